# Optimizing a Trainium2 kernel written in Bass

```python
import math
import jax, jax.numpy as jnp
from jax import lax
import numpy as np

D_MODEL = 1024
BATCH = 32
SEQ = 256
DEPTH = 2
DEC_BATCH = 2
DEC_SEQ = 2048
PAST_LEN = 512

GRID_W = 64
N_AH_LAYERS = (DEPTH + 1) // 2
N_ML_LAYERS = DEPTH // 2
A_HEADS = 4
A_KV_HEADS = 2
A_GROUP = A_HEADS // A_KV_HEADS
A_HEAD_DIM = 128
A_Q = A_HEADS * A_HEAD_DIM
A_KV = A_KV_HEADS * A_HEAD_DIM
Q_BLOCK = 128
ROPE_THETA = 10000.0
HY_CH = D_MODEL // 2
HY_BANDS = 8
HY_EMB = 1 + 2 * HY_BANDS
HY_W = 64
HY_TARGET = 1e-2
HY_FAST_PCT = 0.3
HY_SLOW_PCT = 1.5
HY_MAX_DECAY = math.log(HY_TARGET) / HY_FAST_PCT
HY_MIN_DECAY = math.log(HY_TARGET) / HY_SLOW_PCT
AH_IN = A_Q + 2 * A_KV + 3 * HY_CH
AH_OUT = A_Q + HY_CH
ML_HEADS = 8
ML_HEAD_DIM = D_MODEL // ML_HEADS
ML_W = ML_HEADS * ML_HEAD_DIM
ML_IN = 4 * ML_W + 4 * ML_HEADS
ML_CHUNK = 64
D_FF = 2816
NORM_EPS = 1e-6
NEG_BIG = -1e30

kernel_name = 'hybrid_diffusion_step'


def rms_norm(x, g):
    xf = x.astype(jnp.float32)
    y = xf * lax.rsqrt(jnp.mean(xf * xf, axis=-1, keepdims=True) + NORM_EPS)
    return (y * g.astype(jnp.float32)).astype(x.dtype)


def modulate(x, g, shift, scale):
    return rms_norm(x, g) * (1.0 + scale) + shift


def dwconv3(x, w, b):
    L = x.shape[1]
    xp = jnp.pad(x, ((0, 0), (1, 1), (0, 0)))
    return xp[:, :L] * w[0] + xp[:, 1:L + 1] * w[1] + xp[:, 2:] * w[2] + b


def axial_rope(L):
    rows = L // GRID_W
    row = jnp.repeat(jnp.arange(rows, dtype=jnp.float32), GRID_W)
    col = jnp.tile(jnp.arange(GRID_W, dtype=jnp.float32), rows)
    n_freq = A_HEAD_DIM // 4
    inv = ROPE_THETA ** (-jnp.arange(n_freq, dtype=jnp.float32) / n_freq)
    ang = jnp.concatenate([row[:, None] * inv, col[:, None] * inv], axis=-1)
    return jnp.cos(ang), jnp.sin(ang)


def apply_rope(x, cos, sin):
    half = x.shape[-1] // 2
    xf = x.astype(jnp.float32)
    x1, x2 = xf[..., :half], xf[..., half:]
    c, s = cos[None, :, None, :], sin[None, :, None, :]
    return jnp.concatenate([x1 * c - x2 * s, x1 * s + x2 * c], axis=-1).astype(x.dtype)


def block_attention(q, k, v):
    B, Lq = q.shape[0], q.shape[1]
    nb = Lq // Q_BLOCK
    qb = q.astype(jnp.float32).reshape(B, nb, Q_BLOCK, A_KV_HEADS, A_GROUP, A_HEAD_DIM)
    qb = jnp.moveaxis(qb, 1, 0)
    kf, vf = k.astype(jnp.float32), v.astype(jnp.float32)
    scale = A_HEAD_DIM ** -0.5

    def one_block(qi):
        s = jnp.einsum('bqkgd,bskd->bkgqs', qi, kf) * scale
        p = jax.nn.softmax(s, axis=-1)
        return jnp.einsum('bkgqs,bskd->bqkgd', p, vf)

    o = lax.map(one_block, qb)
    return jnp.moveaxis(o, 0, 1).reshape(B, Lq, A_Q).astype(q.dtype)


def hyena_filters(L, w1, b1, w2, b2, w3, b3, sin_freq):
    t = jnp.linspace(0.0, 1.0, L, dtype=jnp.float32)
    bands = jnp.arange(1, HY_BANDS + 1, dtype=jnp.float32)
    ang = 2.0 * jnp.pi * t[:, None] * bands
    z = jnp.concatenate([t[:, None], jnp.cos(ang), jnp.sin(ang)], axis=-1)
    h = jnp.sin(sin_freq[0] * (z @ w1 + b1))
    h = jnp.sin(sin_freq[1] * (h @ w2 + b2))
    filt = (h @ w3 + b3).astype(jnp.float32)
    deltas = jnp.abs(jnp.linspace(HY_MIN_DECAY, HY_MAX_DECAY, HY_CH, dtype=jnp.float32))
    window = jnp.exp(-t[:, None] * deltas)
    filt = filt.reshape(L, 2, HY_CH) * window[:, None, :]
    return filt[:, 0], filt[:, 1]


def long_conv(v, h):
    L = v.shape[1]
    n = 2 * L
    V = jnp.fft.rfft(v, n=n, axis=1)
    Hf = jnp.fft.rfft(h, n=n, axis=0)
    return jnp.fft.irfft(V * Hf[None], n=n, axis=1)[:, :L]


def hyena_mixer(u, conv_w, conv_b, w1, b1, w2, b2, w3, b3, sin_freq, skip):
    L = u.shape[1]
    uc = dwconv3(u, conv_w, conv_b)
    x0, x1, v = uc[..., :HY_CH], uc[..., HY_CH:2 * HY_CH], uc[..., 2 * HY_CH:]
    v = (v * x1).astype(jnp.float32)
    h_f, h_b = hyena_filters(L, w1, b1, w2, b2, w3, b3, sin_freq)
    y = long_conv(v, h_f) + jnp.flip(long_conv(jnp.flip(v, axis=1), h_b), axis=1) + skip * v
    return (y * x0).astype(u.dtype)


def attn_hyena_mixer(x, lat, w_in, w_out, q_norm, k_norm, hy):
    B, L, _ = x.shape
    proj = x @ w_in
    q = rms_norm(proj[..., :A_Q].reshape(B, L, A_HEADS, A_HEAD_DIM), q_norm)
    k = rms_norm(proj[..., A_Q:A_Q + A_KV].reshape(B, L, A_KV_HEADS, A_HEAD_DIM), k_norm)
    v = proj[..., A_Q + A_KV:A_Q + 2 * A_KV].reshape(B, L, A_KV_HEADS, A_HEAD_DIM)
    u = proj[..., A_Q + 2 * A_KV:]
    if lat is None:
        k_all, v_all = k, v
    else:
        (cos, sin), ctx_k, ctx_v = lat
        q = apply_rope(q, cos, sin)
        k_all = jnp.concatenate([apply_rope(k, cos, sin), ctx_k.astype(k.dtype)], axis=1)
        v_all = jnp.concatenate([v, ctx_v.astype(v.dtype)], axis=1)
    attn = block_attention(q, k_all, v_all)
    hyo = hyena_mixer(u, *hy)
    out = jnp.concatenate([attn, hyo], axis=-1) @ w_out
    return out, k, v


def mlstm_chunked(q, k, v, ig, fg, C0, n0, m0):
    B, H, L, dh = q.shape
    nc = L // ML_CHUNK

    def chunks(a):
        return jnp.moveaxis(a.reshape(B, H, nc, ML_CHUNK, *a.shape[3:]), 2, 0)

    lower = jnp.tril(jnp.ones((ML_CHUNK, ML_CHUNK), dtype=bool))

    def step(carry, inp):
        C, n, m = carry
        qc, kc, vc, ic, lfc = inp
        b = jnp.cumsum(lfc, axis=-1)
        dmat = jnp.where(lower, b[..., :, None] - b[..., None, :] + ic[..., None, :], NEG_BIG)
        inter = b + m[..., None]
        mt = jnp.maximum(inter, jnp.max(dmat, axis=-1))
        w_intra = jnp.exp(dmat - mt[..., None])
        w_inter = jnp.exp(inter - mt)
        s = jnp.einsum('bhtd,bhsd->bhts', qc, kc) * w_intra
        num = w_inter[..., None] * jnp.einsum('bhvk,bhtk->bhtv', C, qc) + jnp.einsum('bhts,bhsv->bhtv', s, vc)
        den = w_inter * jnp.einsum('bhk,bhtk->bht', n, qc) + jnp.sum(s, axis=-1)
        h = num / jnp.maximum(jnp.abs(den), jnp.exp(-mt))[..., None]
        m_new = mt[..., -1]
        w_state = jnp.exp(b[..., -1] + m - m_new)
        w_tok = jnp.exp(b[..., -1:] - b + ic - m_new[..., None])
        C_new = w_state[..., None, None] * C + jnp.einsum('bhs,bhsv,bhsk->bhvk', w_tok, vc, kc)
        n_new = w_state[..., None] * n + jnp.einsum('bhs,bhsk->bhk', w_tok, kc)
        return (C_new, n_new, m_new), h

    (C, n, m), h = lax.scan(step, (C0, n0, m0),
                            (chunks(q), chunks(k), chunks(v), chunks(ig), chunks(jax.nn.log_sigmoid(fg))))
    h = jnp.moveaxis(h, 0, 2).reshape(B, H, L, dh)
    return h, (C, n, m)


def mlstm_mixer(x, C0, n0, m0, w_in, b_gates, conv_w, conv_b, head_norm, w_out):
    B, L, _ = x.shape
    proj = x @ w_in
    qk = jax.nn.silu(dwconv3(proj[..., :2 * ML_W], conv_w, conv_b))
    v = proj[..., 2 * ML_W:3 * ML_W]
    o = proj[..., 3 * ML_W:4 * ML_W]
    gates = (proj[..., 4 * ML_W:] + b_gates).astype(jnp.float32)
    gates = gates.reshape(B, L, 4, ML_HEADS).transpose(2, 0, 3, 1)

    def heads(a):
        return a.reshape(B, L, ML_HEADS, ML_HEAD_DIM).transpose(0, 2, 1, 3).astype(jnp.float32)

    q = heads(qk[..., :ML_W])
    k = heads(qk[..., ML_W:]) * (ML_HEAD_DIM ** -0.5)
    vh = heads(v)
    C0, n0, m0 = C0.astype(jnp.float32), n0.astype(jnp.float32), m0.astype(jnp.float32)
    h_f, (Cf, nf, mf) = mlstm_chunked(q, k, vh, gates[0], gates[2], C0[:, 0], n0[:, 0], m0[:, 0])

    def rev(a):
        return jnp.flip(a, axis=2)

    h_b, (Cb, nb, mb) = mlstm_chunked(rev(q), rev(k), rev(vh), rev(gates[1]), rev(gates[3]),
                                      C0[:, 1], n0[:, 1], m0[:, 1])
    h = (h_f + rev(h_b)).transpose(0, 2, 1, 3)
    h = rms_norm(h, head_norm.reshape(ML_HEADS, ML_HEAD_DIM)).reshape(B, L, ML_W).astype(x.dtype)
    out = (h * jax.nn.sigmoid(o)) @ w_out
    state = (jnp.stack([Cf, Cb], axis=1), jnp.stack([nf, nb], axis=1), jnp.stack([mf, mb], axis=1))
    return out, state


def conv_ffn(x, w_up, conv_w, conv_b, w_down):
    h = dwconv3(x @ w_up, conv_w, conv_b)
    return (jax.nn.gelu(h[..., :D_FF], approximate=False) * h[..., D_FF:]) @ w_down


def setup_inputs(seed: int = 0) -> dict:
    key = jax.random.key(seed)
    keys = iter(jax.random.split(key, 64))

    def nrm(shape, scale):
        return scale * jax.random.normal(next(keys), shape, jnp.float32)

    def gain(shape):
        return 1.0 + nrm(shape, 0.05)

    LA, LM = N_AH_LAYERS, N_ML_LAYERS
    inp = {}
    inp['x_prompt'] = nrm((BATCH, SEQ, D_MODEL), 1.0)
    inp['x_sample'] = nrm((DEC_BATCH, DEC_SEQ, D_MODEL), 1.0)
    inp['cache_attn_k'] = nrm((DEC_BATCH, LA, PAST_LEN, A_KV_HEADS, A_HEAD_DIM), 1.0)
    inp['cache_attn_v'] = nrm((DEC_BATCH, LA, PAST_LEN, A_KV_HEADS, A_HEAD_DIM), 1.0)
    inp['state_mlstm_C'] = nrm((DEC_BATCH, LM, 2, ML_HEADS, ML_HEAD_DIM, ML_HEAD_DIM), 0.3)
    inp['state_mlstm_n'] = nrm((DEC_BATCH, LM, 2, ML_HEADS, ML_HEAD_DIM), 0.3)
    inp['state_mlstm_m'] = jax.random.uniform(next(keys), (DEC_BATCH, LM, 2, ML_HEADS), jnp.float32, 0.0, 3.0)
    inp['c'] = nrm((DEC_BATCH, D_MODEL), 1.0)
    inp['c_ctx'] = nrm((D_MODEL,), 1.0)
    inp['w_mod'] = nrm((DEPTH, D_MODEL, 6 * D_MODEL), 0.5 * D_MODEL ** -0.5)
    inp['b_mod'] = nrm((DEPTH, 6 * D_MODEL), 0.1)
    inp['norm_mix_pre'] = gain((DEPTH, D_MODEL))
    inp['norm_mix_post'] = gain((DEPTH, D_MODEL))
    inp['norm_ffn_pre'] = gain((DEPTH, D_MODEL))
    inp['norm_ffn_post'] = gain((DEPTH, D_MODEL))
    inp['ffn_w_up'] = nrm((DEPTH, D_MODEL, 2 * D_FF), D_MODEL ** -0.5)
    inp['ffn_conv_w'] = nrm((DEPTH, 3, 2 * D_FF), 0.5)
    inp['ffn_conv_b'] = nrm((DEPTH, 2 * D_FF), 0.02)
    inp['ffn_w_down'] = nrm((DEPTH, D_FF, D_MODEL), D_FF ** -0.5)
    inp['ah_w_in'] = nrm((LA, D_MODEL, AH_IN), D_MODEL ** -0.5)
    inp['ah_w_out'] = nrm((LA, AH_OUT, D_MODEL), AH_OUT ** -0.5)
    inp['attn_q_norm'] = gain((LA, A_HEAD_DIM))
    inp['attn_k_norm'] = gain((LA, A_HEAD_DIM))
    inp['hy_conv_w'] = nrm((LA, 3, 3 * HY_CH), 0.5)
    inp['hy_conv_b'] = nrm((LA, 3 * HY_CH), 0.02)
    inp['hy_w1'] = nrm((LA, HY_EMB, HY_W), HY_EMB ** -0.5)
    inp['hy_b1'] = nrm((LA, HY_W), 0.1)
    inp['hy_w2'] = nrm((LA, HY_W, HY_W), HY_W ** -0.5)
    inp['hy_b2'] = nrm((LA, HY_W), 0.1)
    inp['hy_w3'] = nrm((LA, HY_W, 2 * HY_CH), 0.1 * HY_W ** -0.5)
    inp['hy_b3'] = nrm((LA, 2 * HY_CH), 0.01)
    inp['hy_sin_freq'] = gain((LA, 2, HY_W))
    inp['hy_skip'] = nrm((LA, HY_CH), 0.5)
    inp['ml_w_in'] = nrm((LM, D_MODEL, ML_IN), D_MODEL ** -0.5)
    b_i = nrm((LM, 2 * ML_HEADS), 0.1)
    b_f = jnp.tile(jnp.linspace(3.0, 6.0, ML_HEADS, dtype=jnp.float32), 2)[None] + nrm((LM, 2 * ML_HEADS), 0.1)
    inp['ml_b_gates'] = jnp.concatenate([b_i, b_f], axis=-1)
    inp['ml_conv_w'] = nrm((LM, 3, 2 * ML_W), 0.5)
    inp['ml_conv_b'] = nrm((LM, 2 * ML_W), 0.02)
    inp['ml_head_norm'] = gain((LM, ML_W))
    inp['ml_w_out'] = nrm((LM, ML_W, D_MODEL), ML_W ** -0.5)
    return inp


def reference(x_prompt, x_sample, cache_attn_k, cache_attn_v, state_mlstm_C, state_mlstm_n, state_mlstm_m,
              c, c_ctx, w_mod, b_mod, norm_mix_pre, norm_mix_post, norm_ffn_pre, norm_ffn_post,
              ffn_w_up, ffn_conv_w, ffn_conv_b, ffn_w_down,
              ah_w_in, ah_w_out, attn_q_norm, attn_k_norm,
              hy_conv_w, hy_conv_b, hy_w1, hy_b1, hy_w2, hy_b2, hy_w3, hy_b3, hy_sin_freq, hy_skip,
              ml_w_in, ml_b_gates, ml_conv_w, ml_conv_b, ml_head_norm, ml_w_out):
    xp, xs = x_prompt, x_sample
    B = xp.shape[0]
    rope = axial_rope(xs.shape[1])
    ks, vs, Cs, ns, ms = [], [], [], [], []
    for l in range(DEPTH):
        j = l // 2
        mod_p = jnp.split(jax.nn.silu(c_ctx) @ w_mod[l] + b_mod[l], 6, axis=-1)
        mod_s = [m[:, None, :] for m in jnp.split(jax.nn.silu(c) @ w_mod[l] + b_mod[l], 6, axis=-1)]
        hp = modulate(xp, norm_mix_pre[l], mod_p[0], mod_p[1])
        hs = modulate(xs, norm_mix_pre[l], mod_s[0], mod_s[1])
        if l % 2 == 0:
            hy = (hy_conv_w[j], hy_conv_b[j], hy_w1[j], hy_b1[j], hy_w2[j], hy_b2[j],
                  hy_w3[j], hy_b3[j], hy_sin_freq[j], hy_skip[j])
            out_p, k_ctx, v_ctx = attn_hyena_mixer(hp, None, ah_w_in[j], ah_w_out[j],
                                                   attn_q_norm[j], attn_k_norm[j], hy)
            out_s, _, _ = attn_hyena_mixer(hs, (rope, cache_attn_k[:, j], cache_attn_v[:, j]),
                                           ah_w_in[j], ah_w_out[j], attn_q_norm[j], attn_k_norm[j], hy)
            ks.append(k_ctx)
            vs.append(v_ctx)
        else:
            zC = jnp.zeros((B, 2, ML_HEADS, ML_HEAD_DIM, ML_HEAD_DIM), jnp.float32)
            zn = jnp.zeros((B, 2, ML_HEADS, ML_HEAD_DIM), jnp.float32)
            zm = jnp.zeros((B, 2, ML_HEADS), jnp.float32)
            out_p, st = mlstm_mixer(hp, zC, zn, zm, ml_w_in[j], ml_b_gates[j], ml_conv_w[j], ml_conv_b[j],
                                    ml_head_norm[j], ml_w_out[j])
            out_s, _ = mlstm_mixer(hs, state_mlstm_C[:, j], state_mlstm_n[:, j], state_mlstm_m[:, j],
                                   ml_w_in[j], ml_b_gates[j], ml_conv_w[j], ml_conv_b[j],
                                   ml_head_norm[j], ml_w_out[j])
            Cs.append(st[0].astype(xp.dtype))
            ns.append(st[1].astype(xp.dtype))
            ms.append(st[2].astype(xp.dtype))
        xp = xp + mod_p[2] * rms_norm(out_p, norm_mix_post[l])
        xs = xs + mod_s[2] * rms_norm(out_s, norm_mix_post[l])
        hp = modulate(xp, norm_ffn_pre[l], mod_p[3], mod_p[4])
        hs = modulate(xs, norm_ffn_pre[l], mod_s[3], mod_s[4])
        xp = xp + mod_p[5] * rms_norm(conv_ffn(hp, ffn_w_up[l], ffn_conv_w[l], ffn_conv_b[l], ffn_w_down[l]),
                                      norm_ffn_post[l])
        xs = xs + mod_s[5] * rms_norm(conv_ffn(hs, ffn_w_up[l], ffn_conv_w[l], ffn_conv_b[l], ffn_w_down[l]),
                                      norm_ffn_post[l])
    new_attn_k = jnp.stack(ks, axis=1)
    new_attn_v = jnp.stack(vs, axis=1)
    new_mlstm_C = jnp.stack(Cs, axis=1)
    new_mlstm_n = jnp.stack(ns, axis=1)
    new_mlstm_m = jnp.stack(ms, axis=1)
    return (xp, xs, new_attn_k, new_attn_v, new_mlstm_C, new_mlstm_n, new_mlstm_m)
```

```python
import math
import os
import numpy as np
import concourse.bass as bass
import concourse.mybir as mybir
from concourse.bass_utils import run_bass_kernel_spmd

AF = mybir.ActivationFunctionType
ALU = mybir.AluOpType
AX = mybir.AxisListType
F32 = mybir.dt.float32
BF16 = mybir.dt.bfloat16

D = 1024
NCORES = 8
SEQ = 256
NPSEQ = 4
DEC_SEQ = 2048
PAST = 512
DFF = 2816
EPS = 1e-6
HY_MAX_DECAY = math.log(1e-2) / 0.3
HY_MIN_DECAY = math.log(1e-2) / 1.5

SERIAL = bool(int(os.environ.get('KSERIAL', '0')))
SELF_SYNC = True


class Buf:
    __slots__ = ("name", "last_w", "readers", "excl")

    def __init__(self, name, excl=False):
        self.name = name
        self.last_w = None
        self.readers = []
        self.excl = excl


class Emit:
    ENG = ("pe", "act", "dve", "pool", "sp")

    def __init__(self, nc, n_dma_sems=40):
        self.nc = nc
        self.h = {"pe": nc.tensor, "act": nc.scalar, "dve": nc.vector, "pool": nc.gpsimd, "sp": nc.sync}
        self.ops = {e: [] for e in self.ENG}
        self.cnt = {e: 0 for e in self.ENG}
        self.seen = {e: {} for e in self.ENG}
        self.esem = {}
        self.dsem = []
        self.dval = []
        self.n_dma_sems = n_dma_sems
        self.dnext = 0
        self.sem_ctx = []
        self.dma_tokens = []

    def open_sems(self, stack):
        for e in self.ENG:
            self.esem[e] = stack.enter_context(self.nc.semaphore("c_" + e))
        for i in range(self.n_dma_sems):
            self.dsem.append(stack.enter_context(self.nc.semaphore("d%d" % i)))
            self.dval.append(0)

    def _deps(self, reads, writes):
        deps = []
        for b in reads:
            if b.last_w is not None:
                deps.append(b.last_w)
        for b in writes:
            if b.last_w is not None:
                deps.append(b.last_w)
            deps.extend(b.readers)
        return deps

    def _waits(self, eng, deps, pe_accum=False):
        need = {}
        for d in deps:
            kind, src, val = d
            if kind == "e" and src == eng and (not SELF_SYNC or (eng == "pe" and pe_accum)):
                continue
            key = (kind, src)
            if self.seen[eng].get(key, 0) >= val:
                continue
            if need.get(key, 0) < val:
                need[key] = val
        waits = []
        for (kind, src), val in need.items():
            self.seen[eng][(kind, src)] = val
            sem = self.esem[src] if kind == "e" else self.dsem[src]
            waits.append((sem, val))
        return waits

    def _commit(self, tok, reads, writes):
        for b in writes:
            b.last_w = tok
            b.readers = []
        for b in reads:
            if b in writes:
                continue
            if tok[0] == "e":
                b.readers = [r for r in b.readers if not (r[0] == "e" and r[1] == tok[1])]
            b.readers.append(tok)

    def op(self, eng, fn, reads=(), writes=(), pe_accum=False):
        ex = [b for b in reads if b.excl and b not in writes]
        if ex:
            reads = [b for b in reads if not b.excl]
            writes = list(writes) + ex
        deps = self._deps(reads, writes)
        if SERIAL:
            for e in self.ENG:
                if self.cnt[e] > 0 and not (e == eng and pe_accum):
                    deps.append(("e", e, self.cnt[e]))
            for i in range(self.n_dma_sems):
                if self.dval[i] > 0:
                    deps.append(("d", i, self.dval[i]))
        waits = self._waits(eng, deps, pe_accum)
        self.cnt[eng] += 1
        tok = ("e", eng, self.cnt[eng])
        h = self.h[eng]
        for s_, v_ in waits:
            h.wait_ge(s_, v_)
        fn(h).then_inc(self.esem[eng], 1)
        self._commit(tok, reads, writes)
        return tok

    def dma(self, eng, out, in_, reads=(), writes=(), **kw):
        deps = self._deps(reads, writes)
        i = self.dnext
        self.dnext = (self.dnext + 1) % self.n_dma_sems
        if self.dval[i] > 0:
            deps.append(("d", i, self.dval[i]))
        waits = self._waits(eng, deps)
        self.dval[i] += 16
        tok = ("d", i, self.dval[i])
        h = self.h[eng]
        for s_, v_ in waits:
            h.wait_ge(s_, v_)
        h.dma_start(out=out, in_=in_, **kw).then_inc(self.dsem[i], 16)
        self._commit(tok, reads, writes)
        return tok

    def finish(self):
        h = self.h["sp"]
        for i in range(self.n_dma_sems):
            if self.dval[i] > 0:
                h.wait_ge(self.dsem[i], self.dval[i])
        for e in self.ENG:
            if e != "sp" and self.cnt[e] > 0:
                h.wait_ge(self.esem[e], self.cnt[e])


class Arena:
    def __init__(self, nc, lo, hi):
        self.nc, self.lo, self.hi, self.p = nc, lo, hi, lo
        self.n = 0
        self.live = []

    def alloc(self, name, shape, dtype, nbufs=1, top=False):
        esz = 4 if dtype == F32 else 2
        per = 1
        for s in shape[1:]:
            per *= s
        nbytes = (per * esz + 31) // 32 * 32
        assert self.p + nbytes <= self.hi, "SBUF arena overflow at %s (%d + %d > %d)" % (name, self.p, nbytes, self.hi)
        if top:
            self.hi -= nbytes
            off = self.hi
        else:
            off = self.p
            self.p += nbytes
        self.n += 1
        t = self.nc.alloc_sbuf_tensor_at("%s_%d" % (name, self.n), list(shape), dtype, offset=off)
        bufs = [Buf("%s.%d" % (name, i)) for i in range(nbufs)]
        keep = []
        for (l, h, bs) in self.live:
            if l < off + nbytes and off < h:
                for ob in bs:
                    for nb in bufs:
                        if ob.last_w is not None:
                            nb.readers.append(ob.last_w)
                        nb.readers.extend(ob.readers)
                if l < off or h > off + nbytes:
                    keep.append((l, h, bs))
            else:
                keep.append((l, h, bs))
        keep.append((off, off + nbytes, bufs))
        self.live = keep
        ap = t.ap()
        return (ap, bufs[0]) if nbufs == 1 else (ap, bufs)

    def mark_top(self):
        return self.hi

    def release_top(self, m):
        self.hi = m

    def mark(self):
        return self.p

    def release(self, m):
        self.p = m


def _bf16(a):
    import ml_dtypes
    return np.ascontiguousarray(a.astype(np.float32)).astype(ml_dtypes.bfloat16)


def dft_tables(L):
    n = 2 * L
    s = np.arange(L, dtype=np.float64)
    fr = np.arange(L + 128, dtype=np.float64)
    fi = np.arange(L, dtype=np.float64)
    FR = np.cos(2 * np.pi * np.outer(s, fr) / n)
    FR[:, L + 1:] = 0.0
    FI = -np.sin(2 * np.pi * np.outer(s, fi) / n)
    cf = np.full(L + 128, 2.0)
    cf[0] = 1.0
    cf[L] = 1.0
    cf[L + 1:] = 0.0
    IR = (cf[:, None] / n) * np.cos(2 * np.pi * np.outer(fr, s) / n)
    II = -(2.0 / n) * np.sin(2 * np.pi * np.outer(fi, s) / n)
    fwd = np.concatenate([FR, FI], axis=1)
    inv = np.concatenate([IR, II], axis=0)
    nsc = L // 128
    nfc = (2 * L + 128) // 128
    fwd_l = fwd.reshape(nsc, 128, nfc * 128).transpose(1, 0, 2)
    inv_l = inv.reshape(nfc, 128, L).transpose(1, 0, 2)
    return _bf16(fwd_l.reshape(128, -1)), _bf16(inv_l.reshape(128, -1))


def hyena_pos_tables(L):
    t = np.linspace(0.0, 1.0, L, dtype=np.float32)
    bands = np.arange(1, 9, dtype=np.float32)
    ang = 2.0 * np.pi * t[:, None] * bands
    z = np.concatenate([t[:, None], np.cos(ang), np.sin(ang)], axis=-1).astype(np.float32)
    zT = np.ascontiguousarray(z.T)
    tcol = np.ascontiguousarray(t.reshape(L // 128, 128).T)
    return zT, tcol


def rope_tables(L):
    rows = L // 64
    row = np.repeat(np.arange(rows, dtype=np.float32), 64)
    col = np.tile(np.arange(64, dtype=np.float32), rows)
    n_freq = 32
    inv = (10000.0 ** (-np.arange(n_freq, dtype=np.float32) / n_freq)).astype(np.float32)
    ang = np.concatenate([row[:, None] * inv, col[:, None] * inv], axis=-1)
    cos, sin = np.cos(ang).astype(np.float32), np.sin(ang).astype(np.float32)
    cosT = np.concatenate([cos.T, cos.T], axis=0)
    sinT = np.concatenate([-sin.T, sin.T], axis=0)
    return np.ascontiguousarray(cosT), np.ascontiguousarray(sinT)


class Prog:
    def __init__(self, stage):
        self.stage = stage
        self.nc = bass.Bass("TRN2", target_bir_lowering=False)
        self.ins = {}
        self.outs = {}

    def din(self, name, shape, dtype=F32):
        t = self.nc.dram_tensor(name, list(shape), dtype, kind="ExternalInput")
        self.ins[name] = t
        return t.ap()

    def dout(self, name, shape, dtype=F32):
        t = self.nc.dram_tensor(name, list(shape), dtype, kind="ExternalOutput")
        self.outs[name] = t
        return t.ap()


TWO_PI = 2.0 * math.pi
MAGIC = 12582912.0


def build_program(stage=99):
    P = Prog(stage)
    nc = P.nc
    TP = NPSEQ * SEQ
    TS = DEC_SEQ

    xp_d = P.din("xp", [128, 8 * TP])
    xs_d = P.din("xs", [128, 8 * TS])
    cvec_d = P.din("cvec", [128, 8 * 2])
    wmod_d = P.din("w_mod", [2, 128, 8 * 6144])
    bmod_d = P.din("b_mod", [128, 2 * 48])
    norms_d = P.din("norms", [128, 4 * 2 * 8])
    ahwin_d = P.din("ah_w_in", [128, 8 * 2560])
    ahwout_d = P.din("ah_w_out", [128, 8 * 1024])
    qkn_d = P.din("qk_norm", [128, 2])
    idb_d = P.din("ident_bf", [128, 128], BF16)
    ones_d = P.din("ones_bf", [128, 128], BF16)
    swap_d = P.din("swap_bf", [128, 128], BF16)
    hcw_d = P.din("hy_conv", [128, 12 * 4])
    hyw1_d = P.din("hy_w1", [17, 64])
    hyw2_d = P.din("hy_w2", [64, 64])
    hyw3_d = P.din("hy_w3", [64, 1024])
    hyb12_d = P.din("hy_b12f", [64, 4])
    hyb3_d = P.din("hy_b3", [128, 1024])
    hyskip_d = P.din("hy_skip", [128, 512])
    hydelta_d = P.din("hy_delta", [128, 512])
    ffnup_d = P.din("ffn_w_up", [2, 128, 8 * 5632])
    ffndn_d = P.din("ffn_w_down", [2, 128, 22 * 1024])
    ffncw_d = P.din("ffn_conv", [128, 2 * 44 * 4])
    tabs = {}
    for L in (SEQ, DEC_SEQ):
        nsc, nfc = L // 128, (2 * L + 128) // 128
        tabs[L] = dict(
            fwd=P.din("dft_fwd_%d" % L, [nfc, 128, nsc * 128], BF16),
            inv=P.din("dft_inv_%d" % L, [max(1, L // 512), 128, nfc * min(L, 512)], BF16),
            zT=P.din("hy_zT_%d" % L, [17, L]),
            tcol=P.din("hy_tcol_%d" % L, [128, L // 128]))
    mlwin_d = P.din("ml_w_in", [128, 8 * 4096])
    mlwg_d = P.din("ml_w_g", [128, 8 * 80])
    mlbg_d = P.din("ml_b_g", [40, 2])
    mlcw_d = P.din("ml_conv", [128, 16 * 4])
    mlhn_d = P.din("ml_head_norm", [128, 8])
    mlwout_d = P.din("ml_w_out", [128, 8 * 1024])
    sel_d = P.din("sel40", [40, 16 * 128])
    masks_d = P.din("masks", [128, 2 * 4 * 512], BF16)
    id32_d = P.din("ident_f32", [128, 128])
    m0_d = P.din("ml_m0", [40, 1])
    c0t_d = P.din("ml_c0t", [128, 16 * 128])
    n0b_d = P.din("ml_n0b", [128, 16 * 128])
    ropec_d = P.din("rope_cos", [128, TS])
    ropes_d = P.din("rope_sin", [128, TS])
    ckT_d = P.din("cache_kT", [128, 2 * PAST])
    cvt_d = P.din("cache_vt", [128, (PAST // 128) * 256])

    kout_d = P.dout("k_out", [128, 2 * TP])
    vout_d = P.dout("v_out", [128, (TP // 128) * 256])
    cst_d = P.dout("c_out", [128, NPSEQ * 16 * 128])
    nst_d = P.dout("n_out", [1, NPSEQ * 16 * 128])
    mst_d = P.dout("m_out", [40, NPSEQ])
    yp_d = P.dout("yp", [128, 8 * TP])
    ys_d = P.dout("ys", [128, 8 * TS])

    from contextlib import ExitStack
    with ExitStack() as stack:
        E = Emit(nc)
        E.open_sems(stack)
        A = Arena(nc, 16640, 229376 - 1024)
        ps_t = nc.alloc_psum_tensor("ps_all", [128, 7 * 512], F32)
        PS = ps_t.ap()
        PSB = [Buf("ps%d" % i, excl=True) for i in range(7)]
        psbf_t = nc.alloc_psum_tensor("ps_bf", [128, 1024], BF16)
        PSbf = {4: psbf_t.ap()[:, 0:128], 5: psbf_t.ap()[:, 128:256]}
        _pb = Buf("psbf", excl=True)
        PSbf_b = {4: _pb, 5: _pb}

        def bank(i, n=512, off=0):
            return PS[:, i * 512 + off: i * 512 + off + n]

        def mm_group(out_ap, out_bufs, terms):
            n = len(terms)
            for i, (l_ap, r_ap, rb) in enumerate(terms):
                E.op("pe", lambda h: h.matmul(out_ap, lhsT=l_ap, rhs=r_ap, start=(i == 0), stop=(i == n - 1)),
                     reads=rb, writes=out_bufs, pe_accum=True)

        dram_bufs = {}

        def dscratch(name, shape, dtype=F32):
            t = nc.dram_tensor(name, list(shape), dtype)
            b = Buf(name)
            dram_bufs[name] = b
            return t.ap(), b

        def const(name, shape, dtype, src, eng="sp"):
            ap, b = A.alloc(name, shape, dtype)
            E.dma(eng, ap, src, writes=[b])
            return ap, b

        ident, ident_b = const("ident", [128, 128], BF16, idb_d)
        ones, ones_b = const("ones", [128, 128], BF16, ones_d)
        swp, swp_b = const("swap", [128, 128], BF16, swap_d)
        epsc, epsc_b = A.alloc("epsc", [128, 2], F32)
        E.op("dve", lambda h: h.memset(epsc, EPS), writes=[epsc_b])
        cv, cv_b = const("cvec", [128, 8, 2], F32, cvec_d.rearrange("p (j v) -> p j v", v=2))
        bm, bm_b = const("bmod", [128, 2, 48], F32, bmod_d.rearrange("p (l m) -> p l m", l=2))
        nrm, nrm_b = const("norms", [128, 4, 2, 8], F32, norms_d.rearrange("p (w l j) -> p w l j", w=4, l=2))
        qkn, qkn_b = const("qkn", [128, 2], F32, qkn_d)
        hcw, hcw_b = const("hcw", [128, 12, 4], F32, hcw_d.rearrange("p (c k) -> p c k", k=4))
        fcw, fcw_b = const("fcw", [128, 2, 44, 4], F32, ffncw_d.rearrange("p (l c k) -> p l c k", l=2, k=4))

        mcw, mcw_b = const("mcw", [128, 16, 4], F32, mlcw_d.rearrange("p (c k) -> p c k", k=4))
        mhn, mhn_b = const("mhn", [128, 8], F32, mlhn_d)
        mbg, mbg_b = const("mbg", [40, 2], F32, mlbg_d)
        sel, sel_b = const("sel", [40, 16, 128], F32, sel_d.rearrange("p (r m) -> p r m", m=128))
        msk, msk_b = const("msk", [128, 2, 4, 512], BF16, masks_d.rearrange("p (d o t) -> p d o t", d=2, o=4))
        id32, id32_b = const("id32", [128, 128], F32, id32_d)
        onec, onec_b = A.alloc("onec", [128, 2], F32)
        E.op("dve", lambda h: h.memset(onec, 1.0), writes=[onec_b])
        sc, sc_b = A.alloc("silu_c", [128, 8, 2], BF16)
        sig, sig_b = A.alloc("sig_c", [128, 8, 2], F32)
        E.op("act", lambda h: h.activation(out=sig, in_=cv, func=AF.Sigmoid), reads=[cv_b], writes=[sig_b])
        E.op("dve", lambda h: h.tensor_tensor(out=sc, in0=cv, in1=sig, op=ALU.mult), reads=[cv_b, sig_b], writes=[sc_b])
        MOD, MOD_b = A.alloc("mod", [128, 2, 48, 2], F32)
        mk = A.mark()
        wm, wm_bs = A.alloc("wmod_st", [128, 2, 8, 512], BF16, nbufs=2)
        it = 0
        for l in range(2):
            for cg in range(12):
                slot = it % 2
                it += 1
                src = wmod_d[l].rearrange("p (kc n) -> p kc n", kc=8)[:, :, cg * 512:(cg + 1) * 512]
                E.dma("pool", wm[:, slot], src, writes=[wm_bs[slot]])
                for mm in range(4):
                    m = cg * 4 + mm
                    mm_group(bank(l, 2, 2 * m), [PSB[l]],
                             [(wm[:, slot, kc, mm * 128:(mm + 1) * 128], sc[:, kc, :], [wm_bs[slot], sc_b])
                              for kc in range(8)])
            E.op("dve", lambda h: h.tensor_tensor(
                out=MOD[:, l], in0=bank(l, 96).rearrange("p (m v) -> p m v", v=2),
                in1=bm[:, l, :].unsqueeze(2).to_broadcast([128, 48, 2]), op=ALU.add),
                reads=[PSB[l], bm_b], writes=[MOD_b])
        A.release(mk)

        COEF, COEF_b = A.alloc("coef", [128, 2, 2, 3, 8, 2], F32)
        for l in range(2):
            for part in range(2):
                sh, scl, gt = 3 * part, 3 * part + 1, 3 * part + 2
                gpre = nrm[:, 2 * part, l, :].unsqueeze(2).to_broadcast([128, 8, 2])
                gpost = nrm[:, 2 * part + 1, l, :].unsqueeze(2).to_broadcast([128, 8, 2])
                E.op("dve", lambda h: h.scalar_tensor_tensor(
                    out=COEF[:, l, part, 0], in0=MOD[:, l, scl * 8:(scl + 1) * 8, :], scalar=1.0, in1=gpre,
                    op0=ALU.add, op1=ALU.mult), reads=[MOD_b, nrm_b], writes=[COEF_b])
                E.op("dve", lambda h: h.tensor_copy(
                    out=COEF[:, l, part, 1], in_=MOD[:, l, sh * 8:(sh + 1) * 8, :]), reads=[MOD_b], writes=[COEF_b])
                E.op("dve", lambda h: h.tensor_tensor(
                    out=COEF[:, l, part, 2], in0=MOD[:, l, gt * 8:(gt + 1) * 8, :], in1=gpost, op=ALU.mult),
                    reads=[MOD_b, nrm_b], writes=[COEF_b])

        def sumsq_rstd(src_chunks, src_bufs, n, dim, rstd, rstd_b, sq, sq_b, psb):
            nch = len(src_chunks)
            for j, s_ in enumerate(src_chunks):
                E.op("act", lambda h: h.activation(out=sq[:, j, :n], in_=s_, func=AF.Square),
                     reads=[src_bufs[j]], writes=[sq_b])
            mm_group(bank(psb, n), [PSB[psb]], [(ones, sq[:, j, :n], [ones_b, sq_b]) for j in range(nch)])
            E.op("act", lambda h: h.activation(out=rstd[:, :n], in_=bank(psb, n), func=AF.Sqrt, bias=epsc[:, 0:1],
                                               scale=1.0 / dim), reads=[PSB[psb], epsc_b], writes=[rstd_b])
            E.op("dve", lambda h: h.reciprocal(out=rstd[:, :n], in_=rstd[:, :n]), reads=[rstd_b], writes=[rstd_b])

        def modulate_block(xb, xb_buf, n, l, part, v, dst_fn, dst_buf, sq, sq_b, rstd, rstd_b, tmp, tmp_b):
            sumsq_rstd([xb[:, j, :n] for j in range(8)], [xb_buf] * 8, n, D, rstd, rstd_b, sq, sq_b, 6)
            for j in range(8):
                E.op("dve", lambda h: h.scalar_tensor_tensor(
                    out=tmp[:, :n], in0=xb[:, j, :n], scalar=COEF[:, l, part, 0, j, v:v + 1], in1=rstd[:, :n],
                    op0=ALU.mult, op1=ALU.mult), reads=[xb_buf, COEF_b, rstd_b], writes=[tmp_b])
                E.op("act", lambda h: h.activation(
                    out=dst_fn(j), in_=tmp[:, :n], func=AF.Identity, bias=COEF[:, l, part, 1, j, v:v + 1], scale=1.0),
                    reads=[tmp_b, COEF_b], writes=[dst_buf])

        def epilogue(O, O_b, n, xsrc, xsrc_b, xdst, xdst_b, tok0, T, l, part, v, nxt, wk):
            (sq, sq_b, rstd, rstd_b, tmp, tmp_b, xb, xb_b) = wk
            E.dma("sp", xb[:, :, :n], xsrc.rearrange("p (j t) -> p j t", j=8)[:, :, tok0:tok0 + n],
                  reads=[xsrc_b], writes=[xb_b])
            sumsq_rstd([O[:, j, :n] for j in range(8)], [O_b] * 8, n, D, rstd, rstd_b, sq, sq_b, 6)
            for j in range(8):
                E.op("dve", lambda h: h.scalar_tensor_tensor(
                    out=tmp[:, :n], in0=O[:, j, :n], scalar=COEF[:, l, part, 2, j, v:v + 1], in1=rstd[:, :n],
                    op0=ALU.mult, op1=ALU.mult), reads=[O_b, COEF_b, rstd_b], writes=[tmp_b])
                E.op("pool", lambda h: h.tensor_tensor(out=xb[:, j, :n], in0=xb[:, j, :n], in1=tmp[:, :n], op=ALU.add),
                     reads=[tmp_b, xb_b], writes=[xb_b])
            E.dma("sp", xdst.rearrange("p (j t) -> p j t", j=8)[:, :, tok0:tok0 + n], xb[:, :, :n],
                  reads=[xb_b], writes=[xdst_b])
            if nxt is not None:
                l2, part2, dst_fn, dst_buf = nxt
                modulate_block(xb, xb_b, n, l2, part2, v, dst_fn, dst_buf, sq, sq_b, rstd, rstd_b, tmp, tmp_b)

        def hyena_filters(L, keep):
            nsc = L // 128
            GR, GR_b = keep.alloc("GR%d" % L, [128, nsc + 1, 512], BF16, top=True)
            GI, GI_b = keep.alloc("GI%d" % L, [128, nsc, 512], BF16, top=True)
            mk_ = A.mark()
            zT, zT_b = const("zT", [17, L], F32, tabs[L]["zT"])
            tcol, tcol_b = const("tcol", [128, nsc], F32, tabs[L]["tcol"])
            w1, w1_b = const("hw1", [17, 64], F32, hyw1_d)
            w2, w2_b = const("hw2", [64, 64], F32, hyw2_d)
            w3, w3_b = const("hw3", [64, 1024], F32, hyw3_d)
            b12, b12_b = const("hb12", [64, 4], F32, hyb12_d)
            b3, b3_b = const("hb3", [128, 1024], F32, hyb3_d)
            skp, skp_b = const("hskip", [128, 512], F32, hyskip_d)
            dlt, dlt_b = const("hdelta", [128, 512], F32, hydelta_d)
            ntc, ntc_b = A.alloc("ntcol", [128, nsc], F32)
            E.op("dve", lambda h: h.tensor_scalar(out=ntc, in0=tcol, scalar1=-1.0, scalar2=None, op0=ALU.mult),
                 reads=[tcol_b], writes=[ntc_b])
            H1, H1_b = A.alloc("h1", [64, L], F32)
            H2, H2_b = A.alloc("h2", [64, L], F32)
            t1, t1_b = A.alloc("ht1", [64, 512], F32)
            t2, t2_b = A.alloc("ht2", [64, 512], F32)

            def sin_layer(dst, dst_b, w_ap, w_b, src, src_b, bcol, fcol):
                for c0 in range(0, L, 512):
                    n = min(512, L - c0)
                    mm_group(PS[0:64, 0:n], [PSB[0]], [(w_ap, src[:, c0:c0 + n], [w_b, src_b])])
                    E.op("dve", lambda h: h.tensor_scalar(out=t1[:, :n], in0=PS[0:64, 0:n], scalar1=b12[:, bcol:bcol + 1],
                                                          scalar2=b12[:, fcol:fcol + 1], op0=ALU.add, op1=ALU.mult),
                         reads=[PSB[0], b12_b], writes=[t1_b])
                    E.op("dve", lambda h: h.tensor_scalar(out=t2[:, :n], in0=t1[:, :n], scalar1=1.0 / TWO_PI, scalar2=MAGIC,
                                                          op0=ALU.mult, op1=ALU.add), reads=[t1_b], writes=[t2_b])
                    E.op("dve", lambda h: h.tensor_scalar(out=t2[:, :n], in0=t2[:, :n], scalar1=MAGIC, scalar2=-TWO_PI,
                                                          op0=ALU.subtract, op1=ALU.mult), reads=[t2_b], writes=[t2_b])
                    E.op("dve", lambda h: h.tensor_tensor(out=t1[:, :n], in0=t1[:, :n], in1=t2[:, :n], op=ALU.add),
                         reads=[t1_b, t2_b], writes=[t1_b])
                    E.op("act", lambda h: h.activation(out=dst[:, c0:c0 + n], in_=t1[:, :n], func=AF.Sin),
                         reads=[t1_b], writes=[dst_b])

            sin_layer(H1, H1_b, w1, w1_b, zT, zT_b, 0, 2)
            sin_layer(H2, H2_b, w2, w2_b, H1, H1_b, 1, 3)
            GS, GS_b = A.alloc("gs", [128, nsc, 512], BF16)
            GD, GD_b = A.alloc("gd", [128, nsc, 512], BF16)
            Fm, Fm_b = A.alloc("fm", [128, 1024], F32)
            win, win_b = A.alloc("win", [128, 512], F32)
            fs, fs_b = A.alloc("fs", [128, 512], F32)
            for tc in range(nsc):
                for hh in range(2):
                    mm_group(bank(hh), [PSB[hh]], [(H2[:, tc * 128:(tc + 1) * 128], w3[:, hh * 512:(hh + 1) * 512],
                                                    [H2_b, w3_b])])
                    E.op("dve", lambda h: h.tensor_tensor(out=Fm[:, hh * 512:(hh + 1) * 512], in0=bank(hh),
                                                          in1=b3[:, hh * 512:(hh + 1) * 512], op=ALU.add),
                         reads=[PSB[hh], b3_b], writes=[Fm_b])
                E.op("act", lambda h: h.activation(out=win, in_=dlt, func=AF.Exp, scale=ntc[:, tc:tc + 1]),
                     reads=[dlt_b, ntc_b], writes=[win_b])
                E.op("dve", lambda h: h.tensor_tensor(out=fs, in0=Fm[:, 0:512], in1=Fm[:, 512:1024], op=ALU.add),
                     reads=[Fm_b], writes=[fs_b])
                E.op("dve", lambda h: h.tensor_tensor(out=GS[:, tc, :], in0=fs, in1=win, op=ALU.mult),
                     reads=[fs_b, win_b], writes=[GS_b])
                E.op("dve", lambda h: h.tensor_tensor(out=fs, in0=Fm[:, 0:512], in1=Fm[:, 512:1024], op=ALU.subtract),
                     reads=[Fm_b], writes=[fs_b])
                E.op("dve", lambda h: h.tensor_tensor(out=GD[:, tc, :], in0=fs, in1=win, op=ALU.mult),
                     reads=[fs_b, win_b], writes=[GD_b])
            fw, fw_bs = A.alloc("fwst", [128, 2, nsc, 128], BF16, nbufs=2)
            nfc = 2 * nsc + 1
            for fc in range(nfc):
                slot = fc % 2
                E.dma("sp", fw[:, slot], tabs[L]["fwd"][fc].rearrange("p (s f) -> p s f", f=128), writes=[fw_bs[slot]])
                src, src_b = (GS, GS_b) if fc <= nsc else (GD, GD_b)
                pb = fc % 2
                mm_group(bank(pb), [PSB[pb]], [(fw[:, slot, s_, :], src[:, s_, :], [fw_bs[slot], src_b]) for s_ in range(nsc)])
                if fc <= nsc:
                    E.op("dve", lambda h: h.tensor_tensor(out=GR[:, fc, :], in0=bank(pb), in1=skp, op=ALU.add),
                         reads=[PSB[pb], skp_b], writes=[GR_b])
                else:
                    E.op("act", lambda h: h.activation(out=GI[:, fc - nsc - 1, :], in_=bank(pb), func=AF.Copy),
                         reads=[PSB[pb]], writes=[GI_b])
            A.release(mk_)
            return GR, GR_b, GI, GI_b

        def run_group(G):
            gname, nseq, L, v = G["name"], G["nseq"], G["L"], G["v"]
            smp = G["sample"]
            T = nseq * L
            nblk = T // 512
            nkv = L + (PAST if smp else 0)
            X0d, X0d_b = G["xin"], Buf(gname + "_xin")
            X1d, X1d_b = dscratch(gname + "_x1", [128, 8 * T])
            X2d, X2d_b = dscratch(gname + "_x2", [128, 8 * T])
            mk_g = A.mark()
            MIX, MIX_bs = A.alloc(gname + "_mix", [128, 8, T], BF16, nbufs=8)
            mk_m = A.mark()
            mk_top = A.mark_top()
            GR, GR_b, GI, GI_b = hyena_filters(L, A)
            HS, HS_bs = A.alloc(gname + "_hs", [128, 8, T], BF16, nbufs=nblk)
            mk_a = A.mark()
            sq, sq_b = A.alloc("sq", [128, 8, 512], BF16)
            rstd, rstd_b = A.alloc("rstd", [128, 512], F32)
            tmp, tmp_b = A.alloc("tmp", [128, 512], F32)
            xb, xb_bs = A.alloc("xb", [128, 2, 8, 512], F32, nbufs=2)
            for blk in range(nblk):
                sl = blk % 2
                E.dma("sp", xb[:, sl], X0d.rearrange("p (j t) -> p j t", j=8)[:, :, blk * 512:(blk + 1) * 512],
                      writes=[xb_bs[sl]])
                modulate_block(xb[:, sl], xb_bs[sl], 512, 0, 0, v,
                               lambda j: HS[:, j, blk * 512:(blk + 1) * 512], HS_bs[blk], sq, sq_b, rstd, rstd_b, tmp, tmp_b)
            A.release(mk_a)
            if stage == 2:
                return

            QT, QT_b = A.alloc("QT", [128, 4, T], BF16)
            KT, KT_b = A.alloc("KT", [128, 2, nseq, nkv], BF16)
            VTb, VTb_b = A.alloc("VTb", [128, nseq * (nkv // 128), 256], BF16)
            mk_q = A.mark()
            wq, wq_b = A.alloc("wq", [128, 8, 1024], BF16)
            E.dma("pool", wq, ahwin_d.rearrange("p (kc n) -> p kc n", kc=8)[:, :, 0:1024], writes=[wq_b])
            sq, sq_b = A.alloc("sq", [128, 1, 512], BF16)
            rstd, rstd_b = A.alloc("rstd", [128, 512], F32)
            qn, qn_b = A.alloc("qn", [128, 512], F32)
            qb16, qb16_b = A.alloc("qb16", [128, 512], BF16)
            r1, r1_b = A.alloc("r1", [128, 512], F32)
            if smp:
                rc, rc_b = const("ropec", [128, TS], F32, ropec_d)
                rs, rs_b = const("ropes", [128, TS], F32, ropes_d)
                E.dma("pool", KT[:, :, 0, L:], ckT_d.rearrange("p (g t) -> p g t", g=2), writes=[KT_b])
                E.dma("pool", VTb[:, L // 128:, :], cvt_d.rearrange("p (c e) -> p c e", e=256), writes=[VTb_b])
            else:
                KN, KN_b = A.alloc("KN", [128, 2, T], F32)
                VT, VT_b = A.alloc("VT", [128, T // 128, 256], F32)
            KSUB = int(os.environ.get("KSUB", "0"))
            if KSUB == 1:
                return
            for hq in range(6):
                if KSUB == 2 and hq >= 4:
                    break
                for blk in range(nblk):
                    ts = slice(blk * 512, (blk + 1) * 512)
                    pb = blk % 2
                    mm_group(bank(pb), [PSB[pb]], [(wq[:, kc, hq * 128:(hq + 1) * 128], HS[:, kc, ts], [wq_b, HS_bs[blk]])
                                                   for kc in range(8)])
                    sumsq_rstd([bank(pb)], [PSB[pb]], 512, 128, rstd, rstd_b, sq, sq_b, 2 + pb)
                    gcol = 0 if hq < 4 else 1
                    if hq < 4:
                        dst = QT[:, hq, ts]
                        dst_b = QT_b
                    else:
                        s_i, t0 = (blk * 512) // L, (blk * 512) % L
                        dst_b = KT_b
                    if not smp:
                        if hq < 4:
                            E.op("dve", lambda h: h.scalar_tensor_tensor(
                                out=dst, in0=bank(pb), scalar=qkn[:, 0:1], in1=rstd, op0=ALU.mult, op1=ALU.mult),
                                reads=[PSB[pb], qkn_b, rstd_b], writes=[dst_b])
                        else:
                            g = hq - 4
                            E.op("dve", lambda h: h.scalar_tensor_tensor(
                                out=KN[:, g, ts], in0=bank(pb), scalar=qkn[:, 1:2], in1=rstd, op0=ALU.mult, op1=ALU.mult),
                                reads=[PSB[pb], qkn_b, rstd_b], writes=[KN_b])
                            nsq = 512 // L
                            E.op("act", lambda h: h.activation(
                                out=KT[:, g, s_i:s_i + nsq, 0:L], in_=KN[:, g, ts].rearrange("p (s t) -> p s t", t=L),
                                func=AF.Copy), reads=[KN_b], writes=[KT_b])
                    else:
                        E.op("dve", lambda h: h.scalar_tensor_tensor(
                            out=qn, in0=bank(pb), scalar=qkn[:, gcol:gcol + 1], in1=rstd, op0=ALU.mult, op1=ALU.mult),
                            reads=[PSB[pb], qkn_b, rstd_b], writes=[qn_b])
                        E.op("act", lambda h: h.activation(out=qb16, in_=qn, func=AF.Copy), reads=[qn_b], writes=[qb16_b])
                        mm_group(bank(4 + pb), [PSB[4 + pb]], [(swp, qb16, [swp_b, qb16_b])])
                        E.op("dve", lambda h: h.tensor_tensor(out=r1, in0=bank(4 + pb), in1=rs[:, ts], op=ALU.mult),
                             reads=[PSB[4 + pb], rs_b], writes=[r1_b])
                        E.op("pool", lambda h: h.tensor_tensor(out=qn, in0=qn, in1=rc[:, ts], op=ALU.mult),
                             reads=[qn_b, rc_b], writes=[qn_b])
                        if hq < 4:
                            d2 = dst
                        else:
                            d2 = KT[:, hq - 4, 0, t0:t0 + 512]
                        E.op("dve", lambda h: h.tensor_tensor(out=d2, in0=qn, in1=r1, op=ALU.add),
                             reads=[qn_b, r1_b], writes=[dst_b])
            if G.get("kout") is not None:
                E.dma("sp", G["kout"].rearrange("p (g t) -> p g t", g=2), KN, reads=[KN_b])
            if KSUB in (2, 3):
                return
            for c in range(T // 128):
                pb = 4 + c % 2
                blk = c // 4
                s_i, cc = (c * 128) // L, ((c * 128) % L) // 128
                mm_group(bank(pb, 256), [PSB[pb]], [(HS[:, kc, c * 128:(c + 1) * 128], wq[:, kc, 768:1024], [wq_b, HS_bs[blk]])
                                                    for kc in range(8)])
                if not smp and KSUB != 5:
                    E.op("dve", lambda h: h.tensor_copy(out=VT[:, c, :], in_=bank(pb, 256)),
                         reads=[PSB[pb]], writes=[VT_b])
                if KSUB != 6:
                    E.op("act", lambda h: h.activation(out=VTb[:, s_i * (nkv // 128) + cc, :], in_=bank(pb, 256), func=AF.Copy),
                         reads=[PSB[pb]], writes=[VTb_b])
            if G.get("vout") is not None:
                E.dma("sp", G["vout"].rearrange("p (c e) -> p c e", e=256), VT, reads=[VT_b])
            A.release(mk_q)
            if stage == 3:
                return
            Pt, Pt_bs = A.alloc("Pt", [128, 2, 512], BF16, nbufs=2)
            rden, rden_b = A.alloc("rden", [128, 512], F32)
            nq = min(512, L)
            nkc = nkv // 128
            att_scale = 1.0 / math.sqrt(128.0)
            for s_i in range(nseq):
                for hd in range(4):
                    g = hd // 2
                    for qb in range(L // nq):
                        q0 = s_i * L + qb * nq
                        for kc in range(nkc):
                            sb = kc % 2
                            mm_group(bank(sb, nq), [PSB[sb]], [(KT[:, g, s_i, kc * 128:(kc + 1) * 128], QT[:, hd, q0:q0 + nq],
                                                                [KT_b, QT_b])])
                            E.op("act", lambda h: h.activation(out=Pt[:, sb, :nq], in_=bank(sb, nq), func=AF.Exp,
                                                               scale=att_scale), reads=[PSB[sb]], writes=[Pt_bs[sb]])
                            E.op("pe", lambda h: h.matmul(bank(2, nq), lhsT=VTb[:, s_i * nkc + kc, g * 128:(g + 1) * 128],
                                                          rhs=Pt[:, sb, :nq], start=(kc == 0), stop=(kc == nkc - 1)),
                                 reads=[VTb_b, Pt_bs[sb]], writes=[PSB[2]], pe_accum=True)
                            E.op("pe", lambda h: h.matmul(bank(3, nq), lhsT=ones, rhs=Pt[:, sb, :nq],
                                                          start=(kc == 0), stop=(kc == nkc - 1)),
                                 reads=[ones_b, Pt_bs[sb]], writes=[PSB[3]], pe_accum=True)
                        E.op("dve", lambda h: h.reciprocal(out=rden[:, :nq], in_=bank(3, nq)), reads=[PSB[3]], writes=[rden_b])
                        E.op("dve", lambda h: h.tensor_tensor(out=MIX[:, hd, q0:q0 + nq], in0=bank(2, nq), in1=rden[:, :nq],
                                                              op=ALU.mult), reads=[PSB[2], rden_b], writes=[MIX_bs[hd]])
            A.release(mk_a)
            if stage == 4:
                return

            nsc = L // 128
            nfc = 2 * nsc + 1
            X0, X0_b = A.alloc("X0", [128, 4, T], BF16, top=True)
            VPT, VPT_b = A.alloc("VPT", [128, nseq, nsc, 512], BF16, top=True)
            mk_h = A.mark()
            wu, wu_bs = A.alloc("wu", [128, 2, 8, 128], BF16, nbufs=2)
            wu_it = [0]
            U, U_b = A.alloc("U", [128, T], F32)
            CU, CU_bs = A.alloc("CU", [128, 2, T], F32, nbufs=2)
            VPc, VPc_b = A.alloc("VPc", [128, T], BF16)
            for c in range(4):
                for ti, which in enumerate((1, 2, 0)):
                    ch = which * 4 + c
                    wsl = wu_it[0] % 2
                    wu_it[0] += 1
                    E.dma("pool", wu[:, wsl], ahwin_d.rearrange("p (kc n) -> p kc n", kc=8)[:, :, 1024 + ch * 128: 1024 + (ch + 1) * 128],
                          writes=[wu_bs[wsl]])
                    for blk in range(nblk):
                        ts = slice(blk * 512, (blk + 1) * 512)
                        pb = blk % 2
                        mm_group(bank(pb), [PSB[pb]], [(wu[:, wsl, kc, :], HS[:, kc, ts], [wu_bs[wsl], HS_bs[blk]])
                                                       for kc in range(8)])
                        E.op("act", lambda h: h.activation(out=U[:, ts], in_=bank(pb), func=AF.Copy),
                             reads=[PSB[pb]], writes=[U_b])
                    ci = ti % 2
                    cu, cu_b = CU[:, ci], CU_bs[ci]
                    E.op("act", lambda h: h.activation(out=cu, in_=U, func=AF.Identity, bias=hcw[:, ch, 3:4],
                                                       scale=hcw[:, ch, 1:2]), reads=[U_b, hcw_b], writes=[cu_b])
                    c3 = cu.rearrange("p (s t) -> p s t", t=L)
                    u3 = U.rearrange("p (s t) -> p s t", t=L)
                    E.op("dve", lambda h: h.scalar_tensor_tensor(out=c3[:, :, 1:L], in0=u3[:, :, 0:L - 1], scalar=hcw[:, ch, 0:1],
                                                                 in1=c3[:, :, 1:L], op0=ALU.mult, op1=ALU.add),
                         reads=[U_b, hcw_b, cu_b], writes=[cu_b])
                    E.op("dve", lambda h: h.scalar_tensor_tensor(out=c3[:, :, 0:L - 1], in0=u3[:, :, 1:L], scalar=hcw[:, ch, 2:3],
                                                                 in1=c3[:, :, 0:L - 1], op0=ALU.mult, op1=ALU.add),
                         reads=[U_b, hcw_b, cu_b], writes=[cu_b])
                    if which == 2:
                        E.op("pool", lambda h: h.tensor_tensor(out=VPc, in0=CU[:, 0], in1=CU[:, 1], op=ALU.mult),
                             reads=[CU_bs[0], CU_bs[1]], writes=[VPc_b])
                    if which == 0:
                        E.op("pool", lambda h: h.tensor_copy(out=X0[:, c, :], in_=cu), reads=[cu_b], writes=[X0_b])
                for tcn in range(T // 128):
                    s_i, cc = (tcn * 128) // L, ((tcn * 128) % L) // 128
                    pb = 4 + tcn % 2
                    E.op("pe", lambda h: h.transpose(PSbf[pb], VPc[:, tcn * 128:(tcn + 1) * 128], ident),
                         reads=[VPc_b, ident_b], writes=[PSbf_b[pb]])
                    E.op("act", lambda h: h.activation(out=VPT[:, s_i, cc, c * 128:(c + 1) * 128], in_=PSbf[pb], func=AF.Copy),
                         reads=[PSbf_b[pb]], writes=[VPT_b])
            A.release(mk_m)
            if stage == 5:
                return
            YR, YR_b = A.alloc("YR", [128, nsc + 1, 512], BF16)
            YI, YI_b = A.alloc("YI", [128, nsc, 512], BF16)
            vr, vr_b = A.alloc("vr", [128, 512], F32)
            vi, vi_b = A.alloc("vi", [128, 512], F32)
            pa, pa_b = A.alloc("pa", [128, 512], F32)
            pb_, pb_b = A.alloc("pb", [128, 512], F32)
            pc, pc_b = A.alloc("pc", [128, 512], F32)
            pd, pd_b = A.alloc("pd", [128, 512], F32)
            fw, fw_bs = A.alloc("fwst", [128, 2, nsc, 128], BF16, nbufs=2)
            nt = min(L, 256)
            ntb = L // nt
            iv, iv_b = A.alloc("invst", [128, nfc, nt], BF16)
            for s_i in range(nseq):
                for fc in range(nsc + 1):
                    E.dma("sp", fw[:, 0], tabs[L]["fwd"][fc].rearrange("p (s f) -> p s f", f=128), writes=[fw_bs[0]])
                    mm_group(bank(0), [PSB[0]], [(fw[:, 0, s_, :], VPT[:, s_i, s_, :], [fw_bs[0], VPT_b]) for s_ in range(nsc)])
                    E.op("act", lambda h: h.activation(out=vr, in_=bank(0), func=AF.Copy), reads=[PSB[0]], writes=[vr_b])
                    if fc < nsc:
                        E.dma("sp", fw[:, 1], tabs[L]["fwd"][nsc + 1 + fc].rearrange("p (s f) -> p s f", f=128),
                              writes=[fw_bs[1]])
                        mm_group(bank(1), [PSB[1]], [(fw[:, 1, s_, :], VPT[:, s_i, s_, :], [fw_bs[1], VPT_b])
                                                     for s_ in range(nsc)])
                        E.op("act", lambda h: h.activation(out=vi, in_=bank(1), func=AF.Copy), reads=[PSB[1]], writes=[vi_b])
                        E.op("dve", lambda h: h.tensor_tensor(out=pa, in0=vr, in1=GR[:, fc, :], op=ALU.mult),
                             reads=[vr_b, GR_b], writes=[pa_b])
                        E.op("pool", lambda h: h.tensor_tensor(out=pb_, in0=vi, in1=GI[:, fc, :], op=ALU.mult),
                             reads=[vi_b, GI_b], writes=[pb_b])
                        E.op("dve", lambda h: h.tensor_tensor(out=YR[:, fc, :], in0=pa, in1=pb_, op=ALU.subtract),
                             reads=[pa_b, pb_b], writes=[YR_b])
                        E.op("pool", lambda h: h.tensor_tensor(out=pc, in0=vr, in1=GI[:, fc, :], op=ALU.mult),
                             reads=[vr_b, GI_b], writes=[pc_b])
                        E.op("dve", lambda h: h.tensor_tensor(out=pd, in0=vi, in1=GR[:, fc, :], op=ALU.mult),
                             reads=[vi_b, GR_b], writes=[pd_b])
                        E.op("pool", lambda h: h.tensor_tensor(out=YI[:, fc, :], in0=pc, in1=pd, op=ALU.add),
                             reads=[pc_b, pd_b], writes=[YI_b])
                    else:
                        E.op("dve", lambda h: h.tensor_tensor(out=YR[:, fc, :], in0=vr, in1=GR[:, fc, :], op=ALU.mult),
                             reads=[vr_b, GR_b], writes=[YR_b])
                for tb in range(ntb):
                    tw = min(L, 512)
                    E.dma("sp", iv, tabs[L]["inv"][(tb * nt) // tw].rearrange("p (f t) -> p f t", t=tw)[:, :, (tb * nt) % tw:(tb * nt) % tw + nt],
                          writes=[iv_b])
                    t0 = s_i * L + tb * nt
                    for cc in range(4):
                        pbk = 4 + cc % 2
                        terms = [(YR[:, f_, cc * 128:(cc + 1) * 128], iv[:, f_, :], [YR_b, iv_b]) for f_ in range(nsc + 1)]
                        terms += [(YI[:, f_, cc * 128:(cc + 1) * 128], iv[:, nsc + 1 + f_, :], [YI_b, iv_b]) for f_ in range(nsc)]
                        mm_group(bank(pbk, nt), [PSB[pbk]], terms)
                        E.op("dve", lambda h: h.tensor_tensor(out=MIX[:, 4 + cc, t0:t0 + nt], in0=bank(pbk, nt),
                                                              in1=X0[:, cc, t0:t0 + nt], op=ALU.mult),
                             reads=[PSB[pbk], X0_b], writes=[MIX_bs[4 + cc]])
            A.release(mk_m)
            A.release_top(mk_top)
            if stage == 6:
                dbg = P.dout("dbg_" + gname, [128, 8 * T], BF16)
                E.dma("sp", dbg.rearrange("p (j t) -> p j t", j=8), MIX, reads=MIX_bs)
                return

            HF, HF_bs = A.alloc(gname + "_hf", [128, 8, T], BF16, nbufs=nblk)
            mk_o = A.mark()
            wo, wo_b = A.alloc("wo", [128, 8, 1024], BF16)
            E.dma("pool", wo, ahwout_d.rearrange("p (kc n) -> p kc n", kc=8), writes=[wo_b])
            O, O_b = A.alloc("O", [128, 8, 512], F32)
            sq, sq_b = A.alloc("sq", [128, 8, 512], BF16)
            rstd, rstd_b = A.alloc("rstd", [128, 512], F32)
            tmp, tmp_b = A.alloc("tmp", [128, 512], F32)
            xb, xb_b = A.alloc("xb", [128, 8, 512], F32)
            wk = (sq, sq_b, rstd, rstd_b, tmp, tmp_b, xb, xb_b)
            for blk in range(nblk):
                ts = slice(blk * 512, (blk + 1) * 512)
                for m in range(8):
                    pbk = m % 2
                    mm_group(bank(pbk), [PSB[pbk]], [(wo[:, kc, m * 128:(m + 1) * 128], MIX[:, kc, ts], [wo_b, MIX_bs[kc]])
                                                     for kc in range(8)])
                    E.op("act", lambda h: h.activation(out=O[:, m, :], in_=bank(pbk), func=AF.Copy), reads=[PSB[pbk]], writes=[O_b])
                epilogue(O, O_b, 512, X0d, X0d_b, X1d, X1d_b, blk * 512, T, 0, 0, v,
                         (0, 1, lambda j: HF[:, j, ts], HF_bs[blk]), wk)
            A.release(mk_o)
            if stage == 7:
                dbg = P.dout("dbg_" + gname, [128, 8 * T], F32)
                E.dma("sp", dbg, X1d, reads=[X1d_b], writes=[])
                return
            p_after_hf = A.p
            A.release(mk_g)
            HS1, HS1_bs = A.alloc(gname + "_hs1", [128, 8, T], BF16, nbufs=nblk)
            p_after_hs1 = A.p
            A.p = p_after_hf
            X2d, X2d_b = dscratch(gname + "_x2b", [128, 8 * T])
            ffn(G, 0, HF, HF_bs, X1d, X1d_b, X2d, X2d_b,
                lambda tok0: (1, 0, (lambda j: HS1[:, j, tok0:tok0 + 512]), HS1_bs[tok0 // 512]), NB=(512 if smp else 1024))
            if stage == 8:
                E.dma("sp", G["y"], X2d, reads=[X2d_b])
                A.release(mk_g)
                return
            A.release(p_after_hs1)
            layer1(G, HS1, HS1_bs, X2d, X2d_b)
            A.release(mk_g)

        def layer1(G, HS1, HS1_bs, X2d, X2d_b):
            gname, nseq, L, v = G["name"], G["nseq"], G["L"], G["v"]
            smp = G["sample"]
            T = nseq * L
            nblk = T // 512
            nch = L // 128
            X3d, X3d_b = dscratch(gname + "_x3", [128, 8 * T])
            mk_l = A.mark()
            MLM, MLM_bs = A.alloc("mlmix", [128, 8, T], BF16, nbufs=8)
            mk_2 = A.mark()
            RT, RT_b = A.alloc("RT", [40, T], F32)
            EM, EM_b = A.alloc("EM", [40, T], F32)
            WI, WI_b = A.alloc("WI", [40, T], F32)
            ATK, ATK_b = A.alloc("ATK", [128, T // 128, 40], F32)
            m0c, m0c_b = A.alloc("m0c", [40, 2], F32)
            onesf, onesf_b = A.alloc("onesf", [40, 128], F32)
            E.op("dve", lambda h: h.memset(onesf, 1.0), writes=[onesf_b])
            if G.get("states"):
                WTK, WTK_b = A.alloc("WTK", [128, T // 128, 40], F32)
            mk_r = A.mark()
            wg, wg_b = A.alloc("wg", [128, 8, 80], BF16)
            E.dma("pool", wg, mlwg_d.rearrange("p (kc n) -> p kc n", kc=8), writes=[wg_b])
            IG, IG_b = A.alloc("IG", [40, T], F32)
            LF, LF_b = A.alloc("LF", [40, T], F32)
            for blk in range(nblk):
                ts = slice(blk * 512, (blk + 1) * 512)
                for gi in range(2):
                    mm_group(PS[0:40, gi * 512:(gi + 1) * 512], [PSB[gi]],
                             [(wg[:, kc, gi * 40:(gi + 1) * 40], HS1[:, kc, ts], [wg_b, HS1_bs[blk]]) for kc in range(8)])
                E.op("act", lambda h: h.activation(out=IG[:, ts], in_=PS[0:40, 0:512], func=AF.Identity, bias=mbg[:, 0:1], scale=1.0),
                     reads=[PSB[0], mbg_b], writes=[IG_b])
                E.op("act", lambda h: h.activation(out=LF[:, ts], in_=PS[0:40, 512:1024], func=AF.Identity, bias=mbg[:, 1:2], scale=1.0),
                     reads=[PSB[1], mbg_b], writes=[LF_b])
            E.op("act", lambda h: h.activation(out=LF, in_=LF, func=AF.Exp, scale=-1.0), reads=[LF_b], writes=[LF_b])
            E.op("act", lambda h: h.activation(out=LF, in_=LF, func=AF.Ln, bias=onec[0:40, 0:1], scale=1.0),
                 reads=[LF_b, onec_b], writes=[LF_b])
            E.op("dve", lambda h: h.tensor_scalar(out=LF, in0=LF, scalar1=-1.0, scalar2=None, op0=ALU.mult), reads=[LF_b], writes=[LF_b])
            if smp:
                E.dma("sp", m0c[:, 0:1], m0_d, writes=[m0c_b])
            else:
                E.op("dve", lambda h: h.memset(m0c, 0.0), writes=[m0c_b])
            BT, BT_b = A.alloc("BT", [40, T], F32)
            AA, AA_b = A.alloc("AA", [40, T], F32)
            CM, CM_b = A.alloc("CM", [40, T], F32)
            onesr, onesr_b = A.alloc("onesr", [40, L], F32)
            E.op("dve", lambda h: h.memset(onesr, 1.0), writes=[onesr_b])
            for (tt, tb_) in ((BT, BT_b), (AA, AA_b), (CM, CM_b)):
                E.op("pool", lambda h: h.memset(tt, 0.0), writes=[tb_])
            for s_i in range(nseq):
                sl = slice(s_i * L, (s_i + 1) * L)
                for (p0, rev) in ((0, False), (32, True)):
                    pr = slice(p0, p0 + 8)

                    def V_(ap):
                        a2 = ap[pr, sl]
                        return a2[:, ::-1] if rev else a2
                    E.op("dve", lambda h: h.tensor_tensor_scan(out=V_(BT), data0=onesr[pr, :], data1=V_(LF), initial=0.0,
                                                               op0=ALU.mult, op1=ALU.add),
                         reads=[LF_b, onesr_b], writes=[BT_b])
                    E.op("dve", lambda h: h.tensor_tensor(out=AA[pr, sl], in0=IG[pr, sl], in1=BT[pr, sl], op=ALU.subtract),
                         reads=[IG_b, BT_b], writes=[AA_b])
                    E.op("dve", lambda h: h.tensor_tensor_scan(out=V_(CM), data0=V_(AA), data1=V_(AA), initial=m0c[pr, 0:1],
                                                               op0=ALU.max, op1=ALU.max), reads=[AA_b, m0c_b], writes=[CM_b])
            E.op("dve", lambda h: h.tensor_scalar(out=RT, in0=CM, scalar1=-1.0, scalar2=None, op0=ALU.mult), reads=[CM_b], writes=[RT_b])
            E.op("dve", lambda h: h.tensor_tensor(out=EM, in0=BT, in1=CM, op=ALU.add), reads=[BT_b, CM_b], writes=[EM_b])
            if G.get("states"):
                MTk, MTk_b = A.alloc("MTk", [40, nseq], F32)
                E.op("dve", lambda h: h.memset(MTk, 0.0), writes=[MTk_b])
                for s_i in range(nseq):
                    E.op("dve", lambda h: h.tensor_copy(out=MTk[0:8, s_i:s_i + 1], in_=EM[0:8, (s_i + 1) * L - 1:(s_i + 1) * L]),
                         reads=[EM_b], writes=[MTk_b])
                    E.op("dve", lambda h: h.tensor_copy(out=MTk[32:40, s_i:s_i + 1], in_=EM[32:40, s_i * L:s_i * L + 1]),
                         reads=[EM_b], writes=[MTk_b])
                E.dma("sp", mst_d, MTk, reads=[MTk_b])
            E.op("act", lambda h: h.activation(out=EM, in_=EM, func=AF.Exp, scale=-1.0), reads=[EM_b], writes=[EM_b])
            E.op("act", lambda h: h.activation(out=WI, in_=CM, func=AF.Exp, scale=-1.0, bias=m0c[:, 0:1]),
                 reads=[CM_b, m0c_b], writes=[WI_b])
            for c in range(T // 128):
                E.op("pe", lambda h: h.transpose(PS[:, 1024:1064], AA[:, c * 128:(c + 1) * 128], id32[0:40, 0:40]),
                     reads=[AA_b, id32_b], writes=[PSB[2]])
                E.op("dve", lambda h: h.tensor_copy(out=ATK[:, c, :], in_=PS[:, 1024:1064]), reads=[PSB[2]], writes=[ATK_b])
            if G.get("states"):
                dg, dg_b = A.alloc("dg", [40, nseq, 40], F32)
                for s_i in range(nseq):
                    E.op("dve", lambda h: h.memset(dg[:, s_i, :], 0.0), writes=[dg_b])
                    E.op("dve", lambda h: h.tensor_scalar(out=dg[0:8, s_i, :], in0=id32[0:8, 0:40], scalar1=RT[0:8, (s_i + 1) * L - 1:(s_i + 1) * L],
                                                          scalar2=None, op0=ALU.mult), reads=[id32_b, RT_b], writes=[dg_b])
                    E.op("dve", lambda h: h.tensor_scalar(out=dg[32:40, s_i, :], in0=id32[32:40, 0:40], scalar1=RT[32:40, s_i * L:s_i * L + 1],
                                                          scalar2=None, op0=ALU.mult), reads=[id32_b, RT_b], writes=[dg_b])
                    mm_group(PS[:, 1024:1064], [PSB[2]], [(onesf, dg[:, s_i, :], [onesf_b, dg_b])])
                    for cc in range(nch):
                        c = s_i * nch + cc
                        E.op("dve", lambda h: h.tensor_tensor(out=WTK[:, c, :], in0=ATK[:, c, :], in1=PS[:, 1024:1064], op=ALU.add),
                             reads=[ATK_b, PSB[2]], writes=[WTK_b])
                E.op("act", lambda h: h.activation(out=WTK, in_=WTK, func=AF.Exp), reads=[WTK_b], writes=[WTK_b])
            A.release(mk_r)
            wh, wh_b = A.alloc("wh", [128, 8, 4, 128], BF16)
            U, U_b = A.alloc("U", [128, T], F32)
            cu, cu_b = A.alloc("cu1", [128, T], F32)
            QT, QT_b = A.alloc("QT1", [128, T], BF16)
            KT, KT_b = A.alloc("KT1", [128, T], BF16)
            SG, SG_b = A.alloc("SG", [128, T], BF16)
            VK, VK_b = A.alloc("VK", [128, T // 128, 128], BF16)
            KK, KK_b = A.alloc("KK", [128, T // 128, 128], BF16)
            HSUM, HSUM_b = A.alloc("HSUM", [128, T], F32)
            nq = min(512, L)
            rtb, rtb_b = A.alloc("rtb", [128, nq], F32)
            emb, emb_b = A.alloc("emb", [128, nq], F32)
            wexp, wexp_bs = A.alloc("wexp", [128, 2, nq], F32, nbufs=2)
            Pm, Pm_bs = A.alloc("Pm", [128, 2, nq], BF16, nbufs=2)
            dn, dn_b = A.alloc("dn", [128, nq], F32)
            ht, ht_b = A.alloc("ht", [128, nq], F32)
            sq, sq_b = A.alloc("sq", [128, 1, 512], BF16)
            rstd, rstd_b = A.alloc("rstd", [128, 512], F32)
            vw, vw_b = A.alloc("vw", [128, 128], BF16)
            cst, cst_bs = A.alloc("cst", [128, 2, 128], F32, nbufs=2)
            nst, nst_bs = A.alloc("nst", [1, 2, 128], F32, nbufs=2)
            if smp:
                qp, qp_b = A.alloc("qp", [128, nq], BF16)
                c0t, c0t_b = A.alloc("c0t", [128, 16, 128], BF16)
                n0b, n0b_b = A.alloc("n0b", [128, 16, 128], BF16)
                E.dma("pool", c0t, c0t_d.rearrange("p (r m) -> p r m", m=128), writes=[c0t_b])
                E.dma("pool", n0b, n0b_d.rearrange("p (r m) -> p r m", m=128), writes=[n0b_b])
            wv = mlwin_d.rearrange("p (kc n) -> p kc n", kc=8)
            for hd in range(8):
                for wi in range(4):
                    E.dma("pool", wh[:, :, wi, :], wv[:, :, wi * 1024 + hd * 128: wi * 1024 + (hd + 1) * 128], writes=[wh_b])
                for wi, (dst, dst_b) in ((0, (QT, QT_b)), (1, (KT, KT_b)), (3, (SG, SG_b))):
                    for blk in range(nblk):
                        ts = slice(blk * 512, (blk + 1) * 512)
                        pb = blk % 2
                        mm_group(bank(pb), [PSB[pb]], [(wh[:, kc, wi, :], HS1[:, kc, ts], [wh_b, HS1_bs[blk]]) for kc in range(8)])
                        if wi == 3:
                            E.op("act", lambda h: h.activation(out=SG[:, ts], in_=bank(pb), func=AF.Sigmoid), reads=[PSB[pb]], writes=[SG_b])
                        else:
                            E.op("act", lambda h: h.activation(out=U[:, ts], in_=bank(pb), func=AF.Copy), reads=[PSB[pb]], writes=[U_b])
                    if wi == 3:
                        continue
                    ch = wi * 8 + hd
                    E.op("act", lambda h: h.activation(out=cu, in_=U, func=AF.Identity, bias=mcw[:, ch, 3:4], scale=mcw[:, ch, 1:2]),
                         reads=[U_b, mcw_b], writes=[cu_b])
                    c3 = cu.rearrange("p (s t) -> p s t", t=L)
                    u3 = U.rearrange("p (s t) -> p s t", t=L)
                    E.op("dve", lambda h: h.scalar_tensor_tensor(out=c3[:, :, 1:L], in0=u3[:, :, 0:L - 1], scalar=mcw[:, ch, 0:1],
                                                                 in1=c3[:, :, 1:L], op0=ALU.mult, op1=ALU.add),
                         reads=[U_b, mcw_b, cu_b], writes=[cu_b])
                    E.op("dve", lambda h: h.scalar_tensor_tensor(out=c3[:, :, 0:L - 1], in0=u3[:, :, 1:L], scalar=mcw[:, ch, 2:3],
                                                                 in1=c3[:, :, 0:L - 1], op0=ALU.mult, op1=ALU.add),
                         reads=[U_b, mcw_b, cu_b], writes=[cu_b])
                    if wi == 0:
                        E.op("act", lambda h: h.activation(out=QT, in_=cu, func=AF.Silu), reads=[cu_b], writes=[QT_b])
                    else:
                        E.op("act", lambda h: h.activation(out=cu, in_=cu, func=AF.Silu), reads=[cu_b], writes=[cu_b])
                        E.op("pool", lambda h: h.tensor_scalar(out=KT, in0=cu, scalar1=128.0 ** -0.5, scalar2=None, op0=ALU.mult),
                             reads=[cu_b], writes=[KT_b])
                for c in range(T // 128):
                    pb = 4 + c % 2
                    mm_group(bank(pb, 128), [PSB[pb]], [(HS1[:, kc, c * 128:(c + 1) * 128], wh[:, kc, 2, :], [wh_b, HS1_bs[c // 4]])
                                                        for kc in range(8)])
                    E.op("act", lambda h: h.activation(out=VK[:, c, :], in_=bank(pb, 128), func=AF.Copy), reads=[PSB[pb]], writes=[VK_b])
                    if G.get("states"):
                        E.op("pe", lambda h: h.transpose(PSbf[4], KT[:, c * 128:(c + 1) * 128], ident), reads=[KT_b, ident_b], writes=[PSbf_b[4]])
                        E.op("act", lambda h: h.activation(out=KK[:, c, :], in_=PSbf[4], func=AF.Copy), reads=[PSbf_b[4]], writes=[KK_b])
                for s_i in range(nseq):
                    for di in range(2):
                        r = di * 32 + hd
                        ridx = di * 8 + hd
                        for qb in range(L // nq):
                            q0 = s_i * L + qb * nq
                            mm_group(bank(4, nq), [PSB[4]], [(sel[:, ridx, :], RT[:, q0:q0 + nq], [sel_b, RT_b])])
                            E.op("act", lambda h: h.activation(out=rtb, in_=bank(4, nq), func=AF.Copy), reads=[PSB[4]], writes=[rtb_b])
                            mm_group(bank(5, nq), [PSB[5]], [(sel[:, ridx, :], EM[:, q0:q0 + nq], [sel_b, EM_b])])
                            E.op("act", lambda h: h.activation(out=emb, in_=bank(5, nq), func=AF.Copy), reads=[PSB[5]], writes=[emb_b])
                            if di == 0:
                                kcs = [kc for kc in range(nch) if kc * 128 <= qb * nq + nq - 1]
                            else:
                                kcs = [kc for kc in range(nch) if kc * 128 + 127 >= qb * nq]
                            nterm = len(kcs) + (1 if smp else 0)
                            ti = 0
                            if smp:
                                mm_group(bank(4, nq), [PSB[4]], [(sel[:, ridx, :], WI[:, q0:q0 + nq], [sel_b, WI_b])])
                                E.op("dve", lambda h: h.tensor_tensor(out=qp, in0=QT[:, q0:q0 + nq], in1=bank(4, nq), op=ALU.mult),
                                     reads=[QT_b, PSB[4]], writes=[qp_b])
                                E.op("pe", lambda h: h.matmul(bank(2, nq), lhsT=c0t[:, ridx, :], rhs=qp, start=True, stop=(nterm == 1)),
                                     reads=[c0t_b, qp_b], writes=[PSB[2]], pe_accum=True)
                                E.op("pe", lambda h: h.matmul(bank(3, nq), lhsT=n0b[:, ridx, :], rhs=qp, start=True, stop=(nterm == 1)),
                                     reads=[n0b_b, qp_b], writes=[PSB[3]], pe_accum=True)
                                ti = 1
                            for kc in kcs:
                                c = s_i * nch + kc
                                sb = ti % 2
                                off = kc * 128 - qb * nq
                                mm_group(bank(sb, nq), [PSB[sb]], [(KT[:, c * 128:(c + 1) * 128], QT[:, q0:q0 + nq], [KT_b, QT_b])])
                                E.op("act", lambda h: h.activation(out=wexp[:, sb, :], in_=rtb, func=AF.Exp, bias=ATK[:, c, r:r + 1], scale=1.0),
                                     reads=[rtb_b, ATK_b], writes=[wexp_bs[sb]])
                                E.op("dve", lambda h: h.tensor_tensor(out=Pm[:, sb, :], in0=bank(sb, nq), in1=wexp[:, sb, :], op=ALU.mult),
                                     reads=[PSB[sb], wexp_bs[sb]], writes=[Pm_bs[sb]])
                                if 0 <= off < nq and (off // 128) < 4:
                                    E.op("pool", lambda h: h.tensor_tensor(out=Pm[:, sb, :], in0=Pm[:, sb, :], in1=msk[:, di, off // 128, 0:nq], op=ALU.mult),
                                         reads=[Pm_bs[sb], msk_b], writes=[Pm_bs[sb]])
                                E.op("pe", lambda h: h.matmul(bank(2, nq), lhsT=VK[:, c, :], rhs=Pm[:, sb, :], start=(ti == 0), stop=(ti == nterm - 1)),
                                     reads=[VK_b, Pm_bs[sb]], writes=[PSB[2]], pe_accum=True)
                                E.op("pe", lambda h: h.matmul(bank(3, nq), lhsT=ones, rhs=Pm[:, sb, :], start=(ti == 0), stop=(ti == nterm - 1)),
                                     reads=[ones_b, Pm_bs[sb]], writes=[PSB[3]], pe_accum=True)
                                ti += 1
                            E.op("dve", lambda h: h.tensor_scalar(out=dn, in0=bank(3, nq), scalar1=-1.0, scalar2=None, op0=ALU.mult),
                                 reads=[PSB[3]], writes=[dn_b])
                            E.op("dve", lambda h: h.tensor_tensor(out=dn, in0=dn, in1=bank(3, nq), op=ALU.max), reads=[dn_b, PSB[3]], writes=[dn_b])
                            E.op("dve", lambda h: h.tensor_tensor(out=dn, in0=dn, in1=emb, op=ALU.max), reads=[dn_b, emb_b], writes=[dn_b])
                            E.op("dve", lambda h: h.reciprocal(out=dn, in_=dn), reads=[dn_b], writes=[dn_b])
                            if di == 0:
                                E.op("dve", lambda h: h.tensor_tensor(out=HSUM[:, q0:q0 + nq], in0=bank(2, nq), in1=dn, op=ALU.mult),
                                     reads=[PSB[2], dn_b], writes=[HSUM_b])
                            else:
                                E.op("dve", lambda h: h.tensor_tensor(out=ht, in0=bank(2, nq), in1=dn, op=ALU.mult),
                                     reads=[PSB[2], dn_b], writes=[ht_b])
                                E.op("pool", lambda h: h.tensor_tensor(out=HSUM[:, q0:q0 + nq], in0=HSUM[:, q0:q0 + nq], in1=ht, op=ALU.add),
                                     reads=[ht_b, HSUM_b], writes=[HSUM_b])
                        if G.get("states"):
                            so = (s_i * 16 + ridx) * 128
                            slot = ridx % 2
                            for cc in range(nch):
                                c = s_i * nch + cc
                                E.op("dve", lambda h: h.tensor_scalar(out=vw, in0=VK[:, c, :], scalar1=WTK[:, c, r:r + 1], scalar2=None, op0=ALU.mult),
                                     reads=[VK_b, WTK_b], writes=[vw_b])
                                E.op("pe", lambda h: h.matmul(bank(5, 128), lhsT=vw, rhs=KK[:, c, :], start=(cc == 0), stop=(cc == nch - 1)),
                                     reads=[vw_b, KK_b], writes=[PSB[5]], pe_accum=True)
                            E.op("act", lambda h: h.activation(out=cst[:, slot, :], in_=bank(5, 128), func=AF.Copy), reads=[PSB[5]], writes=[cst_bs[slot]])
                            E.dma("sp", cst_d[:, so:so + 128], cst[:, slot, :], reads=[cst_bs[slot]])
                            wtb, wtb_b = vw, vw_b
                            for cc in range(nch):
                                c = s_i * nch + cc
                                E.op("dve", lambda h: h.tensor_copy(out=vw[:, 0:1], in_=WTK[:, c, r:r + 1]), reads=[WTK_b], writes=[vw_b])
                                E.op("pe", lambda h: h.matmul(PS[0:1, 5 * 512 + 128: 5 * 512 + 256], lhsT=vw[:, 0:1], rhs=KK[:, c, :],
                                                              start=(cc == 0), stop=(cc == nch - 1)),
                                     reads=[vw_b, KK_b], writes=[PSB[5]], pe_accum=True)
                            E.op("act", lambda h: h.activation(out=nst[0:1, slot, :], in_=PS[0:1, 5 * 512 + 128: 5 * 512 + 256], func=AF.Copy),
                                 reads=[PSB[5]], writes=[nst_bs[slot]])
                            E.dma("sp", nst_d[0:1, so:so + 128], nst[0:1, slot, :], reads=[nst_bs[slot]])
                for blk in range(nblk):
                    ts = slice(blk * 512, (blk + 1) * 512)
                    sumsq_rstd([HSUM[:, ts]], [HSUM_b], 512, 128, rstd, rstd_b, sq, sq_b, 6)
                    E.op("dve", lambda h: h.scalar_tensor_tensor(out=U[:, ts], in0=HSUM[:, ts], scalar=mhn[:, hd:hd + 1], in1=rstd,
                                                                 op0=ALU.mult, op1=ALU.mult), reads=[HSUM_b, mhn_b, rstd_b], writes=[U_b])
                    E.op("pool", lambda h: h.tensor_tensor(out=MLM[:, hd, ts], in0=U[:, ts], in1=SG[:, ts], op=ALU.mult),
                         reads=[U_b, SG_b], writes=[MLM_bs[hd]])
            A.release(mk_2)
            if stage == 9:
                dbg = P.dout("dbg_" + gname, [128, 8 * T], BF16)
                E.dma("sp", dbg.rearrange("p (j t) -> p j t", j=8), MLM, reads=MLM_bs)
                A.release(mk_l)
                return
            mk_top1 = A.mark_top()
            HF1, HF1_bs = A.alloc(gname + "_hf1", [128, 8, T], BF16, nbufs=nblk, top=True)
            mk_o = A.mark()
            wo, wo_b = A.alloc("wo1", [128, 8, 1024], BF16)
            E.dma("pool", wo, mlwout_d.rearrange("p (kc n) -> p kc n", kc=8), writes=[wo_b])
            O, O_b = A.alloc("O", [128, 8, 512], F32)
            sq, sq_b = A.alloc("sq", [128, 8, 512], BF16)
            rstd, rstd_b = A.alloc("rstd", [128, 512], F32)
            tmp, tmp_b = A.alloc("tmp", [128, 512], F32)
            xb, xb_b = A.alloc("xb", [128, 8, 512], F32)
            wk = (sq, sq_b, rstd, rstd_b, tmp, tmp_b, xb, xb_b)
            for blk in range(nblk):
                ts = slice(blk * 512, (blk + 1) * 512)
                for m in range(8):
                    pbk = m % 2
                    mm_group(bank(pbk), [PSB[pbk]], [(wo[:, kc, m * 128:(m + 1) * 128], MLM[:, kc, ts], [wo_b, MLM_bs[kc]])
                                                     for kc in range(8)])
                    E.op("act", lambda h: h.activation(out=O[:, m, :], in_=bank(pbk), func=AF.Copy), reads=[PSB[pbk]], writes=[O_b])
                epilogue(O, O_b, 512, X2d, X2d_b, X3d, X3d_b, blk * 512, T, 1, 0, v,
                         (1, 1, lambda j: HF1[:, j, ts], HF1_bs[blk]), wk)
            A.release(mk_l)
            ffn(G, 1, HF1, HF1_bs, X3d, X3d_b, G["y"], Buf("y"), None, NB=(512 if smp else 1024))
            A.release_top(mk_top1)

        def ffn(G, l, HF, HF_bs, xsrc, xsrc_b, xdst, xdst_b, nxt, NB=1024):
            gname, nseq, L, v = G["name"], G["nseq"], G["L"], G["v"]
            T = nseq * L
            mk_f = A.mark()
            Gb, Gb_b = A.alloc("G", [128, 22, NB], BF16)
            wu, wu_bs = A.alloc("wup", [128, 2, 8, 2, 256], BF16, nbufs=2)
            wd, wd_bs = A.alloc("wdn", [128, 2, 22, 256], BF16, nbufs=2)
            cu, cu_bs = A.alloc("cu", [128, 2, NB], F32, nbufs=2)
            O, O_b = A.alloc("O", [128, 8, 512], F32)
            sq, sq_b = A.alloc("sq", [128, 8, 512], BF16)
            rstd, rstd_b = A.alloc("rstd", [128, 512], F32)
            tmp, tmp_b = A.alloc("tmp", [128, 512], F32)
            xb, xb_b = A.alloc("xb", [128, 8, 512], F32)
            wk = (sq, sq_b, rstd, rstd_b, tmp, tmp_b, xb, xb_b)
            upv = ffnup_d[l].rearrange("p (kc n) -> p kc n", kc=8)
            dnv = ffndn_d[l].rearrange("p (kc n) -> p kc n", kc=22)
            for a in range(0, T, NB):
                n = NB
                seq_start = (a % L == 0)
                seq_end = ((a + n) % L == 0)
                aligned = (L <= n)
                lo = 1 if (seq_start or aligned) else 0
                hi = n + 1 if (seq_end or aligned) else n + 2
                pieces = []
                c0 = lo
                while c0 < hi:
                    c1 = min(hi, (c0 // 512 + 1) * 512)
                    pieces.append((c0, c1))
                    c0 = c1
                for jp in range(11):
                    slot = jp % 2
                    for half in range(2):
                        E.dma("pool", wu[:, slot, :, half, :], upv[:, :, half * DFF + jp * 256: half * DFF + (jp + 1) * 256],
                              writes=[wu_bs[slot]])
                    for jj in range(2):
                        j = jp * 2 + jj
                        for half in range(2):
                            hb = 3 * half
                            hp = PS[:, hb * 512: hb * 512 + 1536]
                            hbufs = [PSB[hb], PSB[hb + 1], PSB[hb + 2]]
                            for (c0, c1) in pieces:
                                bk = hb + c0 // 512
                                mm_group(hp[:, c0:c1], [PSB[bk]],
                                         [(wu[:, slot, kc, half, jj * 128:(jj + 1) * 128], HF[:, kc, a - 1 + c0: a - 1 + c1],
                                           [wu_bs[slot]] + HF_bs) for kc in range(8)])
                            ch = half * 22 + j
                            cw = fcw[:, l, ch, :]
                            E.op("act", lambda h: h.activation(out=cu[:, half, :n], in_=hp[:, 1:n + 1], func=AF.Identity,
                                                               bias=cw[:, 3:4], scale=cw[:, 1:2]),
                                 reads=hbufs + [fcw_b], writes=[cu_bs[half]])
                            if aligned:
                                c3 = cu[:, half, :n].rearrange("p (s t) -> p s t", t=L)
                                h3 = hp[:, 1:n + 1].rearrange("p (s t) -> p s t", t=L)
                                E.op("dve", lambda h: h.scalar_tensor_tensor(
                                    out=c3[:, :, 1:L], in0=h3[:, :, 0:L - 1], scalar=cw[:, 0:1], in1=c3[:, :, 1:L],
                                    op0=ALU.mult, op1=ALU.add), reads=hbufs + [fcw_b, cu_bs[half]], writes=[cu_bs[half]])
                                E.op("dve", lambda h: h.scalar_tensor_tensor(
                                    out=c3[:, :, 0:L - 1], in0=h3[:, :, 1:L], scalar=cw[:, 2:3], in1=c3[:, :, 0:L - 1],
                                    op0=ALU.mult, op1=ALU.add), reads=hbufs + [fcw_b, cu_bs[half]], writes=[cu_bs[half]])
                            else:
                                i0 = 1 if seq_start else 0
                                i1 = n - 1 if seq_end else n
                                E.op("dve", lambda h: h.scalar_tensor_tensor(
                                    out=cu[:, half, i0:n], in0=hp[:, i0:n], scalar=cw[:, 0:1], in1=cu[:, half, i0:n],
                                    op0=ALU.mult, op1=ALU.add), reads=hbufs + [fcw_b, cu_bs[half]], writes=[cu_bs[half]])
                                E.op("dve", lambda h: h.scalar_tensor_tensor(
                                    out=cu[:, half, 0:i1], in0=hp[:, 2:i1 + 2], scalar=cw[:, 2:3], in1=cu[:, half, 0:i1],
                                    op0=ALU.mult, op1=ALU.add), reads=hbufs + [fcw_b, cu_bs[half]], writes=[cu_bs[half]])
                        E.op("act", lambda h: h.activation(out=cu[:, 0, :n], in_=cu[:, 0, :n], func=AF.Gelu),
                             reads=[cu_bs[0]], writes=[cu_bs[0]])
                        E.op("pool", lambda h: h.tensor_tensor(out=Gb[:, j, :n], in0=cu[:, 0, :n], in1=cu[:, 1, :n], op=ALU.mult),
                             reads=[cu_bs[0], cu_bs[1]], writes=[Gb_b])
                for piece in range(n // 512):
                    tsl = slice(piece * 512, (piece + 1) * 512)
                    for mp in range(4):
                        slot = mp % 2
                        E.dma("pool", wd[:, slot], dnv[:, :, mp * 256:(mp + 1) * 256], writes=[wd_bs[slot]])
                        for mm_ in range(2):
                            m = mp * 2 + mm_
                            pbk = m % 2
                            mm_group(bank(pbk), [PSB[pbk]], [(wd[:, slot, j, mm_ * 128:(mm_ + 1) * 128], Gb[:, j, tsl],
                                                              [wd_bs[slot], Gb_b]) for j in range(22)])
                            E.op("act", lambda h: h.activation(out=O[:, m, :], in_=bank(pbk), func=AF.Copy),
                                 reads=[PSB[pbk]], writes=[O_b])
                    epilogue(O, O_b, 512, xsrc, xsrc_b, xdst, xdst_b, a + piece * 512, T, l, 1, v,
                             None if nxt is None else nxt(a + piece * 512), wk)
            A.release(mk_f)

        GP = dict(name="p", nseq=NPSEQ, L=SEQ, v=0, sample=False, xin=xp_d, kout=kout_d, vout=vout_d, y=yp_d, states=True)
        gstage = [stage]
        if stage >= 20:
            stage = 99
        run_group(GP)
        stage = gstage[0] - 20 if 20 <= gstage[0] < 40 else stage
        if gstage[0] >= 20:
            GS = dict(name="s", nseq=1, L=DEC_SEQ, v=1, sample=True, xin=xs_d, kout=None, vout=None, y=ys_d, states=False)
            run_group(GS)
        E.finish()
    return P


def _fm(x2d):
    T, F = x2d.shape
    return np.ascontiguousarray(x2d.reshape(T, F // 128, 128).transpose(2, 1, 0).reshape(128, -1))


def _unfm(a, T):
    return np.ascontiguousarray(a.reshape(128, 8, T).transpose(2, 1, 0).reshape(T, 1024))


def _wl(w):
    K, N = w.shape
    return np.ascontiguousarray(w.reshape(K // 128, 128, N).transpose(1, 0, 2).reshape(128, -1))


_PROG_CACHE = {}
_CONST_CACHE = {}
DBG = None


def _consts():
    if _CONST_CACHE:
        return _CONST_CACHE
    import ml_dtypes
    c = {}
    c["ident_bf"] = np.eye(128, dtype=np.float32).astype(ml_dtypes.bfloat16)
    c["ones_bf"] = np.ones((128, 128), dtype=np.float32).astype(ml_dtypes.bfloat16)
    sw = np.zeros((128, 128), np.float32)
    for m in range(128):
        sw[(m + 64) % 128, m] = 1.0
    c["swap_bf"] = sw.astype(ml_dtypes.bfloat16)
    for L in (SEQ, DEC_SEQ):
        fwd, inv = dft_tables(L)
        nsc, nfc = L // 128, (2 * L + 128) // 128
        f3 = fwd.reshape(128, nsc, nfc, 128).transpose(2, 0, 1, 3).reshape(nfc, 128, nsc * 128)
        c["dft_fwd_%d" % L] = np.ascontiguousarray(f3)
        nt = min(L, 512)
        ntb = max(1, L // 512)
        i3 = inv.reshape(128, nfc, ntb, nt).transpose(2, 0, 1, 3).reshape(ntb, 128, nfc * nt)
        c["dft_inv_%d" % L] = np.ascontiguousarray(i3)
        zT, tcol = hyena_pos_tables(L)
        c["hy_zT_%d" % L] = zT
        c["hy_tcol_%d" % L] = tcol
    selm = np.zeros((40, 16, 128), np.float32)
    for ridx in range(16):
        selm[(ridx // 8) * 32 + ridx % 8, ridx, :] = 1.0
    c["sel40"] = selm.reshape(40, -1)
    sp = np.arange(128)[:, None]
    tp = np.arange(512)[None, :]
    mk = np.zeros((128, 2, 4, 512), np.float32)
    for o in range(4):
        mk[:, 0, o, :] = (sp + 128 * o <= tp)
        mk[:, 1, o, :] = (sp + 128 * o >= tp)
    c["masks"] = mk.reshape(128, -1).astype(ml_dtypes.bfloat16)
    c["ident_f32"] = np.eye(128, dtype=np.float32)
    cosT, sinT = rope_tables(DEC_SEQ)
    c["rope_cos"] = cosT
    c["rope_sin"] = sinT
    c["hy_delta"] = np.ascontiguousarray(np.broadcast_to(np.abs(np.linspace(HY_MIN_DECAY, HY_MAX_DECAY, 512, dtype=np.float32)).reshape(1, 512), (128, 512)))
    _CONST_CACHE.update(c)
    return _CONST_CACHE


def kernel(**inp):
    global DBG
    inp = {k: np.asarray(v) for k, v in inp.items()}
    stage = int(os.environ.get("KSTAGE", "99"))
    if stage not in _PROG_CACHE:
        _PROG_CACHE[stage] = build_program(stage)
    P = _PROG_CACHE[stage]
    TP = NPSEQ * SEQ
    B = inp["x_prompt"].shape[0]
    f32 = np.float32

    sh = dict(_consts())
    sh["w_mod"] = np.stack([_wl(inp["w_mod"][l]) for l in range(2)], axis=0)
    sh["b_mod"] = np.ascontiguousarray(inp["b_mod"].reshape(2, 48, 128).transpose(2, 0, 1).reshape(128, -1))
    norms = np.stack([inp["norm_mix_pre"], inp["norm_mix_post"], inp["norm_ffn_pre"], inp["norm_ffn_post"]], 0)
    sh["norms"] = np.ascontiguousarray(norms.reshape(4, 2, 8, 128).transpose(3, 0, 1, 2).reshape(128, -1))
    sh["ah_w_in"] = _wl(inp["ah_w_in"][0])
    sh["ah_w_out"] = _wl(inp["ah_w_out"][0])
    sh["qk_norm"] = np.ascontiguousarray(np.stack([inp["attn_q_norm"][0], inp["attn_k_norm"][0]], axis=1))
    hc = np.concatenate([inp["hy_conv_w"][0], inp["hy_conv_b"][0][None]], axis=0)
    sh["hy_conv"] = np.ascontiguousarray(hc.reshape(4, 12, 128).transpose(2, 1, 0).reshape(128, -1))
    sh["hy_w1"] = np.ascontiguousarray(inp["hy_w1"][0])
    sh["hy_w2"] = np.ascontiguousarray(inp["hy_w2"][0])
    sh["hy_w3"] = np.ascontiguousarray(inp["hy_w3"][0])
    sh["hy_b12f"] = np.ascontiguousarray(np.stack([inp["hy_b1"][0], inp["hy_b2"][0], inp["hy_sin_freq"][0, 0],
                                                   inp["hy_sin_freq"][0, 1]], axis=1))
    sh["hy_b3"] = np.ascontiguousarray(np.broadcast_to(inp["hy_b3"][0].reshape(1, 1024), (128, 1024)))
    sh["hy_skip"] = np.ascontiguousarray(np.broadcast_to(inp["hy_skip"][0].reshape(1, 512), (128, 512)))
    sh["ffn_w_up"] = np.stack([_wl(inp["ffn_w_up"][l]) for l in range(2)], axis=0)
    sh["ffn_w_down"] = np.stack([_wl(inp["ffn_w_down"][l]) for l in range(2)], axis=0)
    fc = np.concatenate([inp["ffn_conv_w"], inp["ffn_conv_b"][:, None, :]], axis=1)
    sh["ffn_conv"] = np.ascontiguousarray(fc.reshape(2, 4, 44, 128).transpose(3, 0, 2, 1).reshape(128, -1))

    mw = inp["ml_w_in"][0]
    sh["ml_w_in"] = _wl(np.ascontiguousarray(mw[:, :4096]))
    wgp = np.zeros((1024, 80), f32)
    bgp = np.zeros((40, 2), f32)
    bg = inp["ml_b_gates"][0]
    for kind in range(4):
        base = (kind // 2) * 40 + (kind % 2) * 32
        wgp[:, base:base + 8] = mw[:, 4096 + kind * 8: 4096 + kind * 8 + 8]
        bgp[(kind % 2) * 32:(kind % 2) * 32 + 8, kind // 2] = bg[kind * 8:kind * 8 + 8]
    sh["ml_w_g"] = _wl(wgp)
    sh["ml_b_g"] = bgp
    mc = np.concatenate([inp["ml_conv_w"][0], inp["ml_conv_b"][0][None]], axis=0)
    sh["ml_conv"] = np.ascontiguousarray(mc.reshape(4, 16, 128).transpose(2, 1, 0).reshape(128, -1))
    sh["ml_head_norm"] = np.ascontiguousarray(inp["ml_head_norm"][0].reshape(8, 128).T)
    sh["ml_w_out"] = _wl(inp["ml_w_out"][0])

    in_maps = []
    for r in range(NCORES):
        b = r // 4
        m = dict(sh)
        m["xp"] = _fm(inp["x_prompt"][NPSEQ * r:NPSEQ * (r + 1)].reshape(TP, D))
        m["xs"] = _fm(inp["x_sample"][b])
        cvec = np.stack([inp["c_ctx"], inp["c"][b]], axis=0)
        m["cvec"] = np.ascontiguousarray(cvec.reshape(2, 8, 128).transpose(2, 1, 0).reshape(128, -1))
        ck = inp["cache_attn_k"][b, 0]
        m["cache_kT"] = np.ascontiguousarray(ck.transpose(2, 1, 0).reshape(128, -1))
        cvv = inp["cache_attn_v"][b, 0]
        m["cache_vt"] = np.ascontiguousarray(cvv.reshape(4, 128, 256).transpose(1, 0, 2).reshape(128, -1))
        m0 = np.zeros((40, 1), f32)
        sm = inp["state_mlstm_m"][b, 0]
        m0[0:8, 0] = sm[0]
        m0[32:40, 0] = sm[1]
        m["ml_m0"] = m0
        sC = inp["state_mlstm_C"][b, 0].reshape(16, 128, 128)
        m["ml_c0t"] = np.ascontiguousarray(sC.transpose(2, 0, 1).reshape(128, -1))
        sn = inp["state_mlstm_n"][b, 0].reshape(16, 128)
        m["ml_n0b"] = np.ascontiguousarray(np.broadcast_to(sn.T[:, :, None], (128, 16, 128)).reshape(128, -1))
        in_maps.append({k: np.ascontiguousarray(m[k]) for k in P.ins})
    res = run_bass_kernel_spmd(P.nc, in_maps, core_ids=list(range(NCORES)))
    R = res.results
    DBG = R

    y_prompt = np.zeros((B, SEQ, D), f32)
    y_sample = np.zeros((2, DEC_SEQ, D), f32)
    new_k = np.zeros((B, 1, SEQ, 2, 128), f32)
    new_v = np.zeros((B, 1, SEQ, 2, 128), f32)
    new_C = np.zeros((B, 1, 2, 8, 128, 128), f32)
    new_n = np.zeros((B, 1, 2, 8, 128), f32)
    new_m = np.zeros((B, 1, 2, 8), f32)
    for r in range(NCORES):
        o = R[r]
        sl = slice(NPSEQ * r, NPSEQ * (r + 1))
        new_k[sl, 0] = o["k_out"].reshape(128, 2, NPSEQ, SEQ).transpose(2, 3, 1, 0)
        new_v[sl, 0] = o["v_out"].reshape(128, TP // 128, 2, 128).transpose(1, 0, 2, 3).reshape(NPSEQ, SEQ, 2, 128)
        y_prompt[sl] = _unfm(o["yp"], TP).reshape(NPSEQ, SEQ, D)
        new_C[sl, 0] = o["c_out"].reshape(128, NPSEQ, 2, 8, 128).transpose(1, 2, 3, 0, 4)
        new_n[sl, 0] = o["n_out"].reshape(NPSEQ, 2, 8, 128)
        mo = o["m_out"]
        new_m[sl, 0, 0] = mo[0:8].T
        new_m[sl, 0, 1] = mo[32:40].T
        if r % 4 == 0:
            y_sample[r // 4] = _unfm(o["ys"], DEC_SEQ)
    return (y_prompt, y_sample, new_k, new_v, new_C, new_n, new_m)
```

```python
import math
import os
import numpy as np
import concourse.bass as bass
import concourse.mybir as mybir
from concourse.bass_utils import run_bass_kernel_spmd

AF = mybir.ActivationFunctionType
ALU = mybir.AluOpType
AX = mybir.AxisListType
F32 = mybir.dt.float32
BF16 = mybir.dt.bfloat16

D = 1024
NCORES = 8
SEQ = 256
NPSEQ = 4
DEC_SEQ = 2048
PAST = 512
DFF = 2816
EPS = 1e-6
HY_MAX_DECAY = math.log(1e-2) / 0.3
HY_MIN_DECAY = math.log(1e-2) / 1.5

SERIAL = bool(int(os.environ.get('KSERIAL', '0')))
SELF_SYNC = True


class Buf:
    __slots__ = ("name", "last_w", "readers", "excl")

    def __init__(self, name, excl=False):
        self.name = name
        self.last_w = None
        self.readers = []
        self.excl = excl


class Emit:
    ENG = ("pe", "act", "dve", "pool", "sp")

    def __init__(self, nc, n_dma_sems=40):
        self.nc = nc
        self.h = {"pe": nc.tensor, "act": nc.scalar, "dve": nc.vector, "pool": nc.gpsimd, "sp": nc.sync}
        self.ops = {e: [] for e in self.ENG}
        self.cnt = {e: 0 for e in self.ENG}
        self.seen = {e: {} for e in self.ENG}
        self.esem = {}
        self.dsem = []
        self.dval = []
        self.n_dma_sems = n_dma_sems
        self.dnext = 0
        self.sem_ctx = []
        self.dma_tokens = []

    def open_sems(self, stack):
        for e in self.ENG:
            self.esem[e] = stack.enter_context(self.nc.semaphore("c_" + e))
        for i in range(self.n_dma_sems):
            self.dsem.append(stack.enter_context(self.nc.semaphore("d%d" % i)))
            self.dval.append(0)

    def _deps(self, reads, writes):
        deps = []
        for b in reads:
            if b.last_w is not None:
                deps.append(b.last_w)
        for b in writes:
            if b.last_w is not None:
                deps.append(b.last_w)
            deps.extend(b.readers)
        return deps

    def _waits(self, eng, deps, pe_accum=False):
        need = {}
        for d in deps:
            kind, src, val = d
            if kind == "e" and src == eng and (not SELF_SYNC or (eng == "pe" and pe_accum)):
                continue
            key = (kind, src)
            if self.seen[eng].get(key, 0) >= val:
                continue
            if need.get(key, 0) < val:
                need[key] = val
        waits = []
        for (kind, src), val in need.items():
            self.seen[eng][(kind, src)] = val
            sem = self.esem[src] if kind == "e" else self.dsem[src]
            waits.append((sem, val))
        return waits

    def _commit(self, tok, reads, writes):
        for b in writes:
            b.last_w = tok
            b.readers = []
        for b in reads:
            if b in writes:
                continue
            if tok[0] == "e":
                b.readers = [r for r in b.readers if not (r[0] == "e" and r[1] == tok[1])]
            b.readers.append(tok)

    def op(self, eng, fn, reads=(), writes=(), pe_accum=False):
        ex = [b for b in reads if b.excl and b not in writes]
        if ex:
            reads = [b for b in reads if not b.excl]
            writes = list(writes) + ex
        deps = self._deps(reads, writes)
        if SERIAL:
            for e in self.ENG:
                if self.cnt[e] > 0 and not (e == eng and pe_accum):
                    deps.append(("e", e, self.cnt[e]))
            for i in range(self.n_dma_sems):
                if self.dval[i] > 0:
                    deps.append(("d", i, self.dval[i]))
        waits = self._waits(eng, deps, pe_accum)
        self.cnt[eng] += 1
        tok = ("e", eng, self.cnt[eng])
        h = self.h[eng]
        for s_, v_ in waits:
            h.wait_ge(s_, v_)
        fn(h).then_inc(self.esem[eng], 1)
        self._commit(tok, reads, writes)
        return tok

    def dma(self, eng, out, in_, reads=(), writes=(), **kw):
        deps = self._deps(reads, writes)
        i = self.dnext
        self.dnext = (self.dnext + 1) % self.n_dma_sems
        if self.dval[i] > 0:
            deps.append(("d", i, self.dval[i]))
        waits = self._waits(eng, deps)
        self.dval[i] += 16
        tok = ("d", i, self.dval[i])
        h = self.h[eng]
        for s_, v_ in waits:
            h.wait_ge(s_, v_)
        h.dma_start(out=out, in_=in_, **kw).then_inc(self.dsem[i], 16)
        self._commit(tok, reads, writes)
        return tok

    def finish(self):
        h = self.h["sp"]
        for i in range(self.n_dma_sems):
            if self.dval[i] > 0:
                h.wait_ge(self.dsem[i], self.dval[i])
        for e in self.ENG:
            if e != "sp" and self.cnt[e] > 0:
                h.wait_ge(self.esem[e], self.cnt[e])


class Arena:
    def __init__(self, nc, lo, hi):
        self.nc, self.lo, self.hi, self.p = nc, lo, hi, lo
        self.n = 0
        self.live = []

    def alloc(self, name, shape, dtype, nbufs=1, top=False):
        esz = 4 if dtype == F32 else 2
        per = 1
        for s in shape[1:]:
            per *= s
        nbytes = (per * esz + 31) // 32 * 32
        assert self.p + nbytes <= self.hi, "SBUF arena overflow at %s (%d + %d > %d)" % (name, self.p, nbytes, self.hi)
        if top:
            self.hi -= nbytes
            off = self.hi
        else:
            off = self.p
            self.p += nbytes
        self.n += 1
        t = self.nc.alloc_sbuf_tensor_at("%s_%d" % (name, self.n), list(shape), dtype, offset=off)
        bufs = [Buf("%s.%d" % (name, i)) for i in range(nbufs)]
        keep = []
        for (l, h, bs) in self.live:
            if l < off + nbytes and off < h:
                for ob in bs:
                    for nb in bufs:
                        if ob.last_w is not None:
                            nb.readers.append(ob.last_w)
                        nb.readers.extend(ob.readers)
                if l < off or h > off + nbytes:
                    keep.append((l, h, bs))
            else:
                keep.append((l, h, bs))
        keep.append((off, off + nbytes, bufs))
        self.live = keep
        ap = t.ap()
        return (ap, bufs[0]) if nbufs == 1 else (ap, bufs)

    def mark_top(self):
        return self.hi

    def release_top(self, m):
        self.hi = m

    def mark(self):
        return self.p

    def release(self, m):
        self.p = m


def _bf16(a):
    import ml_dtypes
    return np.ascontiguousarray(a.astype(np.float32)).astype(ml_dtypes.bfloat16)


def dft_tables(L):
    n = 2 * L
    s = np.arange(L, dtype=np.float64)
    fr = np.arange(L + 128, dtype=np.float64)
    fi = np.arange(L, dtype=np.float64)
    FR = np.cos(2 * np.pi * np.outer(s, fr) / n)
    FR[:, L + 1:] = 0.0
    FI = -np.sin(2 * np.pi * np.outer(s, fi) / n)
    cf = np.full(L + 128, 2.0)
    cf[0] = 1.0
    cf[L] = 1.0
    cf[L + 1:] = 0.0
    IR = (cf[:, None] / n) * np.cos(2 * np.pi * np.outer(fr, s) / n)
    II = -(2.0 / n) * np.sin(2 * np.pi * np.outer(fi, s) / n)
    fwd = np.concatenate([FR, FI], axis=1)
    inv = np.concatenate([IR, II], axis=0)
    nsc = L // 128
    nfc = (2 * L + 128) // 128
    fwd_l = fwd.reshape(nsc, 128, nfc * 128).transpose(1, 0, 2)
    inv_l = inv.reshape(nfc, 128, L).transpose(1, 0, 2)
    return _bf16(fwd_l.reshape(128, -1)), _bf16(inv_l.reshape(128, -1))


def hyena_pos_tables(L):
    t = np.linspace(0.0, 1.0, L, dtype=np.float32)
    bands = np.arange(1, 9, dtype=np.float32)
    ang = 2.0 * np.pi * t[:, None] * bands
    z = np.concatenate([t[:, None], np.cos(ang), np.sin(ang)], axis=-1).astype(np.float32)
    zT = np.ascontiguousarray(z.T)
    tcol = np.ascontiguousarray(t.reshape(L // 128, 128).T)
    return zT, tcol


def rope_tables(L):
    rows = L // 64
    row = np.repeat(np.arange(rows, dtype=np.float32), 64)
    col = np.tile(np.arange(64, dtype=np.float32), rows)
    n_freq = 32
    inv = (10000.0 ** (-np.arange(n_freq, dtype=np.float32) / n_freq)).astype(np.float32)
    ang = np.concatenate([row[:, None] * inv, col[:, None] * inv], axis=-1)
    cos, sin = np.cos(ang).astype(np.float32), np.sin(ang).astype(np.float32)
    cosT = np.concatenate([cos.T, cos.T], axis=0)
    sinT = np.concatenate([-sin.T, sin.T], axis=0)
    return np.ascontiguousarray(cosT), np.ascontiguousarray(sinT)


class Prog:
    def __init__(self, stage):
        self.stage = stage
        self.nc = bass.Bass("TRN2", target_bir_lowering=False)
        self.ins = {}
        self.outs = {}

    def din(self, name, shape, dtype=F32):
        t = self.nc.dram_tensor(name, list(shape), dtype, kind="ExternalInput")
        self.ins[name] = t
        return t.ap()

    def dout(self, name, shape, dtype=F32):
        t = self.nc.dram_tensor(name, list(shape), dtype, kind="ExternalOutput")
        self.outs[name] = t
        return t.ap()


TWO_PI = 2.0 * math.pi
MAGIC = 12582912.0


def build_program(stage=99):
    P = Prog(stage)
    nc = P.nc
    TP = NPSEQ * SEQ
    TS = DEC_SEQ

    xp_d = P.din("xp", [128, 8 * TP])
    xs_d = P.din("xs", [128, 8 * TS])
    cvec_d = P.din("cvec", [128, 8 * 2])
    wmod_d = P.din("w_mod", [2, 128, 8 * 6144])
    bmod_d = P.din("b_mod", [128, 2 * 48])
    norms_d = P.din("norms", [128, 4 * 2 * 8])
    ahwin_d = P.din("ah_w_in", [128, 8 * 2560])
    ahwout_d = P.din("ah_w_out", [128, 8 * 1024])
    qkn_d = P.din("qk_norm", [128, 2])
    idb_d = P.din("ident_bf", [128, 128], BF16)
    ones_d = P.din("ones_bf", [128, 128], BF16)
    swap_d = P.din("swap_bf", [128, 128], BF16)
    hcw_d = P.din("hy_conv", [128, 12 * 4])
    hyw1_d = P.din("hy_w1", [17, 64])
    hyw2_d = P.din("hy_w2", [64, 64])
    hyw3_d = P.din("hy_w3", [64, 1024])
    hyb12_d = P.din("hy_b12f", [64, 4])
    hyb3_d = P.din("hy_b3", [128, 1024])
    hyskip_d = P.din("hy_skip", [128, 512])
    hydelta_d = P.din("hy_delta", [128, 512])
    ffnup_d = P.din("ffn_w_up", [2, 128, 8 * 5632])
    ffndn_d = P.din("ffn_w_down", [2, 128, 22 * 1024])
    ffncw_d = P.din("ffn_conv", [128, 2 * 44 * 4])
    tabs = {}
    for L in (SEQ, DEC_SEQ):
        nsc, nfc = L // 128, (2 * L + 128) // 128
        tabs[L] = dict(
            fwd=P.din("dft_fwd_%d" % L, [nfc, 128, nsc * 128], BF16),
            inv=P.din("dft_inv_%d" % L, [max(1, L // 512), 128, nfc * min(L, 512)], BF16),
            zT=P.din("hy_zT_%d" % L, [17, L]),
            tcol=P.din("hy_tcol_%d" % L, [128, L // 128]))
    mlwin_d = P.din("ml_w_in", [128, 8 * 4096])
    mlwg_d = P.din("ml_w_g", [128, 8 * 80])
    mlbg_d = P.din("ml_b_g", [40, 2])
    mlcw_d = P.din("ml_conv", [128, 16 * 4])
    mlhn_d = P.din("ml_head_norm", [128, 8])
    mlwout_d = P.din("ml_w_out", [128, 8 * 1024])
    sel_d = P.din("sel40", [40, 16 * 128])
    masks_d = P.din("masks", [128, 2 * 4 * 512], BF16)
    id32_d = P.din("ident_f32", [128, 128])
    m0_d = P.din("ml_m0", [40, 1])
    c0t_d = P.din("ml_c0t", [128, 16 * 128])
    n0b_d = P.din("ml_n0b", [128, 16 * 128])
    ropec_d = P.din("rope_cos", [128, TS])
    ropes_d = P.din("rope_sin", [128, TS])
    ckT_d = P.din("cache_kT", [128, 2 * PAST])
    cvt_d = P.din("cache_vt", [128, (PAST // 128) * 256])

    kout_d = P.dout("k_out", [128, 2 * TP])
    vout_d = P.dout("v_out", [128, (TP // 128) * 256])
    cst_d = P.dout("c_out", [128, NPSEQ * 16 * 128])
    nst_d = P.dout("n_out", [1, NPSEQ * 16 * 128])
    mst_d = P.dout("m_out", [40, NPSEQ])
    yp_d = P.dout("yp", [128, 8 * TP])
    ys_d = P.dout("ys", [128, 8 * TS])

    from contextlib import ExitStack
    with ExitStack() as stack:
        E = Emit(nc)
        E.open_sems(stack)
        A = Arena(nc, 16640, 229376 - 1024)
        ps_t = nc.alloc_psum_tensor("ps_all", [128, 7 * 512], F32)
        PS = ps_t.ap()
        PSB = [Buf("ps%d" % i, excl=True) for i in range(7)]
        psbf_t = nc.alloc_psum_tensor("ps_bf", [128, 1024], BF16)
        PSbf = {4: psbf_t.ap()[:, 0:128], 5: psbf_t.ap()[:, 128:256]}
        _pb = Buf("psbf", excl=True)
        PSbf_b = {4: _pb, 5: _pb}

        def bank(i, n=512, off=0):
            return PS[:, i * 512 + off: i * 512 + off + n]

        def mm_group(out_ap, out_bufs, terms):
            n = len(terms)
            for i, (l_ap, r_ap, rb) in enumerate(terms):
                E.op("pe", lambda h: h.matmul(out_ap, lhsT=l_ap, rhs=r_ap, start=(i == 0), stop=(i == n - 1)),
                     reads=rb, writes=out_bufs, pe_accum=True)

        dram_bufs = {}

        def dscratch(name, shape, dtype=F32):
            t = nc.dram_tensor(name, list(shape), dtype)
            b = Buf(name)
            dram_bufs[name] = b
            return t.ap(), b

        WB = {}

        def conv_weight(name, src_ap, shape):
            t_ap, t_b = dscratch(name + "_bf", shape, BF16)
            E.dma("pool", t_ap, src_ap, writes=[t_b])
            WB[name] = (t_ap, t_b)

        conv_weight("ah_w_in", ahwin_d, [128, 8 * 2560])
        conv_weight("ah_w_out", ahwout_d, [128, 8 * 1024])
        conv_weight("ffn_up0", ffnup_d[0], [128, 8 * 5632])
        conv_weight("ffn_dn0", ffndn_d[0], [128, 22 * 1024])
        conv_weight("ml_w_in", mlwin_d, [128, 8 * 4096])
        conv_weight("ml_w_out", mlwout_d, [128, 8 * 1024])
        conv_weight("ffn_up1", ffnup_d[1], [128, 8 * 5632])
        conv_weight("ffn_dn1", ffndn_d[1], [128, 22 * 1024])

        def const(name, shape, dtype, src, eng="sp"):
            ap, b = A.alloc(name, shape, dtype)
            E.dma(eng, ap, src, writes=[b])
            return ap, b

        ident, ident_b = const("ident", [128, 128], BF16, idb_d)
        ones, ones_b = const("ones", [128, 128], BF16, ones_d)
        swp, swp_b = const("swap", [128, 128], BF16, swap_d)
        epsc, epsc_b = A.alloc("epsc", [128, 2], F32)
        E.op("dve", lambda h: h.memset(epsc, EPS), writes=[epsc_b])
        cv, cv_b = const("cvec", [128, 8, 2], F32, cvec_d.rearrange("p (j v) -> p j v", v=2))
        bm, bm_b = const("bmod", [128, 2, 48], F32, bmod_d.rearrange("p (l m) -> p l m", l=2))
        nrm, nrm_b = const("norms", [128, 4, 2, 8], F32, norms_d.rearrange("p (w l j) -> p w l j", w=4, l=2))
        qkn, qkn_b = const("qkn", [128, 2], F32, qkn_d)
        hcw, hcw_b = const("hcw", [128, 12, 4], F32, hcw_d.rearrange("p (c k) -> p c k", k=4))
        fcw, fcw_b = const("fcw", [128, 2, 44, 4], F32, ffncw_d.rearrange("p (l c k) -> p l c k", l=2, k=4))

        mcw, mcw_b = const("mcw", [128, 16, 4], F32, mlcw_d.rearrange("p (c k) -> p c k", k=4))
        mhn, mhn_b = const("mhn", [128, 8], F32, mlhn_d)
        mbg, mbg_b = const("mbg", [40, 2], F32, mlbg_d)
        sel, sel_b = const("sel", [40, 16, 128], F32, sel_d.rearrange("p (r m) -> p r m", m=128))
        msk, msk_b = const("msk", [128, 2, 4, 512], BF16, masks_d.rearrange("p (d o t) -> p d o t", d=2, o=4))
        id32, id32_b = const("id32", [128, 128], F32, id32_d)
        onec, onec_b = A.alloc("onec", [128, 2], F32)
        E.op("dve", lambda h: h.memset(onec, 1.0), writes=[onec_b])
        sc, sc_b = A.alloc("silu_c", [128, 8, 2], BF16)
        sig, sig_b = A.alloc("sig_c", [128, 8, 2], F32)
        E.op("act", lambda h: h.activation(out=sig, in_=cv, func=AF.Sigmoid), reads=[cv_b], writes=[sig_b])
        E.op("dve", lambda h: h.tensor_tensor(out=sc, in0=cv, in1=sig, op=ALU.mult), reads=[cv_b, sig_b], writes=[sc_b])
        MOD, MOD_b = A.alloc("mod", [128, 2, 48, 2], F32)
        mk = A.mark()
        wm, wm_bs = A.alloc("wmod_st", [128, 2, 8, 512], BF16, nbufs=2)
        it = 0
        for l in range(2):
            for cg in range(12):
                slot = it % 2
                it += 1
                src = wmod_d[l].rearrange("p (kc n) -> p kc n", kc=8)[:, :, cg * 512:(cg + 1) * 512]
                E.dma("pool", wm[:, slot], src, writes=[wm_bs[slot]])
                for mm in range(4):
                    m = cg * 4 + mm
                    mm_group(bank(l, 2, 2 * m), [PSB[l]],
                             [(wm[:, slot, kc, mm * 128:(mm + 1) * 128], sc[:, kc, :], [wm_bs[slot], sc_b])
                              for kc in range(8)])
            E.op("dve", lambda h: h.tensor_tensor(
                out=MOD[:, l], in0=bank(l, 96).rearrange("p (m v) -> p m v", v=2),
                in1=bm[:, l, :].unsqueeze(2).to_broadcast([128, 48, 2]), op=ALU.add),
                reads=[PSB[l], bm_b], writes=[MOD_b])
        A.release(mk)

        COEF, COEF_b = A.alloc("coef", [128, 2, 2, 3, 8, 2], F32)
        for l in range(2):
            for part in range(2):
                sh, scl, gt = 3 * part, 3 * part + 1, 3 * part + 2
                gpre = nrm[:, 2 * part, l, :].unsqueeze(2).to_broadcast([128, 8, 2])
                gpost = nrm[:, 2 * part + 1, l, :].unsqueeze(2).to_broadcast([128, 8, 2])
                E.op("dve", lambda h: h.scalar_tensor_tensor(
                    out=COEF[:, l, part, 0], in0=MOD[:, l, scl * 8:(scl + 1) * 8, :], scalar=1.0, in1=gpre,
                    op0=ALU.add, op1=ALU.mult), reads=[MOD_b, nrm_b], writes=[COEF_b])
                E.op("dve", lambda h: h.tensor_copy(
                    out=COEF[:, l, part, 1], in_=MOD[:, l, sh * 8:(sh + 1) * 8, :]), reads=[MOD_b], writes=[COEF_b])
                E.op("dve", lambda h: h.tensor_tensor(
                    out=COEF[:, l, part, 2], in0=MOD[:, l, gt * 8:(gt + 1) * 8, :], in1=gpost, op=ALU.mult),
                    reads=[MOD_b, nrm_b], writes=[COEF_b])

        def sumsq_rstd(src_chunks, src_bufs, n, dim, rstd, rstd_b, sq, sq_b, psb):
            nch = len(src_chunks)
            for j, s_ in enumerate(src_chunks):
                E.op("act", lambda h: h.activation(out=sq[:, j, :n], in_=s_, func=AF.Square),
                     reads=[src_bufs[j]], writes=[sq_b])
            mm_group(bank(psb, n), [PSB[psb]], [(ones, sq[:, j, :n], [ones_b, sq_b]) for j in range(nch)])
            E.op("act", lambda h: h.activation(out=rstd[:, :n], in_=bank(psb, n), func=AF.Sqrt, bias=epsc[:, 0:1],
                                               scale=1.0 / dim), reads=[PSB[psb], epsc_b], writes=[rstd_b])
            E.op("dve", lambda h: h.reciprocal(out=rstd[:, :n], in_=rstd[:, :n]), reads=[rstd_b], writes=[rstd_b])

        def modulate_block(xb, xb_buf, n, l, part, v, dst_fn, dst_buf, sq, sq_b, rstd, rstd_b, tmp, tmp_b):
            sumsq_rstd([xb[:, j, :n] for j in range(8)], [xb_buf] * 8, n, D, rstd, rstd_b, sq, sq_b, 6)
            for j in range(8):
                E.op("dve", lambda h: h.scalar_tensor_tensor(
                    out=tmp[:, :n], in0=xb[:, j, :n], scalar=COEF[:, l, part, 0, j, v:v + 1], in1=rstd[:, :n],
                    op0=ALU.mult, op1=ALU.mult), reads=[xb_buf, COEF_b, rstd_b], writes=[tmp_b])
                E.op("act", lambda h: h.activation(
                    out=dst_fn(j), in_=tmp[:, :n], func=AF.Identity, bias=COEF[:, l, part, 1, j, v:v + 1], scale=1.0),
                    reads=[tmp_b, COEF_b], writes=[dst_buf])

        def epilogue(O, O_b, n, xsrc, xsrc_b, xdst, xdst_b, tok0, T, l, part, v, nxt, wk):
            (sq, sq_b, rstd, rstd_b, tmp, tmp_b, xb, xb_b) = wk
            E.dma("sp", xb[:, :, :n], xsrc.rearrange("p (j t) -> p j t", j=8)[:, :, tok0:tok0 + n],
                  reads=[xsrc_b], writes=[xb_b])
            sumsq_rstd([O[:, j, :n] for j in range(8)], [O_b] * 8, n, D, rstd, rstd_b, sq, sq_b, 6)
            for j in range(8):
                E.op("dve", lambda h: h.scalar_tensor_tensor(
                    out=tmp[:, :n], in0=O[:, j, :n], scalar=COEF[:, l, part, 2, j, v:v + 1], in1=rstd[:, :n],
                    op0=ALU.mult, op1=ALU.mult), reads=[O_b, COEF_b, rstd_b], writes=[tmp_b])
                E.op("pool", lambda h: h.tensor_tensor(out=xb[:, j, :n], in0=xb[:, j, :n], in1=tmp[:, :n], op=ALU.add),
                     reads=[tmp_b, xb_b], writes=[xb_b])
            E.dma("sp", xdst.rearrange("p (j t) -> p j t", j=8)[:, :, tok0:tok0 + n], xb[:, :, :n],
                  reads=[xb_b], writes=[xdst_b])
            if nxt is not None:
                l2, part2, dst_fn, dst_buf = nxt
                modulate_block(xb, xb_b, n, l2, part2, v, dst_fn, dst_buf, sq, sq_b, rstd, rstd_b, tmp, tmp_b)

        def hyena_filters(L, keep):
            nsc = L // 128
            GR, GR_b = keep.alloc("GR%d" % L, [128, nsc + 1, 512], BF16, top=True)
            GI, GI_b = keep.alloc("GI%d" % L, [128, nsc, 512], BF16, top=True)
            mk_ = A.mark()
            zT, zT_b = const("zT", [17, L], F32, tabs[L]["zT"])
            tcol, tcol_b = const("tcol", [128, nsc], F32, tabs[L]["tcol"])
            w1, w1_b = const("hw1", [17, 64], F32, hyw1_d)
            w2, w2_b = const("hw2", [64, 64], F32, hyw2_d)
            w3, w3_b = const("hw3", [64, 1024], F32, hyw3_d)
            b12, b12_b = const("hb12", [64, 4], F32, hyb12_d)
            b3, b3_b = const("hb3", [128, 1024], F32, hyb3_d)
            skp, skp_b = const("hskip", [128, 512], F32, hyskip_d)
            dlt, dlt_b = const("hdelta", [128, 512], F32, hydelta_d)
            ntc, ntc_b = A.alloc("ntcol", [128, nsc], F32)
            E.op("dve", lambda h: h.tensor_scalar(out=ntc, in0=tcol, scalar1=-1.0, scalar2=None, op0=ALU.mult),
                 reads=[tcol_b], writes=[ntc_b])
            H1, H1_b = A.alloc("h1", [64, L], F32)
            H2, H2_b = A.alloc("h2", [64, L], F32)
            t1, t1_b = A.alloc("ht1", [64, 512], F32)
            t2, t2_b = A.alloc("ht2", [64, 512], F32)

            def sin_layer(dst, dst_b, w_ap, w_b, src, src_b, bcol, fcol):
                for c0 in range(0, L, 512):
                    n = min(512, L - c0)
                    mm_group(PS[0:64, 0:n], [PSB[0]], [(w_ap, src[:, c0:c0 + n], [w_b, src_b])])
                    E.op("dve", lambda h: h.tensor_scalar(out=t1[:, :n], in0=PS[0:64, 0:n], scalar1=b12[:, bcol:bcol + 1],
                                                          scalar2=b12[:, fcol:fcol + 1], op0=ALU.add, op1=ALU.mult),
                         reads=[PSB[0], b12_b], writes=[t1_b])
                    E.op("dve", lambda h: h.tensor_scalar(out=t2[:, :n], in0=t1[:, :n], scalar1=1.0 / TWO_PI, scalar2=MAGIC,
                                                          op0=ALU.mult, op1=ALU.add), reads=[t1_b], writes=[t2_b])
                    E.op("dve", lambda h: h.tensor_scalar(out=t2[:, :n], in0=t2[:, :n], scalar1=MAGIC, scalar2=-TWO_PI,
                                                          op0=ALU.subtract, op1=ALU.mult), reads=[t2_b], writes=[t2_b])
                    E.op("dve", lambda h: h.tensor_tensor(out=t1[:, :n], in0=t1[:, :n], in1=t2[:, :n], op=ALU.add),
                         reads=[t1_b, t2_b], writes=[t1_b])
                    E.op("act", lambda h: h.activation(out=dst[:, c0:c0 + n], in_=t1[:, :n], func=AF.Sin),
                         reads=[t1_b], writes=[dst_b])

            sin_layer(H1, H1_b, w1, w1_b, zT, zT_b, 0, 2)
            sin_layer(H2, H2_b, w2, w2_b, H1, H1_b, 1, 3)
            GS, GS_b = A.alloc("gs", [128, nsc, 512], BF16)
            GD, GD_b = A.alloc("gd", [128, nsc, 512], BF16)
            Fm, Fm_b = A.alloc("fm", [128, 1024], F32)
            win, win_b = A.alloc("win", [128, 512], F32)
            fs, fs_b = A.alloc("fs", [128, 512], F32)
            for tc in range(nsc):
                for hh in range(2):
                    mm_group(bank(hh), [PSB[hh]], [(H2[:, tc * 128:(tc + 1) * 128], w3[:, hh * 512:(hh + 1) * 512],
                                                    [H2_b, w3_b])])
                    E.op("dve", lambda h: h.tensor_tensor(out=Fm[:, hh * 512:(hh + 1) * 512], in0=bank(hh),
                                                          in1=b3[:, hh * 512:(hh + 1) * 512], op=ALU.add),
                         reads=[PSB[hh], b3_b], writes=[Fm_b])
                E.op("act", lambda h: h.activation(out=win, in_=dlt, func=AF.Exp, scale=ntc[:, tc:tc + 1]),
                     reads=[dlt_b, ntc_b], writes=[win_b])
                E.op("dve", lambda h: h.tensor_tensor(out=fs, in0=Fm[:, 0:512], in1=Fm[:, 512:1024], op=ALU.add),
                     reads=[Fm_b], writes=[fs_b])
                E.op("dve", lambda h: h.tensor_tensor(out=GS[:, tc, :], in0=fs, in1=win, op=ALU.mult),
                     reads=[fs_b, win_b], writes=[GS_b])
                E.op("dve", lambda h: h.tensor_tensor(out=fs, in0=Fm[:, 0:512], in1=Fm[:, 512:1024], op=ALU.subtract),
                     reads=[Fm_b], writes=[fs_b])
                E.op("dve", lambda h: h.tensor_tensor(out=GD[:, tc, :], in0=fs, in1=win, op=ALU.mult),
                     reads=[fs_b, win_b], writes=[GD_b])
            fw, fw_bs = A.alloc("fwst", [128, 2, nsc, 128], BF16, nbufs=2)
            nfc = 2 * nsc + 1
            for fc in range(nfc):
                slot = fc % 2
                E.dma("sp", fw[:, slot], tabs[L]["fwd"][fc].rearrange("p (s f) -> p s f", f=128), writes=[fw_bs[slot]])
                src, src_b = (GS, GS_b) if fc <= nsc else (GD, GD_b)
                pb = fc % 2
                mm_group(bank(pb), [PSB[pb]], [(fw[:, slot, s_, :], src[:, s_, :], [fw_bs[slot], src_b]) for s_ in range(nsc)])
                if fc <= nsc:
                    E.op("dve", lambda h: h.tensor_tensor(out=GR[:, fc, :], in0=bank(pb), in1=skp, op=ALU.add),
                         reads=[PSB[pb], skp_b], writes=[GR_b])
                else:
                    E.op("act", lambda h: h.activation(out=GI[:, fc - nsc - 1, :], in_=bank(pb), func=AF.Copy),
                         reads=[PSB[pb]], writes=[GI_b])
            A.release(mk_)
            return GR, GR_b, GI, GI_b

        def run_group(G):
            gname, nseq, L, v = G["name"], G["nseq"], G["L"], G["v"]
            smp = G["sample"]
            T = nseq * L
            nblk = T // 512
            nkv = L + (PAST if smp else 0)
            X0d, X0d_b = G["xin"], Buf(gname + "_xin")
            X1d, X1d_b = dscratch(gname + "_x1", [128, 8 * T])
            X2d, X2d_b = dscratch(gname + "_x2", [128, 8 * T])
            mk_g = A.mark()
            MIX, MIX_bs = A.alloc(gname + "_mix", [128, 8, T], BF16, nbufs=8)
            mk_m = A.mark()
            mk_top = A.mark_top()
            GR, GR_b, GI, GI_b = hyena_filters(L, A)
            HS, HS_bs = A.alloc(gname + "_hs", [128, 8, T], BF16, nbufs=nblk)
            mk_a = A.mark()
            sq, sq_b = A.alloc("sq", [128, 8, 512], BF16)
            rstd, rstd_b = A.alloc("rstd", [128, 512], F32)
            tmp, tmp_b = A.alloc("tmp", [128, 512], F32)
            xb, xb_bs = A.alloc("xb", [128, 2, 8, 512], F32, nbufs=2)
            for blk in range(nblk):
                sl = blk % 2
                E.dma("sp", xb[:, sl], X0d.rearrange("p (j t) -> p j t", j=8)[:, :, blk * 512:(blk + 1) * 512],
                      writes=[xb_bs[sl]])
                modulate_block(xb[:, sl], xb_bs[sl], 512, 0, 0, v,
                               lambda j: HS[:, j, blk * 512:(blk + 1) * 512], HS_bs[blk], sq, sq_b, rstd, rstd_b, tmp, tmp_b)
            A.release(mk_a)
            if stage == 2:
                return

            QT, QT_b = A.alloc("QT", [128, 4, T], BF16)
            KT, KT_b = A.alloc("KT", [128, 2, nseq, nkv], BF16)
            VTb, VTb_b = A.alloc("VTb", [128, nseq * (nkv // 128), 256], BF16)
            mk_q = A.mark()
            wq, wq_b = A.alloc("wq", [128, 8, 1024], BF16)
            E.dma("sp", wq, WB["ah_w_in"][0].rearrange("p (kc n) -> p kc n", kc=8)[:, :, 0:1024], reads=[WB["ah_w_in"][1]], writes=[wq_b])
            sq, sq_b = A.alloc("sq", [128, 1, 512], BF16)
            rstd, rstd_b = A.alloc("rstd", [128, 512], F32)
            qn, qn_b = A.alloc("qn", [128, 512], F32)
            qb16, qb16_b = A.alloc("qb16", [128, 512], BF16)
            r1, r1_b = A.alloc("r1", [128, 512], F32)
            if smp:
                rc, rc_b = const("ropec", [128, TS], F32, ropec_d)
                rs, rs_b = const("ropes", [128, TS], F32, ropes_d)
                E.dma("pool", KT[:, :, 0, L:], ckT_d.rearrange("p (g t) -> p g t", g=2), writes=[KT_b])
                E.dma("pool", VTb[:, L // 128:, :], cvt_d.rearrange("p (c e) -> p c e", e=256), writes=[VTb_b])
            else:
                KN, KN_b = A.alloc("KN", [128, 2, T], F32)
                VT, VT_b = A.alloc("VT", [128, T // 128, 256], F32)
            KSUB = int(os.environ.get("KSUB", "0"))
            if KSUB == 1:
                return
            for hq in range(6):
                if KSUB == 2 and hq >= 4:
                    break
                for blk in range(nblk):
                    ts = slice(blk * 512, (blk + 1) * 512)
                    pb = blk % 2
                    mm_group(bank(pb), [PSB[pb]], [(wq[:, kc, hq * 128:(hq + 1) * 128], HS[:, kc, ts], [wq_b, HS_bs[blk]])
                                                   for kc in range(8)])
                    sumsq_rstd([bank(pb)], [PSB[pb]], 512, 128, rstd, rstd_b, sq, sq_b, 2 + pb)
                    gcol = 0 if hq < 4 else 1
                    if hq < 4:
                        dst = QT[:, hq, ts]
                        dst_b = QT_b
                    else:
                        s_i, t0 = (blk * 512) // L, (blk * 512) % L
                        dst_b = KT_b
                    if not smp:
                        if hq < 4:
                            E.op("dve", lambda h: h.scalar_tensor_tensor(
                                out=dst, in0=bank(pb), scalar=qkn[:, 0:1], in1=rstd, op0=ALU.mult, op1=ALU.mult),
                                reads=[PSB[pb], qkn_b, rstd_b], writes=[dst_b])
                        else:
                            g = hq - 4
                            E.op("dve", lambda h: h.scalar_tensor_tensor(
                                out=KN[:, g, ts], in0=bank(pb), scalar=qkn[:, 1:2], in1=rstd, op0=ALU.mult, op1=ALU.mult),
                                reads=[PSB[pb], qkn_b, rstd_b], writes=[KN_b])
                            nsq = 512 // L
                            E.op("act", lambda h: h.activation(
                                out=KT[:, g, s_i:s_i + nsq, 0:L], in_=KN[:, g, ts].rearrange("p (s t) -> p s t", t=L),
                                func=AF.Copy), reads=[KN_b], writes=[KT_b])
                    else:
                        E.op("dve", lambda h: h.scalar_tensor_tensor(
                            out=qn, in0=bank(pb), scalar=qkn[:, gcol:gcol + 1], in1=rstd, op0=ALU.mult, op1=ALU.mult),
                            reads=[PSB[pb], qkn_b, rstd_b], writes=[qn_b])
                        E.op("act", lambda h: h.activation(out=qb16, in_=qn, func=AF.Copy), reads=[qn_b], writes=[qb16_b])
                        mm_group(bank(4 + pb), [PSB[4 + pb]], [(swp, qb16, [swp_b, qb16_b])])
                        E.op("dve", lambda h: h.tensor_tensor(out=r1, in0=bank(4 + pb), in1=rs[:, ts], op=ALU.mult),
                             reads=[PSB[4 + pb], rs_b], writes=[r1_b])
                        E.op("pool", lambda h: h.tensor_tensor(out=qn, in0=qn, in1=rc[:, ts], op=ALU.mult),
                             reads=[qn_b, rc_b], writes=[qn_b])
                        if hq < 4:
                            d2 = dst
                        else:
                            d2 = KT[:, hq - 4, 0, t0:t0 + 512]
                        E.op("dve", lambda h: h.tensor_tensor(out=d2, in0=qn, in1=r1, op=ALU.add),
                             reads=[qn_b, r1_b], writes=[dst_b])
            if G.get("kout") is not None:
                E.dma("sp", G["kout"].rearrange("p (g t) -> p g t", g=2), KN, reads=[KN_b])
            if KSUB in (2, 3):
                return
            for c in range(T // 128):
                pb = 4 + c % 2
                blk = c // 4
                s_i, cc = (c * 128) // L, ((c * 128) % L) // 128
                mm_group(bank(pb, 256), [PSB[pb]], [(HS[:, kc, c * 128:(c + 1) * 128], wq[:, kc, 768:1024], [wq_b, HS_bs[blk]])
                                                    for kc in range(8)])
                if not smp and KSUB != 5:
                    E.op("dve", lambda h: h.tensor_copy(out=VT[:, c, :], in_=bank(pb, 256)),
                         reads=[PSB[pb]], writes=[VT_b])
                if KSUB != 6:
                    E.op("act", lambda h: h.activation(out=VTb[:, s_i * (nkv // 128) + cc, :], in_=bank(pb, 256), func=AF.Copy),
                         reads=[PSB[pb]], writes=[VTb_b])
            if G.get("vout") is not None:
                E.dma("sp", G["vout"].rearrange("p (c e) -> p c e", e=256), VT, reads=[VT_b])
            A.release(mk_q)
            if stage == 3:
                return
            Pt, Pt_bs = A.alloc("Pt", [128, 2, 512], BF16, nbufs=2)
            rden, rden_b = A.alloc("rden", [128, 512], F32)
            nq = min(512, L)
            nkc = nkv // 128
            att_scale = 1.0 / math.sqrt(128.0)
            for s_i in range(nseq):
                for hd in range(4):
                    g = hd // 2
                    for qb in range(L // nq):
                        q0 = s_i * L + qb * nq
                        for kc in range(nkc):
                            sb = kc % 2
                            mm_group(bank(sb, nq), [PSB[sb]], [(KT[:, g, s_i, kc * 128:(kc + 1) * 128], QT[:, hd, q0:q0 + nq],
                                                                [KT_b, QT_b])])
                            E.op("act", lambda h: h.activation(out=Pt[:, sb, :nq], in_=bank(sb, nq), func=AF.Exp,
                                                               scale=att_scale), reads=[PSB[sb]], writes=[Pt_bs[sb]])
                            E.op("pe", lambda h: h.matmul(bank(2, nq), lhsT=VTb[:, s_i * nkc + kc, g * 128:(g + 1) * 128],
                                                          rhs=Pt[:, sb, :nq], start=(kc == 0), stop=(kc == nkc - 1)),
                                 reads=[VTb_b, Pt_bs[sb]], writes=[PSB[2]], pe_accum=True)
                            E.op("pe", lambda h: h.matmul(bank(3, nq), lhsT=ones, rhs=Pt[:, sb, :nq],
                                                          start=(kc == 0), stop=(kc == nkc - 1)),
                                 reads=[ones_b, Pt_bs[sb]], writes=[PSB[3]], pe_accum=True)
                        E.op("dve", lambda h: h.reciprocal(out=rden[:, :nq], in_=bank(3, nq)), reads=[PSB[3]], writes=[rden_b])
                        E.op("dve", lambda h: h.tensor_tensor(out=MIX[:, hd, q0:q0 + nq], in0=bank(2, nq), in1=rden[:, :nq],
                                                              op=ALU.mult), reads=[PSB[2], rden_b], writes=[MIX_bs[hd]])
            A.release(mk_a)
            if stage == 4:
                return

            nsc = L // 128
            nfc = 2 * nsc + 1
            X0, X0_b = A.alloc("X0", [128, 4, T], BF16, top=True)
            VPT, VPT_b = A.alloc("VPT", [128, nseq, nsc, 512], BF16, top=True)
            mk_h = A.mark()
            wu, wu_bs = A.alloc("wu", [128, 2, 8, 128], BF16, nbufs=2)
            wu_it = [0]
            U, U_b = A.alloc("U", [128, T], F32)
            CU, CU_bs = A.alloc("CU", [128, 2, T], F32, nbufs=2)
            VPc, VPc_b = A.alloc("VPc", [128, T], BF16)
            for c in range(4):
                for ti, which in enumerate((1, 2, 0)):
                    ch = which * 4 + c
                    wsl = wu_it[0] % 2
                    wu_it[0] += 1
                    E.dma("sp", wu[:, wsl], WB["ah_w_in"][0].rearrange("p (kc n) -> p kc n", kc=8)[:, :, 1024 + ch * 128: 1024 + (ch + 1) * 128],
                          reads=[WB["ah_w_in"][1]], writes=[wu_bs[wsl]])
                    for blk in range(nblk):
                        ts = slice(blk * 512, (blk + 1) * 512)
                        pb = blk % 2
                        mm_group(bank(pb), [PSB[pb]], [(wu[:, wsl, kc, :], HS[:, kc, ts], [wu_bs[wsl], HS_bs[blk]])
                                                       for kc in range(8)])
                        E.op("act", lambda h: h.activation(out=U[:, ts], in_=bank(pb), func=AF.Copy),
                             reads=[PSB[pb]], writes=[U_b])
                    ci = ti % 2
                    cu, cu_b = CU[:, ci], CU_bs[ci]
                    E.op("act", lambda h: h.activation(out=cu, in_=U, func=AF.Identity, bias=hcw[:, ch, 3:4],
                                                       scale=hcw[:, ch, 1:2]), reads=[U_b, hcw_b], writes=[cu_b])
                    c3 = cu.rearrange("p (s t) -> p s t", t=L)
                    u3 = U.rearrange("p (s t) -> p s t", t=L)
                    E.op("dve", lambda h: h.scalar_tensor_tensor(out=c3[:, :, 1:L], in0=u3[:, :, 0:L - 1], scalar=hcw[:, ch, 0:1],
                                                                 in1=c3[:, :, 1:L], op0=ALU.mult, op1=ALU.add),
                         reads=[U_b, hcw_b, cu_b], writes=[cu_b])
                    E.op("dve", lambda h: h.scalar_tensor_tensor(out=c3[:, :, 0:L - 1], in0=u3[:, :, 1:L], scalar=hcw[:, ch, 2:3],
                                                                 in1=c3[:, :, 0:L - 1], op0=ALU.mult, op1=ALU.add),
                         reads=[U_b, hcw_b, cu_b], writes=[cu_b])
                    if which == 2:
                        E.op("pool", lambda h: h.tensor_tensor(out=VPc, in0=CU[:, 0], in1=CU[:, 1], op=ALU.mult),
                             reads=[CU_bs[0], CU_bs[1]], writes=[VPc_b])
                    if which == 0:
                        E.op("pool", lambda h: h.tensor_copy(out=X0[:, c, :], in_=cu), reads=[cu_b], writes=[X0_b])
                for tcn in range(T // 128):
                    s_i, cc = (tcn * 128) // L, ((tcn * 128) % L) // 128
                    pb = 4 + tcn % 2
                    E.op("pe", lambda h: h.transpose(PSbf[pb], VPc[:, tcn * 128:(tcn + 1) * 128], ident),
                         reads=[VPc_b, ident_b], writes=[PSbf_b[pb]])
                    E.op("act", lambda h: h.activation(out=VPT[:, s_i, cc, c * 128:(c + 1) * 128], in_=PSbf[pb], func=AF.Copy),
                         reads=[PSbf_b[pb]], writes=[VPT_b])
            A.release(mk_m)
            if stage == 5:
                return
            YR, YR_b = A.alloc("YR", [128, nsc + 1, 512], BF16)
            YI, YI_b = A.alloc("YI", [128, nsc, 512], BF16)
            vr, vr_b = A.alloc("vr", [128, 512], F32)
            vi, vi_b = A.alloc("vi", [128, 512], F32)
            pa, pa_b = A.alloc("pa", [128, 512], F32)
            pb_, pb_b = A.alloc("pb", [128, 512], F32)
            pc, pc_b = A.alloc("pc", [128, 512], F32)
            pd, pd_b = A.alloc("pd", [128, 512], F32)
            fw, fw_bs = A.alloc("fwst", [128, 2, nsc, 128], BF16, nbufs=2)
            nt = min(L, 256)
            ntb = L // nt
            iv, iv_b = A.alloc("invst", [128, nfc, nt], BF16)
            for s_i in range(nseq):
                for fc in range(nsc + 1):
                    E.dma("sp", fw[:, 0], tabs[L]["fwd"][fc].rearrange("p (s f) -> p s f", f=128), writes=[fw_bs[0]])
                    mm_group(bank(0), [PSB[0]], [(fw[:, 0, s_, :], VPT[:, s_i, s_, :], [fw_bs[0], VPT_b]) for s_ in range(nsc)])
                    E.op("act", lambda h: h.activation(out=vr, in_=bank(0), func=AF.Copy), reads=[PSB[0]], writes=[vr_b])
                    if fc < nsc:
                        E.dma("sp", fw[:, 1], tabs[L]["fwd"][nsc + 1 + fc].rearrange("p (s f) -> p s f", f=128),
                              writes=[fw_bs[1]])
                        mm_group(bank(1), [PSB[1]], [(fw[:, 1, s_, :], VPT[:, s_i, s_, :], [fw_bs[1], VPT_b])
                                                     for s_ in range(nsc)])
                        E.op("act", lambda h: h.activation(out=vi, in_=bank(1), func=AF.Copy), reads=[PSB[1]], writes=[vi_b])
                        E.op("dve", lambda h: h.tensor_tensor(out=pa, in0=vr, in1=GR[:, fc, :], op=ALU.mult),
                             reads=[vr_b, GR_b], writes=[pa_b])
                        E.op("pool", lambda h: h.tensor_tensor(out=pb_, in0=vi, in1=GI[:, fc, :], op=ALU.mult),
                             reads=[vi_b, GI_b], writes=[pb_b])
                        E.op("dve", lambda h: h.tensor_tensor(out=YR[:, fc, :], in0=pa, in1=pb_, op=ALU.subtract),
                             reads=[pa_b, pb_b], writes=[YR_b])
                        E.op("pool", lambda h: h.tensor_tensor(out=pc, in0=vr, in1=GI[:, fc, :], op=ALU.mult),
                             reads=[vr_b, GI_b], writes=[pc_b])
                        E.op("dve", lambda h: h.tensor_tensor(out=pd, in0=vi, in1=GR[:, fc, :], op=ALU.mult),
                             reads=[vi_b, GR_b], writes=[pd_b])
                        E.op("pool", lambda h: h.tensor_tensor(out=YI[:, fc, :], in0=pc, in1=pd, op=ALU.add),
                             reads=[pc_b, pd_b], writes=[YI_b])
                    else:
                        E.op("dve", lambda h: h.tensor_tensor(out=YR[:, fc, :], in0=vr, in1=GR[:, fc, :], op=ALU.mult),
                             reads=[vr_b, GR_b], writes=[YR_b])
                for tb in range(ntb):
                    tw = min(L, 512)
                    E.dma("sp", iv, tabs[L]["inv"][(tb * nt) // tw].rearrange("p (f t) -> p f t", t=tw)[:, :, (tb * nt) % tw:(tb * nt) % tw + nt],
                          writes=[iv_b])
                    t0 = s_i * L + tb * nt
                    for cc in range(4):
                        pbk = 4 + cc % 2
                        terms = [(YR[:, f_, cc * 128:(cc + 1) * 128], iv[:, f_, :], [YR_b, iv_b]) for f_ in range(nsc + 1)]
                        terms += [(YI[:, f_, cc * 128:(cc + 1) * 128], iv[:, nsc + 1 + f_, :], [YI_b, iv_b]) for f_ in range(nsc)]
                        mm_group(bank(pbk, nt), [PSB[pbk]], terms)
                        E.op("dve", lambda h: h.tensor_tensor(out=MIX[:, 4 + cc, t0:t0 + nt], in0=bank(pbk, nt),
                                                              in1=X0[:, cc, t0:t0 + nt], op=ALU.mult),
                             reads=[PSB[pbk], X0_b], writes=[MIX_bs[4 + cc]])
            A.release(mk_m)
            A.release_top(mk_top)
            if stage == 6:
                dbg = P.dout("dbg_" + gname, [128, 8 * T], BF16)
                E.dma("sp", dbg.rearrange("p (j t) -> p j t", j=8), MIX, reads=MIX_bs)
                return

            HF, HF_bs = A.alloc(gname + "_hf", [128, 8, T], BF16, nbufs=nblk)
            mk_o = A.mark()
            wo, wo_b = A.alloc("wo", [128, 8, 1024], BF16)
            E.dma("sp", wo, WB["ah_w_out"][0].rearrange("p (kc n) -> p kc n", kc=8), reads=[WB["ah_w_out"][1]], writes=[wo_b])
            O, O_b = A.alloc("O", [128, 8, 512], F32)
            sq, sq_b = A.alloc("sq", [128, 8, 512], BF16)
            rstd, rstd_b = A.alloc("rstd", [128, 512], F32)
            tmp, tmp_b = A.alloc("tmp", [128, 512], F32)
            xb, xb_b = A.alloc("xb", [128, 8, 512], F32)
            wk = (sq, sq_b, rstd, rstd_b, tmp, tmp_b, xb, xb_b)
            for blk in range(nblk):
                ts = slice(blk * 512, (blk + 1) * 512)
                for m in range(8):
                    pbk = m % 2
                    mm_group(bank(pbk), [PSB[pbk]], [(wo[:, kc, m * 128:(m + 1) * 128], MIX[:, kc, ts], [wo_b, MIX_bs[kc]])
                                                     for kc in range(8)])
                    E.op("act", lambda h: h.activation(out=O[:, m, :], in_=bank(pbk), func=AF.Copy), reads=[PSB[pbk]], writes=[O_b])
                epilogue(O, O_b, 512, X0d, X0d_b, X1d, X1d_b, blk * 512, T, 0, 0, v,
                         (0, 1, lambda j: HF[:, j, ts], HF_bs[blk]), wk)
            A.release(mk_o)
            if stage == 7:
                dbg = P.dout("dbg_" + gname, [128, 8 * T], F32)
                E.dma("sp", dbg, X1d, reads=[X1d_b], writes=[])
                return
            p_after_hf = A.p
            A.release(mk_g)
            HS1, HS1_bs = A.alloc(gname + "_hs1", [128, 8, T], BF16, nbufs=nblk)
            p_after_hs1 = A.p
            A.p = p_after_hf
            X2d, X2d_b = dscratch(gname + "_x2b", [128, 8 * T])
            ffn(G, 0, HF, HF_bs, X1d, X1d_b, X2d, X2d_b,
                lambda tok0: (1, 0, (lambda j: HS1[:, j, tok0:tok0 + 512]), HS1_bs[tok0 // 512]), NB=(512 if smp else 1024))
            if stage == 8:
                E.dma("sp", G["y"], X2d, reads=[X2d_b])
                A.release(mk_g)
                return
            A.release(p_after_hs1)
            layer1(G, HS1, HS1_bs, X2d, X2d_b)
            A.release(mk_g)

        def layer1(G, HS1, HS1_bs, X2d, X2d_b):
            gname, nseq, L, v = G["name"], G["nseq"], G["L"], G["v"]
            smp = G["sample"]
            T = nseq * L
            nblk = T // 512
            nch = L // 128
            X3d, X3d_b = dscratch(gname + "_x3", [128, 8 * T])
            mk_l = A.mark()
            MLM, MLM_bs = A.alloc("mlmix", [128, 8, T], BF16, nbufs=8)
            mk_2 = A.mark()
            RT, RT_b = A.alloc("RT", [40, T], F32)
            EM, EM_b = A.alloc("EM", [40, T], F32)
            WI, WI_b = A.alloc("WI", [40, T], F32)
            ATK, ATK_b = A.alloc("ATK", [128, T // 128, 40], F32)
            m0c, m0c_b = A.alloc("m0c", [40, 2], F32)
            onesf, onesf_b = A.alloc("onesf", [40, 128], F32)
            E.op("dve", lambda h: h.memset(onesf, 1.0), writes=[onesf_b])
            if G.get("states"):
                WTK, WTK_b = A.alloc("WTK", [128, T // 128, 40], F32)
            mk_r = A.mark()
            wg, wg_b = A.alloc("wg", [128, 8, 80], BF16)
            E.dma("pool", wg, mlwg_d.rearrange("p (kc n) -> p kc n", kc=8), writes=[wg_b])
            IG, IG_b = A.alloc("IG", [40, T], F32)
            LF, LF_b = A.alloc("LF", [40, T], F32)
            for blk in range(nblk):
                ts = slice(blk * 512, (blk + 1) * 512)
                for gi in range(2):
                    mm_group(PS[0:40, gi * 512:(gi + 1) * 512], [PSB[gi]],
                             [(wg[:, kc, gi * 40:(gi + 1) * 40], HS1[:, kc, ts], [wg_b, HS1_bs[blk]]) for kc in range(8)])
                E.op("act", lambda h: h.activation(out=IG[:, ts], in_=PS[0:40, 0:512], func=AF.Identity, bias=mbg[:, 0:1], scale=1.0),
                     reads=[PSB[0], mbg_b], writes=[IG_b])
                E.op("act", lambda h: h.activation(out=LF[:, ts], in_=PS[0:40, 512:1024], func=AF.Identity, bias=mbg[:, 1:2], scale=1.0),
                     reads=[PSB[1], mbg_b], writes=[LF_b])
            E.op("act", lambda h: h.activation(out=LF, in_=LF, func=AF.Exp, scale=-1.0), reads=[LF_b], writes=[LF_b])
            E.op("act", lambda h: h.activation(out=LF, in_=LF, func=AF.Ln, bias=onec[0:40, 0:1], scale=1.0),
                 reads=[LF_b, onec_b], writes=[LF_b])
            E.op("dve", lambda h: h.tensor_scalar(out=LF, in0=LF, scalar1=-1.0, scalar2=None, op0=ALU.mult), reads=[LF_b], writes=[LF_b])
            if smp:
                E.dma("sp", m0c[:, 0:1], m0_d, writes=[m0c_b])
            else:
                E.op("dve", lambda h: h.memset(m0c, 0.0), writes=[m0c_b])
            BT, BT_b = A.alloc("BT", [40, T], F32)
            AA, AA_b = A.alloc("AA", [40, T], F32)
            CM, CM_b = A.alloc("CM", [40, T], F32)
            onesr, onesr_b = A.alloc("onesr", [40, L], F32)
            E.op("dve", lambda h: h.memset(onesr, 1.0), writes=[onesr_b])
            for (tt, tb_) in ((BT, BT_b), (AA, AA_b), (CM, CM_b)):
                E.op("pool", lambda h: h.memset(tt, 0.0), writes=[tb_])
            for s_i in range(nseq):
                sl = slice(s_i * L, (s_i + 1) * L)
                for (p0, rev) in ((0, False), (32, True)):
                    pr = slice(p0, p0 + 8)

                    def V_(ap):
                        a2 = ap[pr, sl]
                        return a2[:, ::-1] if rev else a2
                    E.op("dve", lambda h: h.tensor_tensor_scan(out=V_(BT), data0=onesr[pr, :], data1=V_(LF), initial=0.0,
                                                               op0=ALU.mult, op1=ALU.add),
                         reads=[LF_b, onesr_b], writes=[BT_b])
                    E.op("dve", lambda h: h.tensor_tensor(out=AA[pr, sl], in0=IG[pr, sl], in1=BT[pr, sl], op=ALU.subtract),
                         reads=[IG_b, BT_b], writes=[AA_b])
                    E.op("dve", lambda h: h.tensor_tensor_scan(out=V_(CM), data0=V_(AA), data1=V_(AA), initial=m0c[pr, 0:1],
                                                               op0=ALU.max, op1=ALU.max), reads=[AA_b, m0c_b], writes=[CM_b])
            E.op("dve", lambda h: h.tensor_scalar(out=RT, in0=CM, scalar1=-1.0, scalar2=None, op0=ALU.mult), reads=[CM_b], writes=[RT_b])
            E.op("dve", lambda h: h.tensor_tensor(out=EM, in0=BT, in1=CM, op=ALU.add), reads=[BT_b, CM_b], writes=[EM_b])
            if G.get("states"):
                MTk, MTk_b = A.alloc("MTk", [40, nseq], F32)
                E.op("dve", lambda h: h.memset(MTk, 0.0), writes=[MTk_b])
                for s_i in range(nseq):
                    E.op("dve", lambda h: h.tensor_copy(out=MTk[0:8, s_i:s_i + 1], in_=EM[0:8, (s_i + 1) * L - 1:(s_i + 1) * L]),
                         reads=[EM_b], writes=[MTk_b])
                    E.op("dve", lambda h: h.tensor_copy(out=MTk[32:40, s_i:s_i + 1], in_=EM[32:40, s_i * L:s_i * L + 1]),
                         reads=[EM_b], writes=[MTk_b])
                E.dma("sp", mst_d, MTk, reads=[MTk_b])
            E.op("act", lambda h: h.activation(out=EM, in_=EM, func=AF.Exp, scale=-1.0), reads=[EM_b], writes=[EM_b])
            E.op("act", lambda h: h.activation(out=WI, in_=CM, func=AF.Exp, scale=-1.0, bias=m0c[:, 0:1]),
                 reads=[CM_b, m0c_b], writes=[WI_b])
            for c in range(T // 128):
                E.op("pe", lambda h: h.transpose(PS[:, 1024:1064], AA[:, c * 128:(c + 1) * 128], id32[0:40, 0:40]),
                     reads=[AA_b, id32_b], writes=[PSB[2]])
                E.op("dve", lambda h: h.tensor_copy(out=ATK[:, c, :], in_=PS[:, 1024:1064]), reads=[PSB[2]], writes=[ATK_b])
            if G.get("states"):
                dg, dg_b = A.alloc("dg", [40, nseq, 40], F32)
                for s_i in range(nseq):
                    E.op("dve", lambda h: h.memset(dg[:, s_i, :], 0.0), writes=[dg_b])
                    E.op("dve", lambda h: h.tensor_scalar(out=dg[0:8, s_i, :], in0=id32[0:8, 0:40], scalar1=RT[0:8, (s_i + 1) * L - 1:(s_i + 1) * L],
                                                          scalar2=None, op0=ALU.mult), reads=[id32_b, RT_b], writes=[dg_b])
                    E.op("dve", lambda h: h.tensor_scalar(out=dg[32:40, s_i, :], in0=id32[32:40, 0:40], scalar1=RT[32:40, s_i * L:s_i * L + 1],
                                                          scalar2=None, op0=ALU.mult), reads=[id32_b, RT_b], writes=[dg_b])
                    mm_group(PS[:, 1024:1064], [PSB[2]], [(onesf, dg[:, s_i, :], [onesf_b, dg_b])])
                    for cc in range(nch):
                        c = s_i * nch + cc
                        E.op("dve", lambda h: h.tensor_tensor(out=WTK[:, c, :], in0=ATK[:, c, :], in1=PS[:, 1024:1064], op=ALU.add),
                             reads=[ATK_b, PSB[2]], writes=[WTK_b])
                E.op("act", lambda h: h.activation(out=WTK, in_=WTK, func=AF.Exp), reads=[WTK_b], writes=[WTK_b])
            A.release(mk_r)
            wh, wh_b = A.alloc("wh", [128, 8, 4, 128], BF16)
            U, U_b = A.alloc("U", [128, T], F32)
            cu, cu_b = A.alloc("cu1", [128, T], F32)
            QT, QT_b = A.alloc("QT1", [128, T], BF16)
            KT, KT_b = A.alloc("KT1", [128, T], BF16)
            SG, SG_b = A.alloc("SG", [128, T], BF16)
            VK, VK_b = A.alloc("VK", [128, T // 128, 128], BF16)
            KK, KK_b = A.alloc("KK", [128, T // 128, 128], BF16)
            HSUM, HSUM_b = A.alloc("HSUM", [128, T], F32)
            nq = min(512, L)
            rtb, rtb_b = A.alloc("rtb", [128, nq], F32)
            emb, emb_b = A.alloc("emb", [128, nq], F32)
            wexp, wexp_bs = A.alloc("wexp", [128, 2, nq], F32, nbufs=2)
            Pm, Pm_bs = A.alloc("Pm", [128, 2, nq], BF16, nbufs=2)
            dn, dn_b = A.alloc("dn", [128, nq], F32)
            ht, ht_b = A.alloc("ht", [128, nq], F32)
            sq, sq_b = A.alloc("sq", [128, 1, 512], BF16)
            rstd, rstd_b = A.alloc("rstd", [128, 512], F32)
            vw, vw_b = A.alloc("vw", [128, 128], BF16)
            cst, cst_bs = A.alloc("cst", [128, 2, 128], F32, nbufs=2)
            nst, nst_bs = A.alloc("nst", [1, 2, 128], F32, nbufs=2)
            if smp:
                qp, qp_b = A.alloc("qp", [128, nq], BF16)
                c0t, c0t_b = A.alloc("c0t", [128, 16, 128], BF16)
                n0b, n0b_b = A.alloc("n0b", [128, 16, 128], BF16)
                E.dma("pool", c0t, c0t_d.rearrange("p (r m) -> p r m", m=128), writes=[c0t_b])
                E.dma("pool", n0b, n0b_d.rearrange("p (r m) -> p r m", m=128), writes=[n0b_b])
            wv = WB["ml_w_in"][0].rearrange("p (kc n) -> p kc n", kc=8)
            for hd in range(8):
                for wi in range(4):
                    E.dma("sp", wh[:, :, wi, :], wv[:, :, wi * 1024 + hd * 128: wi * 1024 + (hd + 1) * 128], reads=[WB["ml_w_in"][1]], writes=[wh_b])
                for wi, (dst, dst_b) in ((0, (QT, QT_b)), (1, (KT, KT_b)), (3, (SG, SG_b))):
                    for blk in range(nblk):
                        ts = slice(blk * 512, (blk + 1) * 512)
                        pb = blk % 2
                        mm_group(bank(pb), [PSB[pb]], [(wh[:, kc, wi, :], HS1[:, kc, ts], [wh_b, HS1_bs[blk]]) for kc in range(8)])
                        if wi == 3:
                            E.op("act", lambda h: h.activation(out=SG[:, ts], in_=bank(pb), func=AF.Sigmoid), reads=[PSB[pb]], writes=[SG_b])
                        else:
                            E.op("act", lambda h: h.activation(out=U[:, ts], in_=bank(pb), func=AF.Copy), reads=[PSB[pb]], writes=[U_b])
                    if wi == 3:
                        continue
                    ch = wi * 8 + hd
                    E.op("act", lambda h: h.activation(out=cu, in_=U, func=AF.Identity, bias=mcw[:, ch, 3:4], scale=mcw[:, ch, 1:2]),
                         reads=[U_b, mcw_b], writes=[cu_b])
                    c3 = cu.rearrange("p (s t) -> p s t", t=L)
                    u3 = U.rearrange("p (s t) -> p s t", t=L)
                    E.op("dve", lambda h: h.scalar_tensor_tensor(out=c3[:, :, 1:L], in0=u3[:, :, 0:L - 1], scalar=mcw[:, ch, 0:1],
                                                                 in1=c3[:, :, 1:L], op0=ALU.mult, op1=ALU.add),
                         reads=[U_b, mcw_b, cu_b], writes=[cu_b])
                    E.op("dve", lambda h: h.scalar_tensor_tensor(out=c3[:, :, 0:L - 1], in0=u3[:, :, 1:L], scalar=mcw[:, ch, 2:3],
                                                                 in1=c3[:, :, 0:L - 1], op0=ALU.mult, op1=ALU.add),
                         reads=[U_b, mcw_b, cu_b], writes=[cu_b])
                    if wi == 0:
                        E.op("act", lambda h: h.activation(out=QT, in_=cu, func=AF.Silu), reads=[cu_b], writes=[QT_b])
                    else:
                        E.op("act", lambda h: h.activation(out=cu, in_=cu, func=AF.Silu), reads=[cu_b], writes=[cu_b])
                        E.op("pool", lambda h: h.tensor_scalar(out=KT, in0=cu, scalar1=128.0 ** -0.5, scalar2=None, op0=ALU.mult),
                             reads=[cu_b], writes=[KT_b])
                for c in range(T // 128):
                    pb = 4 + c % 2
                    mm_group(bank(pb, 128), [PSB[pb]], [(HS1[:, kc, c * 128:(c + 1) * 128], wh[:, kc, 2, :], [wh_b, HS1_bs[c // 4]])
                                                        for kc in range(8)])
                    E.op("act", lambda h: h.activation(out=VK[:, c, :], in_=bank(pb, 128), func=AF.Copy), reads=[PSB[pb]], writes=[VK_b])
                    if G.get("states"):
                        E.op("pe", lambda h: h.transpose(PSbf[4], KT[:, c * 128:(c + 1) * 128], ident), reads=[KT_b, ident_b], writes=[PSbf_b[4]])
                        E.op("act", lambda h: h.activation(out=KK[:, c, :], in_=PSbf[4], func=AF.Copy), reads=[PSbf_b[4]], writes=[KK_b])
                for s_i in range(nseq):
                    for di in range(2):
                        r = di * 32 + hd
                        ridx = di * 8 + hd
                        for qb in range(L // nq):
                            q0 = s_i * L + qb * nq
                            mm_group(bank(4, nq), [PSB[4]], [(sel[:, ridx, :], RT[:, q0:q0 + nq], [sel_b, RT_b])])
                            E.op("act", lambda h: h.activation(out=rtb, in_=bank(4, nq), func=AF.Copy), reads=[PSB[4]], writes=[rtb_b])
                            mm_group(bank(5, nq), [PSB[5]], [(sel[:, ridx, :], EM[:, q0:q0 + nq], [sel_b, EM_b])])
                            E.op("act", lambda h: h.activation(out=emb, in_=bank(5, nq), func=AF.Copy), reads=[PSB[5]], writes=[emb_b])
                            if di == 0:
                                kcs = [kc for kc in range(nch) if kc * 128 <= qb * nq + nq - 1]
                            else:
                                kcs = [kc for kc in range(nch) if kc * 128 + 127 >= qb * nq]
                            nterm = len(kcs) + (1 if smp else 0)
                            ti = 0
                            if smp:
                                mm_group(bank(4, nq), [PSB[4]], [(sel[:, ridx, :], WI[:, q0:q0 + nq], [sel_b, WI_b])])
                                E.op("dve", lambda h: h.tensor_tensor(out=qp, in0=QT[:, q0:q0 + nq], in1=bank(4, nq), op=ALU.mult),
                                     reads=[QT_b, PSB[4]], writes=[qp_b])
                                E.op("pe", lambda h: h.matmul(bank(2, nq), lhsT=c0t[:, ridx, :], rhs=qp, start=True, stop=(nterm == 1)),
                                     reads=[c0t_b, qp_b], writes=[PSB[2]], pe_accum=True)
                                E.op("pe", lambda h: h.matmul(bank(3, nq), lhsT=n0b[:, ridx, :], rhs=qp, start=True, stop=(nterm == 1)),
                                     reads=[n0b_b, qp_b], writes=[PSB[3]], pe_accum=True)
                                ti = 1
                            for kc in kcs:
                                c = s_i * nch + kc
                                sb = ti % 2
                                off = kc * 128 - qb * nq
                                mm_group(bank(sb, nq), [PSB[sb]], [(KT[:, c * 128:(c + 1) * 128], QT[:, q0:q0 + nq], [KT_b, QT_b])])
                                E.op("act", lambda h: h.activation(out=wexp[:, sb, :], in_=rtb, func=AF.Exp, bias=ATK[:, c, r:r + 1], scale=1.0),
                                     reads=[rtb_b, ATK_b], writes=[wexp_bs[sb]])
                                E.op("dve", lambda h: h.tensor_tensor(out=Pm[:, sb, :], in0=bank(sb, nq), in1=wexp[:, sb, :], op=ALU.mult),
                                     reads=[PSB[sb], wexp_bs[sb]], writes=[Pm_bs[sb]])
                                if 0 <= off < nq and (off // 128) < 4:
                                    E.op("pool", lambda h: h.tensor_tensor(out=Pm[:, sb, :], in0=Pm[:, sb, :], in1=msk[:, di, off // 128, 0:nq], op=ALU.mult),
                                         reads=[Pm_bs[sb], msk_b], writes=[Pm_bs[sb]])
                                E.op("pe", lambda h: h.matmul(bank(2, nq), lhsT=VK[:, c, :], rhs=Pm[:, sb, :], start=(ti == 0), stop=(ti == nterm - 1)),
                                     reads=[VK_b, Pm_bs[sb]], writes=[PSB[2]], pe_accum=True)
                                E.op("pe", lambda h: h.matmul(bank(3, nq), lhsT=ones, rhs=Pm[:, sb, :], start=(ti == 0), stop=(ti == nterm - 1)),
                                     reads=[ones_b, Pm_bs[sb]], writes=[PSB[3]], pe_accum=True)
                                ti += 1
                            E.op("dve", lambda h: h.tensor_scalar(out=dn, in0=bank(3, nq), scalar1=-1.0, scalar2=None, op0=ALU.mult),
                                 reads=[PSB[3]], writes=[dn_b])
                            E.op("dve", lambda h: h.tensor_tensor(out=dn, in0=dn, in1=bank(3, nq), op=ALU.max), reads=[dn_b, PSB[3]], writes=[dn_b])
                            E.op("dve", lambda h: h.tensor_tensor(out=dn, in0=dn, in1=emb, op=ALU.max), reads=[dn_b, emb_b], writes=[dn_b])
                            E.op("dve", lambda h: h.reciprocal(out=dn, in_=dn), reads=[dn_b], writes=[dn_b])
                            if di == 0:
                                E.op("dve", lambda h: h.tensor_tensor(out=HSUM[:, q0:q0 + nq], in0=bank(2, nq), in1=dn, op=ALU.mult),
                                     reads=[PSB[2], dn_b], writes=[HSUM_b])
                            else:
                                E.op("dve", lambda h: h.tensor_tensor(out=ht, in0=bank(2, nq), in1=dn, op=ALU.mult),
                                     reads=[PSB[2], dn_b], writes=[ht_b])
                                E.op("pool", lambda h: h.tensor_tensor(out=HSUM[:, q0:q0 + nq], in0=HSUM[:, q0:q0 + nq], in1=ht, op=ALU.add),
                                     reads=[ht_b, HSUM_b], writes=[HSUM_b])
                        if G.get("states"):
                            so = (s_i * 16 + ridx) * 128
                            slot = ridx % 2
                            for cc in range(nch):
                                c = s_i * nch + cc
                                E.op("dve", lambda h: h.tensor_scalar(out=vw, in0=VK[:, c, :], scalar1=WTK[:, c, r:r + 1], scalar2=None, op0=ALU.mult),
                                     reads=[VK_b, WTK_b], writes=[vw_b])
                                E.op("pe", lambda h: h.matmul(bank(5, 128), lhsT=vw, rhs=KK[:, c, :], start=(cc == 0), stop=(cc == nch - 1)),
                                     reads=[vw_b, KK_b], writes=[PSB[5]], pe_accum=True)
                            E.op("act", lambda h: h.activation(out=cst[:, slot, :], in_=bank(5, 128), func=AF.Copy), reads=[PSB[5]], writes=[cst_bs[slot]])
                            E.dma("sp", cst_d[:, so:so + 128], cst[:, slot, :], reads=[cst_bs[slot]])
                            wtb, wtb_b = vw, vw_b
                            for cc in range(nch):
                                c = s_i * nch + cc
                                E.op("dve", lambda h: h.tensor_copy(out=vw[:, 0:1], in_=WTK[:, c, r:r + 1]), reads=[WTK_b], writes=[vw_b])
                                E.op("pe", lambda h: h.matmul(PS[0:1, 5 * 512 + 128: 5 * 512 + 256], lhsT=vw[:, 0:1], rhs=KK[:, c, :],
                                                              start=(cc == 0), stop=(cc == nch - 1)),
                                     reads=[vw_b, KK_b], writes=[PSB[5]], pe_accum=True)
                            E.op("act", lambda h: h.activation(out=nst[0:1, slot, :], in_=PS[0:1, 5 * 512 + 128: 5 * 512 + 256], func=AF.Copy),
                                 reads=[PSB[5]], writes=[nst_bs[slot]])
                            E.dma("sp", nst_d[0:1, so:so + 128], nst[0:1, slot, :], reads=[nst_bs[slot]])
                for blk in range(nblk):
                    ts = slice(blk * 512, (blk + 1) * 512)
                    sumsq_rstd([HSUM[:, ts]], [HSUM_b], 512, 128, rstd, rstd_b, sq, sq_b, 6)
                    E.op("dve", lambda h: h.scalar_tensor_tensor(out=U[:, ts], in0=HSUM[:, ts], scalar=mhn[:, hd:hd + 1], in1=rstd,
                                                                 op0=ALU.mult, op1=ALU.mult), reads=[HSUM_b, mhn_b, rstd_b], writes=[U_b])
                    E.op("pool", lambda h: h.tensor_tensor(out=MLM[:, hd, ts], in0=U[:, ts], in1=SG[:, ts], op=ALU.mult),
                         reads=[U_b, SG_b], writes=[MLM_bs[hd]])
            A.release(mk_2)
            if stage == 9:
                dbg = P.dout("dbg_" + gname, [128, 8 * T], BF16)
                E.dma("sp", dbg.rearrange("p (j t) -> p j t", j=8), MLM, reads=MLM_bs)
                A.release(mk_l)
                return
            mk_top1 = A.mark_top()
            HF1, HF1_bs = A.alloc(gname + "_hf1", [128, 8, T], BF16, nbufs=nblk, top=True)
            mk_o = A.mark()
            wo, wo_b = A.alloc("wo1", [128, 8, 1024], BF16)
            E.dma("sp", wo, WB["ml_w_out"][0].rearrange("p (kc n) -> p kc n", kc=8), reads=[WB["ml_w_out"][1]], writes=[wo_b])
            O, O_b = A.alloc("O", [128, 8, 512], F32)
            sq, sq_b = A.alloc("sq", [128, 8, 512], BF16)
            rstd, rstd_b = A.alloc("rstd", [128, 512], F32)
            tmp, tmp_b = A.alloc("tmp", [128, 512], F32)
            xb, xb_b = A.alloc("xb", [128, 8, 512], F32)
            wk = (sq, sq_b, rstd, rstd_b, tmp, tmp_b, xb, xb_b)
            for blk in range(nblk):
                ts = slice(blk * 512, (blk + 1) * 512)
                for m in range(8):
                    pbk = m % 2
                    mm_group(bank(pbk), [PSB[pbk]], [(wo[:, kc, m * 128:(m + 1) * 128], MLM[:, kc, ts], [wo_b, MLM_bs[kc]])
                                                     for kc in range(8)])
                    E.op("act", lambda h: h.activation(out=O[:, m, :], in_=bank(pbk), func=AF.Copy), reads=[PSB[pbk]], writes=[O_b])
                epilogue(O, O_b, 512, X2d, X2d_b, X3d, X3d_b, blk * 512, T, 1, 0, v,
                         (1, 1, lambda j: HF1[:, j, ts], HF1_bs[blk]), wk)
            A.release(mk_l)
            ffn(G, 1, HF1, HF1_bs, X3d, X3d_b, G["y"], Buf("y"), None, NB=(512 if smp else 1024))
            A.release_top(mk_top1)

        def ffn(G, l, HF, HF_bs, xsrc, xsrc_b, xdst, xdst_b, nxt, NB=1024):
            gname, nseq, L, v = G["name"], G["nseq"], G["L"], G["v"]
            T = nseq * L
            mk_f = A.mark()
            Gb, Gb_b = A.alloc("G", [128, 22, NB], BF16)
            wu, wu_bs = A.alloc("wup", [128, 2, 8, 2, 256], BF16, nbufs=2)
            wd, wd_bs = A.alloc("wdn", [128, 2, 22, 256], BF16, nbufs=2)
            cu, cu_bs = A.alloc("cu", [128, 2, NB], F32, nbufs=2)
            O, O_b = A.alloc("O", [128, 8, 512], F32)
            sq, sq_b = A.alloc("sq", [128, 8, 512], BF16)
            rstd, rstd_b = A.alloc("rstd", [128, 512], F32)
            tmp, tmp_b = A.alloc("tmp", [128, 512], F32)
            xb, xb_b = A.alloc("xb", [128, 8, 512], F32)
            wk = (sq, sq_b, rstd, rstd_b, tmp, tmp_b, xb, xb_b)
            upv = WB["ffn_up%d" % l][0].rearrange("p (kc n) -> p kc n", kc=8)
            dnv = WB["ffn_dn%d" % l][0].rearrange("p (kc n) -> p kc n", kc=22)
            upb, dnb = WB["ffn_up%d" % l][1], WB["ffn_dn%d" % l][1]
            for a in range(0, T, NB):
                n = NB
                seq_start = (a % L == 0)
                seq_end = ((a + n) % L == 0)
                aligned = (L <= n)
                lo = 1 if (seq_start or aligned) else 0
                hi = n + 1 if (seq_end or aligned) else n + 2
                pieces = []
                c0 = lo
                while c0 < hi:
                    c1 = min(hi, (c0 // 512 + 1) * 512)
                    pieces.append((c0, c1))
                    c0 = c1
                for jp in range(11):
                    slot = jp % 2
                    for half in range(2):
                        E.dma("sp", wu[:, slot, :, half, :], upv[:, :, half * DFF + jp * 256: half * DFF + (jp + 1) * 256],
                              reads=[upb], writes=[wu_bs[slot]])
                    for jj in range(2):
                        j = jp * 2 + jj
                        for half in range(2):
                            hb = 3 * half
                            hp = PS[:, hb * 512: hb * 512 + 1536]
                            hbufs = [PSB[hb], PSB[hb + 1], PSB[hb + 2]]
                            for (c0, c1) in pieces:
                                bk = hb + c0 // 512
                                mm_group(hp[:, c0:c1], [PSB[bk]],
                                         [(wu[:, slot, kc, half, jj * 128:(jj + 1) * 128], HF[:, kc, a - 1 + c0: a - 1 + c1],
                                           [wu_bs[slot]] + HF_bs) for kc in range(8)])
                            ch = half * 22 + j
                            cw = fcw[:, l, ch, :]
                            E.op("act", lambda h: h.activation(out=cu[:, half, :n], in_=hp[:, 1:n + 1], func=AF.Identity,
                                                               bias=cw[:, 3:4], scale=cw[:, 1:2]),
                                 reads=hbufs + [fcw_b], writes=[cu_bs[half]])
                            if aligned:
                                c3 = cu[:, half, :n].rearrange("p (s t) -> p s t", t=L)
                                h3 = hp[:, 1:n + 1].rearrange("p (s t) -> p s t", t=L)
                                E.op("dve", lambda h: h.scalar_tensor_tensor(
                                    out=c3[:, :, 1:L], in0=h3[:, :, 0:L - 1], scalar=cw[:, 0:1], in1=c3[:, :, 1:L],
                                    op0=ALU.mult, op1=ALU.add), reads=hbufs + [fcw_b, cu_bs[half]], writes=[cu_bs[half]])
                                E.op("dve", lambda h: h.scalar_tensor_tensor(
                                    out=c3[:, :, 0:L - 1], in0=h3[:, :, 1:L], scalar=cw[:, 2:3], in1=c3[:, :, 0:L - 1],
                                    op0=ALU.mult, op1=ALU.add), reads=hbufs + [fcw_b, cu_bs[half]], writes=[cu_bs[half]])
                            else:
                                i0 = 1 if seq_start else 0
                                i1 = n - 1 if seq_end else n
                                E.op("dve", lambda h: h.scalar_tensor_tensor(
                                    out=cu[:, half, i0:n], in0=hp[:, i0:n], scalar=cw[:, 0:1], in1=cu[:, half, i0:n],
                                    op0=ALU.mult, op1=ALU.add), reads=hbufs + [fcw_b, cu_bs[half]], writes=[cu_bs[half]])
                                E.op("dve", lambda h: h.scalar_tensor_tensor(
                                    out=cu[:, half, 0:i1], in0=hp[:, 2:i1 + 2], scalar=cw[:, 2:3], in1=cu[:, half, 0:i1],
                                    op0=ALU.mult, op1=ALU.add), reads=hbufs + [fcw_b, cu_bs[half]], writes=[cu_bs[half]])
                        E.op("act", lambda h: h.activation(out=cu[:, 0, :n], in_=cu[:, 0, :n], func=AF.Gelu),
                             reads=[cu_bs[0]], writes=[cu_bs[0]])
                        E.op("pool", lambda h: h.tensor_tensor(out=Gb[:, j, :n], in0=cu[:, 0, :n], in1=cu[:, 1, :n], op=ALU.mult),
                             reads=[cu_bs[0], cu_bs[1]], writes=[Gb_b])
                for piece in range(n // 512):
                    tsl = slice(piece * 512, (piece + 1) * 512)
                    for mp in range(4):
                        slot = mp % 2
                        E.dma("sp", wd[:, slot], dnv[:, :, mp * 256:(mp + 1) * 256], reads=[dnb], writes=[wd_bs[slot]])
                        for mm_ in range(2):
                            m = mp * 2 + mm_
                            pbk = m % 2
                            mm_group(bank(pbk), [PSB[pbk]], [(wd[:, slot, j, mm_ * 128:(mm_ + 1) * 128], Gb[:, j, tsl],
                                                              [wd_bs[slot], Gb_b]) for j in range(22)])
                            E.op("act", lambda h: h.activation(out=O[:, m, :], in_=bank(pbk), func=AF.Copy),
                                 reads=[PSB[pbk]], writes=[O_b])
                    epilogue(O, O_b, 512, xsrc, xsrc_b, xdst, xdst_b, a + piece * 512, T, l, 1, v,
                             None if nxt is None else nxt(a + piece * 512), wk)
            A.release(mk_f)

        GP = dict(name="p", nseq=NPSEQ, L=SEQ, v=0, sample=False, xin=xp_d, kout=kout_d, vout=vout_d, y=yp_d, states=True)
        gstage = [stage]
        if stage >= 20:
            stage = 99
        run_group(GP)
        stage = gstage[0] - 20 if 20 <= gstage[0] < 40 else stage
        if gstage[0] >= 20:
            GS = dict(name="s", nseq=1, L=DEC_SEQ, v=1, sample=True, xin=xs_d, kout=None, vout=None, y=ys_d, states=False)
            run_group(GS)
        E.finish()
    return P


def _fm(x2d):
    T, F = x2d.shape
    return np.ascontiguousarray(x2d.reshape(T, F // 128, 128).transpose(2, 1, 0).reshape(128, -1))


def _unfm(a, T):
    return np.ascontiguousarray(a.reshape(128, 8, T).transpose(2, 1, 0).reshape(T, 1024))


def _wl(w):
    K, N = w.shape
    return np.ascontiguousarray(w.reshape(K // 128, 128, N).transpose(1, 0, 2).reshape(128, -1))


_PROG_CACHE = {}
_CONST_CACHE = {}
DBG = None


def _consts():
    if _CONST_CACHE:
        return _CONST_CACHE
    import ml_dtypes
    c = {}
    c["ident_bf"] = np.eye(128, dtype=np.float32).astype(ml_dtypes.bfloat16)
    c["ones_bf"] = np.ones((128, 128), dtype=np.float32).astype(ml_dtypes.bfloat16)
    sw = np.zeros((128, 128), np.float32)
    for m in range(128):
        sw[(m + 64) % 128, m] = 1.0
    c["swap_bf"] = sw.astype(ml_dtypes.bfloat16)
    for L in (SEQ, DEC_SEQ):
        fwd, inv = dft_tables(L)
        nsc, nfc = L // 128, (2 * L + 128) // 128
        f3 = fwd.reshape(128, nsc, nfc, 128).transpose(2, 0, 1, 3).reshape(nfc, 128, nsc * 128)
        c["dft_fwd_%d" % L] = np.ascontiguousarray(f3)
        nt = min(L, 512)
        ntb = max(1, L // 512)
        i3 = inv.reshape(128, nfc, ntb, nt).transpose(2, 0, 1, 3).reshape(ntb, 128, nfc * nt)
        c["dft_inv_%d" % L] = np.ascontiguousarray(i3)
        zT, tcol = hyena_pos_tables(L)
        c["hy_zT_%d" % L] = zT
        c["hy_tcol_%d" % L] = tcol
    selm = np.zeros((40, 16, 128), np.float32)
    for ridx in range(16):
        selm[(ridx // 8) * 32 + ridx % 8, ridx, :] = 1.0
    c["sel40"] = selm.reshape(40, -1)
    sp = np.arange(128)[:, None]
    tp = np.arange(512)[None, :]
    mk = np.zeros((128, 2, 4, 512), np.float32)
    for o in range(4):
        mk[:, 0, o, :] = (sp + 128 * o <= tp)
        mk[:, 1, o, :] = (sp + 128 * o >= tp)
    c["masks"] = mk.reshape(128, -1).astype(ml_dtypes.bfloat16)
    c["ident_f32"] = np.eye(128, dtype=np.float32)
    cosT, sinT = rope_tables(DEC_SEQ)
    c["rope_cos"] = cosT
    c["rope_sin"] = sinT
    c["hy_delta"] = np.ascontiguousarray(np.broadcast_to(np.abs(np.linspace(HY_MIN_DECAY, HY_MAX_DECAY, 512, dtype=np.float32)).reshape(1, 512), (128, 512)))
    _CONST_CACHE.update(c)
    return _CONST_CACHE


def kernel(**inp):
    global DBG
    inp = {k: np.asarray(v) for k, v in inp.items()}
    stage = int(os.environ.get("KSTAGE", "99"))
    if stage not in _PROG_CACHE:
        _PROG_CACHE[stage] = build_program(stage)
    P = _PROG_CACHE[stage]
    TP = NPSEQ * SEQ
    B = inp["x_prompt"].shape[0]
    f32 = np.float32

    sh = dict(_consts())
    sh["w_mod"] = np.stack([_wl(inp["w_mod"][l]) for l in range(2)], axis=0)
    sh["b_mod"] = np.ascontiguousarray(inp["b_mod"].reshape(2, 48, 128).transpose(2, 0, 1).reshape(128, -1))
    norms = np.stack([inp["norm_mix_pre"], inp["norm_mix_post"], inp["norm_ffn_pre"], inp["norm_ffn_post"]], 0)
    sh["norms"] = np.ascontiguousarray(norms.reshape(4, 2, 8, 128).transpose(3, 0, 1, 2).reshape(128, -1))
    sh["ah_w_in"] = _wl(inp["ah_w_in"][0])
    sh["ah_w_out"] = _wl(inp["ah_w_out"][0])
    sh["qk_norm"] = np.ascontiguousarray(np.stack([inp["attn_q_norm"][0], inp["attn_k_norm"][0]], axis=1))
    hc = np.concatenate([inp["hy_conv_w"][0], inp["hy_conv_b"][0][None]], axis=0)
    sh["hy_conv"] = np.ascontiguousarray(hc.reshape(4, 12, 128).transpose(2, 1, 0).reshape(128, -1))
    sh["hy_w1"] = np.ascontiguousarray(inp["hy_w1"][0])
    sh["hy_w2"] = np.ascontiguousarray(inp["hy_w2"][0])
    sh["hy_w3"] = np.ascontiguousarray(inp["hy_w3"][0])
    sh["hy_b12f"] = np.ascontiguousarray(np.stack([inp["hy_b1"][0], inp["hy_b2"][0], inp["hy_sin_freq"][0, 0],
                                                   inp["hy_sin_freq"][0, 1]], axis=1))
    sh["hy_b3"] = np.ascontiguousarray(np.broadcast_to(inp["hy_b3"][0].reshape(1, 1024), (128, 1024)))
    sh["hy_skip"] = np.ascontiguousarray(np.broadcast_to(inp["hy_skip"][0].reshape(1, 512), (128, 512)))
    sh["ffn_w_up"] = np.stack([_wl(inp["ffn_w_up"][l]) for l in range(2)], axis=0)
    sh["ffn_w_down"] = np.stack([_wl(inp["ffn_w_down"][l]) for l in range(2)], axis=0)
    fc = np.concatenate([inp["ffn_conv_w"], inp["ffn_conv_b"][:, None, :]], axis=1)
    sh["ffn_conv"] = np.ascontiguousarray(fc.reshape(2, 4, 44, 128).transpose(3, 0, 2, 1).reshape(128, -1))

    mw = inp["ml_w_in"][0]
    sh["ml_w_in"] = _wl(np.ascontiguousarray(mw[:, :4096]))
    wgp = np.zeros((1024, 80), f32)
    bgp = np.zeros((40, 2), f32)
    bg = inp["ml_b_gates"][0]
    for kind in range(4):
        base = (kind // 2) * 40 + (kind % 2) * 32
        wgp[:, base:base + 8] = mw[:, 4096 + kind * 8: 4096 + kind * 8 + 8]
        bgp[(kind % 2) * 32:(kind % 2) * 32 + 8, kind // 2] = bg[kind * 8:kind * 8 + 8]
    sh["ml_w_g"] = _wl(wgp)
    sh["ml_b_g"] = bgp
    mc = np.concatenate([inp["ml_conv_w"][0], inp["ml_conv_b"][0][None]], axis=0)
    sh["ml_conv"] = np.ascontiguousarray(mc.reshape(4, 16, 128).transpose(2, 1, 0).reshape(128, -1))
    sh["ml_head_norm"] = np.ascontiguousarray(inp["ml_head_norm"][0].reshape(8, 128).T)
    sh["ml_w_out"] = _wl(inp["ml_w_out"][0])

    in_maps = []
    for r in range(NCORES):
        b = r // 4
        m = dict(sh)
        m["xp"] = _fm(inp["x_prompt"][NPSEQ * r:NPSEQ * (r + 1)].reshape(TP, D))
        m["xs"] = _fm(inp["x_sample"][b])
        cvec = np.stack([inp["c_ctx"], inp["c"][b]], axis=0)
        m["cvec"] = np.ascontiguousarray(cvec.reshape(2, 8, 128).transpose(2, 1, 0).reshape(128, -1))
        ck = inp["cache_attn_k"][b, 0]
        m["cache_kT"] = np.ascontiguousarray(ck.transpose(2, 1, 0).reshape(128, -1))
        cvv = inp["cache_attn_v"][b, 0]
        m["cache_vt"] = np.ascontiguousarray(cvv.reshape(4, 128, 256).transpose(1, 0, 2).reshape(128, -1))
        m0 = np.zeros((40, 1), f32)
        sm = inp["state_mlstm_m"][b, 0]
        m0[0:8, 0] = sm[0]
        m0[32:40, 0] = sm[1]
        m["ml_m0"] = m0
        sC = inp["state_mlstm_C"][b, 0].reshape(16, 128, 128)
        m["ml_c0t"] = np.ascontiguousarray(sC.transpose(2, 0, 1).reshape(128, -1))
        sn = inp["state_mlstm_n"][b, 0].reshape(16, 128)
        m["ml_n0b"] = np.ascontiguousarray(np.broadcast_to(sn.T[:, :, None], (128, 16, 128)).reshape(128, -1))
        in_maps.append({k: np.ascontiguousarray(m[k]) for k in P.ins})
    res = run_bass_kernel_spmd(P.nc, in_maps, core_ids=list(range(NCORES)))
    R = res.results
    DBG = R

    y_prompt = np.zeros((B, SEQ, D), f32)
    y_sample = np.zeros((2, DEC_SEQ, D), f32)
    new_k = np.zeros((B, 1, SEQ, 2, 128), f32)
    new_v = np.zeros((B, 1, SEQ, 2, 128), f32)
    new_C = np.zeros((B, 1, 2, 8, 128, 128), f32)
    new_n = np.zeros((B, 1, 2, 8, 128), f32)
    new_m = np.zeros((B, 1, 2, 8), f32)
    for r in range(NCORES):
        o = R[r]
        sl = slice(NPSEQ * r, NPSEQ * (r + 1))
        new_k[sl, 0] = o["k_out"].reshape(128, 2, NPSEQ, SEQ).transpose(2, 3, 1, 0)
        new_v[sl, 0] = o["v_out"].reshape(128, TP // 128, 2, 128).transpose(1, 0, 2, 3).reshape(NPSEQ, SEQ, 2, 128)
        y_prompt[sl] = _unfm(o["yp"], TP).reshape(NPSEQ, SEQ, D)
        new_C[sl, 0] = o["c_out"].reshape(128, NPSEQ, 2, 8, 128).transpose(1, 2, 3, 0, 4)
        new_n[sl, 0] = o["n_out"].reshape(NPSEQ, 2, 8, 128)
        mo = o["m_out"]
        new_m[sl, 0, 0] = mo[0:8].T
        new_m[sl, 0, 1] = mo[32:40].T
        if r % 4 == 0:
            y_sample[r // 4] = _unfm(o["ys"], DEC_SEQ)
    return (y_prompt, y_sample, new_k, new_v, new_C, new_n, new_m)
```

```python
import math
import os
import numpy as np
import concourse.bass as bass
import concourse.mybir as mybir
from concourse.bass_utils import run_bass_kernel_spmd

AF = mybir.ActivationFunctionType
ALU = mybir.AluOpType
AX = mybir.AxisListType
F32 = mybir.dt.float32
BF16 = mybir.dt.bfloat16

D = 1024
NCORES = 8
SEQ = 256
NPSEQ = 4
DEC_SEQ = 2048
PAST = 512
DFF = 2816
EPS = 1e-6
HY_MAX_DECAY = math.log(1e-2) / 0.3
HY_MIN_DECAY = math.log(1e-2) / 1.5

SERIAL = bool(int(os.environ.get('KSERIAL', '0')))
SELF_SYNC = True


class Buf:
    __slots__ = ("name", "last_w", "readers", "excl")

    def __init__(self, name, excl=False):
        self.name = name
        self.last_w = None
        self.readers = []
        self.excl = excl


class Emit:
    ENG = ("pe", "act", "dve", "pool", "sp")

    def __init__(self, nc, n_dma_sems=40):
        self.nc = nc
        self.h = {"pe": nc.tensor, "act": nc.scalar, "dve": nc.vector, "pool": nc.gpsimd, "sp": nc.sync}
        self.ops = {e: [] for e in self.ENG}
        self.cnt = {e: 0 for e in self.ENG}
        self.seen = {e: {} for e in self.ENG}
        self.esem = {}
        self.dsem = []
        self.dval = []
        self.n_dma_sems = n_dma_sems
        self.dnext = 0
        self.sem_ctx = []
        self.dma_tokens = []

    def open_sems(self, stack):
        for e in self.ENG:
            self.esem[e] = stack.enter_context(self.nc.semaphore("c_" + e))
        for i in range(self.n_dma_sems):
            self.dsem.append(stack.enter_context(self.nc.semaphore("d%d" % i)))
            self.dval.append(0)

    def _deps(self, reads, writes):
        deps = []
        for b in reads:
            if b.last_w is not None:
                deps.append(b.last_w)
        for b in writes:
            if b.last_w is not None:
                deps.append(b.last_w)
            deps.extend(b.readers)
        return deps

    def _waits(self, eng, deps, pe_accum=False):
        need = {}
        for d in deps:
            kind, src, val = d
            if kind == "e" and src == eng and (not SELF_SYNC or (eng == "pe" and pe_accum)):
                continue
            key = (kind, src)
            if self.seen[eng].get(key, 0) >= val:
                continue
            if need.get(key, 0) < val:
                need[key] = val
        waits = []
        for (kind, src), val in need.items():
            self.seen[eng][(kind, src)] = val
            sem = self.esem[src] if kind == "e" else self.dsem[src]
            waits.append((sem, val))
        return waits

    def _commit(self, tok, reads, writes):
        for b in writes:
            b.last_w = tok
            b.readers = []
        for b in reads:
            if b in writes:
                continue
            if tok[0] == "e":
                b.readers = [r for r in b.readers if not (r[0] == "e" and r[1] == tok[1])]
            b.readers.append(tok)

    def op(self, eng, fn, reads=(), writes=(), pe_accum=False):
        ex = [b for b in reads if b.excl and b not in writes]
        if ex:
            reads = [b for b in reads if not b.excl]
            writes = list(writes) + ex
        deps = self._deps(reads, writes)
        if SERIAL:
            for e in self.ENG:
                if self.cnt[e] > 0 and not (e == eng and pe_accum):
                    deps.append(("e", e, self.cnt[e]))
            for i in range(self.n_dma_sems):
                if self.dval[i] > 0:
                    deps.append(("d", i, self.dval[i]))
        waits = self._waits(eng, deps, pe_accum)
        self.cnt[eng] += 1
        tok = ("e", eng, self.cnt[eng])
        h = self.h[eng]
        for s_, v_ in waits:
            h.wait_ge(s_, v_)
        fn(h).then_inc(self.esem[eng], 1)
        self._commit(tok, reads, writes)
        return tok

    def pe_group(self, fns, reads, writes):
        ex = [b for b in reads if b.excl and b not in writes]
        if ex:
            reads = [b for b in reads if not b.excl]
            writes = list(writes) + ex
        deps = self._deps(reads, writes)
        waits = self._waits("pe", deps, True)
        self.cnt["pe"] += 1
        tok = ("e", "pe", self.cnt["pe"])
        h = self.h["pe"]
        for s_, v_ in waits:
            h.wait_ge(s_, v_)
        last = None
        for fn in fns:
            last = fn(h)
        last.then_inc(self.esem["pe"], 1)
        self._commit(tok, reads, writes)
        return tok

    def dma(self, eng, out, in_, reads=(), writes=(), **kw):
        deps = self._deps(reads, writes)
        i = self.dnext
        self.dnext = (self.dnext + 1) % self.n_dma_sems
        if self.dval[i] > 0:
            deps.append(("d", i, self.dval[i]))
        waits = self._waits(eng, deps)
        self.dval[i] += 16
        tok = ("d", i, self.dval[i])
        h = self.h[eng]
        for s_, v_ in waits:
            h.wait_ge(s_, v_)
        h.dma_start(out=out, in_=in_, **kw).then_inc(self.dsem[i], 16)
        self._commit(tok, reads, writes)
        return tok

    def finish(self):
        h = self.h["sp"]
        for i in range(self.n_dma_sems):
            if self.dval[i] > 0:
                h.wait_ge(self.dsem[i], self.dval[i])
        for e in self.ENG:
            if e != "sp" and self.cnt[e] > 0:
                h.wait_ge(self.esem[e], self.cnt[e])


class Arena:
    def __init__(self, nc, lo, hi):
        self.nc, self.lo, self.hi, self.p = nc, lo, hi, lo
        self.n = 0
        self.live = []

    def alloc(self, name, shape, dtype, nbufs=1, top=False):
        esz = 4 if dtype == F32 else 2
        per = 1
        for s in shape[1:]:
            per *= s
        nbytes = (per * esz + 31) // 32 * 32
        assert self.p + nbytes <= self.hi, "SBUF arena overflow at %s (%d + %d > %d)" % (name, self.p, nbytes, self.hi)
        if top:
            self.hi -= nbytes
            off = self.hi
        else:
            off = self.p
            self.p += nbytes
        self.n += 1
        t = self.nc.alloc_sbuf_tensor_at("%s_%d" % (name, self.n), list(shape), dtype, offset=off)
        bufs = [Buf("%s.%d" % (name, i)) for i in range(nbufs)]
        keep = []
        for (l, h, bs) in self.live:
            if l < off + nbytes and off < h:
                for ob in bs:
                    for nb in bufs:
                        if ob.last_w is not None:
                            nb.readers.append(ob.last_w)
                        nb.readers.extend(ob.readers)
                if l < off or h > off + nbytes:
                    keep.append((l, h, bs))
            else:
                keep.append((l, h, bs))
        keep.append((off, off + nbytes, bufs))
        self.live = keep
        ap = t.ap()
        return (ap, bufs[0]) if nbufs == 1 else (ap, bufs)

    def mark_top(self):
        return self.hi

    def release_top(self, m):
        self.hi = m

    def mark(self):
        return self.p

    def release(self, m):
        self.p = m


def _bf16(a):
    import ml_dtypes
    return np.ascontiguousarray(a.astype(np.float32)).astype(ml_dtypes.bfloat16)


def dft_tables(L):
    n = 2 * L
    s = np.arange(L, dtype=np.float64)
    fr = np.arange(L + 128, dtype=np.float64)
    fi = np.arange(L, dtype=np.float64)
    FR = np.cos(2 * np.pi * np.outer(s, fr) / n)
    FR[:, L + 1:] = 0.0
    FI = -np.sin(2 * np.pi * np.outer(s, fi) / n)
    cf = np.full(L + 128, 2.0)
    cf[0] = 1.0
    cf[L] = 1.0
    cf[L + 1:] = 0.0
    IR = (cf[:, None] / n) * np.cos(2 * np.pi * np.outer(fr, s) / n)
    II = -(2.0 / n) * np.sin(2 * np.pi * np.outer(fi, s) / n)
    fwd = np.concatenate([FR, FI], axis=1)
    inv = np.concatenate([IR, II], axis=0)
    nsc = L // 128
    nfc = (2 * L + 128) // 128
    fwd_l = fwd.reshape(nsc, 128, nfc * 128).transpose(1, 0, 2)
    inv_l = inv.reshape(nfc, 128, L).transpose(1, 0, 2)
    return _bf16(fwd_l.reshape(128, -1)), _bf16(inv_l.reshape(128, -1))


def hyena_pos_tables(L):
    t = np.linspace(0.0, 1.0, L, dtype=np.float32)
    bands = np.arange(1, 9, dtype=np.float32)
    ang = 2.0 * np.pi * t[:, None] * bands
    z = np.concatenate([t[:, None], np.cos(ang), np.sin(ang)], axis=-1).astype(np.float32)
    zT = np.ascontiguousarray(z.T)
    tcol = np.ascontiguousarray(t.reshape(L // 128, 128).T)
    return zT, tcol


def rope_tables(L):
    rows = L // 64
    row = np.repeat(np.arange(rows, dtype=np.float32), 64)
    col = np.tile(np.arange(64, dtype=np.float32), rows)
    n_freq = 32
    inv = (10000.0 ** (-np.arange(n_freq, dtype=np.float32) / n_freq)).astype(np.float32)
    ang = np.concatenate([row[:, None] * inv, col[:, None] * inv], axis=-1)
    cos, sin = np.cos(ang).astype(np.float32), np.sin(ang).astype(np.float32)
    cosT = np.concatenate([cos.T, cos.T], axis=0)
    sinT = np.concatenate([-sin.T, sin.T], axis=0)
    return np.ascontiguousarray(cosT), np.ascontiguousarray(sinT)


class Prog:
    def __init__(self, stage):
        self.stage = stage
        self.nc = bass.Bass("TRN2", target_bir_lowering=False)
        self.ins = {}
        self.outs = {}

    def din(self, name, shape, dtype=F32):
        t = self.nc.dram_tensor(name, list(shape), dtype, kind="ExternalInput")
        self.ins[name] = t
        return t.ap()

    def dout(self, name, shape, dtype=F32):
        t = self.nc.dram_tensor(name, list(shape), dtype, kind="ExternalOutput")
        self.outs[name] = t
        return t.ap()


TWO_PI = 2.0 * math.pi
MAGIC = 12582912.0


def build_program(stage=99):
    P = Prog(stage)
    nc = P.nc
    TP = NPSEQ * SEQ
    TS = DEC_SEQ

    xp_d = P.din("xp", [128, 8 * TP])
    xs_d = P.din("xs", [128, 8 * TS])
    cvec_d = P.din("cvec", [128, 8 * 2])
    wmod_d = P.din("w_mod", [2, 128, 8 * 6144])
    bmod_d = P.din("b_mod", [128, 2 * 48])
    norms_d = P.din("norms", [128, 4 * 2 * 8])
    ahwin_d = P.din("ah_w_in", [128, 8 * 2560])
    ahwout_d = P.din("ah_w_out", [128, 8 * 1024])
    qkn_d = P.din("qk_norm", [128, 2])
    idb_d = P.din("ident_bf", [128, 128], BF16)
    ones_d = P.din("ones_bf", [128, 128], BF16)
    swap_d = P.din("swap_bf", [128, 128], BF16)
    hcw_d = P.din("hy_conv", [128, 12 * 4])
    hyw1_d = P.din("hy_w1", [17, 64])
    hyw2_d = P.din("hy_w2", [64, 64])
    hyw3_d = P.din("hy_w3", [64, 1024])
    hyb12_d = P.din("hy_b12f", [64, 4])
    hyb3_d = P.din("hy_b3", [128, 1024])
    hyskip_d = P.din("hy_skip", [128, 512])
    hydelta_d = P.din("hy_delta", [128, 512])
    ffnup_d = P.din("ffn_w_up", [2, 128, 8 * 5632])
    ffndn_d = P.din("ffn_w_down", [2, 128, 22 * 1024])
    ffncw_d = P.din("ffn_conv", [128, 2 * 44 * 4])
    tabs = {}
    for L in (SEQ, DEC_SEQ):
        nsc, nfc = L // 128, (2 * L + 128) // 128
        tabs[L] = dict(
            fwd=P.din("dft_fwd_%d" % L, [nfc, 128, nsc * 128], BF16),
            inv=P.din("dft_inv_%d" % L, [max(1, L // 512), 128, nfc * min(L, 512)], BF16),
            zT=P.din("hy_zT_%d" % L, [17, L]),
            tcol=P.din("hy_tcol_%d" % L, [128, L // 128]))
    mlwin_d = P.din("ml_w_in", [128, 8 * 4096])
    mlwg_d = P.din("ml_w_g", [128, 8 * 80])
    mlbg_d = P.din("ml_b_g", [40, 2])
    mlcw_d = P.din("ml_conv", [128, 16 * 4])
    mlhn_d = P.din("ml_head_norm", [128, 8])
    mlwout_d = P.din("ml_w_out", [128, 8 * 1024])
    sel_d = P.din("sel40", [40, 16 * 128])
    masks_d = P.din("masks", [128, 2 * 4 * 512], BF16)
    id32_d = P.din("ident_f32", [128, 128])
    m0_d = P.din("ml_m0", [40, 1])
    c0t_d = P.din("ml_c0t", [128, 16 * 128])
    n0b_d = P.din("ml_n0b", [128, 16 * 128])
    ropec_d = P.din("rope_cos", [128, TS])
    ropes_d = P.din("rope_sin", [128, TS])
    ckT_d = P.din("cache_kT", [128, 2 * PAST])
    cvt_d = P.din("cache_vt", [128, (PAST // 128) * 256])

    kout_d = P.dout("k_out", [128, 2 * TP])
    vout_d = P.dout("v_out", [128, (TP // 128) * 256])
    cst_d = P.dout("c_out", [128, NPSEQ * 16 * 128])
    nst_d = P.dout("n_out", [1, NPSEQ * 16 * 128])
    mst_d = P.dout("m_out", [40, NPSEQ])
    yp_d = P.dout("yp", [128, 8 * TP])
    ys_d = P.dout("ys", [128, 8 * TS])

    from contextlib import ExitStack
    with ExitStack() as stack:
        E = Emit(nc)
        E.open_sems(stack)
        A = Arena(nc, 16640, 229376 - 1024)
        ps_t = nc.alloc_psum_tensor("ps_all", [128, 7 * 512], F32)
        PS = ps_t.ap()
        PSB = [Buf("ps%d" % i, excl=True) for i in range(7)]
        psbf_t = nc.alloc_psum_tensor("ps_bf", [128, 1024], BF16)
        PSbf = {4: psbf_t.ap()[:, 0:128], 5: psbf_t.ap()[:, 128:256]}
        _pb = Buf("psbf", excl=True)
        PSbf_b = {4: _pb, 5: _pb}

        def bank(i, n=512, off=0):
            return PS[:, i * 512 + off: i * 512 + off + n]

        def mm_group(out_ap, out_bufs, terms):
            n = len(terms)
            rset = []
            for (_, _, rb) in terms:
                for x_ in rb:
                    if x_ not in rset:
                        rset.append(x_)
            fns = [(lambda h, i=i, l_ap=l_ap, r_ap=r_ap: h.matmul(out_ap, lhsT=l_ap, rhs=r_ap, start=(i == 0), stop=(i == n - 1)))
                   for i, (l_ap, r_ap, rb) in enumerate(terms)]
            E.pe_group(fns, rset, out_bufs)

        dram_bufs = {}

        def dscratch(name, shape, dtype=F32):
            t = nc.dram_tensor(name, list(shape), dtype)
            b = Buf(name)
            dram_bufs[name] = b
            return t.ap(), b

        WB = {}

        def conv_weight(name, src_ap, shape):
            t_ap, t_b = dscratch(name + "_bf", shape, BF16)
            E.dma("pool", t_ap, src_ap, writes=[t_b])
            WB[name] = (t_ap, t_b)

        conv_weight("ah_w_in", ahwin_d, [128, 8 * 2560])
        conv_weight("ah_w_out", ahwout_d, [128, 8 * 1024])
        conv_weight("ffn_up0", ffnup_d[0], [128, 8 * 5632])
        conv_weight("ffn_dn0", ffndn_d[0], [128, 22 * 1024])
        conv_weight("ml_w_in", mlwin_d, [128, 8 * 4096])
        conv_weight("ml_w_out", mlwout_d, [128, 8 * 1024])
        conv_weight("ffn_up1", ffnup_d[1], [128, 8 * 5632])
        conv_weight("ffn_dn1", ffndn_d[1], [128, 22 * 1024])

        def const(name, shape, dtype, src, eng="sp"):
            ap, b = A.alloc(name, shape, dtype)
            E.dma(eng, ap, src, writes=[b])
            return ap, b

        ident, ident_b = const("ident", [128, 128], BF16, idb_d)
        ones, ones_b = const("ones", [128, 128], BF16, ones_d)
        swp, swp_b = const("swap", [128, 128], BF16, swap_d)
        epsc, epsc_b = A.alloc("epsc", [128, 2], F32)
        E.op("dve", lambda h: h.memset(epsc, EPS), writes=[epsc_b])
        cv, cv_b = const("cvec", [128, 8, 2], F32, cvec_d.rearrange("p (j v) -> p j v", v=2))
        bm, bm_b = const("bmod", [128, 2, 48], F32, bmod_d.rearrange("p (l m) -> p l m", l=2))
        nrm, nrm_b = const("norms", [128, 4, 2, 8], F32, norms_d.rearrange("p (w l j) -> p w l j", w=4, l=2))
        qkn, qkn_b = const("qkn", [128, 2], F32, qkn_d)
        hcw, hcw_b = const("hcw", [128, 12, 4], F32, hcw_d.rearrange("p (c k) -> p c k", k=4))
        fcw, fcw_b = const("fcw", [128, 2, 44, 4], F32, ffncw_d.rearrange("p (l c k) -> p l c k", l=2, k=4))

        mcw, mcw_b = const("mcw", [128, 16, 4], F32, mlcw_d.rearrange("p (c k) -> p c k", k=4))
        mhn, mhn_b = const("mhn", [128, 8], F32, mlhn_d)
        mbg, mbg_b = const("mbg", [40, 2], F32, mlbg_d)
        sel, sel_b = const("sel", [40, 16, 128], F32, sel_d.rearrange("p (r m) -> p r m", m=128))
        msk, msk_b = const("msk", [128, 2, 4, 512], BF16, masks_d.rearrange("p (d o t) -> p d o t", d=2, o=4))
        id32, id32_b = const("id32", [128, 128], F32, id32_d)
        onec, onec_b = A.alloc("onec", [128, 2], F32)
        E.op("dve", lambda h: h.memset(onec, 1.0), writes=[onec_b])
        sc, sc_b = A.alloc("silu_c", [128, 8, 2], BF16)
        sig, sig_b = A.alloc("sig_c", [128, 8, 2], F32)
        E.op("act", lambda h: h.activation(out=sig, in_=cv, func=AF.Sigmoid), reads=[cv_b], writes=[sig_b])
        E.op("dve", lambda h: h.tensor_tensor(out=sc, in0=cv, in1=sig, op=ALU.mult), reads=[cv_b, sig_b], writes=[sc_b])
        MOD, MOD_b = A.alloc("mod", [128, 2, 48, 2], F32)
        mk = A.mark()
        wm, wm_bs = A.alloc("wmod_st", [128, 2, 8, 512], BF16, nbufs=2)
        it = 0
        for l in range(2):
            for cg in range(12):
                slot = it % 2
                it += 1
                src = wmod_d[l].rearrange("p (kc n) -> p kc n", kc=8)[:, :, cg * 512:(cg + 1) * 512]
                E.dma("pool", wm[:, slot], src, writes=[wm_bs[slot]])
                for mm in range(4):
                    m = cg * 4 + mm
                    mm_group(bank(l, 2, 2 * m), [PSB[l]],
                             [(wm[:, slot, kc, mm * 128:(mm + 1) * 128], sc[:, kc, :], [wm_bs[slot], sc_b])
                              for kc in range(8)])
            E.op("dve", lambda h: h.tensor_tensor(
                out=MOD[:, l], in0=bank(l, 96).rearrange("p (m v) -> p m v", v=2),
                in1=bm[:, l, :].unsqueeze(2).to_broadcast([128, 48, 2]), op=ALU.add),
                reads=[PSB[l], bm_b], writes=[MOD_b])
        A.release(mk)

        COEF, COEF_b = A.alloc("coef", [128, 2, 2, 3, 8, 2], F32)
        for l in range(2):
            for part in range(2):
                sh, scl, gt = 3 * part, 3 * part + 1, 3 * part + 2
                gpre = nrm[:, 2 * part, l, :].unsqueeze(2).to_broadcast([128, 8, 2])
                gpost = nrm[:, 2 * part + 1, l, :].unsqueeze(2).to_broadcast([128, 8, 2])
                E.op("dve", lambda h: h.scalar_tensor_tensor(
                    out=COEF[:, l, part, 0], in0=MOD[:, l, scl * 8:(scl + 1) * 8, :], scalar=1.0, in1=gpre,
                    op0=ALU.add, op1=ALU.mult), reads=[MOD_b, nrm_b], writes=[COEF_b])
                E.op("dve", lambda h: h.tensor_copy(
                    out=COEF[:, l, part, 1], in_=MOD[:, l, sh * 8:(sh + 1) * 8, :]), reads=[MOD_b], writes=[COEF_b])
                E.op("dve", lambda h: h.tensor_tensor(
                    out=COEF[:, l, part, 2], in0=MOD[:, l, gt * 8:(gt + 1) * 8, :], in1=gpost, op=ALU.mult),
                    reads=[MOD_b, nrm_b], writes=[COEF_b])

        def sumsq_rstd(src_chunks, src_bufs, n, dim, rstd, rstd_b, sq, sq_b, psb):
            nch = len(src_chunks)
            for j, s_ in enumerate(src_chunks):
                E.op("act", lambda h: h.activation(out=sq[:, j, :n], in_=s_, func=AF.Square),
                     reads=[src_bufs[j]], writes=[sq_b])
            mm_group(bank(psb, n), [PSB[psb]], [(ones, sq[:, j, :n], [ones_b, sq_b]) for j in range(nch)])
            E.op("act", lambda h: h.activation(out=rstd[:, :n], in_=bank(psb, n), func=AF.Sqrt, bias=epsc[:, 0:1],
                                               scale=1.0 / dim), reads=[PSB[psb], epsc_b], writes=[rstd_b])
            E.op("dve", lambda h: h.reciprocal(out=rstd[:, :n], in_=rstd[:, :n]), reads=[rstd_b], writes=[rstd_b])

        def modulate_block(xb, xb_buf, n, l, part, v, dst_fn, dst_buf, sq, sq_b, rstd, rstd_b, tmp, tmp_b):
            sumsq_rstd([xb[:, j, :n] for j in range(8)], [xb_buf] * 8, n, D, rstd, rstd_b, sq, sq_b, 6)
            for j in range(8):
                E.op("dve", lambda h: h.scalar_tensor_tensor(
                    out=tmp[:, :n], in0=xb[:, j, :n], scalar=COEF[:, l, part, 0, j, v:v + 1], in1=rstd[:, :n],
                    op0=ALU.mult, op1=ALU.mult), reads=[xb_buf, COEF_b, rstd_b], writes=[tmp_b])
                E.op("act", lambda h: h.activation(
                    out=dst_fn(j), in_=tmp[:, :n], func=AF.Identity, bias=COEF[:, l, part, 1, j, v:v + 1], scale=1.0),
                    reads=[tmp_b, COEF_b], writes=[dst_buf])

        def epilogue(O, O_b, n, xsrc, xsrc_b, xdst, xdst_b, tok0, T, l, part, v, nxt, wk):
            (sq, sq_b, rstd, rstd_b, tmp, tmp_b, xb, xb_b) = wk
            E.dma("sp", xb[:, :, :n], xsrc.rearrange("p (j t) -> p j t", j=8)[:, :, tok0:tok0 + n],
                  reads=[xsrc_b], writes=[xb_b])
            sumsq_rstd([O[:, j, :n] for j in range(8)], [O_b] * 8, n, D, rstd, rstd_b, sq, sq_b, 6)
            for j in range(8):
                E.op("dve", lambda h: h.scalar_tensor_tensor(
                    out=tmp[:, :n], in0=O[:, j, :n], scalar=COEF[:, l, part, 2, j, v:v + 1], in1=rstd[:, :n],
                    op0=ALU.mult, op1=ALU.mult), reads=[O_b, COEF_b, rstd_b], writes=[tmp_b])
                E.op("pool", lambda h: h.tensor_tensor(out=xb[:, j, :n], in0=xb[:, j, :n], in1=tmp[:, :n], op=ALU.add),
                     reads=[tmp_b, xb_b], writes=[xb_b])
            E.dma("sp", xdst.rearrange("p (j t) -> p j t", j=8)[:, :, tok0:tok0 + n], xb[:, :, :n],
                  reads=[xb_b], writes=[xdst_b])
            if nxt is not None:
                l2, part2, dst_fn, dst_buf = nxt
                modulate_block(xb, xb_b, n, l2, part2, v, dst_fn, dst_buf, sq, sq_b, rstd, rstd_b, tmp, tmp_b)

        def hyena_filters(L, keep):
            nsc = L // 128
            GR, GR_b = keep.alloc("GR%d" % L, [128, nsc + 1, 512], BF16, top=True)
            GI, GI_b = keep.alloc("GI%d" % L, [128, nsc, 512], BF16, top=True)
            mk_ = A.mark()
            zT, zT_b = const("zT", [17, L], F32, tabs[L]["zT"])
            tcol, tcol_b = const("tcol", [128, nsc], F32, tabs[L]["tcol"])
            w1, w1_b = const("hw1", [17, 64], F32, hyw1_d)
            w2, w2_b = const("hw2", [64, 64], F32, hyw2_d)
            w3, w3_b = const("hw3", [64, 1024], F32, hyw3_d)
            b12, b12_b = const("hb12", [64, 4], F32, hyb12_d)
            b3, b3_b = const("hb3", [128, 1024], F32, hyb3_d)
            skp, skp_b = const("hskip", [128, 512], F32, hyskip_d)
            dlt, dlt_b = const("hdelta", [128, 512], F32, hydelta_d)
            ntc, ntc_b = A.alloc("ntcol", [128, nsc], F32)
            E.op("dve", lambda h: h.tensor_scalar(out=ntc, in0=tcol, scalar1=-1.0, scalar2=None, op0=ALU.mult),
                 reads=[tcol_b], writes=[ntc_b])
            H1, H1_b = A.alloc("h1", [64, L], F32)
            H2, H2_b = A.alloc("h2", [64, L], F32)
            t1, t1_b = A.alloc("ht1", [64, 512], F32)
            t2, t2_b = A.alloc("ht2", [64, 512], F32)

            def sin_layer(dst, dst_b, w_ap, w_b, src, src_b, bcol, fcol):
                for c0 in range(0, L, 512):
                    n = min(512, L - c0)
                    mm_group(PS[0:64, 0:n], [PSB[0]], [(w_ap, src[:, c0:c0 + n], [w_b, src_b])])
                    E.op("dve", lambda h: h.tensor_scalar(out=t1[:, :n], in0=PS[0:64, 0:n], scalar1=b12[:, bcol:bcol + 1],
                                                          scalar2=b12[:, fcol:fcol + 1], op0=ALU.add, op1=ALU.mult),
                         reads=[PSB[0], b12_b], writes=[t1_b])
                    E.op("dve", lambda h: h.tensor_scalar(out=t2[:, :n], in0=t1[:, :n], scalar1=1.0 / TWO_PI, scalar2=MAGIC,
                                                          op0=ALU.mult, op1=ALU.add), reads=[t1_b], writes=[t2_b])
                    E.op("dve", lambda h: h.tensor_scalar(out=t2[:, :n], in0=t2[:, :n], scalar1=MAGIC, scalar2=-TWO_PI,
                                                          op0=ALU.subtract, op1=ALU.mult), reads=[t2_b], writes=[t2_b])
                    E.op("dve", lambda h: h.tensor_tensor(out=t1[:, :n], in0=t1[:, :n], in1=t2[:, :n], op=ALU.add),
                         reads=[t1_b, t2_b], writes=[t1_b])
                    E.op("act", lambda h: h.activation(out=dst[:, c0:c0 + n], in_=t1[:, :n], func=AF.Sin),
                         reads=[t1_b], writes=[dst_b])

            sin_layer(H1, H1_b, w1, w1_b, zT, zT_b, 0, 2)
            sin_layer(H2, H2_b, w2, w2_b, H1, H1_b, 1, 3)
            GS, GS_b = A.alloc("gs", [128, nsc, 512], BF16)
            GD, GD_b = A.alloc("gd", [128, nsc, 512], BF16)
            Fm, Fm_b = A.alloc("fm", [128, 1024], F32)
            win, win_b = A.alloc("win", [128, 512], F32)
            fs, fs_b = A.alloc("fs", [128, 512], F32)
            for tc in range(nsc):
                for hh in range(2):
                    mm_group(bank(hh), [PSB[hh]], [(H2[:, tc * 128:(tc + 1) * 128], w3[:, hh * 512:(hh + 1) * 512],
                                                    [H2_b, w3_b])])
                    E.op("dve", lambda h: h.tensor_tensor(out=Fm[:, hh * 512:(hh + 1) * 512], in0=bank(hh),
                                                          in1=b3[:, hh * 512:(hh + 1) * 512], op=ALU.add),
                         reads=[PSB[hh], b3_b], writes=[Fm_b])
                E.op("act", lambda h: h.activation(out=win, in_=dlt, func=AF.Exp, scale=ntc[:, tc:tc + 1]),
                     reads=[dlt_b, ntc_b], writes=[win_b])
                E.op("dve", lambda h: h.tensor_tensor(out=fs, in0=Fm[:, 0:512], in1=Fm[:, 512:1024], op=ALU.add),
                     reads=[Fm_b], writes=[fs_b])
                E.op("dve", lambda h: h.tensor_tensor(out=GS[:, tc, :], in0=fs, in1=win, op=ALU.mult),
                     reads=[fs_b, win_b], writes=[GS_b])
                E.op("dve", lambda h: h.tensor_tensor(out=fs, in0=Fm[:, 0:512], in1=Fm[:, 512:1024], op=ALU.subtract),
                     reads=[Fm_b], writes=[fs_b])
                E.op("dve", lambda h: h.tensor_tensor(out=GD[:, tc, :], in0=fs, in1=win, op=ALU.mult),
                     reads=[fs_b, win_b], writes=[GD_b])
            fw, fw_bs = A.alloc("fwst", [128, 2, nsc, 128], BF16, nbufs=2)
            nfc = 2 * nsc + 1
            for fc in range(nfc):
                slot = fc % 2
                E.dma("sp", fw[:, slot], tabs[L]["fwd"][fc].rearrange("p (s f) -> p s f", f=128), writes=[fw_bs[slot]])
                src, src_b = (GS, GS_b) if fc <= nsc else (GD, GD_b)
                pb = fc % 2
                mm_group(bank(pb), [PSB[pb]], [(fw[:, slot, s_, :], src[:, s_, :], [fw_bs[slot], src_b]) for s_ in range(nsc)])
                if fc <= nsc:
                    E.op("dve", lambda h: h.tensor_tensor(out=GR[:, fc, :], in0=bank(pb), in1=skp, op=ALU.add),
                         reads=[PSB[pb], skp_b], writes=[GR_b])
                else:
                    E.op("act", lambda h: h.activation(out=GI[:, fc - nsc - 1, :], in_=bank(pb), func=AF.Copy),
                         reads=[PSB[pb]], writes=[GI_b])
            A.release(mk_)
            return GR, GR_b, GI, GI_b

        def run_group(G):
            gname, nseq, L, v = G["name"], G["nseq"], G["L"], G["v"]
            smp = G["sample"]
            T = nseq * L
            nblk = T // 512
            nkv = L + (PAST if smp else 0)
            X0d, X0d_b = G["xin"], Buf(gname + "_xin")
            X1d, X1d_b = dscratch(gname + "_x1", [128, 8 * T])
            X2d, X2d_b = dscratch(gname + "_x2", [128, 8 * T])
            mk_g = A.mark()
            MIX, MIX_bs = A.alloc(gname + "_mix", [128, 8, T], BF16, nbufs=8)
            mk_m = A.mark()
            mk_top = A.mark_top()
            GR, GR_b, GI, GI_b = hyena_filters(L, A)
            HS, HS_bs = A.alloc(gname + "_hs", [128, 8, T], BF16, nbufs=nblk)
            mk_a = A.mark()
            sq, sq_b = A.alloc("sq", [128, 8, 512], BF16)
            rstd, rstd_b = A.alloc("rstd", [128, 512], F32)
            tmp, tmp_b = A.alloc("tmp", [128, 512], F32)
            xb, xb_bs = A.alloc("xb", [128, 2, 8, 512], F32, nbufs=2)
            for blk in range(nblk):
                sl = blk % 2
                E.dma("sp", xb[:, sl], X0d.rearrange("p (j t) -> p j t", j=8)[:, :, blk * 512:(blk + 1) * 512],
                      writes=[xb_bs[sl]])
                modulate_block(xb[:, sl], xb_bs[sl], 512, 0, 0, v,
                               lambda j: HS[:, j, blk * 512:(blk + 1) * 512], HS_bs[blk], sq, sq_b, rstd, rstd_b, tmp, tmp_b)
            A.release(mk_a)
            if stage == 2:
                return

            QT, QT_b = A.alloc("QT", [128, 4, T], BF16)
            KT, KT_b = A.alloc("KT", [128, 2, nseq, nkv], BF16)
            VTb, VTb_b = A.alloc("VTb", [128, nseq * (nkv // 128), 256], BF16)
            mk_q = A.mark()
            wq, wq_b = A.alloc("wq", [128, 8, 1024], BF16)
            E.dma("sp", wq, WB["ah_w_in"][0].rearrange("p (kc n) -> p kc n", kc=8)[:, :, 0:1024], reads=[WB["ah_w_in"][1]], writes=[wq_b])
            sq, sq_b = A.alloc("sq", [128, 1, 512], BF16)
            rstd, rstd_b = A.alloc("rstd", [128, 512], F32)
            qn, qn_b = A.alloc("qn", [128, 512], F32)
            qb16, qb16_b = A.alloc("qb16", [128, 512], BF16)
            r1, r1_b = A.alloc("r1", [128, 512], F32)
            if smp:
                rc, rc_b = const("ropec", [128, TS], F32, ropec_d)
                rs, rs_b = const("ropes", [128, TS], F32, ropes_d)
                E.dma("pool", KT[:, :, 0, L:], ckT_d.rearrange("p (g t) -> p g t", g=2), writes=[KT_b])
                E.dma("pool", VTb[:, L // 128:, :], cvt_d.rearrange("p (c e) -> p c e", e=256), writes=[VTb_b])
            else:
                KN, KN_b = A.alloc("KN", [128, 2, T], F32)
                VT, VT_b = A.alloc("VT", [128, T // 128, 256], F32)
            KSUB = int(os.environ.get("KSUB", "0"))
            if KSUB == 1:
                return
            for hq in range(6):
                if KSUB == 2 and hq >= 4:
                    break
                for blk in range(nblk):
                    ts = slice(blk * 512, (blk + 1) * 512)
                    pb = blk % 2
                    mm_group(bank(pb), [PSB[pb]], [(wq[:, kc, hq * 128:(hq + 1) * 128], HS[:, kc, ts], [wq_b, HS_bs[blk]])
                                                   for kc in range(8)])
                    sumsq_rstd([bank(pb)], [PSB[pb]], 512, 128, rstd, rstd_b, sq, sq_b, 2 + pb)
                    gcol = 0 if hq < 4 else 1
                    if hq < 4:
                        dst = QT[:, hq, ts]
                        dst_b = QT_b
                    else:
                        s_i, t0 = (blk * 512) // L, (blk * 512) % L
                        dst_b = KT_b
                    if not smp:
                        if hq < 4:
                            E.op("dve", lambda h: h.scalar_tensor_tensor(
                                out=dst, in0=bank(pb), scalar=qkn[:, 0:1], in1=rstd, op0=ALU.mult, op1=ALU.mult),
                                reads=[PSB[pb], qkn_b, rstd_b], writes=[dst_b])
                        else:
                            g = hq - 4
                            E.op("dve", lambda h: h.scalar_tensor_tensor(
                                out=KN[:, g, ts], in0=bank(pb), scalar=qkn[:, 1:2], in1=rstd, op0=ALU.mult, op1=ALU.mult),
                                reads=[PSB[pb], qkn_b, rstd_b], writes=[KN_b])
                            nsq = 512 // L
                            E.op("act", lambda h: h.activation(
                                out=KT[:, g, s_i:s_i + nsq, 0:L], in_=KN[:, g, ts].rearrange("p (s t) -> p s t", t=L),
                                func=AF.Copy), reads=[KN_b], writes=[KT_b])
                    else:
                        E.op("dve", lambda h: h.scalar_tensor_tensor(
                            out=qn, in0=bank(pb), scalar=qkn[:, gcol:gcol + 1], in1=rstd, op0=ALU.mult, op1=ALU.mult),
                            reads=[PSB[pb], qkn_b, rstd_b], writes=[qn_b])
                        E.op("act", lambda h: h.activation(out=qb16, in_=qn, func=AF.Copy), reads=[qn_b], writes=[qb16_b])
                        mm_group(bank(4 + pb), [PSB[4 + pb]], [(swp, qb16, [swp_b, qb16_b])])
                        E.op("dve", lambda h: h.tensor_tensor(out=r1, in0=bank(4 + pb), in1=rs[:, ts], op=ALU.mult),
                             reads=[PSB[4 + pb], rs_b], writes=[r1_b])
                        E.op("pool", lambda h: h.tensor_tensor(out=qn, in0=qn, in1=rc[:, ts], op=ALU.mult),
                             reads=[qn_b, rc_b], writes=[qn_b])
                        if hq < 4:
                            d2 = dst
                        else:
                            d2 = KT[:, hq - 4, 0, t0:t0 + 512]
                        E.op("dve", lambda h: h.tensor_tensor(out=d2, in0=qn, in1=r1, op=ALU.add),
                             reads=[qn_b, r1_b], writes=[dst_b])
            if G.get("kout") is not None:
                E.dma("sp", G["kout"].rearrange("p (g t) -> p g t", g=2), KN, reads=[KN_b])
            if KSUB in (2, 3):
                return
            for c in range(T // 128):
                pb = 4 + c % 2
                blk = c // 4
                s_i, cc = (c * 128) // L, ((c * 128) % L) // 128
                mm_group(bank(pb, 256), [PSB[pb]], [(HS[:, kc, c * 128:(c + 1) * 128], wq[:, kc, 768:1024], [wq_b, HS_bs[blk]])
                                                    for kc in range(8)])
                if not smp and KSUB != 5:
                    E.op("dve", lambda h: h.tensor_copy(out=VT[:, c, :], in_=bank(pb, 256)),
                         reads=[PSB[pb]], writes=[VT_b])
                if KSUB != 6:
                    E.op("act", lambda h: h.activation(out=VTb[:, s_i * (nkv // 128) + cc, :], in_=bank(pb, 256), func=AF.Copy),
                         reads=[PSB[pb]], writes=[VTb_b])
            if G.get("vout") is not None:
                E.dma("sp", G["vout"].rearrange("p (c e) -> p c e", e=256), VT, reads=[VT_b])
            A.release(mk_q)
            if stage == 3:
                return
            Pt, Pt_bs = A.alloc("Pt", [128, 2, 512], BF16, nbufs=2)
            rden, rden_b = A.alloc("rden", [128, 512], F32)
            nq = min(512, L)
            nkc = nkv // 128
            att_scale = 1.0 / math.sqrt(128.0)
            for s_i in range(nseq):
                for hd in range(4):
                    g = hd // 2
                    for qb in range(L // nq):
                        q0 = s_i * L + qb * nq
                        for kc in range(nkc):
                            sb = kc % 2
                            mm_group(bank(sb, nq), [PSB[sb]], [(KT[:, g, s_i, kc * 128:(kc + 1) * 128], QT[:, hd, q0:q0 + nq],
                                                                [KT_b, QT_b])])
                            E.op("act", lambda h: h.activation(out=Pt[:, sb, :nq], in_=bank(sb, nq), func=AF.Exp,
                                                               scale=att_scale), reads=[PSB[sb]], writes=[Pt_bs[sb]])
                            E.op("pe", lambda h: h.matmul(bank(2, nq), lhsT=VTb[:, s_i * nkc + kc, g * 128:(g + 1) * 128],
                                                          rhs=Pt[:, sb, :nq], start=(kc == 0), stop=(kc == nkc - 1)),
                                 reads=[VTb_b, Pt_bs[sb]], writes=[PSB[2]], pe_accum=True)
                            E.op("pe", lambda h: h.matmul(bank(3, nq), lhsT=ones, rhs=Pt[:, sb, :nq],
                                                          start=(kc == 0), stop=(kc == nkc - 1)),
                                 reads=[ones_b, Pt_bs[sb]], writes=[PSB[3]], pe_accum=True)
                        E.op("dve", lambda h: h.reciprocal(out=rden[:, :nq], in_=bank(3, nq)), reads=[PSB[3]], writes=[rden_b])
                        E.op("dve", lambda h: h.tensor_tensor(out=MIX[:, hd, q0:q0 + nq], in0=bank(2, nq), in1=rden[:, :nq],
                                                              op=ALU.mult), reads=[PSB[2], rden_b], writes=[MIX_bs[hd]])
            A.release(mk_a)
            if stage == 4:
                return

            nsc = L // 128
            nfc = 2 * nsc + 1
            X0, X0_b = A.alloc("X0", [128, 4, T], BF16, top=True)
            VPT, VPT_b = A.alloc("VPT", [128, nseq, nsc, 512], BF16, top=True)
            mk_h = A.mark()
            wu, wu_bs = A.alloc("wu", [128, 2, 8, 128], BF16, nbufs=2)
            wu_it = [0]
            U, U_b = A.alloc("U", [128, T], F32)
            CU, CU_bs = A.alloc("CU", [128, 2, T], F32, nbufs=2)
            VPc, VPc_b = A.alloc("VPc", [128, T], BF16)
            for c in range(4):
                for ti, which in enumerate((1, 2, 0)):
                    ch = which * 4 + c
                    wsl = wu_it[0] % 2
                    wu_it[0] += 1
                    E.dma("sp", wu[:, wsl], WB["ah_w_in"][0].rearrange("p (kc n) -> p kc n", kc=8)[:, :, 1024 + ch * 128: 1024 + (ch + 1) * 128],
                          reads=[WB["ah_w_in"][1]], writes=[wu_bs[wsl]])
                    for blk in range(nblk):
                        ts = slice(blk * 512, (blk + 1) * 512)
                        pb = blk % 2
                        mm_group(bank(pb), [PSB[pb]], [(wu[:, wsl, kc, :], HS[:, kc, ts], [wu_bs[wsl], HS_bs[blk]])
                                                       for kc in range(8)])
                        E.op("act", lambda h: h.activation(out=U[:, ts], in_=bank(pb), func=AF.Copy),
                             reads=[PSB[pb]], writes=[U_b])
                    ci = ti % 2
                    cu, cu_b = CU[:, ci], CU_bs[ci]
                    E.op("act", lambda h: h.activation(out=cu, in_=U, func=AF.Identity, bias=hcw[:, ch, 3:4],
                                                       scale=hcw[:, ch, 1:2]), reads=[U_b, hcw_b], writes=[cu_b])
                    c3 = cu.rearrange("p (s t) -> p s t", t=L)
                    u3 = U.rearrange("p (s t) -> p s t", t=L)
                    E.op("dve", lambda h: h.scalar_tensor_tensor(out=c3[:, :, 1:L], in0=u3[:, :, 0:L - 1], scalar=hcw[:, ch, 0:1],
                                                                 in1=c3[:, :, 1:L], op0=ALU.mult, op1=ALU.add),
                         reads=[U_b, hcw_b, cu_b], writes=[cu_b])
                    E.op("dve", lambda h: h.scalar_tensor_tensor(out=c3[:, :, 0:L - 1], in0=u3[:, :, 1:L], scalar=hcw[:, ch, 2:3],
                                                                 in1=c3[:, :, 0:L - 1], op0=ALU.mult, op1=ALU.add),
                         reads=[U_b, hcw_b, cu_b], writes=[cu_b])
                    if which == 2:
                        E.op("pool", lambda h: h.tensor_tensor(out=VPc, in0=CU[:, 0], in1=CU[:, 1], op=ALU.mult),
                             reads=[CU_bs[0], CU_bs[1]], writes=[VPc_b])
                    if which == 0:
                        E.op("pool", lambda h: h.tensor_copy(out=X0[:, c, :], in_=cu), reads=[cu_b], writes=[X0_b])
                for tcn in range(T // 128):
                    s_i, cc = (tcn * 128) // L, ((tcn * 128) % L) // 128
                    pb = 4 + tcn % 2
                    E.op("pe", lambda h: h.transpose(PSbf[pb], VPc[:, tcn * 128:(tcn + 1) * 128], ident),
                         reads=[VPc_b, ident_b], writes=[PSbf_b[pb]])
                    E.op("act", lambda h: h.activation(out=VPT[:, s_i, cc, c * 128:(c + 1) * 128], in_=PSbf[pb], func=AF.Copy),
                         reads=[PSbf_b[pb]], writes=[VPT_b])
            A.release(mk_m)
            if stage == 5:
                return
            YR, YR_b = A.alloc("YR", [128, nsc + 1, 512], BF16)
            YI, YI_b = A.alloc("YI", [128, nsc, 512], BF16)
            vr, vr_b = A.alloc("vr", [128, 512], F32)
            vi, vi_b = A.alloc("vi", [128, 512], F32)
            pa, pa_b = A.alloc("pa", [128, 512], F32)
            pb_, pb_b = A.alloc("pb", [128, 512], F32)
            pc, pc_b = A.alloc("pc", [128, 512], F32)
            pd, pd_b = A.alloc("pd", [128, 512], F32)
            fw, fw_bs = A.alloc("fwst", [128, 2, nsc, 128], BF16, nbufs=2)
            nt = min(L, 256)
            ntb = L // nt
            iv, iv_b = A.alloc("invst", [128, nfc, nt], BF16)
            for s_i in range(nseq):
                for fc in range(nsc + 1):
                    E.dma("sp", fw[:, 0], tabs[L]["fwd"][fc].rearrange("p (s f) -> p s f", f=128), writes=[fw_bs[0]])
                    mm_group(bank(0), [PSB[0]], [(fw[:, 0, s_, :], VPT[:, s_i, s_, :], [fw_bs[0], VPT_b]) for s_ in range(nsc)])
                    E.op("act", lambda h: h.activation(out=vr, in_=bank(0), func=AF.Copy), reads=[PSB[0]], writes=[vr_b])
                    if fc < nsc:
                        E.dma("sp", fw[:, 1], tabs[L]["fwd"][nsc + 1 + fc].rearrange("p (s f) -> p s f", f=128),
                              writes=[fw_bs[1]])
                        mm_group(bank(1), [PSB[1]], [(fw[:, 1, s_, :], VPT[:, s_i, s_, :], [fw_bs[1], VPT_b])
                                                     for s_ in range(nsc)])
                        E.op("act", lambda h: h.activation(out=vi, in_=bank(1), func=AF.Copy), reads=[PSB[1]], writes=[vi_b])
                        E.op("dve", lambda h: h.tensor_tensor(out=pa, in0=vr, in1=GR[:, fc, :], op=ALU.mult),
                             reads=[vr_b, GR_b], writes=[pa_b])
                        E.op("pool", lambda h: h.tensor_tensor(out=pb_, in0=vi, in1=GI[:, fc, :], op=ALU.mult),
                             reads=[vi_b, GI_b], writes=[pb_b])
                        E.op("dve", lambda h: h.tensor_tensor(out=YR[:, fc, :], in0=pa, in1=pb_, op=ALU.subtract),
                             reads=[pa_b, pb_b], writes=[YR_b])
                        E.op("pool", lambda h: h.tensor_tensor(out=pc, in0=vr, in1=GI[:, fc, :], op=ALU.mult),
                             reads=[vr_b, GI_b], writes=[pc_b])
                        E.op("dve", lambda h: h.tensor_tensor(out=pd, in0=vi, in1=GR[:, fc, :], op=ALU.mult),
                             reads=[vi_b, GR_b], writes=[pd_b])
                        E.op("pool", lambda h: h.tensor_tensor(out=YI[:, fc, :], in0=pc, in1=pd, op=ALU.add),
                             reads=[pc_b, pd_b], writes=[YI_b])
                    else:
                        E.op("dve", lambda h: h.tensor_tensor(out=YR[:, fc, :], in0=vr, in1=GR[:, fc, :], op=ALU.mult),
                             reads=[vr_b, GR_b], writes=[YR_b])
                for tb in range(ntb):
                    tw = min(L, 512)
                    E.dma("sp", iv, tabs[L]["inv"][(tb * nt) // tw].rearrange("p (f t) -> p f t", t=tw)[:, :, (tb * nt) % tw:(tb * nt) % tw + nt],
                          writes=[iv_b])
                    t0 = s_i * L + tb * nt
                    for cc in range(4):
                        pbk = 4 + cc % 2
                        terms = [(YR[:, f_, cc * 128:(cc + 1) * 128], iv[:, f_, :], [YR_b, iv_b]) for f_ in range(nsc + 1)]
                        terms += [(YI[:, f_, cc * 128:(cc + 1) * 128], iv[:, nsc + 1 + f_, :], [YI_b, iv_b]) for f_ in range(nsc)]
                        mm_group(bank(pbk, nt), [PSB[pbk]], terms)
                        E.op("dve", lambda h: h.tensor_tensor(out=MIX[:, 4 + cc, t0:t0 + nt], in0=bank(pbk, nt),
                                                              in1=X0[:, cc, t0:t0 + nt], op=ALU.mult),
                             reads=[PSB[pbk], X0_b], writes=[MIX_bs[4 + cc]])
            A.release(mk_m)
            A.release_top(mk_top)
            if stage == 6:
                dbg = P.dout("dbg_" + gname, [128, 8 * T], BF16)
                E.dma("sp", dbg.rearrange("p (j t) -> p j t", j=8), MIX, reads=MIX_bs)
                return

            HF, HF_bs = A.alloc(gname + "_hf", [128, 8, T], BF16, nbufs=nblk)
            mk_o = A.mark()
            wo, wo_b = A.alloc("wo", [128, 8, 1024], BF16)
            E.dma("sp", wo, WB["ah_w_out"][0].rearrange("p (kc n) -> p kc n", kc=8), reads=[WB["ah_w_out"][1]], writes=[wo_b])
            O, O_b = A.alloc("O", [128, 8, 512], F32)
            sq, sq_b = A.alloc("sq", [128, 8, 512], BF16)
            rstd, rstd_b = A.alloc("rstd", [128, 512], F32)
            tmp, tmp_b = A.alloc("tmp", [128, 512], F32)
            xb, xb_b = A.alloc("xb", [128, 8, 512], F32)
            wk = (sq, sq_b, rstd, rstd_b, tmp, tmp_b, xb, xb_b)
            for blk in range(nblk):
                ts = slice(blk * 512, (blk + 1) * 512)
                for m in range(8):
                    pbk = m % 2
                    mm_group(bank(pbk), [PSB[pbk]], [(wo[:, kc, m * 128:(m + 1) * 128], MIX[:, kc, ts], [wo_b, MIX_bs[kc]])
                                                     for kc in range(8)])
                    E.op("act", lambda h: h.activation(out=O[:, m, :], in_=bank(pbk), func=AF.Copy), reads=[PSB[pbk]], writes=[O_b])
                epilogue(O, O_b, 512, X0d, X0d_b, X1d, X1d_b, blk * 512, T, 0, 0, v,
                         (0, 1, lambda j: HF[:, j, ts], HF_bs[blk]), wk)
            A.release(mk_o)
            if stage == 7:
                dbg = P.dout("dbg_" + gname, [128, 8 * T], F32)
                E.dma("sp", dbg, X1d, reads=[X1d_b], writes=[])
                return
            p_after_hf = A.p
            A.release(mk_g)
            HS1, HS1_bs = A.alloc(gname + "_hs1", [128, 8, T], BF16, nbufs=nblk)
            p_after_hs1 = A.p
            A.p = p_after_hf
            X2d, X2d_b = dscratch(gname + "_x2b", [128, 8 * T])
            ffn(G, 0, HF, HF_bs, X1d, X1d_b, X2d, X2d_b,
                lambda tok0: (1, 0, (lambda j: HS1[:, j, tok0:tok0 + 512]), HS1_bs[tok0 // 512]), NB=(512 if smp else 1024))
            if stage == 8:
                E.dma("sp", G["y"], X2d, reads=[X2d_b])
                A.release(mk_g)
                return
            A.release(p_after_hs1)
            layer1(G, HS1, HS1_bs, X2d, X2d_b)
            A.release(mk_g)

        def layer1(G, HS1, HS1_bs, X2d, X2d_b):
            gname, nseq, L, v = G["name"], G["nseq"], G["L"], G["v"]
            smp = G["sample"]
            T = nseq * L
            nblk = T // 512
            nch = L // 128
            X3d, X3d_b = dscratch(gname + "_x3", [128, 8 * T])
            mk_l = A.mark()
            MLM, MLM_bs = A.alloc("mlmix", [128, 8, T], BF16, nbufs=8)
            mk_2 = A.mark()
            RT, RT_b = A.alloc("RT", [40, T], F32)
            EM, EM_b = A.alloc("EM", [40, T], F32)
            WI, WI_b = A.alloc("WI", [40, T], F32)
            ATK, ATK_b = A.alloc("ATK", [128, T // 128, 40], F32)
            m0c, m0c_b = A.alloc("m0c", [40, 2], F32)
            onesf, onesf_b = A.alloc("onesf", [40, 128], F32)
            E.op("dve", lambda h: h.memset(onesf, 1.0), writes=[onesf_b])
            if G.get("states"):
                WTK, WTK_b = A.alloc("WTK", [128, T // 128, 40], F32)
            mk_r = A.mark()
            wg, wg_b = A.alloc("wg", [128, 8, 80], BF16)
            E.dma("pool", wg, mlwg_d.rearrange("p (kc n) -> p kc n", kc=8), writes=[wg_b])
            IG, IG_b = A.alloc("IG", [40, T], F32)
            LF, LF_b = A.alloc("LF", [40, T], F32)
            for blk in range(nblk):
                ts = slice(blk * 512, (blk + 1) * 512)
                for gi in range(2):
                    mm_group(PS[0:40, gi * 512:(gi + 1) * 512], [PSB[gi]],
                             [(wg[:, kc, gi * 40:(gi + 1) * 40], HS1[:, kc, ts], [wg_b, HS1_bs[blk]]) for kc in range(8)])
                E.op("act", lambda h: h.activation(out=IG[:, ts], in_=PS[0:40, 0:512], func=AF.Identity, bias=mbg[:, 0:1], scale=1.0),
                     reads=[PSB[0], mbg_b], writes=[IG_b])
                E.op("act", lambda h: h.activation(out=LF[:, ts], in_=PS[0:40, 512:1024], func=AF.Identity, bias=mbg[:, 1:2], scale=1.0),
                     reads=[PSB[1], mbg_b], writes=[LF_b])
            E.op("act", lambda h: h.activation(out=LF, in_=LF, func=AF.Exp, scale=-1.0), reads=[LF_b], writes=[LF_b])
            E.op("act", lambda h: h.activation(out=LF, in_=LF, func=AF.Ln, bias=onec[0:40, 0:1], scale=1.0),
                 reads=[LF_b, onec_b], writes=[LF_b])
            E.op("dve", lambda h: h.tensor_scalar(out=LF, in0=LF, scalar1=-1.0, scalar2=None, op0=ALU.mult), reads=[LF_b], writes=[LF_b])
            if smp:
                E.dma("sp", m0c[:, 0:1], m0_d, writes=[m0c_b])
            else:
                E.op("dve", lambda h: h.memset(m0c, 0.0), writes=[m0c_b])
            BT, BT_b = A.alloc("BT", [40, T], F32)
            AA, AA_b = A.alloc("AA", [40, T], F32)
            CM, CM_b = A.alloc("CM", [40, T], F32)
            onesr, onesr_b = A.alloc("onesr", [40, L], F32)
            E.op("dve", lambda h: h.memset(onesr, 1.0), writes=[onesr_b])
            for (tt, tb_) in ((BT, BT_b), (AA, AA_b), (CM, CM_b)):
                E.op("pool", lambda h: h.memset(tt, 0.0), writes=[tb_])
            for s_i in range(nseq):
                sl = slice(s_i * L, (s_i + 1) * L)
                for (p0, rev) in ((0, False), (32, True)):
                    pr = slice(p0, p0 + 8)

                    def V_(ap):
                        a2 = ap[pr, sl]
                        return a2[:, ::-1] if rev else a2
                    E.op("dve", lambda h: h.tensor_tensor_scan(out=V_(BT), data0=onesr[pr, :], data1=V_(LF), initial=0.0,
                                                               op0=ALU.mult, op1=ALU.add),
                         reads=[LF_b, onesr_b], writes=[BT_b])
                    E.op("dve", lambda h: h.tensor_tensor(out=AA[pr, sl], in0=IG[pr, sl], in1=BT[pr, sl], op=ALU.subtract),
                         reads=[IG_b, BT_b], writes=[AA_b])
                    E.op("dve", lambda h: h.tensor_tensor_scan(out=V_(CM), data0=V_(AA), data1=V_(AA), initial=m0c[pr, 0:1],
                                                               op0=ALU.max, op1=ALU.max), reads=[AA_b, m0c_b], writes=[CM_b])
            E.op("dve", lambda h: h.tensor_scalar(out=RT, in0=CM, scalar1=-1.0, scalar2=None, op0=ALU.mult), reads=[CM_b], writes=[RT_b])
            E.op("dve", lambda h: h.tensor_tensor(out=EM, in0=BT, in1=CM, op=ALU.add), reads=[BT_b, CM_b], writes=[EM_b])
            if G.get("states"):
                MTk, MTk_b = A.alloc("MTk", [40, nseq], F32)
                E.op("dve", lambda h: h.memset(MTk, 0.0), writes=[MTk_b])
                for s_i in range(nseq):
                    E.op("dve", lambda h: h.tensor_copy(out=MTk[0:8, s_i:s_i + 1], in_=EM[0:8, (s_i + 1) * L - 1:(s_i + 1) * L]),
                         reads=[EM_b], writes=[MTk_b])
                    E.op("dve", lambda h: h.tensor_copy(out=MTk[32:40, s_i:s_i + 1], in_=EM[32:40, s_i * L:s_i * L + 1]),
                         reads=[EM_b], writes=[MTk_b])
                E.dma("sp", mst_d, MTk, reads=[MTk_b])
            E.op("act", lambda h: h.activation(out=EM, in_=EM, func=AF.Exp, scale=-1.0), reads=[EM_b], writes=[EM_b])
            E.op("act", lambda h: h.activation(out=WI, in_=CM, func=AF.Exp, scale=-1.0, bias=m0c[:, 0:1]),
                 reads=[CM_b, m0c_b], writes=[WI_b])
            for c in range(T // 128):
                E.op("pe", lambda h: h.transpose(PS[:, 1024:1064], AA[:, c * 128:(c + 1) * 128], id32[0:40, 0:40]),
                     reads=[AA_b, id32_b], writes=[PSB[2]])
                E.op("dve", lambda h: h.tensor_copy(out=ATK[:, c, :], in_=PS[:, 1024:1064]), reads=[PSB[2]], writes=[ATK_b])
            if G.get("states"):
                dg, dg_b = A.alloc("dg", [40, nseq, 40], F32)
                for s_i in range(nseq):
                    E.op("dve", lambda h: h.memset(dg[:, s_i, :], 0.0), writes=[dg_b])
                    E.op("dve", lambda h: h.tensor_scalar(out=dg[0:8, s_i, :], in0=id32[0:8, 0:40], scalar1=RT[0:8, (s_i + 1) * L - 1:(s_i + 1) * L],
                                                          scalar2=None, op0=ALU.mult), reads=[id32_b, RT_b], writes=[dg_b])
                    E.op("dve", lambda h: h.tensor_scalar(out=dg[32:40, s_i, :], in0=id32[32:40, 0:40], scalar1=RT[32:40, s_i * L:s_i * L + 1],
                                                          scalar2=None, op0=ALU.mult), reads=[id32_b, RT_b], writes=[dg_b])
                    mm_group(PS[:, 1024:1064], [PSB[2]], [(onesf, dg[:, s_i, :], [onesf_b, dg_b])])
                    for cc in range(nch):
                        c = s_i * nch + cc
                        E.op("dve", lambda h: h.tensor_tensor(out=WTK[:, c, :], in0=ATK[:, c, :], in1=PS[:, 1024:1064], op=ALU.add),
                             reads=[ATK_b, PSB[2]], writes=[WTK_b])
                E.op("act", lambda h: h.activation(out=WTK, in_=WTK, func=AF.Exp), reads=[WTK_b], writes=[WTK_b])
            A.release(mk_r)
            wh, wh_b = A.alloc("wh", [128, 8, 4, 128], BF16)
            U, U_b = A.alloc("U", [128, T], F32)
            cu, cu_b = A.alloc("cu1", [128, T], F32)
            QT, QT_b = A.alloc("QT1", [128, T], BF16)
            KT, KT_b = A.alloc("KT1", [128, T], BF16)
            SG, SG_b = A.alloc("SG", [128, T], BF16)
            VK, VK_b = A.alloc("VK", [128, T // 128, 128], BF16)
            KK, KK_b = A.alloc("KK", [128, T // 128, 128], BF16)
            HSUM, HSUM_b = A.alloc("HSUM", [128, T], F32)
            nq = min(512, L)
            rtb, rtb_b = A.alloc("rtb", [128, nq], F32)
            emb, emb_b = A.alloc("emb", [128, nq], F32)
            wexp, wexp_bs = A.alloc("wexp", [128, 2, nq], F32, nbufs=2)
            Pm, Pm_bs = A.alloc("Pm", [128, 2, nq], BF16, nbufs=2)
            dn, dn_b = A.alloc("dn", [128, nq], F32)
            ht, ht_b = A.alloc("ht", [128, nq], F32)
            sq, sq_b = A.alloc("sq", [128, 1, 512], BF16)
            rstd, rstd_b = A.alloc("rstd", [128, 512], F32)
            vw, vw_b = A.alloc("vw", [128, 128], BF16)
            cst, cst_bs = A.alloc("cst", [128, 2, 128], F32, nbufs=2)
            nst, nst_bs = A.alloc("nst", [1, 2, 128], F32, nbufs=2)
            if smp:
                qp, qp_b = A.alloc("qp", [128, nq], BF16)
                c0t, c0t_b = A.alloc("c0t", [128, 16, 128], BF16)
                n0b, n0b_b = A.alloc("n0b", [128, 16, 128], BF16)
                E.dma("pool", c0t, c0t_d.rearrange("p (r m) -> p r m", m=128), writes=[c0t_b])
                E.dma("pool", n0b, n0b_d.rearrange("p (r m) -> p r m", m=128), writes=[n0b_b])
            wv = WB["ml_w_in"][0].rearrange("p (kc n) -> p kc n", kc=8)
            for hd in range(8):
                for wi in range(4):
                    E.dma("sp", wh[:, :, wi, :], wv[:, :, wi * 1024 + hd * 128: wi * 1024 + (hd + 1) * 128], reads=[WB["ml_w_in"][1]], writes=[wh_b])
                for wi, (dst, dst_b) in ((0, (QT, QT_b)), (1, (KT, KT_b)), (3, (SG, SG_b))):
                    for blk in range(nblk):
                        ts = slice(blk * 512, (blk + 1) * 512)
                        pb = blk % 2
                        mm_group(bank(pb), [PSB[pb]], [(wh[:, kc, wi, :], HS1[:, kc, ts], [wh_b, HS1_bs[blk]]) for kc in range(8)])
                        if wi == 3:
                            E.op("act", lambda h: h.activation(out=SG[:, ts], in_=bank(pb), func=AF.Sigmoid), reads=[PSB[pb]], writes=[SG_b])
                        else:
                            E.op("act", lambda h: h.activation(out=U[:, ts], in_=bank(pb), func=AF.Copy), reads=[PSB[pb]], writes=[U_b])
                    if wi == 3:
                        continue
                    ch = wi * 8 + hd
                    E.op("act", lambda h: h.activation(out=cu, in_=U, func=AF.Identity, bias=mcw[:, ch, 3:4], scale=mcw[:, ch, 1:2]),
                         reads=[U_b, mcw_b], writes=[cu_b])
                    c3 = cu.rearrange("p (s t) -> p s t", t=L)
                    u3 = U.rearrange("p (s t) -> p s t", t=L)
                    E.op("dve", lambda h: h.scalar_tensor_tensor(out=c3[:, :, 1:L], in0=u3[:, :, 0:L - 1], scalar=mcw[:, ch, 0:1],
                                                                 in1=c3[:, :, 1:L], op0=ALU.mult, op1=ALU.add),
                         reads=[U_b, mcw_b, cu_b], writes=[cu_b])
                    E.op("dve", lambda h: h.scalar_tensor_tensor(out=c3[:, :, 0:L - 1], in0=u3[:, :, 1:L], scalar=mcw[:, ch, 2:3],
                                                                 in1=c3[:, :, 0:L - 1], op0=ALU.mult, op1=ALU.add),
                         reads=[U_b, mcw_b, cu_b], writes=[cu_b])
                    if wi == 0:
                        E.op("act", lambda h: h.activation(out=QT, in_=cu, func=AF.Silu), reads=[cu_b], writes=[QT_b])
                    else:
                        E.op("act", lambda h: h.activation(out=cu, in_=cu, func=AF.Silu), reads=[cu_b], writes=[cu_b])
                        E.op("pool", lambda h: h.tensor_scalar(out=KT, in0=cu, scalar1=128.0 ** -0.5, scalar2=None, op0=ALU.mult),
                             reads=[cu_b], writes=[KT_b])
                for c in range(T // 128):
                    pb = 4 + c % 2
                    mm_group(bank(pb, 128), [PSB[pb]], [(HS1[:, kc, c * 128:(c + 1) * 128], wh[:, kc, 2, :], [wh_b, HS1_bs[c // 4]])
                                                        for kc in range(8)])
                    E.op("act", lambda h: h.activation(out=VK[:, c, :], in_=bank(pb, 128), func=AF.Copy), reads=[PSB[pb]], writes=[VK_b])
                    if G.get("states"):
                        E.op("pe", lambda h: h.transpose(PSbf[4], KT[:, c * 128:(c + 1) * 128], ident), reads=[KT_b, ident_b], writes=[PSbf_b[4]])
                        E.op("act", lambda h: h.activation(out=KK[:, c, :], in_=PSbf[4], func=AF.Copy), reads=[PSbf_b[4]], writes=[KK_b])
                for s_i in range(nseq):
                    for di in range(2):
                        r = di * 32 + hd
                        ridx = di * 8 + hd
                        for qb in range(L // nq):
                            q0 = s_i * L + qb * nq
                            mm_group(bank(4, nq), [PSB[4]], [(sel[:, ridx, :], RT[:, q0:q0 + nq], [sel_b, RT_b])])
                            E.op("act", lambda h: h.activation(out=rtb, in_=bank(4, nq), func=AF.Copy), reads=[PSB[4]], writes=[rtb_b])
                            mm_group(bank(5, nq), [PSB[5]], [(sel[:, ridx, :], EM[:, q0:q0 + nq], [sel_b, EM_b])])
                            E.op("act", lambda h: h.activation(out=emb, in_=bank(5, nq), func=AF.Copy), reads=[PSB[5]], writes=[emb_b])
                            if di == 0:
                                kcs = [kc for kc in range(nch) if kc * 128 <= qb * nq + nq - 1]
                            else:
                                kcs = [kc for kc in range(nch) if kc * 128 + 127 >= qb * nq]
                            nterm = len(kcs) + (1 if smp else 0)
                            ti = 0
                            if smp:
                                mm_group(bank(4, nq), [PSB[4]], [(sel[:, ridx, :], WI[:, q0:q0 + nq], [sel_b, WI_b])])
                                E.op("dve", lambda h: h.tensor_tensor(out=qp, in0=QT[:, q0:q0 + nq], in1=bank(4, nq), op=ALU.mult),
                                     reads=[QT_b, PSB[4]], writes=[qp_b])
                                E.op("pe", lambda h: h.matmul(bank(2, nq), lhsT=c0t[:, ridx, :], rhs=qp, start=True, stop=(nterm == 1)),
                                     reads=[c0t_b, qp_b], writes=[PSB[2]], pe_accum=True)
                                E.op("pe", lambda h: h.matmul(bank(3, nq), lhsT=n0b[:, ridx, :], rhs=qp, start=True, stop=(nterm == 1)),
                                     reads=[n0b_b, qp_b], writes=[PSB[3]], pe_accum=True)
                                ti = 1
                            for kc in kcs:
                                c = s_i * nch + kc
                                sb = ti % 2
                                off = kc * 128 - qb * nq
                                mm_group(bank(sb, nq), [PSB[sb]], [(KT[:, c * 128:(c + 1) * 128], QT[:, q0:q0 + nq], [KT_b, QT_b])])
                                E.op("act", lambda h: h.activation(out=wexp[:, sb, :], in_=rtb, func=AF.Exp, bias=ATK[:, c, r:r + 1], scale=1.0),
                                     reads=[rtb_b, ATK_b], writes=[wexp_bs[sb]])
                                E.op("dve", lambda h: h.tensor_tensor(out=Pm[:, sb, :], in0=bank(sb, nq), in1=wexp[:, sb, :], op=ALU.mult),
                                     reads=[PSB[sb], wexp_bs[sb]], writes=[Pm_bs[sb]])
                                if 0 <= off < nq and (off // 128) < 4:
                                    E.op("pool", lambda h: h.tensor_tensor(out=Pm[:, sb, :], in0=Pm[:, sb, :], in1=msk[:, di, off // 128, 0:nq], op=ALU.mult),
                                         reads=[Pm_bs[sb], msk_b], writes=[Pm_bs[sb]])
                                E.op("pe", lambda h: h.matmul(bank(2, nq), lhsT=VK[:, c, :], rhs=Pm[:, sb, :], start=(ti == 0), stop=(ti == nterm - 1)),
                                     reads=[VK_b, Pm_bs[sb]], writes=[PSB[2]], pe_accum=True)
                                E.op("pe", lambda h: h.matmul(bank(3, nq), lhsT=ones, rhs=Pm[:, sb, :], start=(ti == 0), stop=(ti == nterm - 1)),
                                     reads=[ones_b, Pm_bs[sb]], writes=[PSB[3]], pe_accum=True)
                                ti += 1
                            E.op("dve", lambda h: h.tensor_scalar(out=dn, in0=bank(3, nq), scalar1=-1.0, scalar2=None, op0=ALU.mult),
                                 reads=[PSB[3]], writes=[dn_b])
                            E.op("dve", lambda h: h.tensor_tensor(out=dn, in0=dn, in1=bank(3, nq), op=ALU.max), reads=[dn_b, PSB[3]], writes=[dn_b])
                            E.op("dve", lambda h: h.tensor_tensor(out=dn, in0=dn, in1=emb, op=ALU.max), reads=[dn_b, emb_b], writes=[dn_b])
                            E.op("dve", lambda h: h.reciprocal(out=dn, in_=dn), reads=[dn_b], writes=[dn_b])
                            if di == 0:
                                E.op("dve", lambda h: h.tensor_tensor(out=HSUM[:, q0:q0 + nq], in0=bank(2, nq), in1=dn, op=ALU.mult),
                                     reads=[PSB[2], dn_b], writes=[HSUM_b])
                            else:
                                E.op("dve", lambda h: h.tensor_tensor(out=ht, in0=bank(2, nq), in1=dn, op=ALU.mult),
                                     reads=[PSB[2], dn_b], writes=[ht_b])
                                E.op("pool", lambda h: h.tensor_tensor(out=HSUM[:, q0:q0 + nq], in0=HSUM[:, q0:q0 + nq], in1=ht, op=ALU.add),
                                     reads=[ht_b, HSUM_b], writes=[HSUM_b])
                        if G.get("states"):
                            so = (s_i * 16 + ridx) * 128
                            slot = ridx % 2
                            for cc in range(nch):
                                c = s_i * nch + cc
                                E.op("dve", lambda h: h.tensor_scalar(out=vw, in0=VK[:, c, :], scalar1=WTK[:, c, r:r + 1], scalar2=None, op0=ALU.mult),
                                     reads=[VK_b, WTK_b], writes=[vw_b])
                                E.op("pe", lambda h: h.matmul(bank(5, 128), lhsT=vw, rhs=KK[:, c, :], start=(cc == 0), stop=(cc == nch - 1)),
                                     reads=[vw_b, KK_b], writes=[PSB[5]], pe_accum=True)
                            E.op("act", lambda h: h.activation(out=cst[:, slot, :], in_=bank(5, 128), func=AF.Copy), reads=[PSB[5]], writes=[cst_bs[slot]])
                            E.dma("sp", cst_d[:, so:so + 128], cst[:, slot, :], reads=[cst_bs[slot]])
                            wtb, wtb_b = vw, vw_b
                            for cc in range(nch):
                                c = s_i * nch + cc
                                E.op("dve", lambda h: h.tensor_copy(out=vw[:, 0:1], in_=WTK[:, c, r:r + 1]), reads=[WTK_b], writes=[vw_b])
                                E.op("pe", lambda h: h.matmul(PS[0:1, 5 * 512 + 128: 5 * 512 + 256], lhsT=vw[:, 0:1], rhs=KK[:, c, :],
                                                              start=(cc == 0), stop=(cc == nch - 1)),
                                     reads=[vw_b, KK_b], writes=[PSB[5]], pe_accum=True)
                            E.op("act", lambda h: h.activation(out=nst[0:1, slot, :], in_=PS[0:1, 5 * 512 + 128: 5 * 512 + 256], func=AF.Copy),
                                 reads=[PSB[5]], writes=[nst_bs[slot]])
                            E.dma("sp", nst_d[0:1, so:so + 128], nst[0:1, slot, :], reads=[nst_bs[slot]])
                for blk in range(nblk):
                    ts = slice(blk * 512, (blk + 1) * 512)
                    sumsq_rstd([HSUM[:, ts]], [HSUM_b], 512, 128, rstd, rstd_b, sq, sq_b, 6)
                    E.op("dve", lambda h: h.scalar_tensor_tensor(out=U[:, ts], in0=HSUM[:, ts], scalar=mhn[:, hd:hd + 1], in1=rstd,
                                                                 op0=ALU.mult, op1=ALU.mult), reads=[HSUM_b, mhn_b, rstd_b], writes=[U_b])
                    E.op("pool", lambda h: h.tensor_tensor(out=MLM[:, hd, ts], in0=U[:, ts], in1=SG[:, ts], op=ALU.mult),
                         reads=[U_b, SG_b], writes=[MLM_bs[hd]])
            A.release(mk_2)
            if stage == 9:
                dbg = P.dout("dbg_" + gname, [128, 8 * T], BF16)
                E.dma("sp", dbg.rearrange("p (j t) -> p j t", j=8), MLM, reads=MLM_bs)
                A.release(mk_l)
                return
            mk_top1 = A.mark_top()
            HF1, HF1_bs = A.alloc(gname + "_hf1", [128, 8, T], BF16, nbufs=nblk, top=True)
            mk_o = A.mark()
            wo, wo_b = A.alloc("wo1", [128, 8, 1024], BF16)
            E.dma("sp", wo, WB["ml_w_out"][0].rearrange("p (kc n) -> p kc n", kc=8), reads=[WB["ml_w_out"][1]], writes=[wo_b])
            O, O_b = A.alloc("O", [128, 8, 512], F32)
            sq, sq_b = A.alloc("sq", [128, 8, 512], BF16)
            rstd, rstd_b = A.alloc("rstd", [128, 512], F32)
            tmp, tmp_b = A.alloc("tmp", [128, 512], F32)
            xb, xb_b = A.alloc("xb", [128, 8, 512], F32)
            wk = (sq, sq_b, rstd, rstd_b, tmp, tmp_b, xb, xb_b)
            for blk in range(nblk):
                ts = slice(blk * 512, (blk + 1) * 512)
                for m in range(8):
                    pbk = m % 2
                    mm_group(bank(pbk), [PSB[pbk]], [(wo[:, kc, m * 128:(m + 1) * 128], MLM[:, kc, ts], [wo_b, MLM_bs[kc]])
                                                     for kc in range(8)])
                    E.op("act", lambda h: h.activation(out=O[:, m, :], in_=bank(pbk), func=AF.Copy), reads=[PSB[pbk]], writes=[O_b])
                epilogue(O, O_b, 512, X2d, X2d_b, X3d, X3d_b, blk * 512, T, 1, 0, v,
                         (1, 1, lambda j: HF1[:, j, ts], HF1_bs[blk]), wk)
            A.release(mk_l)
            ffn(G, 1, HF1, HF1_bs, X3d, X3d_b, G["y"], Buf("y"), None, NB=(512 if smp else 1024))
            A.release_top(mk_top1)

        def ffn(G, l, HF, HF_bs, xsrc, xsrc_b, xdst, xdst_b, nxt, NB=1024):
            gname, nseq, L, v = G["name"], G["nseq"], G["L"], G["v"]
            T = nseq * L
            mk_f = A.mark()
            Gb, Gb_b = A.alloc("G", [128, 22, NB], BF16)
            wu, wu_bs = A.alloc("wup", [128, 2, 8, 2, 256], BF16, nbufs=2)
            wd, wd_bs = A.alloc("wdn", [128, 2, 22, 256], BF16, nbufs=2)
            cu, cu_bs = A.alloc("cu", [128, 2, NB], F32, nbufs=2)
            O, O_b = A.alloc("O", [128, 8, 512], F32)
            sq, sq_b = A.alloc("sq", [128, 8, 512], BF16)
            rstd, rstd_b = A.alloc("rstd", [128, 512], F32)
            tmp, tmp_b = A.alloc("tmp", [128, 512], F32)
            xb, xb_b = A.alloc("xb", [128, 8, 512], F32)
            wk = (sq, sq_b, rstd, rstd_b, tmp, tmp_b, xb, xb_b)
            upv = WB["ffn_up%d" % l][0].rearrange("p (kc n) -> p kc n", kc=8)
            dnv = WB["ffn_dn%d" % l][0].rearrange("p (kc n) -> p kc n", kc=22)
            upb, dnb = WB["ffn_up%d" % l][1], WB["ffn_dn%d" % l][1]
            for a in range(0, T, NB):
                n = NB
                seq_start = (a % L == 0)
                seq_end = ((a + n) % L == 0)
                aligned = (L <= n)
                lo = 1 if (seq_start or aligned) else 0
                hi = n + 1 if (seq_end or aligned) else n + 2
                pieces = []
                c0 = lo
                while c0 < hi:
                    c1 = min(hi, (c0 // 512 + 1) * 512)
                    pieces.append((c0, c1))
                    c0 = c1
                for jp in range(11):
                    slot = jp % 2
                    for half in range(2):
                        E.dma("sp", wu[:, slot, :, half, :], upv[:, :, half * DFF + jp * 256: half * DFF + (jp + 1) * 256],
                              reads=[upb], writes=[wu_bs[slot]])
                    for jj in range(2):
                        j = jp * 2 + jj
                        for half in range(2):
                            hb = 3 * half
                            hp = PS[:, hb * 512: hb * 512 + 1536]
                            hbufs = [PSB[hb], PSB[hb + 1], PSB[hb + 2]]
                            for (c0, c1) in pieces:
                                bk = hb + c0 // 512
                                mm_group(hp[:, c0:c1], [PSB[bk]],
                                         [(wu[:, slot, kc, half, jj * 128:(jj + 1) * 128], HF[:, kc, a - 1 + c0: a - 1 + c1],
                                           [wu_bs[slot]] + HF_bs) for kc in range(8)])
                            ch = half * 22 + j
                            cw = fcw[:, l, ch, :]
                            E.op("act", lambda h: h.activation(out=cu[:, half, :n], in_=hp[:, 1:n + 1], func=AF.Identity,
                                                               bias=cw[:, 3:4], scale=cw[:, 1:2]),
                                 reads=hbufs + [fcw_b], writes=[cu_bs[half]])
                            if aligned:
                                c3 = cu[:, half, :n].rearrange("p (s t) -> p s t", t=L)
                                h3 = hp[:, 1:n + 1].rearrange("p (s t) -> p s t", t=L)
                                E.op("dve", lambda h: h.scalar_tensor_tensor(
                                    out=c3[:, :, 1:L], in0=h3[:, :, 0:L - 1], scalar=cw[:, 0:1], in1=c3[:, :, 1:L],
                                    op0=ALU.mult, op1=ALU.add), reads=hbufs + [fcw_b, cu_bs[half]], writes=[cu_bs[half]])
                                E.op("dve", lambda h: h.scalar_tensor_tensor(
                                    out=c3[:, :, 0:L - 1], in0=h3[:, :, 1:L], scalar=cw[:, 2:3], in1=c3[:, :, 0:L - 1],
                                    op0=ALU.mult, op1=ALU.add), reads=hbufs + [fcw_b, cu_bs[half]], writes=[cu_bs[half]])
                            else:
                                i0 = 1 if seq_start else 0
                                i1 = n - 1 if seq_end else n
                                E.op("dve", lambda h: h.scalar_tensor_tensor(
                                    out=cu[:, half, i0:n], in0=hp[:, i0:n], scalar=cw[:, 0:1], in1=cu[:, half, i0:n],
                                    op0=ALU.mult, op1=ALU.add), reads=hbufs + [fcw_b, cu_bs[half]], writes=[cu_bs[half]])
                                E.op("dve", lambda h: h.scalar_tensor_tensor(
                                    out=cu[:, half, 0:i1], in0=hp[:, 2:i1 + 2], scalar=cw[:, 2:3], in1=cu[:, half, 0:i1],
                                    op0=ALU.mult, op1=ALU.add), reads=hbufs + [fcw_b, cu_bs[half]], writes=[cu_bs[half]])
                        E.op("act", lambda h: h.activation(out=cu[:, 0, :n], in_=cu[:, 0, :n], func=AF.Gelu),
                             reads=[cu_bs[0]], writes=[cu_bs[0]])
                        E.op("pool", lambda h: h.tensor_tensor(out=Gb[:, j, :n], in0=cu[:, 0, :n], in1=cu[:, 1, :n], op=ALU.mult),
                             reads=[cu_bs[0], cu_bs[1]], writes=[Gb_b])
                for piece in range(n // 512):
                    tsl = slice(piece * 512, (piece + 1) * 512)
                    for mp in range(4):
                        slot = mp % 2
                        E.dma("sp", wd[:, slot], dnv[:, :, mp * 256:(mp + 1) * 256], reads=[dnb], writes=[wd_bs[slot]])
                        for mm_ in range(2):
                            m = mp * 2 + mm_
                            pbk = m % 2
                            mm_group(bank(pbk), [PSB[pbk]], [(wd[:, slot, j, mm_ * 128:(mm_ + 1) * 128], Gb[:, j, tsl],
                                                              [wd_bs[slot], Gb_b]) for j in range(22)])
                            E.op("act", lambda h: h.activation(out=O[:, m, :], in_=bank(pbk), func=AF.Copy),
                                 reads=[PSB[pbk]], writes=[O_b])
                    epilogue(O, O_b, 512, xsrc, xsrc_b, xdst, xdst_b, a + piece * 512, T, l, 1, v,
                             None if nxt is None else nxt(a + piece * 512), wk)
            A.release(mk_f)

        GP = dict(name="p", nseq=NPSEQ, L=SEQ, v=0, sample=False, xin=xp_d, kout=kout_d, vout=vout_d, y=yp_d, states=True)
        gstage = [stage]
        if stage >= 20:
            stage = 99
        run_group(GP)
        stage = gstage[0] - 20 if 20 <= gstage[0] < 40 else stage
        if gstage[0] >= 20:
            GS = dict(name="s", nseq=1, L=DEC_SEQ, v=1, sample=True, xin=xs_d, kout=None, vout=None, y=ys_d, states=False)
            run_group(GS)
        E.finish()
    return P


def _fm(x2d):
    T, F = x2d.shape
    return np.ascontiguousarray(x2d.reshape(T, F // 128, 128).transpose(2, 1, 0).reshape(128, -1))


def _unfm(a, T):
    return np.ascontiguousarray(a.reshape(128, 8, T).transpose(2, 1, 0).reshape(T, 1024))


def _wl(w):
    K, N = w.shape
    return np.ascontiguousarray(w.reshape(K // 128, 128, N).transpose(1, 0, 2).reshape(128, -1))


_PROG_CACHE = {}
_CONST_CACHE = {}
DBG = None


def _consts():
    if _CONST_CACHE:
        return _CONST_CACHE
    import ml_dtypes
    c = {}
    c["ident_bf"] = np.eye(128, dtype=np.float32).astype(ml_dtypes.bfloat16)
    c["ones_bf"] = np.ones((128, 128), dtype=np.float32).astype(ml_dtypes.bfloat16)
    sw = np.zeros((128, 128), np.float32)
    for m in range(128):
        sw[(m + 64) % 128, m] = 1.0
    c["swap_bf"] = sw.astype(ml_dtypes.bfloat16)
    for L in (SEQ, DEC_SEQ):
        fwd, inv = dft_tables(L)
        nsc, nfc = L // 128, (2 * L + 128) // 128
        f3 = fwd.reshape(128, nsc, nfc, 128).transpose(2, 0, 1, 3).reshape(nfc, 128, nsc * 128)
        c["dft_fwd_%d" % L] = np.ascontiguousarray(f3)
        nt = min(L, 512)
        ntb = max(1, L // 512)
        i3 = inv.reshape(128, nfc, ntb, nt).transpose(2, 0, 1, 3).reshape(ntb, 128, nfc * nt)
        c["dft_inv_%d" % L] = np.ascontiguousarray(i3)
        zT, tcol = hyena_pos_tables(L)
        c["hy_zT_%d" % L] = zT
        c["hy_tcol_%d" % L] = tcol
    selm = np.zeros((40, 16, 128), np.float32)
    for ridx in range(16):
        selm[(ridx // 8) * 32 + ridx % 8, ridx, :] = 1.0
    c["sel40"] = selm.reshape(40, -1)
    sp = np.arange(128)[:, None]
    tp = np.arange(512)[None, :]
    mk = np.zeros((128, 2, 4, 512), np.float32)
    for o in range(4):
        mk[:, 0, o, :] = (sp + 128 * o <= tp)
        mk[:, 1, o, :] = (sp + 128 * o >= tp)
    c["masks"] = mk.reshape(128, -1).astype(ml_dtypes.bfloat16)
    c["ident_f32"] = np.eye(128, dtype=np.float32)
    cosT, sinT = rope_tables(DEC_SEQ)
    c["rope_cos"] = cosT
    c["rope_sin"] = sinT
    c["hy_delta"] = np.ascontiguousarray(np.broadcast_to(np.abs(np.linspace(HY_MIN_DECAY, HY_MAX_DECAY, 512, dtype=np.float32)).reshape(1, 512), (128, 512)))
    _CONST_CACHE.update(c)
    return _CONST_CACHE


def kernel(**inp):
    global DBG
    inp = {k: np.asarray(v) for k, v in inp.items()}
    stage = int(os.environ.get("KSTAGE", "99"))
    if stage not in _PROG_CACHE:
        _PROG_CACHE[stage] = build_program(stage)
    P = _PROG_CACHE[stage]
    TP = NPSEQ * SEQ
    B = inp["x_prompt"].shape[0]
    f32 = np.float32

    sh = dict(_consts())
    sh["w_mod"] = np.stack([_wl(inp["w_mod"][l]) for l in range(2)], axis=0)
    sh["b_mod"] = np.ascontiguousarray(inp["b_mod"].reshape(2, 48, 128).transpose(2, 0, 1).reshape(128, -1))
    norms = np.stack([inp["norm_mix_pre"], inp["norm_mix_post"], inp["norm_ffn_pre"], inp["norm_ffn_post"]], 0)
    sh["norms"] = np.ascontiguousarray(norms.reshape(4, 2, 8, 128).transpose(3, 0, 1, 2).reshape(128, -1))
    sh["ah_w_in"] = _wl(inp["ah_w_in"][0])
    sh["ah_w_out"] = _wl(inp["ah_w_out"][0])
    sh["qk_norm"] = np.ascontiguousarray(np.stack([inp["attn_q_norm"][0], inp["attn_k_norm"][0]], axis=1))
    hc = np.concatenate([inp["hy_conv_w"][0], inp["hy_conv_b"][0][None]], axis=0)
    sh["hy_conv"] = np.ascontiguousarray(hc.reshape(4, 12, 128).transpose(2, 1, 0).reshape(128, -1))
    sh["hy_w1"] = np.ascontiguousarray(inp["hy_w1"][0])
    sh["hy_w2"] = np.ascontiguousarray(inp["hy_w2"][0])
    sh["hy_w3"] = np.ascontiguousarray(inp["hy_w3"][0])
    sh["hy_b12f"] = np.ascontiguousarray(np.stack([inp["hy_b1"][0], inp["hy_b2"][0], inp["hy_sin_freq"][0, 0],
                                                   inp["hy_sin_freq"][0, 1]], axis=1))
    sh["hy_b3"] = np.ascontiguousarray(np.broadcast_to(inp["hy_b3"][0].reshape(1, 1024), (128, 1024)))
    sh["hy_skip"] = np.ascontiguousarray(np.broadcast_to(inp["hy_skip"][0].reshape(1, 512), (128, 512)))
    sh["ffn_w_up"] = np.stack([_wl(inp["ffn_w_up"][l]) for l in range(2)], axis=0)
    sh["ffn_w_down"] = np.stack([_wl(inp["ffn_w_down"][l]) for l in range(2)], axis=0)
    fc = np.concatenate([inp["ffn_conv_w"], inp["ffn_conv_b"][:, None, :]], axis=1)
    sh["ffn_conv"] = np.ascontiguousarray(fc.reshape(2, 4, 44, 128).transpose(3, 0, 2, 1).reshape(128, -1))

    mw = inp["ml_w_in"][0]
    sh["ml_w_in"] = _wl(np.ascontiguousarray(mw[:, :4096]))
    wgp = np.zeros((1024, 80), f32)
    bgp = np.zeros((40, 2), f32)
    bg = inp["ml_b_gates"][0]
    for kind in range(4):
        base = (kind // 2) * 40 + (kind % 2) * 32
        wgp[:, base:base + 8] = mw[:, 4096 + kind * 8: 4096 + kind * 8 + 8]
        bgp[(kind % 2) * 32:(kind % 2) * 32 + 8, kind // 2] = bg[kind * 8:kind * 8 + 8]
    sh["ml_w_g"] = _wl(wgp)
    sh["ml_b_g"] = bgp
    mc = np.concatenate([inp["ml_conv_w"][0], inp["ml_conv_b"][0][None]], axis=0)
    sh["ml_conv"] = np.ascontiguousarray(mc.reshape(4, 16, 128).transpose(2, 1, 0).reshape(128, -1))
    sh["ml_head_norm"] = np.ascontiguousarray(inp["ml_head_norm"][0].reshape(8, 128).T)
    sh["ml_w_out"] = _wl(inp["ml_w_out"][0])

    in_maps = []
    for r in range(NCORES):
        b = r // 4
        m = dict(sh)
        m["xp"] = _fm(inp["x_prompt"][NPSEQ * r:NPSEQ * (r + 1)].reshape(TP, D))
        m["xs"] = _fm(inp["x_sample"][b])
        cvec = np.stack([inp["c_ctx"], inp["c"][b]], axis=0)
        m["cvec"] = np.ascontiguousarray(cvec.reshape(2, 8, 128).transpose(2, 1, 0).reshape(128, -1))
        ck = inp["cache_attn_k"][b, 0]
        m["cache_kT"] = np.ascontiguousarray(ck.transpose(2, 1, 0).reshape(128, -1))
        cvv = inp["cache_attn_v"][b, 0]
        m["cache_vt"] = np.ascontiguousarray(cvv.reshape(4, 128, 256).transpose(1, 0, 2).reshape(128, -1))
        m0 = np.zeros((40, 1), f32)
        sm = inp["state_mlstm_m"][b, 0]
        m0[0:8, 0] = sm[0]
        m0[32:40, 0] = sm[1]
        m["ml_m0"] = m0
        sC = inp["state_mlstm_C"][b, 0].reshape(16, 128, 128)
        m["ml_c0t"] = np.ascontiguousarray(sC.transpose(2, 0, 1).reshape(128, -1))
        sn = inp["state_mlstm_n"][b, 0].reshape(16, 128)
        m["ml_n0b"] = np.ascontiguousarray(np.broadcast_to(sn.T[:, :, None], (128, 16, 128)).reshape(128, -1))
        in_maps.append({k: np.ascontiguousarray(m[k]) for k in P.ins})
    res = run_bass_kernel_spmd(P.nc, in_maps, core_ids=list(range(NCORES)))
    R = res.results
    DBG = R

    y_prompt = np.zeros((B, SEQ, D), f32)
    y_sample = np.zeros((2, DEC_SEQ, D), f32)
    new_k = np.zeros((B, 1, SEQ, 2, 128), f32)
    new_v = np.zeros((B, 1, SEQ, 2, 128), f32)
    new_C = np.zeros((B, 1, 2, 8, 128, 128), f32)
    new_n = np.zeros((B, 1, 2, 8, 128), f32)
    new_m = np.zeros((B, 1, 2, 8), f32)
    for r in range(NCORES):
        o = R[r]
        sl = slice(NPSEQ * r, NPSEQ * (r + 1))
        new_k[sl, 0] = o["k_out"].reshape(128, 2, NPSEQ, SEQ).transpose(2, 3, 1, 0)
        new_v[sl, 0] = o["v_out"].reshape(128, TP // 128, 2, 128).transpose(1, 0, 2, 3).reshape(NPSEQ, SEQ, 2, 128)
        y_prompt[sl] = _unfm(o["yp"], TP).reshape(NPSEQ, SEQ, D)
        new_C[sl, 0] = o["c_out"].reshape(128, NPSEQ, 2, 8, 128).transpose(1, 2, 3, 0, 4)
        new_n[sl, 0] = o["n_out"].reshape(NPSEQ, 2, 8, 128)
        mo = o["m_out"]
        new_m[sl, 0, 0] = mo[0:8].T
        new_m[sl, 0, 1] = mo[32:40].T
        if r % 4 == 0:
            y_sample[r // 4] = _unfm(o["ys"], DEC_SEQ)
    return (y_prompt, y_sample, new_k, new_v, new_C, new_n, new_m)
```

```python
import math
import os
import numpy as np
import concourse.bass as bass
import concourse.mybir as mybir
from concourse.bass_utils import run_bass_kernel_spmd

AF = mybir.ActivationFunctionType
ALU = mybir.AluOpType
AX = mybir.AxisListType
F32 = mybir.dt.float32
BF16 = mybir.dt.bfloat16

D = 1024
NCORES = 8
SEQ = 256
NPSEQ = 4
DEC_SEQ = 2048
PAST = 512
DFF = 2816
EPS = 1e-6
HY_MAX_DECAY = math.log(1e-2) / 0.3
HY_MIN_DECAY = math.log(1e-2) / 1.5

SERIAL = bool(int(os.environ.get('KSERIAL', '0')))
SELF_SYNC = True


class Buf:
    __slots__ = ("name", "last_w", "readers", "excl")

    def __init__(self, name, excl=False):
        self.name = name
        self.last_w = None
        self.readers = []
        self.excl = excl


class Emit:
    ENG = ("pe", "act", "dve", "pool", "sp")

    def __init__(self, nc, n_dma_sems=48):
        self.nc = nc
        self.h = {"pe": nc.tensor, "act": nc.scalar, "dve": nc.vector, "pool": nc.gpsimd, "sp": nc.sync}
        self.ops = {e: [] for e in self.ENG}
        self.cnt = {e: 0 for e in self.ENG}
        self.seen = {e: {} for e in self.ENG}
        self.esem = {}
        self.dsem = []
        self.dval = []
        self.n_dma_sems = n_dma_sems
        self.dnext = 0
        self.dnext_sw = 0
        self.n_sw = 12
        self.sem_ctx = []
        self.dma_tokens = []

    def open_sems(self, stack):
        for e in self.ENG:
            self.esem[e] = stack.enter_context(self.nc.semaphore("c_" + e))
        for i in range(self.n_dma_sems):
            self.dsem.append(stack.enter_context(self.nc.semaphore("d%d" % i)))
            self.dval.append(0)

    def _deps(self, reads, writes):
        deps = []
        for b in reads:
            if b.last_w is not None:
                deps.append(b.last_w)
        for b in writes:
            if b.last_w is not None:
                deps.append(b.last_w)
            deps.extend(b.readers)
        return deps

    def _waits(self, eng, deps, pe_accum=False):
        need = {}
        for d in deps:
            kind, src, val = d
            if kind == "e" and src == eng and (not SELF_SYNC or (eng == "pe" and pe_accum)):
                continue
            key = (kind, src)
            if self.seen[eng].get(key, 0) >= val:
                continue
            if need.get(key, 0) < val:
                need[key] = val
        waits = []
        for (kind, src), val in need.items():
            self.seen[eng][(kind, src)] = val
            sem = self.esem[src] if kind == "e" else self.dsem[src]
            waits.append((sem, val))
        return waits

    def _commit(self, tok, reads, writes):
        for b in writes:
            b.last_w = tok
            b.readers = []
        for b in reads:
            if b in writes:
                continue
            if tok[0] == "e":
                b.readers = [r for r in b.readers if not (r[0] == "e" and r[1] == tok[1])]
            b.readers.append(tok)

    def op(self, eng, fn, reads=(), writes=(), pe_accum=False):
        ex = [b for b in reads if b.excl and b not in writes]
        if ex:
            reads = [b for b in reads if not b.excl]
            writes = list(writes) + ex
        deps = self._deps(reads, writes)
        if SERIAL:
            for e in self.ENG:
                if self.cnt[e] > 0 and not (e == eng and pe_accum):
                    deps.append(("e", e, self.cnt[e]))
            for i in range(self.n_dma_sems):
                if self.dval[i] > 0:
                    deps.append(("d", i, self.dval[i]))
        waits = self._waits(eng, deps, pe_accum)
        self.cnt[eng] += 1
        tok = ("e", eng, self.cnt[eng])
        h = self.h[eng]
        for s_, v_ in waits:
            h.wait_ge(s_, v_)
        fn(h).then_inc(self.esem[eng], 1)
        self._commit(tok, reads, writes)
        return tok

    def pe_group(self, fns, reads, writes):
        ex = [b for b in reads if b.excl and b not in writes]
        if ex:
            reads = [b for b in reads if not b.excl]
            writes = list(writes) + ex
        deps = self._deps(reads, writes)
        waits = self._waits("pe", deps, True)
        self.cnt["pe"] += 1
        tok = ("e", "pe", self.cnt["pe"])
        h = self.h["pe"]
        for s_, v_ in waits:
            h.wait_ge(s_, v_)
        last = None
        for fn in fns:
            last = fn(h)
        last.then_inc(self.esem["pe"], 1)
        self._commit(tok, reads, writes)
        return tok

    def dma(self, eng, out, in_, reads=(), writes=(), **kw):
        deps = self._deps(reads, writes)
        if eng == "pool":
            i = self.dnext_sw
            self.dnext_sw = (self.dnext_sw + 1) % self.n_sw
        else:
            i = self.n_sw + self.dnext
            self.dnext = (self.dnext + 1) % (self.n_dma_sems - self.n_sw)
        if self.dval[i] > 0:
            deps.append(("d", i, self.dval[i]))
        waits = self._waits(eng, deps)
        self.dval[i] += 16
        tok = ("d", i, self.dval[i])
        h = self.h[eng]
        for s_, v_ in waits:
            h.wait_ge(s_, v_)
        h.dma_start(out=out, in_=in_, **kw).then_inc(self.dsem[i], 16)
        self._commit(tok, reads, writes)
        return tok

    def finish(self):
        h = self.h["sp"]
        for i in range(self.n_dma_sems):
            if self.dval[i] > 0:
                h.wait_ge(self.dsem[i], self.dval[i])
        for e in self.ENG:
            if e != "sp" and self.cnt[e] > 0:
                h.wait_ge(self.esem[e], self.cnt[e])


class Arena:
    def __init__(self, nc, lo, hi):
        self.nc, self.lo, self.hi, self.p = nc, lo, hi, lo
        self.n = 0
        self.live = []

    def alloc(self, name, shape, dtype, nbufs=1, top=False):
        esz = 4 if dtype == F32 else 2
        per = 1
        for s in shape[1:]:
            per *= s
        nbytes = (per * esz + 31) // 32 * 32
        assert self.p + nbytes <= self.hi, "SBUF arena overflow at %s (%d + %d > %d)" % (name, self.p, nbytes, self.hi)
        if top:
            self.hi -= nbytes
            off = self.hi
        else:
            off = self.p
            self.p += nbytes
        self.n += 1
        t = self.nc.alloc_sbuf_tensor_at("%s_%d" % (name, self.n), list(shape), dtype, offset=off)
        bufs = [Buf("%s.%d" % (name, i)) for i in range(nbufs)]
        keep = []
        for (l, h, bs) in self.live:
            if l < off + nbytes and off < h:
                for ob in bs:
                    for nb in bufs:
                        if ob.last_w is not None:
                            nb.readers.append(ob.last_w)
                        nb.readers.extend(ob.readers)
                if l < off or h > off + nbytes:
                    keep.append((l, h, bs))
            else:
                keep.append((l, h, bs))
        keep.append((off, off + nbytes, bufs))
        self.live = keep
        ap = t.ap()
        return (ap, bufs[0]) if nbufs == 1 else (ap, bufs)

    def mark_top(self):
        return self.hi

    def release_top(self, m):
        self.hi = m

    def mark(self):
        return self.p

    def release(self, m):
        self.p = m


def _bf16(a):
    import ml_dtypes
    return np.ascontiguousarray(a.astype(np.float32)).astype(ml_dtypes.bfloat16)


def dft_tables(L):
    n = 2 * L
    s = np.arange(L, dtype=np.float64)
    fr = np.arange(L + 128, dtype=np.float64)
    fi = np.arange(L, dtype=np.float64)
    FR = np.cos(2 * np.pi * np.outer(s, fr) / n)
    FR[:, L + 1:] = 0.0
    FI = -np.sin(2 * np.pi * np.outer(s, fi) / n)
    cf = np.full(L + 128, 2.0)
    cf[0] = 1.0
    cf[L] = 1.0
    cf[L + 1:] = 0.0
    IR = (cf[:, None] / n) * np.cos(2 * np.pi * np.outer(fr, s) / n)
    II = -(2.0 / n) * np.sin(2 * np.pi * np.outer(fi, s) / n)
    fwd = np.concatenate([FR, FI], axis=1)
    inv = np.concatenate([IR, II], axis=0)
    nsc = L // 128
    nfc = (2 * L + 128) // 128
    fwd_l = fwd.reshape(nsc, 128, nfc * 128).transpose(1, 0, 2)
    inv_l = inv.reshape(nfc, 128, L).transpose(1, 0, 2)
    return _bf16(fwd_l.reshape(128, -1)), _bf16(inv_l.reshape(128, -1))


def hyena_pos_tables(L):
    t = np.linspace(0.0, 1.0, L, dtype=np.float32)
    bands = np.arange(1, 9, dtype=np.float32)
    ang = 2.0 * np.pi * t[:, None] * bands
    z = np.concatenate([t[:, None], np.cos(ang), np.sin(ang)], axis=-1).astype(np.float32)
    zT = np.ascontiguousarray(z.T)
    tcol = np.ascontiguousarray(t.reshape(L // 128, 128).T)
    return zT, tcol


def rope_tables(L):
    rows = L // 64
    row = np.repeat(np.arange(rows, dtype=np.float32), 64)
    col = np.tile(np.arange(64, dtype=np.float32), rows)
    n_freq = 32
    inv = (10000.0 ** (-np.arange(n_freq, dtype=np.float32) / n_freq)).astype(np.float32)
    ang = np.concatenate([row[:, None] * inv, col[:, None] * inv], axis=-1)
    cos, sin = np.cos(ang).astype(np.float32), np.sin(ang).astype(np.float32)
    cosT = np.concatenate([cos.T, cos.T], axis=0)
    sinT = np.concatenate([-sin.T, sin.T], axis=0)
    return np.ascontiguousarray(cosT), np.ascontiguousarray(sinT)


class Prog:
    def __init__(self, stage):
        self.stage = stage
        self.nc = bass.Bass("TRN2", target_bir_lowering=False)
        self.ins = {}
        self.outs = {}

    def din(self, name, shape, dtype=F32):
        t = self.nc.dram_tensor(name, list(shape), dtype, kind="ExternalInput")
        self.ins[name] = t
        return t.ap()

    def dout(self, name, shape, dtype=F32):
        t = self.nc.dram_tensor(name, list(shape), dtype, kind="ExternalOutput")
        self.outs[name] = t
        return t.ap()


TWO_PI = 2.0 * math.pi
MAGIC = 12582912.0


def build_program(stage=99):
    P = Prog(stage)
    nc = P.nc
    TP = NPSEQ * SEQ
    TS = DEC_SEQ

    xp_d = P.din("xp", [128, 8 * TP])
    xs_d = P.din("xs", [128, 8 * TS])
    cvec_d = P.din("cvec", [128, 8 * 2])
    wmod_d = P.din("w_mod", [2, 128, 8 * 6144])
    bmod_d = P.din("b_mod", [128, 2 * 48])
    norms_d = P.din("norms", [128, 4 * 2 * 8])
    ahwin_d = P.din("ah_w_in", [128, 8 * 2560])
    ahwout_d = P.din("ah_w_out", [128, 8 * 1024])
    qkn_d = P.din("qk_norm", [128, 2])
    idb_d = P.din("ident_bf", [128, 128], BF16)
    ones_d = P.din("ones_bf", [128, 128], BF16)
    swap_d = P.din("swap_bf", [128, 128], BF16)
    hcw_d = P.din("hy_conv", [128, 12 * 4])
    hyw1_d = P.din("hy_w1", [17, 64])
    hyw2_d = P.din("hy_w2", [64, 64])
    hyw3_d = P.din("hy_w3", [64, 1024])
    hyb12_d = P.din("hy_b12f", [64, 4])
    hyb3_d = P.din("hy_b3", [128, 1024])
    hyskip_d = P.din("hy_skip", [128, 512])
    hydelta_d = P.din("hy_delta", [128, 512])
    ffnup_d = P.din("ffn_w_up", [2, 128, 8 * 5632])
    ffndn_d = P.din("ffn_w_down", [2, 128, 22 * 1024])
    ffncw_d = P.din("ffn_conv", [128, 2 * 44 * 4])
    tabs = {}
    for L in (SEQ, DEC_SEQ):
        nsc, nfc = L // 128, (2 * L + 128) // 128
        tabs[L] = dict(
            fwd=P.din("dft_fwd_%d" % L, [nfc, 128, nsc * 128], BF16),
            inv=P.din("dft_inv_%d" % L, [max(1, L // 512), 128, nfc * min(L, 512)], BF16),
            zT=P.din("hy_zT_%d" % L, [17, L]),
            tcol=P.din("hy_tcol_%d" % L, [128, L // 128]))
    mlwin_d = P.din("ml_w_in", [128, 8 * 4096])
    mlwg_d = P.din("ml_w_g", [128, 8 * 80])
    mlbg_d = P.din("ml_b_g", [40, 2])
    mlcw_d = P.din("ml_conv", [128, 16 * 4])
    mlhn_d = P.din("ml_head_norm", [128, 8])
    mlwout_d = P.din("ml_w_out", [128, 8 * 1024])
    sel_d = P.din("sel40", [40, 16 * 128])
    masks_d = P.din("masks", [128, 2 * 4 * 512], BF16)
    id32_d = P.din("ident_f32", [128, 128])
    m0_d = P.din("ml_m0", [40, 1])
    c0t_d = P.din("ml_c0t", [128, 16 * 128])
    n0b_d = P.din("ml_n0b", [128, 16 * 128])
    ropec_d = P.din("rope_cos", [128, TS])
    ropes_d = P.din("rope_sin", [128, TS])
    ckT_d = P.din("cache_kT", [128, 2 * PAST])
    cvt_d = P.din("cache_vt", [128, (PAST // 128) * 256])

    kout_d = P.dout("k_out", [128, 2 * TP])
    vout_d = P.dout("v_out", [128, (TP // 128) * 256])
    cst_d = P.dout("c_out", [128, NPSEQ * 16 * 128])
    nst_d = P.dout("n_out", [1, NPSEQ * 16 * 128])
    mst_d = P.dout("m_out", [40, NPSEQ])
    yp_d = P.dout("yp", [128, 8 * TP])
    ys_d = P.dout("ys", [128, 8 * TS])

    from contextlib import ExitStack
    with ExitStack() as stack:
        E = Emit(nc)
        E.open_sems(stack)
        A = Arena(nc, 16640, 229376 - 1024)
        ps_t = nc.alloc_psum_tensor("ps_all", [128, 7 * 512], F32)
        PS = ps_t.ap()
        PSB = [Buf("ps%d" % i, excl=True) for i in range(7)]
        psbf_t = nc.alloc_psum_tensor("ps_bf", [128, 1024], BF16)
        PSbf = {4: psbf_t.ap()[:, 0:128], 5: psbf_t.ap()[:, 128:256]}
        _pb = Buf("psbf", excl=True)
        PSbf_b = {4: _pb, 5: _pb}

        def bank(i, n=512, off=0):
            return PS[:, i * 512 + off: i * 512 + off + n]

        def mm_group(out_ap, out_bufs, terms):
            n = len(terms)
            rset = []
            for (_, _, rb) in terms:
                for x_ in rb:
                    if x_ not in rset:
                        rset.append(x_)
            fns = [(lambda h, i=i, l_ap=l_ap, r_ap=r_ap: h.matmul(out_ap, lhsT=l_ap, rhs=r_ap, start=(i == 0), stop=(i == n - 1)))
                   for i, (l_ap, r_ap, rb) in enumerate(terms)]
            E.pe_group(fns, rset, out_bufs)

        dram_bufs = {}

        def dscratch(name, shape, dtype=F32):
            t = nc.dram_tensor(name, list(shape), dtype)
            b = Buf(name)
            dram_bufs[name] = b
            return t.ap(), b

        WB = {}

        def conv_weight(name, src_ap, shape):
            t_ap, t_b = dscratch(name + "_bf", shape, BF16)
            E.dma("pool", t_ap, src_ap, writes=[t_b])
            WB[name] = (t_ap, t_b)

        conv_weight("ah_w_in", ahwin_d, [128, 8 * 2560])
        conv_weight("ah_w_out", ahwout_d, [128, 8 * 1024])
        conv_weight("ffn_up0", ffnup_d[0], [128, 8 * 5632])
        conv_weight("ffn_dn0", ffndn_d[0], [128, 22 * 1024])
        conv_weight("ml_w_in", mlwin_d, [128, 8 * 4096])
        conv_weight("ml_w_out", mlwout_d, [128, 8 * 1024])
        conv_weight("ffn_up1", ffnup_d[1], [128, 8 * 5632])
        conv_weight("ffn_dn1", ffndn_d[1], [128, 22 * 1024])

        def const(name, shape, dtype, src, eng="sp"):
            ap, b = A.alloc(name, shape, dtype)
            E.dma(eng, ap, src, writes=[b])
            return ap, b

        ident, ident_b = const("ident", [128, 128], BF16, idb_d)
        ones, ones_b = const("ones", [128, 128], BF16, ones_d)
        swp, swp_b = const("swap", [128, 128], BF16, swap_d)
        epsc, epsc_b = A.alloc("epsc", [128, 2], F32)
        E.op("dve", lambda h: h.memset(epsc, EPS), writes=[epsc_b])
        cv, cv_b = const("cvec", [128, 8, 2], F32, cvec_d.rearrange("p (j v) -> p j v", v=2))
        bm, bm_b = const("bmod", [128, 2, 48], F32, bmod_d.rearrange("p (l m) -> p l m", l=2))
        nrm, nrm_b = const("norms", [128, 4, 2, 8], F32, norms_d.rearrange("p (w l j) -> p w l j", w=4, l=2))
        qkn, qkn_b = const("qkn", [128, 2], F32, qkn_d)
        hcw, hcw_b = const("hcw", [128, 12, 4], F32, hcw_d.rearrange("p (c k) -> p c k", k=4))
        fcw, fcw_b = const("fcw", [128, 2, 44, 4], F32, ffncw_d.rearrange("p (l c k) -> p l c k", l=2, k=4))

        mcw, mcw_b = const("mcw", [128, 16, 4], F32, mlcw_d.rearrange("p (c k) -> p c k", k=4))
        mhn, mhn_b = const("mhn", [128, 8], F32, mlhn_d)
        mbg, mbg_b = const("mbg", [40, 2], F32, mlbg_d)
        sel, sel_b = const("sel", [40, 16, 128], F32, sel_d.rearrange("p (r m) -> p r m", m=128))
        msk, msk_b = const("msk", [128, 2, 4, 512], BF16, masks_d.rearrange("p (d o t) -> p d o t", d=2, o=4))
        id32, id32_b = const("id32", [128, 128], F32, id32_d)
        onec, onec_b = A.alloc("onec", [128, 2], F32)
        E.op("dve", lambda h: h.memset(onec, 1.0), writes=[onec_b])
        sc, sc_b = A.alloc("silu_c", [128, 8, 2], BF16)
        sig, sig_b = A.alloc("sig_c", [128, 8, 2], F32)
        E.op("act", lambda h: h.activation(out=sig, in_=cv, func=AF.Sigmoid), reads=[cv_b], writes=[sig_b])
        E.op("dve", lambda h: h.tensor_tensor(out=sc, in0=cv, in1=sig, op=ALU.mult), reads=[cv_b, sig_b], writes=[sc_b])
        MOD, MOD_b = A.alloc("mod", [128, 2, 48, 2], F32)
        mk = A.mark()
        wm, wm_bs = A.alloc("wmod_st", [128, 2, 8, 512], BF16, nbufs=2)
        it = 0
        for l in range(2):
            for cg in range(12):
                slot = it % 2
                it += 1
                src = wmod_d[l].rearrange("p (kc n) -> p kc n", kc=8)[:, :, cg * 512:(cg + 1) * 512]
                E.dma("pool", wm[:, slot], src, writes=[wm_bs[slot]])
                for mm in range(4):
                    m = cg * 4 + mm
                    mm_group(bank(l, 2, 2 * m), [PSB[l]],
                             [(wm[:, slot, kc, mm * 128:(mm + 1) * 128], sc[:, kc, :], [wm_bs[slot], sc_b])
                              for kc in range(8)])
            E.op("dve", lambda h: h.tensor_tensor(
                out=MOD[:, l], in0=bank(l, 96).rearrange("p (m v) -> p m v", v=2),
                in1=bm[:, l, :].unsqueeze(2).to_broadcast([128, 48, 2]), op=ALU.add),
                reads=[PSB[l], bm_b], writes=[MOD_b])
        A.release(mk)

        COEF, COEF_b = A.alloc("coef", [128, 2, 2, 3, 8, 2], F32)
        for l in range(2):
            for part in range(2):
                sh, scl, gt = 3 * part, 3 * part + 1, 3 * part + 2
                gpre = nrm[:, 2 * part, l, :].unsqueeze(2).to_broadcast([128, 8, 2])
                gpost = nrm[:, 2 * part + 1, l, :].unsqueeze(2).to_broadcast([128, 8, 2])
                E.op("dve", lambda h: h.scalar_tensor_tensor(
                    out=COEF[:, l, part, 0], in0=MOD[:, l, scl * 8:(scl + 1) * 8, :], scalar=1.0, in1=gpre,
                    op0=ALU.add, op1=ALU.mult), reads=[MOD_b, nrm_b], writes=[COEF_b])
                E.op("dve", lambda h: h.tensor_copy(
                    out=COEF[:, l, part, 1], in_=MOD[:, l, sh * 8:(sh + 1) * 8, :]), reads=[MOD_b], writes=[COEF_b])
                E.op("dve", lambda h: h.tensor_tensor(
                    out=COEF[:, l, part, 2], in0=MOD[:, l, gt * 8:(gt + 1) * 8, :], in1=gpost, op=ALU.mult),
                    reads=[MOD_b, nrm_b], writes=[COEF_b])

        def sumsq_rstd(src_chunks, src_bufs, n, dim, rstd, rstd_b, sq, sq_b, psb):
            nch = len(src_chunks)
            for j, s_ in enumerate(src_chunks):
                E.op("act", lambda h: h.activation(out=sq[:, j, :n], in_=s_, func=AF.Square),
                     reads=[src_bufs[j]], writes=[sq_b])
            mm_group(bank(psb, n), [PSB[psb]], [(ones, sq[:, j, :n], [ones_b, sq_b]) for j in range(nch)])
            E.op("act", lambda h: h.activation(out=rstd[:, :n], in_=bank(psb, n), func=AF.Ln, bias=epsc[:, 0:1],
                                               scale=1.0 / dim), reads=[PSB[psb], epsc_b], writes=[rstd_b])
            E.op("act", lambda h: h.activation(out=rstd[:, :n], in_=rstd[:, :n], func=AF.Exp, scale=-0.5),
                 reads=[rstd_b], writes=[rstd_b])

        def modulate_block(xb, xb_buf, n, l, part, v, dst_fn, dst_buf, sq, sq_b, rstd, rstd_b, tmp, tmp_b):
            sumsq_rstd([xb[:, j, :n] for j in range(8)], [xb_buf] * 8, n, D, rstd, rstd_b, sq, sq_b, 6)
            for j in range(8):
                E.op("dve", lambda h: h.scalar_tensor_tensor(
                    out=tmp[:, :n], in0=xb[:, j, :n], scalar=COEF[:, l, part, 0, j, v:v + 1], in1=rstd[:, :n],
                    op0=ALU.mult, op1=ALU.mult), reads=[xb_buf, COEF_b, rstd_b], writes=[tmp_b])
                E.op("act", lambda h: h.activation(
                    out=dst_fn(j), in_=tmp[:, :n], func=AF.Identity, bias=COEF[:, l, part, 1, j, v:v + 1], scale=1.0),
                    reads=[tmp_b, COEF_b], writes=[dst_buf])

        def epilogue(O, O_b, n, xsrc, xsrc_b, xdst, xdst_b, tok0, T, l, part, v, nxt, wk):
            (sq, sq_b, rstd, rstd_b, tmp, tmp_b, xb, xb_b) = wk
            E.dma("sp", xb[:, :, :n], xsrc.rearrange("p (j t) -> p j t", j=8)[:, :, tok0:tok0 + n],
                  reads=[xsrc_b], writes=[xb_b])
            sumsq_rstd([O[:, j, :n] for j in range(8)], [O_b] * 8, n, D, rstd, rstd_b, sq, sq_b, 6)
            for j in range(8):
                E.op("dve", lambda h: h.scalar_tensor_tensor(
                    out=tmp[:, :n], in0=O[:, j, :n], scalar=COEF[:, l, part, 2, j, v:v + 1], in1=rstd[:, :n],
                    op0=ALU.mult, op1=ALU.mult), reads=[O_b, COEF_b, rstd_b], writes=[tmp_b])
                E.op("pool", lambda h: h.tensor_tensor(out=xb[:, j, :n], in0=xb[:, j, :n], in1=tmp[:, :n], op=ALU.add),
                     reads=[tmp_b, xb_b], writes=[xb_b])
            E.dma("sp", xdst.rearrange("p (j t) -> p j t", j=8)[:, :, tok0:tok0 + n], xb[:, :, :n],
                  reads=[xb_b], writes=[xdst_b])
            if nxt is not None:
                l2, part2, dst_fn, dst_buf = nxt
                modulate_block(xb, xb_b, n, l2, part2, v, dst_fn, dst_buf, sq, sq_b, rstd, rstd_b, tmp, tmp_b)

        def hyena_filters(L, keep):
            nsc = L // 128
            GR, GR_b = keep.alloc("GR%d" % L, [128, nsc + 1, 512], BF16, top=True)
            GI, GI_b = keep.alloc("GI%d" % L, [128, nsc, 512], BF16, top=True)
            mk_ = A.mark()
            zT, zT_b = const("zT", [17, L], F32, tabs[L]["zT"])
            tcol, tcol_b = const("tcol", [128, nsc], F32, tabs[L]["tcol"])
            w1, w1_b = const("hw1", [17, 64], F32, hyw1_d)
            w2, w2_b = const("hw2", [64, 64], F32, hyw2_d)
            w3, w3_b = const("hw3", [64, 1024], F32, hyw3_d)
            b12, b12_b = const("hb12", [64, 4], F32, hyb12_d)
            b3, b3_b = const("hb3", [128, 1024], F32, hyb3_d)
            skp, skp_b = const("hskip", [128, 512], F32, hyskip_d)
            dlt, dlt_b = const("hdelta", [128, 512], F32, hydelta_d)
            ntc, ntc_b = A.alloc("ntcol", [128, nsc], F32)
            E.op("dve", lambda h: h.tensor_scalar(out=ntc, in0=tcol, scalar1=-1.0, scalar2=None, op0=ALU.mult),
                 reads=[tcol_b], writes=[ntc_b])
            H1, H1_b = A.alloc("h1", [64, L], F32)
            H2, H2_b = A.alloc("h2", [64, L], F32)
            t1, t1_b = A.alloc("ht1", [64, 512], F32)
            t2, t2_b = A.alloc("ht2", [64, 512], F32)

            def sin_layer(dst, dst_b, w_ap, w_b, src, src_b, bcol, fcol):
                for c0 in range(0, L, 512):
                    n = min(512, L - c0)
                    mm_group(PS[0:64, 0:n], [PSB[0]], [(w_ap, src[:, c0:c0 + n], [w_b, src_b])])
                    E.op("dve", lambda h: h.tensor_scalar(out=t1[:, :n], in0=PS[0:64, 0:n], scalar1=b12[:, bcol:bcol + 1],
                                                          scalar2=b12[:, fcol:fcol + 1], op0=ALU.add, op1=ALU.mult),
                         reads=[PSB[0], b12_b], writes=[t1_b])
                    E.op("dve", lambda h: h.tensor_scalar(out=t2[:, :n], in0=t1[:, :n], scalar1=1.0 / TWO_PI, scalar2=MAGIC,
                                                          op0=ALU.mult, op1=ALU.add), reads=[t1_b], writes=[t2_b])
                    E.op("dve", lambda h: h.tensor_scalar(out=t2[:, :n], in0=t2[:, :n], scalar1=MAGIC, scalar2=-TWO_PI,
                                                          op0=ALU.subtract, op1=ALU.mult), reads=[t2_b], writes=[t2_b])
                    E.op("dve", lambda h: h.tensor_tensor(out=t1[:, :n], in0=t1[:, :n], in1=t2[:, :n], op=ALU.add),
                         reads=[t1_b, t2_b], writes=[t1_b])
                    E.op("act", lambda h: h.activation(out=dst[:, c0:c0 + n], in_=t1[:, :n], func=AF.Sin),
                         reads=[t1_b], writes=[dst_b])

            sin_layer(H1, H1_b, w1, w1_b, zT, zT_b, 0, 2)
            sin_layer(H2, H2_b, w2, w2_b, H1, H1_b, 1, 3)
            GS, GS_b = A.alloc("gs", [128, nsc, 512], BF16)
            GD, GD_b = A.alloc("gd", [128, nsc, 512], BF16)
            Fm, Fm_b = A.alloc("fm", [128, 1024], F32)
            win, win_b = A.alloc("win", [128, 512], F32)
            fs, fs_b = A.alloc("fs", [128, 512], F32)
            for tc in range(nsc):
                for hh in range(2):
                    mm_group(bank(hh), [PSB[hh]], [(H2[:, tc * 128:(tc + 1) * 128], w3[:, hh * 512:(hh + 1) * 512],
                                                    [H2_b, w3_b])])
                    E.op("dve", lambda h: h.tensor_tensor(out=Fm[:, hh * 512:(hh + 1) * 512], in0=bank(hh),
                                                          in1=b3[:, hh * 512:(hh + 1) * 512], op=ALU.add),
                         reads=[PSB[hh], b3_b], writes=[Fm_b])
                E.op("act", lambda h: h.activation(out=win, in_=dlt, func=AF.Exp, scale=ntc[:, tc:tc + 1]),
                     reads=[dlt_b, ntc_b], writes=[win_b])
                E.op("dve", lambda h: h.tensor_tensor(out=fs, in0=Fm[:, 0:512], in1=Fm[:, 512:1024], op=ALU.add),
                     reads=[Fm_b], writes=[fs_b])
                E.op("dve", lambda h: h.tensor_tensor(out=GS[:, tc, :], in0=fs, in1=win, op=ALU.mult),
                     reads=[fs_b, win_b], writes=[GS_b])
                E.op("dve", lambda h: h.tensor_tensor(out=fs, in0=Fm[:, 0:512], in1=Fm[:, 512:1024], op=ALU.subtract),
                     reads=[Fm_b], writes=[fs_b])
                E.op("dve", lambda h: h.tensor_tensor(out=GD[:, tc, :], in0=fs, in1=win, op=ALU.mult),
                     reads=[fs_b, win_b], writes=[GD_b])
            fw, fw_bs = A.alloc("fwst", [128, 2, nsc, 128], BF16, nbufs=2)
            nfc = 2 * nsc + 1
            for fc in range(nfc):
                slot = fc % 2
                E.dma("sp", fw[:, slot], tabs[L]["fwd"][fc].rearrange("p (s f) -> p s f", f=128), writes=[fw_bs[slot]])
                src, src_b = (GS, GS_b) if fc <= nsc else (GD, GD_b)
                pb = fc % 2
                mm_group(bank(pb), [PSB[pb]], [(fw[:, slot, s_, :], src[:, s_, :], [fw_bs[slot], src_b]) for s_ in range(nsc)])
                if fc <= nsc:
                    E.op("dve", lambda h: h.tensor_tensor(out=GR[:, fc, :], in0=bank(pb), in1=skp, op=ALU.add),
                         reads=[PSB[pb], skp_b], writes=[GR_b])
                else:
                    E.op("act", lambda h: h.activation(out=GI[:, fc - nsc - 1, :], in_=bank(pb), func=AF.Copy),
                         reads=[PSB[pb]], writes=[GI_b])
            A.release(mk_)
            return GR, GR_b, GI, GI_b

        def run_group(G):
            gname, nseq, L, v = G["name"], G["nseq"], G["L"], G["v"]
            smp = G["sample"]
            T = nseq * L
            nblk = T // 512
            nkv = L + (PAST if smp else 0)
            X0d, X0d_b = G["xin"], Buf(gname + "_xin")
            X1d, X1d_b = dscratch(gname + "_x1", [128, 8 * T])
            X2d, X2d_b = dscratch(gname + "_x2", [128, 8 * T])
            mk_g = A.mark()
            MIX, MIX_bs = A.alloc(gname + "_mix", [128, 8, T], BF16, nbufs=8)
            mk_m = A.mark()
            mk_top = A.mark_top()
            GR, GR_b, GI, GI_b = hyena_filters(L, A)
            HS, HS_bs = A.alloc(gname + "_hs", [128, 8, T], BF16, nbufs=nblk)
            mk_a = A.mark()
            sq, sq_b = A.alloc("sq", [128, 8, 512], BF16)
            rstd, rstd_b = A.alloc("rstd", [128, 512], F32)
            tmp, tmp_b = A.alloc("tmp", [128, 512], F32)
            xb, xb_bs = A.alloc("xb", [128, 2, 8, 512], F32, nbufs=2)
            for blk in range(nblk):
                sl = blk % 2
                E.dma("sp", xb[:, sl], X0d.rearrange("p (j t) -> p j t", j=8)[:, :, blk * 512:(blk + 1) * 512],
                      writes=[xb_bs[sl]])
                modulate_block(xb[:, sl], xb_bs[sl], 512, 0, 0, v,
                               lambda j: HS[:, j, blk * 512:(blk + 1) * 512], HS_bs[blk], sq, sq_b, rstd, rstd_b, tmp, tmp_b)
            A.release(mk_a)
            if stage == 2:
                return

            QT, QT_b = A.alloc("QT", [128, 4, T], BF16)
            KT, KT_b = A.alloc("KT", [128, 2, nseq, nkv], BF16)
            VTb, VTb_b = A.alloc("VTb", [128, nseq * (nkv // 128), 256], BF16)
            mk_q = A.mark()
            wq, wq_b = A.alloc("wq", [128, 8, 1024], BF16)
            E.dma("sp", wq, WB["ah_w_in"][0].rearrange("p (kc n) -> p kc n", kc=8)[:, :, 0:1024], reads=[WB["ah_w_in"][1]], writes=[wq_b])
            sq, sq_b = A.alloc("sq", [128, 1, 512], BF16)
            rstd, rstd_b = A.alloc("rstd", [128, 512], F32)
            qn, qn_b = A.alloc("qn", [128, 512], F32)
            qb16, qb16_b = A.alloc("qb16", [128, 512], BF16)
            r1, r1_b = A.alloc("r1", [128, 512], F32)
            if smp:
                rc, rc_b = const("ropec", [128, TS], F32, ropec_d)
                rs, rs_b = const("ropes", [128, TS], F32, ropes_d)
                E.dma("pool", KT[:, :, 0, L:], ckT_d.rearrange("p (g t) -> p g t", g=2), writes=[KT_b])
                E.dma("pool", VTb[:, L // 128:, :], cvt_d.rearrange("p (c e) -> p c e", e=256), writes=[VTb_b])
            else:
                KN, KN_b = A.alloc("KN", [128, 2, T], F32)
                VT, VT_b = A.alloc("VT", [128, T // 128, 256], F32)
            KSUB = int(os.environ.get("KSUB", "0"))
            if KSUB == 1:
                return
            for hq in range(6):
                if KSUB == 2 and hq >= 4:
                    break
                for blk in range(nblk):
                    ts = slice(blk * 512, (blk + 1) * 512)
                    pb = blk % 2
                    mm_group(bank(pb), [PSB[pb]], [(wq[:, kc, hq * 128:(hq + 1) * 128], HS[:, kc, ts], [wq_b, HS_bs[blk]])
                                                   for kc in range(8)])
                    sumsq_rstd([bank(pb)], [PSB[pb]], 512, 128, rstd, rstd_b, sq, sq_b, 2 + pb)
                    gcol = 0 if hq < 4 else 1
                    if hq < 4:
                        dst = QT[:, hq, ts]
                        dst_b = QT_b
                    else:
                        s_i, t0 = (blk * 512) // L, (blk * 512) % L
                        dst_b = KT_b
                    if not smp:
                        if hq < 4:
                            E.op("dve", lambda h: h.scalar_tensor_tensor(
                                out=dst, in0=bank(pb), scalar=qkn[:, 0:1], in1=rstd, op0=ALU.mult, op1=ALU.mult),
                                reads=[PSB[pb], qkn_b, rstd_b], writes=[dst_b])
                        else:
                            g = hq - 4
                            E.op("dve", lambda h: h.scalar_tensor_tensor(
                                out=KN[:, g, ts], in0=bank(pb), scalar=qkn[:, 1:2], in1=rstd, op0=ALU.mult, op1=ALU.mult),
                                reads=[PSB[pb], qkn_b, rstd_b], writes=[KN_b])
                            nsq = 512 // L
                            E.op("act", lambda h: h.activation(
                                out=KT[:, g, s_i:s_i + nsq, 0:L], in_=KN[:, g, ts].rearrange("p (s t) -> p s t", t=L),
                                func=AF.Copy), reads=[KN_b], writes=[KT_b])
                    else:
                        E.op("dve", lambda h: h.scalar_tensor_tensor(
                            out=qn, in0=bank(pb), scalar=qkn[:, gcol:gcol + 1], in1=rstd, op0=ALU.mult, op1=ALU.mult),
                            reads=[PSB[pb], qkn_b, rstd_b], writes=[qn_b])
                        E.op("act", lambda h: h.activation(out=qb16, in_=qn, func=AF.Copy), reads=[qn_b], writes=[qb16_b])
                        mm_group(bank(4 + pb), [PSB[4 + pb]], [(swp, qb16, [swp_b, qb16_b])])
                        E.op("dve", lambda h: h.tensor_tensor(out=r1, in0=bank(4 + pb), in1=rs[:, ts], op=ALU.mult),
                             reads=[PSB[4 + pb], rs_b], writes=[r1_b])
                        E.op("pool", lambda h: h.tensor_tensor(out=qn, in0=qn, in1=rc[:, ts], op=ALU.mult),
                             reads=[qn_b, rc_b], writes=[qn_b])
                        if hq < 4:
                            d2 = dst
                        else:
                            d2 = KT[:, hq - 4, 0, t0:t0 + 512]
                        E.op("dve", lambda h: h.tensor_tensor(out=d2, in0=qn, in1=r1, op=ALU.add),
                             reads=[qn_b, r1_b], writes=[dst_b])
            if G.get("kout") is not None:
                E.dma("sp", G["kout"].rearrange("p (g t) -> p g t", g=2), KN, reads=[KN_b])
            if KSUB in (2, 3):
                return
            for c in range(T // 128):
                pb = 4 + c % 2
                blk = c // 4
                s_i, cc = (c * 128) // L, ((c * 128) % L) // 128
                mm_group(bank(pb, 256), [PSB[pb]], [(HS[:, kc, c * 128:(c + 1) * 128], wq[:, kc, 768:1024], [wq_b, HS_bs[blk]])
                                                    for kc in range(8)])
                if not smp and KSUB != 5:
                    E.op("dve", lambda h: h.tensor_copy(out=VT[:, c, :], in_=bank(pb, 256)),
                         reads=[PSB[pb]], writes=[VT_b])
                if KSUB != 6:
                    E.op("act", lambda h: h.activation(out=VTb[:, s_i * (nkv // 128) + cc, :], in_=bank(pb, 256), func=AF.Copy),
                         reads=[PSB[pb]], writes=[VTb_b])
            if G.get("vout") is not None:
                E.dma("sp", G["vout"].rearrange("p (c e) -> p c e", e=256), VT, reads=[VT_b])
            A.release(mk_q)
            if stage == 3:
                return
            Pt, Pt_bs = A.alloc("Pt", [128, 2, 512], BF16, nbufs=2)
            rden, rden_b = A.alloc("rden", [128, 512], F32)
            nq = min(512, L)
            nkc = nkv // 128
            att_scale = 1.0 / math.sqrt(128.0)
            for s_i in range(nseq):
                for hd in range(4):
                    g = hd // 2
                    for qb in range(L // nq):
                        q0 = s_i * L + qb * nq
                        for kc in range(nkc):
                            sb = kc % 2
                            mm_group(bank(sb, nq), [PSB[sb]], [(KT[:, g, s_i, kc * 128:(kc + 1) * 128], QT[:, hd, q0:q0 + nq],
                                                                [KT_b, QT_b])])
                            E.op("act", lambda h: h.activation(out=Pt[:, sb, :nq], in_=bank(sb, nq), func=AF.Exp,
                                                               scale=att_scale), reads=[PSB[sb]], writes=[Pt_bs[sb]])
                            E.op("pe", lambda h: h.matmul(bank(2, nq), lhsT=VTb[:, s_i * nkc + kc, g * 128:(g + 1) * 128],
                                                          rhs=Pt[:, sb, :nq], start=(kc == 0), stop=(kc == nkc - 1)),
                                 reads=[VTb_b, Pt_bs[sb]], writes=[PSB[2]], pe_accum=True)
                            E.op("pe", lambda h: h.matmul(bank(3, nq), lhsT=ones, rhs=Pt[:, sb, :nq],
                                                          start=(kc == 0), stop=(kc == nkc - 1)),
                                 reads=[ones_b, Pt_bs[sb]], writes=[PSB[3]], pe_accum=True)
                        E.op("act", lambda h: h.activation(out=rden[:, :nq], in_=bank(3, nq), func=AF.Ln), reads=[PSB[3]], writes=[rden_b])
                        E.op("act", lambda h: h.activation(out=rden[:, :nq], in_=rden[:, :nq], func=AF.Exp, scale=-1.0),
                             reads=[rden_b], writes=[rden_b])
                        E.op("dve", lambda h: h.tensor_tensor(out=MIX[:, hd, q0:q0 + nq], in0=bank(2, nq), in1=rden[:, :nq],
                                                              op=ALU.mult), reads=[PSB[2], rden_b], writes=[MIX_bs[hd]])
            A.release(mk_a)
            if stage == 4:
                return

            nsc = L // 128
            nfc = 2 * nsc + 1
            X0, X0_b = A.alloc("X0", [128, 4, T], BF16, top=True)
            VPT, VPT_b = A.alloc("VPT", [128, nseq, nsc, 512], BF16, top=True)
            mk_h = A.mark()
            wu, wu_bs = A.alloc("wu", [128, 2, 8, 128], BF16, nbufs=2)
            wu_it = [0]
            U, U_b = A.alloc("U", [128, T], F32)
            CU, CU_bs = A.alloc("CU", [128, 2, T], F32, nbufs=2)
            VPc, VPc_b = A.alloc("VPc", [128, T], BF16)
            for c in range(4):
                for ti, which in enumerate((1, 2, 0)):
                    ch = which * 4 + c
                    wsl = wu_it[0] % 2
                    wu_it[0] += 1
                    E.dma("sp", wu[:, wsl], WB["ah_w_in"][0].rearrange("p (kc n) -> p kc n", kc=8)[:, :, 1024 + ch * 128: 1024 + (ch + 1) * 128],
                          reads=[WB["ah_w_in"][1]], writes=[wu_bs[wsl]])
                    for blk in range(nblk):
                        ts = slice(blk * 512, (blk + 1) * 512)
                        pb = blk % 2
                        mm_group(bank(pb), [PSB[pb]], [(wu[:, wsl, kc, :], HS[:, kc, ts], [wu_bs[wsl], HS_bs[blk]])
                                                       for kc in range(8)])
                        E.op("act", lambda h: h.activation(out=U[:, ts], in_=bank(pb), func=AF.Copy),
                             reads=[PSB[pb]], writes=[U_b])
                    ci = ti % 2
                    cu, cu_b = CU[:, ci], CU_bs[ci]
                    E.op("act", lambda h: h.activation(out=cu, in_=U, func=AF.Identity, bias=hcw[:, ch, 3:4],
                                                       scale=hcw[:, ch, 1:2]), reads=[U_b, hcw_b], writes=[cu_b])
                    c3 = cu.rearrange("p (s t) -> p s t", t=L)
                    u3 = U.rearrange("p (s t) -> p s t", t=L)
                    E.op("dve", lambda h: h.scalar_tensor_tensor(out=c3[:, :, 1:L], in0=u3[:, :, 0:L - 1], scalar=hcw[:, ch, 0:1],
                                                                 in1=c3[:, :, 1:L], op0=ALU.mult, op1=ALU.add),
                         reads=[U_b, hcw_b, cu_b], writes=[cu_b])
                    E.op("dve", lambda h: h.scalar_tensor_tensor(out=c3[:, :, 0:L - 1], in0=u3[:, :, 1:L], scalar=hcw[:, ch, 2:3],
                                                                 in1=c3[:, :, 0:L - 1], op0=ALU.mult, op1=ALU.add),
                         reads=[U_b, hcw_b, cu_b], writes=[cu_b])
                    if which == 2:
                        E.op("pool", lambda h: h.tensor_tensor(out=VPc, in0=CU[:, 0], in1=CU[:, 1], op=ALU.mult),
                             reads=[CU_bs[0], CU_bs[1]], writes=[VPc_b])
                    if which == 0:
                        E.op("pool", lambda h: h.tensor_copy(out=X0[:, c, :], in_=cu), reads=[cu_b], writes=[X0_b])
                for tcn in range(T // 128):
                    s_i, cc = (tcn * 128) // L, ((tcn * 128) % L) // 128
                    pb = 4 + tcn % 2
                    E.op("pe", lambda h: h.transpose(PSbf[pb], VPc[:, tcn * 128:(tcn + 1) * 128], ident),
                         reads=[VPc_b, ident_b], writes=[PSbf_b[pb]])
                    E.op("act", lambda h: h.activation(out=VPT[:, s_i, cc, c * 128:(c + 1) * 128], in_=PSbf[pb], func=AF.Copy),
                         reads=[PSbf_b[pb]], writes=[VPT_b])
            A.release(mk_m)
            if stage == 5:
                return
            YR, YR_b = A.alloc("YR", [128, nsc + 1, 512], BF16)
            YI, YI_b = A.alloc("YI", [128, nsc, 512], BF16)
            vr, vr_b = A.alloc("vr", [128, 512], F32)
            vi, vi_b = A.alloc("vi", [128, 512], F32)
            pa, pa_b = A.alloc("pa", [128, 512], F32)
            pb_, pb_b = A.alloc("pb", [128, 512], F32)
            pc, pc_b = A.alloc("pc", [128, 512], F32)
            pd, pd_b = A.alloc("pd", [128, 512], F32)
            fw, fw_bs = A.alloc("fwst", [128, 2, nsc, 128], BF16, nbufs=2)
            nt = min(L, 256)
            ntb = L // nt
            iv, iv_b = A.alloc("invst", [128, nfc, nt], BF16)
            for s_i in range(nseq):
                for fc in range(nsc + 1):
                    E.dma("sp", fw[:, 0], tabs[L]["fwd"][fc].rearrange("p (s f) -> p s f", f=128), writes=[fw_bs[0]])
                    mm_group(bank(0), [PSB[0]], [(fw[:, 0, s_, :], VPT[:, s_i, s_, :], [fw_bs[0], VPT_b]) for s_ in range(nsc)])
                    E.op("act", lambda h: h.activation(out=vr, in_=bank(0), func=AF.Copy), reads=[PSB[0]], writes=[vr_b])
                    if fc < nsc:
                        E.dma("sp", fw[:, 1], tabs[L]["fwd"][nsc + 1 + fc].rearrange("p (s f) -> p s f", f=128),
                              writes=[fw_bs[1]])
                        mm_group(bank(1), [PSB[1]], [(fw[:, 1, s_, :], VPT[:, s_i, s_, :], [fw_bs[1], VPT_b])
                                                     for s_ in range(nsc)])
                        E.op("act", lambda h: h.activation(out=vi, in_=bank(1), func=AF.Copy), reads=[PSB[1]], writes=[vi_b])
                        E.op("dve", lambda h: h.tensor_tensor(out=pa, in0=vr, in1=GR[:, fc, :], op=ALU.mult),
                             reads=[vr_b, GR_b], writes=[pa_b])
                        E.op("pool", lambda h: h.tensor_tensor(out=pb_, in0=vi, in1=GI[:, fc, :], op=ALU.mult),
                             reads=[vi_b, GI_b], writes=[pb_b])
                        E.op("dve", lambda h: h.tensor_tensor(out=YR[:, fc, :], in0=pa, in1=pb_, op=ALU.subtract),
                             reads=[pa_b, pb_b], writes=[YR_b])
                        E.op("pool", lambda h: h.tensor_tensor(out=pc, in0=vr, in1=GI[:, fc, :], op=ALU.mult),
                             reads=[vr_b, GI_b], writes=[pc_b])
                        E.op("dve", lambda h: h.tensor_tensor(out=pd, in0=vi, in1=GR[:, fc, :], op=ALU.mult),
                             reads=[vi_b, GR_b], writes=[pd_b])
                        E.op("pool", lambda h: h.tensor_tensor(out=YI[:, fc, :], in0=pc, in1=pd, op=ALU.add),
                             reads=[pc_b, pd_b], writes=[YI_b])
                    else:
                        E.op("dve", lambda h: h.tensor_tensor(out=YR[:, fc, :], in0=vr, in1=GR[:, fc, :], op=ALU.mult),
                             reads=[vr_b, GR_b], writes=[YR_b])
                for tb in range(ntb):
                    tw = min(L, 512)
                    E.dma("sp", iv, tabs[L]["inv"][(tb * nt) // tw].rearrange("p (f t) -> p f t", t=tw)[:, :, (tb * nt) % tw:(tb * nt) % tw + nt],
                          writes=[iv_b])
                    t0 = s_i * L + tb * nt
                    for cc in range(4):
                        pbk = 4 + cc % 2
                        terms = [(YR[:, f_, cc * 128:(cc + 1) * 128], iv[:, f_, :], [YR_b, iv_b]) for f_ in range(nsc + 1)]
                        terms += [(YI[:, f_, cc * 128:(cc + 1) * 128], iv[:, nsc + 1 + f_, :], [YI_b, iv_b]) for f_ in range(nsc)]
                        mm_group(bank(pbk, nt), [PSB[pbk]], terms)
                        E.op("dve", lambda h: h.tensor_tensor(out=MIX[:, 4 + cc, t0:t0 + nt], in0=bank(pbk, nt),
                                                              in1=X0[:, cc, t0:t0 + nt], op=ALU.mult),
                             reads=[PSB[pbk], X0_b], writes=[MIX_bs[4 + cc]])
            A.release(mk_m)
            A.release_top(mk_top)
            if stage == 6:
                dbg = P.dout("dbg_" + gname, [128, 8 * T], BF16)
                E.dma("sp", dbg.rearrange("p (j t) -> p j t", j=8), MIX, reads=MIX_bs)
                return

            HF, HF_bs = A.alloc(gname + "_hf", [128, 8, T], BF16, nbufs=nblk)
            mk_o = A.mark()
            wo, wo_b = A.alloc("wo", [128, 8, 1024], BF16)
            E.dma("sp", wo, WB["ah_w_out"][0].rearrange("p (kc n) -> p kc n", kc=8), reads=[WB["ah_w_out"][1]], writes=[wo_b])
            O, O_b = A.alloc("O", [128, 8, 512], F32)
            sq, sq_b = A.alloc("sq", [128, 8, 512], BF16)
            rstd, rstd_b = A.alloc("rstd", [128, 512], F32)
            tmp, tmp_b = A.alloc("tmp", [128, 512], F32)
            xb, xb_b = A.alloc("xb", [128, 8, 512], F32)
            wk = (sq, sq_b, rstd, rstd_b, tmp, tmp_b, xb, xb_b)
            for blk in range(nblk):
                ts = slice(blk * 512, (blk + 1) * 512)
                for m in range(8):
                    pbk = m % 2
                    mm_group(bank(pbk), [PSB[pbk]], [(wo[:, kc, m * 128:(m + 1) * 128], MIX[:, kc, ts], [wo_b, MIX_bs[kc]])
                                                     for kc in range(8)])
                    E.op("act", lambda h: h.activation(out=O[:, m, :], in_=bank(pbk), func=AF.Copy), reads=[PSB[pbk]], writes=[O_b])
                epilogue(O, O_b, 512, X0d, X0d_b, X1d, X1d_b, blk * 512, T, 0, 0, v,
                         (0, 1, lambda j: HF[:, j, ts], HF_bs[blk]), wk)
            A.release(mk_o)
            if stage == 7:
                dbg = P.dout("dbg_" + gname, [128, 8 * T], F32)
                E.dma("sp", dbg, X1d, reads=[X1d_b], writes=[])
                return
            p_after_hf = A.p
            A.release(mk_g)
            HS1, HS1_bs = A.alloc(gname + "_hs1", [128, 8, T], BF16, nbufs=nblk)
            p_after_hs1 = A.p
            A.p = p_after_hf
            X2d, X2d_b = dscratch(gname + "_x2b", [128, 8 * T])
            ffn(G, 0, HF, HF_bs, X1d, X1d_b, X2d, X2d_b,
                lambda tok0: (1, 0, (lambda j: HS1[:, j, tok0:tok0 + 512]), HS1_bs[tok0 // 512]), NB=(512 if smp else 1024))
            if stage == 8:
                E.dma("sp", G["y"], X2d, reads=[X2d_b])
                A.release(mk_g)
                return
            A.release(p_after_hs1)
            layer1(G, HS1, HS1_bs, X2d, X2d_b)
            A.release(mk_g)

        def layer1(G, HS1, HS1_bs, X2d, X2d_b):
            gname, nseq, L, v = G["name"], G["nseq"], G["L"], G["v"]
            smp = G["sample"]
            T = nseq * L
            nblk = T // 512
            nch = L // 128
            X3d, X3d_b = dscratch(gname + "_x3", [128, 8 * T])
            mk_l = A.mark()
            MLM, MLM_bs = A.alloc("mlmix", [128, 8, T], BF16, nbufs=8)
            mk_2 = A.mark()
            RT, RT_b = A.alloc("RT", [40, T], F32)
            EM, EM_b = A.alloc("EM", [40, T], F32)
            WI, WI_b = A.alloc("WI", [40, T], F32)
            ATK, ATK_b = A.alloc("ATK", [128, T // 128, 40], F32)
            m0c, m0c_b = A.alloc("m0c", [40, 2], F32)
            onesf, onesf_b = A.alloc("onesf", [40, 128], F32)
            E.op("dve", lambda h: h.memset(onesf, 1.0), writes=[onesf_b])
            if G.get("states"):
                WTK, WTK_b = A.alloc("WTK", [128, T // 128, 40], F32)
            mk_r = A.mark()
            wg, wg_b = A.alloc("wg", [128, 8, 80], BF16)
            E.dma("pool", wg, mlwg_d.rearrange("p (kc n) -> p kc n", kc=8), writes=[wg_b])
            IG, IG_b = A.alloc("IG", [40, T], F32)
            LF, LF_b = A.alloc("LF", [40, T], F32)
            for blk in range(nblk):
                ts = slice(blk * 512, (blk + 1) * 512)
                for gi in range(2):
                    mm_group(PS[0:40, gi * 512:(gi + 1) * 512], [PSB[gi]],
                             [(wg[:, kc, gi * 40:(gi + 1) * 40], HS1[:, kc, ts], [wg_b, HS1_bs[blk]]) for kc in range(8)])
                E.op("act", lambda h: h.activation(out=IG[:, ts], in_=PS[0:40, 0:512], func=AF.Identity, bias=mbg[:, 0:1], scale=1.0),
                     reads=[PSB[0], mbg_b], writes=[IG_b])
                E.op("act", lambda h: h.activation(out=LF[:, ts], in_=PS[0:40, 512:1024], func=AF.Identity, bias=mbg[:, 1:2], scale=1.0),
                     reads=[PSB[1], mbg_b], writes=[LF_b])
            E.op("act", lambda h: h.activation(out=LF, in_=LF, func=AF.Exp, scale=-1.0), reads=[LF_b], writes=[LF_b])
            E.op("act", lambda h: h.activation(out=LF, in_=LF, func=AF.Ln, bias=onec[0:40, 0:1], scale=1.0),
                 reads=[LF_b, onec_b], writes=[LF_b])
            E.op("dve", lambda h: h.tensor_scalar(out=LF, in0=LF, scalar1=-1.0, scalar2=None, op0=ALU.mult), reads=[LF_b], writes=[LF_b])
            if smp:
                E.dma("sp", m0c[:, 0:1], m0_d, writes=[m0c_b])
            else:
                E.op("dve", lambda h: h.memset(m0c, 0.0), writes=[m0c_b])
            BT, BT_b = A.alloc("BT", [40, T], F32)
            AA, AA_b = A.alloc("AA", [40, T], F32)
            CM, CM_b = A.alloc("CM", [40, T], F32)
            onesr, onesr_b = A.alloc("onesr", [40, L], F32)
            E.op("dve", lambda h: h.memset(onesr, 1.0), writes=[onesr_b])
            for (tt, tb_) in ((BT, BT_b), (AA, AA_b), (CM, CM_b)):
                E.op("pool", lambda h: h.memset(tt, 0.0), writes=[tb_])
            for s_i in range(nseq):
                sl = slice(s_i * L, (s_i + 1) * L)
                for (p0, rev) in ((0, False), (32, True)):
                    pr = slice(p0, p0 + 8)

                    def V_(ap):
                        a2 = ap[pr, sl]
                        return a2[:, ::-1] if rev else a2
                    E.op("dve", lambda h: h.tensor_tensor_scan(out=V_(BT), data0=onesr[pr, :], data1=V_(LF), initial=0.0,
                                                               op0=ALU.mult, op1=ALU.add),
                         reads=[LF_b, onesr_b], writes=[BT_b])
                    E.op("dve", lambda h: h.tensor_tensor(out=AA[pr, sl], in0=IG[pr, sl], in1=BT[pr, sl], op=ALU.subtract),
                         reads=[IG_b, BT_b], writes=[AA_b])
                    E.op("dve", lambda h: h.tensor_tensor_scan(out=V_(CM), data0=V_(AA), data1=V_(AA), initial=m0c[pr, 0:1],
                                                               op0=ALU.max, op1=ALU.max), reads=[AA_b, m0c_b], writes=[CM_b])
            E.op("dve", lambda h: h.tensor_scalar(out=RT, in0=CM, scalar1=-1.0, scalar2=None, op0=ALU.mult), reads=[CM_b], writes=[RT_b])
            E.op("dve", lambda h: h.tensor_tensor(out=EM, in0=BT, in1=CM, op=ALU.add), reads=[BT_b, CM_b], writes=[EM_b])
            if G.get("states"):
                MTk, MTk_b = A.alloc("MTk", [40, nseq], F32)
                E.op("dve", lambda h: h.memset(MTk, 0.0), writes=[MTk_b])
                for s_i in range(nseq):
                    E.op("dve", lambda h: h.tensor_copy(out=MTk[0:8, s_i:s_i + 1], in_=EM[0:8, (s_i + 1) * L - 1:(s_i + 1) * L]),
                         reads=[EM_b], writes=[MTk_b])
                    E.op("dve", lambda h: h.tensor_copy(out=MTk[32:40, s_i:s_i + 1], in_=EM[32:40, s_i * L:s_i * L + 1]),
                         reads=[EM_b], writes=[MTk_b])
                E.dma("sp", mst_d, MTk, reads=[MTk_b])
            E.op("act", lambda h: h.activation(out=EM, in_=EM, func=AF.Exp, scale=-1.0), reads=[EM_b], writes=[EM_b])
            E.op("act", lambda h: h.activation(out=WI, in_=CM, func=AF.Exp, scale=-1.0, bias=m0c[:, 0:1]),
                 reads=[CM_b, m0c_b], writes=[WI_b])
            for c in range(T // 128):
                E.op("pe", lambda h: h.transpose(PS[:, 1024:1064], AA[:, c * 128:(c + 1) * 128], id32[0:40, 0:40]),
                     reads=[AA_b, id32_b], writes=[PSB[2]])
                E.op("dve", lambda h: h.tensor_copy(out=ATK[:, c, :], in_=PS[:, 1024:1064]), reads=[PSB[2]], writes=[ATK_b])
            if G.get("states"):
                dg, dg_b = A.alloc("dg", [40, nseq, 40], F32)
                for s_i in range(nseq):
                    E.op("dve", lambda h: h.memset(dg[:, s_i, :], 0.0), writes=[dg_b])
                    E.op("dve", lambda h: h.tensor_scalar(out=dg[0:8, s_i, :], in0=id32[0:8, 0:40], scalar1=RT[0:8, (s_i + 1) * L - 1:(s_i + 1) * L],
                                                          scalar2=None, op0=ALU.mult), reads=[id32_b, RT_b], writes=[dg_b])
                    E.op("dve", lambda h: h.tensor_scalar(out=dg[32:40, s_i, :], in0=id32[32:40, 0:40], scalar1=RT[32:40, s_i * L:s_i * L + 1],
                                                          scalar2=None, op0=ALU.mult), reads=[id32_b, RT_b], writes=[dg_b])
                    mm_group(PS[:, 1024:1064], [PSB[2]], [(onesf, dg[:, s_i, :], [onesf_b, dg_b])])
                    for cc in range(nch):
                        c = s_i * nch + cc
                        E.op("dve", lambda h: h.tensor_tensor(out=WTK[:, c, :], in0=ATK[:, c, :], in1=PS[:, 1024:1064], op=ALU.add),
                             reads=[ATK_b, PSB[2]], writes=[WTK_b])
                E.op("act", lambda h: h.activation(out=WTK, in_=WTK, func=AF.Exp), reads=[WTK_b], writes=[WTK_b])
            A.release(mk_r)
            wh, wh_b = A.alloc("wh", [128, 8, 4, 128], BF16)
            U, U_b = A.alloc("U", [128, T], F32)
            cu, cu_b = A.alloc("cu1", [128, T], F32)
            QT, QT_b = A.alloc("QT1", [128, T], BF16)
            KT, KT_b = A.alloc("KT1", [128, T], BF16)
            SG, SG_b = A.alloc("SG", [128, T], BF16)
            VK, VK_b = A.alloc("VK", [128, T // 128, 128], BF16)
            KK, KK_b = A.alloc("KK", [128, T // 128, 128], BF16)
            HSUM, HSUM_b = A.alloc("HSUM", [128, T], F32)
            nq = min(512, L)
            rtb, rtb_b = A.alloc("rtb", [128, nq], F32)
            emb, emb_b = A.alloc("emb", [128, nq], F32)
            wexp, wexp_bs = A.alloc("wexp", [128, 2, nq], F32, nbufs=2)
            Pm, Pm_bs = A.alloc("Pm", [128, 2, nq], BF16, nbufs=2)
            dn, dn_b = A.alloc("dn", [128, nq], F32)
            ht, ht_b = A.alloc("ht", [128, nq], F32)
            sq, sq_b = A.alloc("sq", [128, 1, 512], BF16)
            rstd, rstd_b = A.alloc("rstd", [128, 512], F32)
            vw, vw_b = A.alloc("vw", [128, 128], BF16)
            cst, cst_bs = A.alloc("cst", [128, 2, 128], F32, nbufs=2)
            nst, nst_bs = A.alloc("nst", [1, 2, 128], F32, nbufs=2)
            if smp:
                qp, qp_b = A.alloc("qp", [128, nq], BF16)
                c0t, c0t_b = A.alloc("c0t", [128, 16, 128], BF16)
                n0b, n0b_b = A.alloc("n0b", [128, 16, 128], BF16)
                E.dma("pool", c0t, c0t_d.rearrange("p (r m) -> p r m", m=128), writes=[c0t_b])
                E.dma("pool", n0b, n0b_d.rearrange("p (r m) -> p r m", m=128), writes=[n0b_b])
            wv = WB["ml_w_in"][0].rearrange("p (kc n) -> p kc n", kc=8)
            for hd in range(8):
                for wi in range(4):
                    E.dma("sp", wh[:, :, wi, :], wv[:, :, wi * 1024 + hd * 128: wi * 1024 + (hd + 1) * 128], reads=[WB["ml_w_in"][1]], writes=[wh_b])
                for wi, (dst, dst_b) in ((0, (QT, QT_b)), (1, (KT, KT_b)), (3, (SG, SG_b))):
                    for blk in range(nblk):
                        ts = slice(blk * 512, (blk + 1) * 512)
                        pb = blk % 2
                        mm_group(bank(pb), [PSB[pb]], [(wh[:, kc, wi, :], HS1[:, kc, ts], [wh_b, HS1_bs[blk]]) for kc in range(8)])
                        if wi == 3:
                            E.op("act", lambda h: h.activation(out=SG[:, ts], in_=bank(pb), func=AF.Sigmoid), reads=[PSB[pb]], writes=[SG_b])
                        else:
                            E.op("act", lambda h: h.activation(out=U[:, ts], in_=bank(pb), func=AF.Copy), reads=[PSB[pb]], writes=[U_b])
                    if wi == 3:
                        continue
                    ch = wi * 8 + hd
                    E.op("act", lambda h: h.activation(out=cu, in_=U, func=AF.Identity, bias=mcw[:, ch, 3:4], scale=mcw[:, ch, 1:2]),
                         reads=[U_b, mcw_b], writes=[cu_b])
                    c3 = cu.rearrange("p (s t) -> p s t", t=L)
                    u3 = U.rearrange("p (s t) -> p s t", t=L)
                    E.op("dve", lambda h: h.scalar_tensor_tensor(out=c3[:, :, 1:L], in0=u3[:, :, 0:L - 1], scalar=mcw[:, ch, 0:1],
                                                                 in1=c3[:, :, 1:L], op0=ALU.mult, op1=ALU.add),
                         reads=[U_b, mcw_b, cu_b], writes=[cu_b])
                    E.op("dve", lambda h: h.scalar_tensor_tensor(out=c3[:, :, 0:L - 1], in0=u3[:, :, 1:L], scalar=mcw[:, ch, 2:3],
                                                                 in1=c3[:, :, 0:L - 1], op0=ALU.mult, op1=ALU.add),
                         reads=[U_b, mcw_b, cu_b], writes=[cu_b])
                    if wi == 0:
                        E.op("act", lambda h: h.activation(out=QT, in_=cu, func=AF.Silu), reads=[cu_b], writes=[QT_b])
                    else:
                        E.op("act", lambda h: h.activation(out=cu, in_=cu, func=AF.Silu), reads=[cu_b], writes=[cu_b])
                        E.op("pool", lambda h: h.tensor_scalar(out=KT, in0=cu, scalar1=128.0 ** -0.5, scalar2=None, op0=ALU.mult),
                             reads=[cu_b], writes=[KT_b])
                for c in range(T // 128):
                    pb = 4 + c % 2
                    mm_group(bank(pb, 128), [PSB[pb]], [(HS1[:, kc, c * 128:(c + 1) * 128], wh[:, kc, 2, :], [wh_b, HS1_bs[c // 4]])
                                                        for kc in range(8)])
                    E.op("act", lambda h: h.activation(out=VK[:, c, :], in_=bank(pb, 128), func=AF.Copy), reads=[PSB[pb]], writes=[VK_b])
                    if G.get("states"):
                        E.op("pe", lambda h: h.transpose(PSbf[4], KT[:, c * 128:(c + 1) * 128], ident), reads=[KT_b, ident_b], writes=[PSbf_b[4]])
                        E.op("act", lambda h: h.activation(out=KK[:, c, :], in_=PSbf[4], func=AF.Copy), reads=[PSbf_b[4]], writes=[KK_b])
                for s_i in range(nseq):
                    for di in range(2):
                        r = di * 32 + hd
                        ridx = di * 8 + hd
                        for qb in range(L // nq):
                            q0 = s_i * L + qb * nq
                            mm_group(bank(4, nq), [PSB[4]], [(sel[:, ridx, :], RT[:, q0:q0 + nq], [sel_b, RT_b])])
                            E.op("act", lambda h: h.activation(out=rtb, in_=bank(4, nq), func=AF.Copy), reads=[PSB[4]], writes=[rtb_b])
                            mm_group(bank(5, nq), [PSB[5]], [(sel[:, ridx, :], EM[:, q0:q0 + nq], [sel_b, EM_b])])
                            E.op("act", lambda h: h.activation(out=emb, in_=bank(5, nq), func=AF.Copy), reads=[PSB[5]], writes=[emb_b])
                            if di == 0:
                                kcs = [kc for kc in range(nch) if kc * 128 <= qb * nq + nq - 1]
                            else:
                                kcs = [kc for kc in range(nch) if kc * 128 + 127 >= qb * nq]
                            nterm = len(kcs) + (1 if smp else 0)
                            ti = 0
                            if smp:
                                mm_group(bank(4, nq), [PSB[4]], [(sel[:, ridx, :], WI[:, q0:q0 + nq], [sel_b, WI_b])])
                                E.op("dve", lambda h: h.tensor_tensor(out=qp, in0=QT[:, q0:q0 + nq], in1=bank(4, nq), op=ALU.mult),
                                     reads=[QT_b, PSB[4]], writes=[qp_b])
                                E.op("pe", lambda h: h.matmul(bank(2, nq), lhsT=c0t[:, ridx, :], rhs=qp, start=True, stop=(nterm == 1)),
                                     reads=[c0t_b, qp_b], writes=[PSB[2]], pe_accum=True)
                                E.op("pe", lambda h: h.matmul(bank(3, nq), lhsT=n0b[:, ridx, :], rhs=qp, start=True, stop=(nterm == 1)),
                                     reads=[n0b_b, qp_b], writes=[PSB[3]], pe_accum=True)
                                ti = 1
                            for kc in kcs:
                                c = s_i * nch + kc
                                sb = ti % 2
                                off = kc * 128 - qb * nq
                                mm_group(bank(sb, nq), [PSB[sb]], [(KT[:, c * 128:(c + 1) * 128], QT[:, q0:q0 + nq], [KT_b, QT_b])])
                                E.op("act", lambda h: h.activation(out=wexp[:, sb, :], in_=rtb, func=AF.Exp, bias=ATK[:, c, r:r + 1], scale=1.0),
                                     reads=[rtb_b, ATK_b], writes=[wexp_bs[sb]])
                                E.op("dve", lambda h: h.tensor_tensor(out=Pm[:, sb, :], in0=bank(sb, nq), in1=wexp[:, sb, :], op=ALU.mult),
                                     reads=[PSB[sb], wexp_bs[sb]], writes=[Pm_bs[sb]])
                                if 0 <= off < nq and (off // 128) < 4:
                                    E.op("pool", lambda h: h.tensor_tensor(out=Pm[:, sb, :], in0=Pm[:, sb, :], in1=msk[:, di, off // 128, 0:nq], op=ALU.mult),
                                         reads=[Pm_bs[sb], msk_b], writes=[Pm_bs[sb]])
                                E.op("pe", lambda h: h.matmul(bank(2, nq), lhsT=VK[:, c, :], rhs=Pm[:, sb, :], start=(ti == 0), stop=(ti == nterm - 1)),
                                     reads=[VK_b, Pm_bs[sb]], writes=[PSB[2]], pe_accum=True)
                                E.op("pe", lambda h: h.matmul(bank(3, nq), lhsT=ones, rhs=Pm[:, sb, :], start=(ti == 0), stop=(ti == nterm - 1)),
                                     reads=[ones_b, Pm_bs[sb]], writes=[PSB[3]], pe_accum=True)
                                ti += 1
                            E.op("dve", lambda h: h.tensor_scalar(out=dn, in0=bank(3, nq), scalar1=-1.0, scalar2=None, op0=ALU.mult),
                                 reads=[PSB[3]], writes=[dn_b])
                            E.op("dve", lambda h: h.tensor_tensor(out=dn, in0=dn, in1=bank(3, nq), op=ALU.max), reads=[dn_b, PSB[3]], writes=[dn_b])
                            E.op("dve", lambda h: h.tensor_tensor(out=dn, in0=dn, in1=emb, op=ALU.max), reads=[dn_b, emb_b], writes=[dn_b])
                            E.op("act", lambda h: h.activation(out=dn, in_=dn, func=AF.Ln), reads=[dn_b], writes=[dn_b])
                            E.op("act", lambda h: h.activation(out=dn, in_=dn, func=AF.Exp, scale=-1.0), reads=[dn_b], writes=[dn_b])
                            if di == 0:
                                E.op("dve", lambda h: h.tensor_tensor(out=HSUM[:, q0:q0 + nq], in0=bank(2, nq), in1=dn, op=ALU.mult),
                                     reads=[PSB[2], dn_b], writes=[HSUM_b])
                            else:
                                E.op("dve", lambda h: h.tensor_tensor(out=ht, in0=bank(2, nq), in1=dn, op=ALU.mult),
                                     reads=[PSB[2], dn_b], writes=[ht_b])
                                E.op("pool", lambda h: h.tensor_tensor(out=HSUM[:, q0:q0 + nq], in0=HSUM[:, q0:q0 + nq], in1=ht, op=ALU.add),
                                     reads=[ht_b, HSUM_b], writes=[HSUM_b])
                        if G.get("states"):
                            so = (s_i * 16 + ridx) * 128
                            slot = ridx % 2
                            for cc in range(nch):
                                c = s_i * nch + cc
                                E.op("dve", lambda h: h.tensor_scalar(out=vw, in0=VK[:, c, :], scalar1=WTK[:, c, r:r + 1], scalar2=None, op0=ALU.mult),
                                     reads=[VK_b, WTK_b], writes=[vw_b])
                                E.op("pe", lambda h: h.matmul(bank(5, 128), lhsT=vw, rhs=KK[:, c, :], start=(cc == 0), stop=(cc == nch - 1)),
                                     reads=[vw_b, KK_b], writes=[PSB[5]], pe_accum=True)
                            E.op("act", lambda h: h.activation(out=cst[:, slot, :], in_=bank(5, 128), func=AF.Copy), reads=[PSB[5]], writes=[cst_bs[slot]])
                            E.dma("sp", cst_d[:, so:so + 128], cst[:, slot, :], reads=[cst_bs[slot]])
                            wtb, wtb_b = vw, vw_b
                            for cc in range(nch):
                                c = s_i * nch + cc
                                E.op("dve", lambda h: h.tensor_copy(out=vw[:, 0:1], in_=WTK[:, c, r:r + 1]), reads=[WTK_b], writes=[vw_b])
                                E.op("pe", lambda h: h.matmul(PS[0:1, 5 * 512 + 128: 5 * 512 + 256], lhsT=vw[:, 0:1], rhs=KK[:, c, :],
                                                              start=(cc == 0), stop=(cc == nch - 1)),
                                     reads=[vw_b, KK_b], writes=[PSB[5]], pe_accum=True)
                            E.op("act", lambda h: h.activation(out=nst[0:1, slot, :], in_=PS[0:1, 5 * 512 + 128: 5 * 512 + 256], func=AF.Copy),
                                 reads=[PSB[5]], writes=[nst_bs[slot]])
                            E.dma("sp", nst_d[0:1, so:so + 128], nst[0:1, slot, :], reads=[nst_bs[slot]])
                for blk in range(nblk):
                    ts = slice(blk * 512, (blk + 1) * 512)
                    sumsq_rstd([HSUM[:, ts]], [HSUM_b], 512, 128, rstd, rstd_b, sq, sq_b, 6)
                    E.op("dve", lambda h: h.scalar_tensor_tensor(out=U[:, ts], in0=HSUM[:, ts], scalar=mhn[:, hd:hd + 1], in1=rstd,
                                                                 op0=ALU.mult, op1=ALU.mult), reads=[HSUM_b, mhn_b, rstd_b], writes=[U_b])
                    E.op("pool", lambda h: h.tensor_tensor(out=MLM[:, hd, ts], in0=U[:, ts], in1=SG[:, ts], op=ALU.mult),
                         reads=[U_b, SG_b], writes=[MLM_bs[hd]])
            A.release(mk_2)
            if stage == 9:
                dbg = P.dout("dbg_" + gname, [128, 8 * T], BF16)
                E.dma("sp", dbg.rearrange("p (j t) -> p j t", j=8), MLM, reads=MLM_bs)
                A.release(mk_l)
                return
            mk_top1 = A.mark_top()
            HF1, HF1_bs = A.alloc(gname + "_hf1", [128, 8, T], BF16, nbufs=nblk, top=True)
            mk_o = A.mark()
            wo, wo_b = A.alloc("wo1", [128, 8, 1024], BF16)
            E.dma("sp", wo, WB["ml_w_out"][0].rearrange("p (kc n) -> p kc n", kc=8), reads=[WB["ml_w_out"][1]], writes=[wo_b])
            O, O_b = A.alloc("O", [128, 8, 512], F32)
            sq, sq_b = A.alloc("sq", [128, 8, 512], BF16)
            rstd, rstd_b = A.alloc("rstd", [128, 512], F32)
            tmp, tmp_b = A.alloc("tmp", [128, 512], F32)
            xb, xb_b = A.alloc("xb", [128, 8, 512], F32)
            wk = (sq, sq_b, rstd, rstd_b, tmp, tmp_b, xb, xb_b)
            for blk in range(nblk):
                ts = slice(blk * 512, (blk + 1) * 512)
                for m in range(8):
                    pbk = m % 2
                    mm_group(bank(pbk), [PSB[pbk]], [(wo[:, kc, m * 128:(m + 1) * 128], MLM[:, kc, ts], [wo_b, MLM_bs[kc]])
                                                     for kc in range(8)])
                    E.op("act", lambda h: h.activation(out=O[:, m, :], in_=bank(pbk), func=AF.Copy), reads=[PSB[pbk]], writes=[O_b])
                epilogue(O, O_b, 512, X2d, X2d_b, X3d, X3d_b, blk * 512, T, 1, 0, v,
                         (1, 1, lambda j: HF1[:, j, ts], HF1_bs[blk]), wk)
            A.release(mk_l)
            ffn(G, 1, HF1, HF1_bs, X3d, X3d_b, G["y"], Buf("y"), None, NB=(512 if smp else 1024))
            A.release_top(mk_top1)

        def ffn(G, l, HF, HF_bs, xsrc, xsrc_b, xdst, xdst_b, nxt, NB=1024):
            gname, nseq, L, v = G["name"], G["nseq"], G["L"], G["v"]
            T = nseq * L
            mk_f = A.mark()
            Gb, Gb_b = A.alloc("G", [128, 22, NB], BF16)
            wu, wu_bs = A.alloc("wup", [128, 2, 8, 2, 256], BF16, nbufs=2)
            wd, wd_bs = A.alloc("wdn", [128, 2, 22, 256], BF16, nbufs=2)
            cu, cu_bs = A.alloc("cu", [128, 2, NB], F32, nbufs=2)
            O, O_b = A.alloc("O", [128, 8, 512], F32)
            sq, sq_b = A.alloc("sq", [128, 8, 512], BF16)
            rstd, rstd_b = A.alloc("rstd", [128, 512], F32)
            tmp, tmp_b = A.alloc("tmp", [128, 512], F32)
            xb, xb_b = A.alloc("xb", [128, 8, 512], F32)
            wk = (sq, sq_b, rstd, rstd_b, tmp, tmp_b, xb, xb_b)
            upv = WB["ffn_up%d" % l][0].rearrange("p (kc n) -> p kc n", kc=8)
            dnv = WB["ffn_dn%d" % l][0].rearrange("p (kc n) -> p kc n", kc=22)
            upb, dnb = WB["ffn_up%d" % l][1], WB["ffn_dn%d" % l][1]
            for a in range(0, T, NB):
                n = NB
                seq_start = (a % L == 0)
                seq_end = ((a + n) % L == 0)
                aligned = (L <= n)
                lo = 1 if (seq_start or aligned) else 0
                hi = n + 1 if (seq_end or aligned) else n + 2
                pieces = []
                c0 = lo
                while c0 < hi:
                    c1 = min(hi, (c0 // 512 + 1) * 512)
                    pieces.append((c0, c1))
                    c0 = c1
                for jp in range(11):
                    slot = jp % 2
                    for half in range(2):
                        E.dma("sp", wu[:, slot, :, half, :], upv[:, :, half * DFF + jp * 256: half * DFF + (jp + 1) * 256],
                              reads=[upb], writes=[wu_bs[slot]])
                    for jj in range(2):
                        j = jp * 2 + jj
                        for half in range(2):
                            hb = 3 * half
                            hp = PS[:, hb * 512: hb * 512 + 1536]
                            hbufs = [PSB[hb], PSB[hb + 1], PSB[hb + 2]]
                            for (c0, c1) in pieces:
                                bk = hb + c0 // 512
                                mm_group(hp[:, c0:c1], [PSB[bk]],
                                         [(wu[:, slot, kc, half, jj * 128:(jj + 1) * 128], HF[:, kc, a - 1 + c0: a - 1 + c1],
                                           [wu_bs[slot]] + HF_bs) for kc in range(8)])
                            ch = half * 22 + j
                            cw = fcw[:, l, ch, :]
                            E.op("act", lambda h: h.activation(out=cu[:, half, :n], in_=hp[:, 1:n + 1], func=AF.Identity,
                                                               bias=cw[:, 3:4], scale=cw[:, 1:2]),
                                 reads=hbufs + [fcw_b], writes=[cu_bs[half]])
                            if aligned:
                                c3 = cu[:, half, :n].rearrange("p (s t) -> p s t", t=L)
                                h3 = hp[:, 1:n + 1].rearrange("p (s t) -> p s t", t=L)
                                E.op("dve", lambda h: h.scalar_tensor_tensor(
                                    out=c3[:, :, 1:L], in0=h3[:, :, 0:L - 1], scalar=cw[:, 0:1], in1=c3[:, :, 1:L],
                                    op0=ALU.mult, op1=ALU.add), reads=hbufs + [fcw_b, cu_bs[half]], writes=[cu_bs[half]])
                                E.op("dve", lambda h: h.scalar_tensor_tensor(
                                    out=c3[:, :, 0:L - 1], in0=h3[:, :, 1:L], scalar=cw[:, 2:3], in1=c3[:, :, 0:L - 1],
                                    op0=ALU.mult, op1=ALU.add), reads=hbufs + [fcw_b, cu_bs[half]], writes=[cu_bs[half]])
                            else:
                                i0 = 1 if seq_start else 0
                                i1 = n - 1 if seq_end else n
                                E.op("dve", lambda h: h.scalar_tensor_tensor(
                                    out=cu[:, half, i0:n], in0=hp[:, i0:n], scalar=cw[:, 0:1], in1=cu[:, half, i0:n],
                                    op0=ALU.mult, op1=ALU.add), reads=hbufs + [fcw_b, cu_bs[half]], writes=[cu_bs[half]])
                                E.op("dve", lambda h: h.scalar_tensor_tensor(
                                    out=cu[:, half, 0:i1], in0=hp[:, 2:i1 + 2], scalar=cw[:, 2:3], in1=cu[:, half, 0:i1],
                                    op0=ALU.mult, op1=ALU.add), reads=hbufs + [fcw_b, cu_bs[half]], writes=[cu_bs[half]])
                        E.op("act", lambda h: h.activation(out=cu[:, 0, :n], in_=cu[:, 0, :n], func=AF.Gelu),
                             reads=[cu_bs[0]], writes=[cu_bs[0]])
                        E.op("pool", lambda h: h.tensor_tensor(out=Gb[:, j, :n], in0=cu[:, 0, :n], in1=cu[:, 1, :n], op=ALU.mult),
                             reads=[cu_bs[0], cu_bs[1]], writes=[Gb_b])
                for piece in range(n // 512):
                    tsl = slice(piece * 512, (piece + 1) * 512)
                    for mp in range(4):
                        slot = mp % 2
                        E.dma("sp", wd[:, slot], dnv[:, :, mp * 256:(mp + 1) * 256], reads=[dnb], writes=[wd_bs[slot]])
                        for mm_ in range(2):
                            m = mp * 2 + mm_
                            pbk = m % 2
                            mm_group(bank(pbk), [PSB[pbk]], [(wd[:, slot, j, mm_ * 128:(mm_ + 1) * 128], Gb[:, j, tsl],
                                                              [wd_bs[slot], Gb_b]) for j in range(22)])
                            E.op("act", lambda h: h.activation(out=O[:, m, :], in_=bank(pbk), func=AF.Copy),
                                 reads=[PSB[pbk]], writes=[O_b])
                    epilogue(O, O_b, 512, xsrc, xsrc_b, xdst, xdst_b, a + piece * 512, T, l, 1, v,
                             None if nxt is None else nxt(a + piece * 512), wk)
            A.release(mk_f)

        GP = dict(name="p", nseq=NPSEQ, L=SEQ, v=0, sample=False, xin=xp_d, kout=kout_d, vout=vout_d, y=yp_d, states=True)
        gstage = [stage]
        if stage >= 20:
            stage = 99
        run_group(GP)
        stage = gstage[0] - 20 if 20 <= gstage[0] < 40 else stage
        if gstage[0] >= 20:
            GS = dict(name="s", nseq=1, L=DEC_SEQ, v=1, sample=True, xin=xs_d, kout=None, vout=None, y=ys_d, states=False)
            run_group(GS)
        E.finish()
    return P


def _fm(x2d):
    T, F = x2d.shape
    return np.ascontiguousarray(x2d.reshape(T, F // 128, 128).transpose(2, 1, 0).reshape(128, -1))


def _unfm(a, T):
    return np.ascontiguousarray(a.reshape(128, 8, T).transpose(2, 1, 0).reshape(T, 1024))


def _wl(w):
    K, N = w.shape
    return np.ascontiguousarray(w.reshape(K // 128, 128, N).transpose(1, 0, 2).reshape(128, -1))


_PROG_CACHE = {}
_CONST_CACHE = {}
DBG = None


def _consts():
    if _CONST_CACHE:
        return _CONST_CACHE
    import ml_dtypes
    c = {}
    c["ident_bf"] = np.eye(128, dtype=np.float32).astype(ml_dtypes.bfloat16)
    c["ones_bf"] = np.ones((128, 128), dtype=np.float32).astype(ml_dtypes.bfloat16)
    sw = np.zeros((128, 128), np.float32)
    for m in range(128):
        sw[(m + 64) % 128, m] = 1.0
    c["swap_bf"] = sw.astype(ml_dtypes.bfloat16)
    for L in (SEQ, DEC_SEQ):
        fwd, inv = dft_tables(L)
        nsc, nfc = L // 128, (2 * L + 128) // 128
        f3 = fwd.reshape(128, nsc, nfc, 128).transpose(2, 0, 1, 3).reshape(nfc, 128, nsc * 128)
        c["dft_fwd_%d" % L] = np.ascontiguousarray(f3)
        nt = min(L, 512)
        ntb = max(1, L // 512)
        i3 = inv.reshape(128, nfc, ntb, nt).transpose(2, 0, 1, 3).reshape(ntb, 128, nfc * nt)
        c["dft_inv_%d" % L] = np.ascontiguousarray(i3)
        zT, tcol = hyena_pos_tables(L)
        c["hy_zT_%d" % L] = zT
        c["hy_tcol_%d" % L] = tcol
    selm = np.zeros((40, 16, 128), np.float32)
    for ridx in range(16):
        selm[(ridx // 8) * 32 + ridx % 8, ridx, :] = 1.0
    c["sel40"] = selm.reshape(40, -1)
    sp = np.arange(128)[:, None]
    tp = np.arange(512)[None, :]
    mk = np.zeros((128, 2, 4, 512), np.float32)
    for o in range(4):
        mk[:, 0, o, :] = (sp + 128 * o <= tp)
        mk[:, 1, o, :] = (sp + 128 * o >= tp)
    c["masks"] = mk.reshape(128, -1).astype(ml_dtypes.bfloat16)
    c["ident_f32"] = np.eye(128, dtype=np.float32)
    cosT, sinT = rope_tables(DEC_SEQ)
    c["rope_cos"] = cosT
    c["rope_sin"] = sinT
    c["hy_delta"] = np.ascontiguousarray(np.broadcast_to(np.abs(np.linspace(HY_MIN_DECAY, HY_MAX_DECAY, 512, dtype=np.float32)).reshape(1, 512), (128, 512)))
    _CONST_CACHE.update(c)
    return _CONST_CACHE


def kernel(**inp):
    global DBG
    inp = {k: np.asarray(v) for k, v in inp.items()}
    stage = int(os.environ.get("KSTAGE", "99"))
    if stage not in _PROG_CACHE:
        _PROG_CACHE[stage] = build_program(stage)
    P = _PROG_CACHE[stage]
    TP = NPSEQ * SEQ
    B = inp["x_prompt"].shape[0]
    f32 = np.float32

    sh = dict(_consts())
    sh["w_mod"] = np.stack([_wl(inp["w_mod"][l]) for l in range(2)], axis=0)
    sh["b_mod"] = np.ascontiguousarray(inp["b_mod"].reshape(2, 48, 128).transpose(2, 0, 1).reshape(128, -1))
    norms = np.stack([inp["norm_mix_pre"], inp["norm_mix_post"], inp["norm_ffn_pre"], inp["norm_ffn_post"]], 0)
    sh["norms"] = np.ascontiguousarray(norms.reshape(4, 2, 8, 128).transpose(3, 0, 1, 2).reshape(128, -1))
    sh["ah_w_in"] = _wl(inp["ah_w_in"][0])
    sh["ah_w_out"] = _wl(inp["ah_w_out"][0])
    sh["qk_norm"] = np.ascontiguousarray(np.stack([inp["attn_q_norm"][0], inp["attn_k_norm"][0]], axis=1))
    hc = np.concatenate([inp["hy_conv_w"][0], inp["hy_conv_b"][0][None]], axis=0)
    sh["hy_conv"] = np.ascontiguousarray(hc.reshape(4, 12, 128).transpose(2, 1, 0).reshape(128, -1))
    sh["hy_w1"] = np.ascontiguousarray(inp["hy_w1"][0])
    sh["hy_w2"] = np.ascontiguousarray(inp["hy_w2"][0])
    sh["hy_w3"] = np.ascontiguousarray(inp["hy_w3"][0])
    sh["hy_b12f"] = np.ascontiguousarray(np.stack([inp["hy_b1"][0], inp["hy_b2"][0], inp["hy_sin_freq"][0, 0],
                                                   inp["hy_sin_freq"][0, 1]], axis=1))
    sh["hy_b3"] = np.ascontiguousarray(np.broadcast_to(inp["hy_b3"][0].reshape(1, 1024), (128, 1024)))
    sh["hy_skip"] = np.ascontiguousarray(np.broadcast_to(inp["hy_skip"][0].reshape(1, 512), (128, 512)))
    sh["ffn_w_up"] = np.stack([_wl(inp["ffn_w_up"][l]) for l in range(2)], axis=0)
    sh["ffn_w_down"] = np.stack([_wl(inp["ffn_w_down"][l]) for l in range(2)], axis=0)
    fc = np.concatenate([inp["ffn_conv_w"], inp["ffn_conv_b"][:, None, :]], axis=1)
    sh["ffn_conv"] = np.ascontiguousarray(fc.reshape(2, 4, 44, 128).transpose(3, 0, 2, 1).reshape(128, -1))

    mw = inp["ml_w_in"][0]
    sh["ml_w_in"] = _wl(np.ascontiguousarray(mw[:, :4096]))
    wgp = np.zeros((1024, 80), f32)
    bgp = np.zeros((40, 2), f32)
    bg = inp["ml_b_gates"][0]
    for kind in range(4):
        base = (kind // 2) * 40 + (kind % 2) * 32
        wgp[:, base:base + 8] = mw[:, 4096 + kind * 8: 4096 + kind * 8 + 8]
        bgp[(kind % 2) * 32:(kind % 2) * 32 + 8, kind // 2] = bg[kind * 8:kind * 8 + 8]
    sh["ml_w_g"] = _wl(wgp)
    sh["ml_b_g"] = bgp
    mc = np.concatenate([inp["ml_conv_w"][0], inp["ml_conv_b"][0][None]], axis=0)
    sh["ml_conv"] = np.ascontiguousarray(mc.reshape(4, 16, 128).transpose(2, 1, 0).reshape(128, -1))
    sh["ml_head_norm"] = np.ascontiguousarray(inp["ml_head_norm"][0].reshape(8, 128).T)
    sh["ml_w_out"] = _wl(inp["ml_w_out"][0])

    in_maps = []
    for r in range(NCORES):
        b = r // 4
        m = dict(sh)
        m["xp"] = _fm(inp["x_prompt"][NPSEQ * r:NPSEQ * (r + 1)].reshape(TP, D))
        m["xs"] = _fm(inp["x_sample"][b])
        cvec = np.stack([inp["c_ctx"], inp["c"][b]], axis=0)
        m["cvec"] = np.ascontiguousarray(cvec.reshape(2, 8, 128).transpose(2, 1, 0).reshape(128, -1))
        ck = inp["cache_attn_k"][b, 0]
        m["cache_kT"] = np.ascontiguousarray(ck.transpose(2, 1, 0).reshape(128, -1))
        cvv = inp["cache_attn_v"][b, 0]
        m["cache_vt"] = np.ascontiguousarray(cvv.reshape(4, 128, 256).transpose(1, 0, 2).reshape(128, -1))
        m0 = np.zeros((40, 1), f32)
        sm = inp["state_mlstm_m"][b, 0]
        m0[0:8, 0] = sm[0]
        m0[32:40, 0] = sm[1]
        m["ml_m0"] = m0
        sC = inp["state_mlstm_C"][b, 0].reshape(16, 128, 128)
        m["ml_c0t"] = np.ascontiguousarray(sC.transpose(2, 0, 1).reshape(128, -1))
        sn = inp["state_mlstm_n"][b, 0].reshape(16, 128)
        m["ml_n0b"] = np.ascontiguousarray(np.broadcast_to(sn.T[:, :, None], (128, 16, 128)).reshape(128, -1))
        in_maps.append({k: np.ascontiguousarray(m[k]) for k in P.ins})
    res = run_bass_kernel_spmd(P.nc, in_maps, core_ids=list(range(NCORES)))
    R = res.results
    DBG = R

    y_prompt = np.zeros((B, SEQ, D), f32)
    y_sample = np.zeros((2, DEC_SEQ, D), f32)
    new_k = np.zeros((B, 1, SEQ, 2, 128), f32)
    new_v = np.zeros((B, 1, SEQ, 2, 128), f32)
    new_C = np.zeros((B, 1, 2, 8, 128, 128), f32)
    new_n = np.zeros((B, 1, 2, 8, 128), f32)
    new_m = np.zeros((B, 1, 2, 8), f32)
    for r in range(NCORES):
        o = R[r]
        sl = slice(NPSEQ * r, NPSEQ * (r + 1))
        new_k[sl, 0] = o["k_out"].reshape(128, 2, NPSEQ, SEQ).transpose(2, 3, 1, 0)
        new_v[sl, 0] = o["v_out"].reshape(128, TP // 128, 2, 128).transpose(1, 0, 2, 3).reshape(NPSEQ, SEQ, 2, 128)
        y_prompt[sl] = _unfm(o["yp"], TP).reshape(NPSEQ, SEQ, D)
        new_C[sl, 0] = o["c_out"].reshape(128, NPSEQ, 2, 8, 128).transpose(1, 2, 3, 0, 4)
        new_n[sl, 0] = o["n_out"].reshape(NPSEQ, 2, 8, 128)
        mo = o["m_out"]
        new_m[sl, 0, 0] = mo[0:8].T
        new_m[sl, 0, 1] = mo[32:40].T
        if r % 4 == 0:
            y_sample[r // 4] = _unfm(o["ys"], DEC_SEQ)
    return (y_prompt, y_sample, new_k, new_v, new_C, new_n, new_m)
```

```python
import math
import os
import numpy as np
import concourse.bass as bass
import concourse.mybir as mybir
from concourse.bass_utils import run_bass_kernel_spmd

AF = mybir.ActivationFunctionType
ALU = mybir.AluOpType
AX = mybir.AxisListType
F32 = mybir.dt.float32
BF16 = mybir.dt.bfloat16

D = 1024
NCORES = 8
SEQ = 256
NPSEQ = 4
DEC_SEQ = 2048
PAST = 512
DFF = 2816
EPS = 1e-6
HY_MAX_DECAY = math.log(1e-2) / 0.3
HY_MIN_DECAY = math.log(1e-2) / 1.5

SERIAL = bool(int(os.environ.get('KSERIAL', '0')))
SELF_SYNC = True


class Buf:
    __slots__ = ("name", "last_w", "readers", "excl")

    def __init__(self, name, excl=False):
        self.name = name
        self.last_w = None
        self.readers = []
        self.excl = excl


class Emit:
    ENG = ("pe", "act", "dve", "pool", "sp")

    def __init__(self, nc, n_dma_sems=48):
        self.nc = nc
        self.h = {"pe": nc.tensor, "act": nc.scalar, "dve": nc.vector, "pool": nc.gpsimd, "sp": nc.sync}
        self.ops = {e: [] for e in self.ENG}
        self.cnt = {e: 0 for e in self.ENG}
        self.seen = {e: {} for e in self.ENG}
        self.esem = {}
        self.dsem = []
        self.dval = []
        self.n_dma_sems = n_dma_sems
        self.dnext = 0
        self.dnext_sw = 0
        self.n_sw = 12
        self.sem_ctx = []
        self.dma_tokens = []

    def open_sems(self, stack):
        for e in self.ENG:
            self.esem[e] = stack.enter_context(self.nc.semaphore("c_" + e))
        for i in range(self.n_dma_sems):
            self.dsem.append(stack.enter_context(self.nc.semaphore("d%d" % i)))
            self.dval.append(0)

    def _deps(self, reads, writes):
        deps = []
        for b in reads:
            if b.last_w is not None:
                deps.append(b.last_w)
        for b in writes:
            if b.last_w is not None:
                deps.append(b.last_w)
            deps.extend(b.readers)
        return deps

    def _waits(self, eng, deps, pe_accum=False):
        need = {}
        for d in deps:
            kind, src, val = d
            if kind == "e" and src == eng and (not SELF_SYNC or (eng == "pe" and pe_accum)):
                continue
            key = (kind, src)
            if self.seen[eng].get(key, 0) >= val:
                continue
            if need.get(key, 0) < val:
                need[key] = val
        waits = []
        for (kind, src), val in need.items():
            self.seen[eng][(kind, src)] = val
            sem = self.esem[src] if kind == "e" else self.dsem[src]
            waits.append((sem, val))
        return waits

    def _commit(self, tok, reads, writes):
        for b in writes:
            b.last_w = tok
            b.readers = []
        for b in reads:
            if b in writes:
                continue
            if tok[0] == "e":
                b.readers = [r for r in b.readers if not (r[0] == "e" and r[1] == tok[1])]
            b.readers.append(tok)

    def op(self, eng, fn, reads=(), writes=(), pe_accum=False):
        ex = [b for b in reads if b.excl and b not in writes]
        if ex:
            reads = [b for b in reads if not b.excl]
            writes = list(writes) + ex
        deps = self._deps(reads, writes)
        if SERIAL:
            for e in self.ENG:
                if self.cnt[e] > 0 and not (e == eng and pe_accum):
                    deps.append(("e", e, self.cnt[e]))
            for i in range(self.n_dma_sems):
                if self.dval[i] > 0:
                    deps.append(("d", i, self.dval[i]))
        waits = self._waits(eng, deps, pe_accum)
        self.cnt[eng] += 1
        tok = ("e", eng, self.cnt[eng])
        h = self.h[eng]
        for s_, v_ in waits:
            h.wait_ge(s_, v_)
        fn(h).then_inc(self.esem[eng], 1)
        self._commit(tok, reads, writes)
        return tok

    def pe_group(self, fns, reads, writes):
        ex = [b for b in reads if b.excl and b not in writes]
        if ex:
            reads = [b for b in reads if not b.excl]
            writes = list(writes) + ex
        deps = self._deps(reads, writes)
        waits = self._waits("pe", deps, True)
        self.cnt["pe"] += 1
        tok = ("e", "pe", self.cnt["pe"])
        h = self.h["pe"]
        for s_, v_ in waits:
            h.wait_ge(s_, v_)
        last = None
        for fn in fns:
            last = fn(h)
        last.then_inc(self.esem["pe"], 1)
        self._commit(tok, reads, writes)
        return tok

    def dma(self, eng, out, in_, reads=(), writes=(), **kw):
        deps = self._deps(reads, writes)
        if eng == "pool":
            i = self.dnext_sw
            self.dnext_sw = (self.dnext_sw + 1) % self.n_sw
        else:
            i = self.n_sw + self.dnext
            self.dnext = (self.dnext + 1) % (self.n_dma_sems - self.n_sw)
        if self.dval[i] > 0:
            deps.append(("d", i, self.dval[i]))
        waits = self._waits(eng, deps)
        self.dval[i] += 16
        tok = ("d", i, self.dval[i])
        h = self.h[eng]
        for s_, v_ in waits:
            h.wait_ge(s_, v_)
        h.dma_start(out=out, in_=in_, **kw).then_inc(self.dsem[i], 16)
        self._commit(tok, reads, writes)
        return tok

    def finish(self):
        h = self.h["sp"]
        for i in range(self.n_dma_sems):
            if self.dval[i] > 0:
                h.wait_ge(self.dsem[i], self.dval[i])
        for e in self.ENG:
            if e != "sp" and self.cnt[e] > 0:
                h.wait_ge(self.esem[e], self.cnt[e])


class Arena:
    def __init__(self, nc, lo, hi):
        self.nc, self.lo, self.hi, self.p = nc, lo, hi, lo
        self.n = 0
        self.live = []

    def alloc(self, name, shape, dtype, nbufs=1, top=False):
        esz = 4 if dtype == F32 else 2
        per = 1
        for s in shape[1:]:
            per *= s
        nbytes = (per * esz + 31) // 32 * 32
        assert self.p + nbytes <= self.hi, "SBUF arena overflow at %s (%d + %d > %d)" % (name, self.p, nbytes, self.hi)
        if top:
            self.hi -= nbytes
            off = self.hi
        else:
            off = self.p
            self.p += nbytes
        self.n += 1
        t = self.nc.alloc_sbuf_tensor_at("%s_%d" % (name, self.n), list(shape), dtype, offset=off)
        bufs = [Buf("%s.%d" % (name, i)) for i in range(nbufs)]
        keep = []
        for (l, h, bs) in self.live:
            if l < off + nbytes and off < h:
                for ob in bs:
                    for nb in bufs:
                        if ob.last_w is not None:
                            nb.readers.append(ob.last_w)
                        nb.readers.extend(ob.readers)
                if l < off or h > off + nbytes:
                    keep.append((l, h, bs))
            else:
                keep.append((l, h, bs))
        keep.append((off, off + nbytes, bufs))
        self.live = keep
        ap = t.ap()
        return (ap, bufs[0]) if nbufs == 1 else (ap, bufs)

    def mark_top(self):
        return self.hi

    def release_top(self, m):
        self.hi = m

    def mark(self):
        return self.p

    def release(self, m):
        self.p = m


def _bf16(a):
    import ml_dtypes
    return np.ascontiguousarray(a.astype(np.float32)).astype(ml_dtypes.bfloat16)


def dft_tables(L):
    n = 2 * L
    s = np.arange(L, dtype=np.float64)
    fr = np.arange(L + 128, dtype=np.float64)
    fi = np.arange(L, dtype=np.float64)
    FR = np.cos(2 * np.pi * np.outer(s, fr) / n)
    FR[:, L + 1:] = 0.0
    FI = -np.sin(2 * np.pi * np.outer(s, fi) / n)
    cf = np.full(L + 128, 2.0)
    cf[0] = 1.0
    cf[L] = 1.0
    cf[L + 1:] = 0.0
    IR = (cf[:, None] / n) * np.cos(2 * np.pi * np.outer(fr, s) / n)
    II = -(2.0 / n) * np.sin(2 * np.pi * np.outer(fi, s) / n)
    fwd = np.concatenate([FR, FI], axis=1)
    inv = np.concatenate([IR, II], axis=0)
    nsc = L // 128
    nfc = (2 * L + 128) // 128
    fwd_l = fwd.reshape(nsc, 128, nfc * 128).transpose(1, 0, 2)
    inv_l = inv.reshape(nfc, 128, L).transpose(1, 0, 2)
    return _bf16(fwd_l.reshape(128, -1)), _bf16(inv_l.reshape(128, -1))


def hyena_pos_tables(L):
    t = np.linspace(0.0, 1.0, L, dtype=np.float32)
    bands = np.arange(1, 9, dtype=np.float32)
    ang = 2.0 * np.pi * t[:, None] * bands
    z = np.concatenate([t[:, None], np.cos(ang), np.sin(ang)], axis=-1).astype(np.float32)
    zT = np.ascontiguousarray(z.T)
    tcol = np.ascontiguousarray(t.reshape(L // 128, 128).T)
    return zT, tcol


def rope_tables(L):
    rows = L // 64
    row = np.repeat(np.arange(rows, dtype=np.float32), 64)
    col = np.tile(np.arange(64, dtype=np.float32), rows)
    n_freq = 32
    inv = (10000.0 ** (-np.arange(n_freq, dtype=np.float32) / n_freq)).astype(np.float32)
    ang = np.concatenate([row[:, None] * inv, col[:, None] * inv], axis=-1)
    cos, sin = np.cos(ang).astype(np.float32), np.sin(ang).astype(np.float32)
    cosT = np.concatenate([cos.T, cos.T], axis=0)
    sinT = np.concatenate([-sin.T, sin.T], axis=0)
    return np.ascontiguousarray(cosT), np.ascontiguousarray(sinT)


class Prog:
    def __init__(self, stage):
        self.stage = stage
        self.nc = bass.Bass("TRN2", target_bir_lowering=False)
        self.ins = {}
        self.outs = {}

    def din(self, name, shape, dtype=F32):
        t = self.nc.dram_tensor(name, list(shape), dtype, kind="ExternalInput")
        self.ins[name] = t
        return t.ap()

    def dout(self, name, shape, dtype=F32):
        t = self.nc.dram_tensor(name, list(shape), dtype, kind="ExternalOutput")
        self.outs[name] = t
        return t.ap()


TWO_PI = 2.0 * math.pi
MAGIC = 12582912.0


def build_program(stage=99):
    P = Prog(stage)
    nc = P.nc
    TP = NPSEQ * SEQ
    TS = DEC_SEQ

    xp_d = P.din("xp", [128, 8 * TP])
    xs_d = P.din("xs", [128, 8 * TS])
    cvec_d = P.din("cvec", [128, 8 * 2])
    wmod_d = P.din("w_mod", [2, 128, 8 * 6144])
    bmod_d = P.din("b_mod", [128, 2 * 48])
    norms_d = P.din("norms", [128, 4 * 2 * 8])
    ahwin_d = P.din("ah_w_in", [128, 8 * 2560])
    ahwout_d = P.din("ah_w_out", [128, 8 * 1024])
    qkn_d = P.din("qk_norm", [128, 2])
    idb_d = P.din("ident_bf", [128, 128], BF16)
    ones_d = P.din("ones_bf", [128, 128], BF16)
    swap_d = P.din("swap_bf", [128, 128], BF16)
    hcw_d = P.din("hy_conv", [128, 12 * 4])
    hyw1_d = P.din("hy_w1", [17, 64])
    hyw2_d = P.din("hy_w2", [64, 64])
    hyw3_d = P.din("hy_w3", [64, 1024])
    hyb12_d = P.din("hy_b12f", [64, 4])
    hyb3_d = P.din("hy_b3", [128, 1024])
    hyskip_d = P.din("hy_skip", [128, 512])
    hydelta_d = P.din("hy_delta", [128, 512])
    ffnup_d = P.din("ffn_w_up", [2, 128, 8 * 5632])
    ffndn_d = P.din("ffn_w_down", [2, 128, 22 * 1024])
    ffncw_d = P.din("ffn_conv", [128, 2 * 44 * 4])
    tabs = {}
    for L in (SEQ, DEC_SEQ):
        nsc, nfc = L // 128, (2 * L + 128) // 128
        tabs[L] = dict(
            fwd=P.din("dft_fwd_%d" % L, [nfc, 128, nsc * 128], BF16),
            inv=P.din("dft_inv_%d" % L, [max(1, L // 512), 128, nfc * min(L, 512)], BF16),
            zT=P.din("hy_zT_%d" % L, [17, L]),
            tcol=P.din("hy_tcol_%d" % L, [128, L // 128]))
    mlwin_d = P.din("ml_w_in", [128, 8 * 4096])
    mlwg_d = P.din("ml_w_g", [128, 8 * 80])
    mlbg_d = P.din("ml_b_g", [40, 2])
    mlcw_d = P.din("ml_conv", [128, 16 * 4])
    mlhn_d = P.din("ml_head_norm", [128, 8])
    mlwout_d = P.din("ml_w_out", [128, 8 * 1024])
    sel_d = P.din("sel40", [40, 16 * 128])
    masks_d = P.din("masks", [128, 2 * 4 * 512], BF16)
    id32_d = P.din("ident_f32", [128, 128])
    m0_d = P.din("ml_m0", [40, 1])
    c0t_d = P.din("ml_c0t", [128, 16 * 128])
    n0b_d = P.din("ml_n0b", [128, 16 * 128])
    ropec_d = P.din("rope_cos", [128, TS])
    ropes_d = P.din("rope_sin", [128, TS])
    ckT_d = P.din("cache_kT", [128, 2 * PAST])
    cvt_d = P.din("cache_vt", [128, (PAST // 128) * 256])

    kout_d = P.dout("k_out", [128, 2 * TP])
    vout_d = P.dout("v_out", [128, (TP // 128) * 256])
    cst_d = P.dout("c_out", [128, NPSEQ * 16 * 128])
    nst_d = P.dout("n_out", [1, NPSEQ * 16 * 128])
    mst_d = P.dout("m_out", [40, NPSEQ])
    yp_d = P.dout("yp", [128, 8 * TP])
    ys_d = P.dout("ys", [128, 8 * TS])

    from contextlib import ExitStack
    with ExitStack() as stack:
        E = Emit(nc)
        E.open_sems(stack)
        A = Arena(nc, 16640, 229376 - 1024)
        ps_t = nc.alloc_psum_tensor("ps_all", [128, 7 * 512], F32)
        PS = ps_t.ap()
        PSB = [Buf("ps%d" % i, excl=True) for i in range(7)]
        psbf_t = nc.alloc_psum_tensor("ps_bf", [128, 1024], BF16)
        PSbf = {4: psbf_t.ap()[:, 0:128], 5: psbf_t.ap()[:, 128:256]}
        _pb = Buf("psbf", excl=True)
        PSbf_b = {4: _pb, 5: _pb}

        def bank(i, n=512, off=0):
            return PS[:, i * 512 + off: i * 512 + off + n]

        def mm_group(out_ap, out_bufs, terms):
            n = len(terms)
            rset = []
            for (_, _, rb) in terms:
                for x_ in rb:
                    if x_ not in rset:
                        rset.append(x_)
            fns = [(lambda h, i=i, l_ap=l_ap, r_ap=r_ap: h.matmul(out_ap, lhsT=l_ap, rhs=r_ap, start=(i == 0), stop=(i == n - 1)))
                   for i, (l_ap, r_ap, rb) in enumerate(terms)]
            E.pe_group(fns, rset, out_bufs)

        dram_bufs = {}

        def dscratch(name, shape, dtype=F32):
            t = nc.dram_tensor(name, list(shape), dtype)
            b = Buf(name)
            dram_bufs[name] = b
            return t.ap(), b

        WB = {}

        def conv_weight(name, src_ap, shape):
            t_ap, t_b = dscratch(name + "_bf", shape, BF16)
            E.dma("pool", t_ap, src_ap, writes=[t_b])
            WB[name] = (t_ap, t_b)

        conv_weight("ah_w_in", ahwin_d, [128, 8 * 2560])
        conv_weight("ah_w_out", ahwout_d, [128, 8 * 1024])
        conv_weight("ffn_up0", ffnup_d[0], [128, 8 * 5632])
        conv_weight("ffn_dn0", ffndn_d[0], [128, 22 * 1024])
        conv_weight("ml_w_in", mlwin_d, [128, 8 * 4096])
        conv_weight("ml_w_out", mlwout_d, [128, 8 * 1024])
        conv_weight("ffn_up1", ffnup_d[1], [128, 8 * 5632])
        conv_weight("ffn_dn1", ffndn_d[1], [128, 22 * 1024])

        def const(name, shape, dtype, src, eng="sp"):
            ap, b = A.alloc(name, shape, dtype)
            E.dma(eng, ap, src, writes=[b])
            return ap, b

        ident, ident_b = const("ident", [128, 128], BF16, idb_d)
        ones, ones_b = const("ones", [128, 128], BF16, ones_d)
        swp, swp_b = const("swap", [128, 128], BF16, swap_d)
        epsc, epsc_b = A.alloc("epsc", [128, 2], F32)
        E.op("dve", lambda h: h.memset(epsc, EPS), writes=[epsc_b])
        cv, cv_b = const("cvec", [128, 8, 2], F32, cvec_d.rearrange("p (j v) -> p j v", v=2))
        bm, bm_b = const("bmod", [128, 2, 48], F32, bmod_d.rearrange("p (l m) -> p l m", l=2))
        nrm, nrm_b = const("norms", [128, 4, 2, 8], F32, norms_d.rearrange("p (w l j) -> p w l j", w=4, l=2))
        qkn, qkn_b = const("qkn", [128, 2], F32, qkn_d)
        hcw, hcw_b = const("hcw", [128, 12, 4], F32, hcw_d.rearrange("p (c k) -> p c k", k=4))
        fcw, fcw_b = const("fcw", [128, 2, 44, 4], F32, ffncw_d.rearrange("p (l c k) -> p l c k", l=2, k=4))

        mcw, mcw_b = const("mcw", [128, 16, 4], F32, mlcw_d.rearrange("p (c k) -> p c k", k=4))
        mhn, mhn_b = const("mhn", [128, 8], F32, mlhn_d)
        mbg, mbg_b = const("mbg", [40, 2], F32, mlbg_d)
        sel, sel_b = const("sel", [40, 16, 128], F32, sel_d.rearrange("p (r m) -> p r m", m=128))
        msk, msk_b = const("msk", [128, 2, 4, 512], BF16, masks_d.rearrange("p (d o t) -> p d o t", d=2, o=4))
        id32, id32_b = const("id32", [128, 128], F32, id32_d)
        onec, onec_b = A.alloc("onec", [128, 2], F32)
        E.op("dve", lambda h: h.memset(onec, 1.0), writes=[onec_b])
        sc, sc_b = A.alloc("silu_c", [128, 8, 2], BF16)
        sig, sig_b = A.alloc("sig_c", [128, 8, 2], F32)
        E.op("act", lambda h: h.activation(out=sig, in_=cv, func=AF.Sigmoid), reads=[cv_b], writes=[sig_b])
        E.op("dve", lambda h: h.tensor_tensor(out=sc, in0=cv, in1=sig, op=ALU.mult), reads=[cv_b, sig_b], writes=[sc_b])
        MOD, MOD_b = A.alloc("mod", [128, 2, 48, 2], F32)
        mk = A.mark()
        wm, wm_bs = A.alloc("wmod_st", [128, 2, 8, 512], BF16, nbufs=2)
        it = 0
        for l in range(2):
            for cg in range(12):
                slot = it % 2
                it += 1
                src = wmod_d[l].rearrange("p (kc n) -> p kc n", kc=8)[:, :, cg * 512:(cg + 1) * 512]
                E.dma("pool", wm[:, slot], src, writes=[wm_bs[slot]])
                for mm in range(4):
                    m = cg * 4 + mm
                    mm_group(bank(l, 2, 2 * m), [PSB[l]],
                             [(wm[:, slot, kc, mm * 128:(mm + 1) * 128], sc[:, kc, :], [wm_bs[slot], sc_b])
                              for kc in range(8)])
            E.op("dve", lambda h: h.tensor_tensor(
                out=MOD[:, l], in0=bank(l, 96).rearrange("p (m v) -> p m v", v=2),
                in1=bm[:, l, :].unsqueeze(2).to_broadcast([128, 48, 2]), op=ALU.add),
                reads=[PSB[l], bm_b], writes=[MOD_b])
        A.release(mk)

        COEF, COEF_b = A.alloc("coef", [128, 2, 2, 3, 8, 2], F32)
        for l in range(2):
            for part in range(2):
                sh, scl, gt = 3 * part, 3 * part + 1, 3 * part + 2
                gpre = nrm[:, 2 * part, l, :].unsqueeze(2).to_broadcast([128, 8, 2])
                gpost = nrm[:, 2 * part + 1, l, :].unsqueeze(2).to_broadcast([128, 8, 2])
                E.op("dve", lambda h: h.scalar_tensor_tensor(
                    out=COEF[:, l, part, 0], in0=MOD[:, l, scl * 8:(scl + 1) * 8, :], scalar=1.0, in1=gpre,
                    op0=ALU.add, op1=ALU.mult), reads=[MOD_b, nrm_b], writes=[COEF_b])
                E.op("dve", lambda h: h.tensor_copy(
                    out=COEF[:, l, part, 1], in_=MOD[:, l, sh * 8:(sh + 1) * 8, :]), reads=[MOD_b], writes=[COEF_b])
                E.op("dve", lambda h: h.tensor_tensor(
                    out=COEF[:, l, part, 2], in0=MOD[:, l, gt * 8:(gt + 1) * 8, :], in1=gpost, op=ALU.mult),
                    reads=[MOD_b, nrm_b], writes=[COEF_b])

        def sumsq_rstd(src_chunks, src_bufs, n, dim, rstd, rstd_b, sq, sq_b, psb):
            nch = len(src_chunks)
            for j, s_ in enumerate(src_chunks):
                E.op("act", lambda h: h.activation(out=sq[:, j, :n], in_=s_, func=AF.Square),
                     reads=[src_bufs[j]], writes=[sq_b])
            mm_group(bank(psb, n), [PSB[psb]], [(ones, sq[:, j, :n], [ones_b, sq_b]) for j in range(nch)])
            E.op("act", lambda h: h.activation(out=rstd[:, :n], in_=bank(psb, n), func=AF.Ln, bias=epsc[:, 0:1],
                                               scale=1.0 / dim), reads=[PSB[psb], epsc_b], writes=[rstd_b])
            E.op("act", lambda h: h.activation(out=rstd[:, :n], in_=rstd[:, :n], func=AF.Exp, scale=-0.5),
                 reads=[rstd_b], writes=[rstd_b])

        def modulate_block(xb, xb_buf, n, l, part, v, dst_fn, dst_buf, sq, sq_b, rstd, rstd_b, tmp, tmp_b):
            sumsq_rstd([xb[:, j, :n] for j in range(8)], [xb_buf] * 8, n, D, rstd, rstd_b, sq, sq_b, 6)
            for j in range(8):
                E.op("dve", lambda h: h.scalar_tensor_tensor(
                    out=tmp[:, :n], in0=xb[:, j, :n], scalar=COEF[:, l, part, 0, j, v:v + 1], in1=rstd[:, :n],
                    op0=ALU.mult, op1=ALU.mult), reads=[xb_buf, COEF_b, rstd_b], writes=[tmp_b])
                E.op("act", lambda h: h.activation(
                    out=dst_fn(j), in_=tmp[:, :n], func=AF.Identity, bias=COEF[:, l, part, 1, j, v:v + 1], scale=1.0),
                    reads=[tmp_b, COEF_b], writes=[dst_buf])

        def epilogue(O, O_b, n, xsrc, xsrc_b, xdst, xdst_b, tok0, T, l, part, v, nxt, wk):
            (sq, sq_b, rstd, rstd_b, tmp, tmp_b, xb, xb_b) = wk
            E.dma("sp", xb[:, :, :n], xsrc.rearrange("p (j t) -> p j t", j=8)[:, :, tok0:tok0 + n],
                  reads=[xsrc_b], writes=[xb_b])
            sumsq_rstd([O[:, j, :n] for j in range(8)], [O_b] * 8, n, D, rstd, rstd_b, sq, sq_b, 6)
            for j in range(8):
                E.op("dve", lambda h: h.scalar_tensor_tensor(
                    out=tmp[:, :n], in0=O[:, j, :n], scalar=COEF[:, l, part, 2, j, v:v + 1], in1=rstd[:, :n],
                    op0=ALU.mult, op1=ALU.mult), reads=[O_b, COEF_b, rstd_b], writes=[tmp_b])
                E.op("pool", lambda h: h.tensor_tensor(out=xb[:, j, :n], in0=xb[:, j, :n], in1=tmp[:, :n], op=ALU.add),
                     reads=[tmp_b, xb_b], writes=[xb_b])
            E.dma("sp", xdst.rearrange("p (j t) -> p j t", j=8)[:, :, tok0:tok0 + n], xb[:, :, :n],
                  reads=[xb_b], writes=[xdst_b])
            if nxt is not None:
                l2, part2, dst_fn, dst_buf = nxt
                modulate_block(xb, xb_b, n, l2, part2, v, dst_fn, dst_buf, sq, sq_b, rstd, rstd_b, tmp, tmp_b)

        def hyena_filters(L, keep):
            nsc = L // 128
            GR, GR_b = keep.alloc("GR%d" % L, [128, nsc + 1, 512], BF16, top=True)
            GI, GI_b = keep.alloc("GI%d" % L, [128, nsc, 512], BF16, top=True)
            mk_ = A.mark()
            zT, zT_b = const("zT", [17, L], F32, tabs[L]["zT"])
            tcol, tcol_b = const("tcol", [128, nsc], F32, tabs[L]["tcol"])
            w1, w1_b = const("hw1", [17, 64], F32, hyw1_d)
            w2, w2_b = const("hw2", [64, 64], F32, hyw2_d)
            w3, w3_b = const("hw3", [64, 1024], F32, hyw3_d)
            b12, b12_b = const("hb12", [64, 4], F32, hyb12_d)
            b3, b3_b = const("hb3", [128, 1024], F32, hyb3_d)
            skp, skp_b = const("hskip", [128, 512], F32, hyskip_d)
            dlt, dlt_b = const("hdelta", [128, 512], F32, hydelta_d)
            ntc, ntc_b = A.alloc("ntcol", [128, nsc], F32)
            E.op("dve", lambda h: h.tensor_scalar(out=ntc, in0=tcol, scalar1=-1.0, scalar2=None, op0=ALU.mult),
                 reads=[tcol_b], writes=[ntc_b])
            H1, H1_b = A.alloc("h1", [64, L], F32)
            H2, H2_b = A.alloc("h2", [64, L], F32)
            t1, t1_b = A.alloc("ht1", [64, 512], F32)
            t2, t2_b = A.alloc("ht2", [64, 512], F32)

            def sin_layer(dst, dst_b, w_ap, w_b, src, src_b, bcol, fcol):
                for c0 in range(0, L, 512):
                    n = min(512, L - c0)
                    mm_group(PS[0:64, 0:n], [PSB[0]], [(w_ap, src[:, c0:c0 + n], [w_b, src_b])])
                    E.op("dve", lambda h: h.tensor_scalar(out=t1[:, :n], in0=PS[0:64, 0:n], scalar1=b12[:, bcol:bcol + 1],
                                                          scalar2=b12[:, fcol:fcol + 1], op0=ALU.add, op1=ALU.mult),
                         reads=[PSB[0], b12_b], writes=[t1_b])
                    E.op("dve", lambda h: h.tensor_scalar(out=t2[:, :n], in0=t1[:, :n], scalar1=1.0 / TWO_PI, scalar2=MAGIC,
                                                          op0=ALU.mult, op1=ALU.add), reads=[t1_b], writes=[t2_b])
                    E.op("dve", lambda h: h.tensor_scalar(out=t2[:, :n], in0=t2[:, :n], scalar1=MAGIC, scalar2=-TWO_PI,
                                                          op0=ALU.subtract, op1=ALU.mult), reads=[t2_b], writes=[t2_b])
                    E.op("dve", lambda h: h.tensor_tensor(out=t1[:, :n], in0=t1[:, :n], in1=t2[:, :n], op=ALU.add),
                         reads=[t1_b, t2_b], writes=[t1_b])
                    E.op("act", lambda h: h.activation(out=dst[:, c0:c0 + n], in_=t1[:, :n], func=AF.Sin),
                         reads=[t1_b], writes=[dst_b])

            sin_layer(H1, H1_b, w1, w1_b, zT, zT_b, 0, 2)
            sin_layer(H2, H2_b, w2, w2_b, H1, H1_b, 1, 3)
            GS, GS_b = A.alloc("gs", [128, nsc, 512], BF16)
            GD, GD_b = A.alloc("gd", [128, nsc, 512], BF16)
            Fm, Fm_b = A.alloc("fm", [128, 1024], F32)
            win, win_b = A.alloc("win", [128, 512], F32)
            fs, fs_b = A.alloc("fs", [128, 512], F32)
            for tc in range(nsc):
                for hh in range(2):
                    mm_group(bank(hh), [PSB[hh]], [(H2[:, tc * 128:(tc + 1) * 128], w3[:, hh * 512:(hh + 1) * 512],
                                                    [H2_b, w3_b])])
                    E.op("dve", lambda h: h.tensor_tensor(out=Fm[:, hh * 512:(hh + 1) * 512], in0=bank(hh),
                                                          in1=b3[:, hh * 512:(hh + 1) * 512], op=ALU.add),
                         reads=[PSB[hh], b3_b], writes=[Fm_b])
                E.op("act", lambda h: h.activation(out=win, in_=dlt, func=AF.Exp, scale=ntc[:, tc:tc + 1]),
                     reads=[dlt_b, ntc_b], writes=[win_b])
                E.op("dve", lambda h: h.tensor_tensor(out=fs, in0=Fm[:, 0:512], in1=Fm[:, 512:1024], op=ALU.add),
                     reads=[Fm_b], writes=[fs_b])
                E.op("dve", lambda h: h.tensor_tensor(out=GS[:, tc, :], in0=fs, in1=win, op=ALU.mult),
                     reads=[fs_b, win_b], writes=[GS_b])
                E.op("dve", lambda h: h.tensor_tensor(out=fs, in0=Fm[:, 0:512], in1=Fm[:, 512:1024], op=ALU.subtract),
                     reads=[Fm_b], writes=[fs_b])
                E.op("dve", lambda h: h.tensor_tensor(out=GD[:, tc, :], in0=fs, in1=win, op=ALU.mult),
                     reads=[fs_b, win_b], writes=[GD_b])
            fw, fw_bs = A.alloc("fwst", [128, 2, nsc, 128], BF16, nbufs=2)
            nfc = 2 * nsc + 1
            for fc in range(nfc):
                slot = fc % 2
                E.dma("sp", fw[:, slot], tabs[L]["fwd"][fc].rearrange("p (s f) -> p s f", f=128), writes=[fw_bs[slot]])
                src, src_b = (GS, GS_b) if fc <= nsc else (GD, GD_b)
                pb = fc % 2
                mm_group(bank(pb), [PSB[pb]], [(fw[:, slot, s_, :], src[:, s_, :], [fw_bs[slot], src_b]) for s_ in range(nsc)])
                if fc <= nsc:
                    E.op("dve", lambda h: h.tensor_tensor(out=GR[:, fc, :], in0=bank(pb), in1=skp, op=ALU.add),
                         reads=[PSB[pb], skp_b], writes=[GR_b])
                else:
                    E.op("act", lambda h: h.activation(out=GI[:, fc - nsc - 1, :], in_=bank(pb), func=AF.Copy),
                         reads=[PSB[pb]], writes=[GI_b])
            A.release(mk_)
            return GR, GR_b, GI, GI_b

        def run_group(G):
            gname, nseq, L, v = G["name"], G["nseq"], G["L"], G["v"]
            smp = G["sample"]
            T = nseq * L
            nblk = T // 512
            nkv = L + (PAST if smp else 0)
            X0d, X0d_b = G["xin"], Buf(gname + "_xin")
            X1d, X1d_b = dscratch(gname + "_x1", [128, 8 * T])
            X2d, X2d_b = dscratch(gname + "_x2", [128, 8 * T])
            mk_g = A.mark()
            MIX, MIX_bs = A.alloc(gname + "_mix", [128, 8, T], BF16, nbufs=8)
            mk_m = A.mark()
            mk_top = A.mark_top()
            GR, GR_b, GI, GI_b = hyena_filters(L, A)
            HS, HS_bs = A.alloc(gname + "_hs", [128, 8, T], BF16, nbufs=nblk)
            mk_a = A.mark()
            sq, sq_b = A.alloc("sq", [128, 8, 512], BF16)
            rstd, rstd_b = A.alloc("rstd", [128, 512], F32)
            tmp, tmp_b = A.alloc("tmp", [128, 512], F32)
            xb, xb_bs = A.alloc("xb", [128, 2, 8, 512], F32, nbufs=2)
            for blk in range(nblk):
                sl = blk % 2
                E.dma("sp", xb[:, sl], X0d.rearrange("p (j t) -> p j t", j=8)[:, :, blk * 512:(blk + 1) * 512],
                      writes=[xb_bs[sl]])
                modulate_block(xb[:, sl], xb_bs[sl], 512, 0, 0, v,
                               lambda j: HS[:, j, blk * 512:(blk + 1) * 512], HS_bs[blk], sq, sq_b, rstd, rstd_b, tmp, tmp_b)
            A.release(mk_a)
            if stage == 2:
                return

            QT, QT_b = A.alloc("QT", [128, 4, T], BF16)
            KT, KT_b = A.alloc("KT", [128, 2, nseq, nkv], BF16)
            VTb, VTb_b = A.alloc("VTb", [128, nseq * (nkv // 128), 256], BF16)
            mk_q = A.mark()
            wq, wq_b = A.alloc("wq", [128, 8, 1024], BF16)
            E.dma("sp", wq, WB["ah_w_in"][0].rearrange("p (kc n) -> p kc n", kc=8)[:, :, 0:1024], reads=[WB["ah_w_in"][1]], writes=[wq_b])
            sq, sq_b = A.alloc("sq", [128, 1, 512], BF16)
            rstd, rstd_b = A.alloc("rstd", [128, 512], F32)
            qn, qn_b = A.alloc("qn", [128, 512], F32)
            qb16, qb16_b = A.alloc("qb16", [128, 512], BF16)
            r1, r1_b = A.alloc("r1", [128, 512], F32)
            if smp:
                rc, rc_b = const("ropec", [128, TS], F32, ropec_d)
                rs, rs_b = const("ropes", [128, TS], F32, ropes_d)
                E.dma("pool", KT[:, :, 0, L:], ckT_d.rearrange("p (g t) -> p g t", g=2), writes=[KT_b])
                E.dma("pool", VTb[:, L // 128:, :], cvt_d.rearrange("p (c e) -> p c e", e=256), writes=[VTb_b])
            else:
                KN, KN_b = A.alloc("KN", [128, 2, T], F32)
                VT, VT_b = A.alloc("VT", [128, T // 128, 256], F32)
            KSUB = int(os.environ.get("KSUB", "0"))
            if KSUB == 1:
                return
            for hq in range(6):
                if KSUB == 2 and hq >= 4:
                    break
                for blk in range(nblk):
                    ts = slice(blk * 512, (blk + 1) * 512)
                    pb = blk % 2
                    mm_group(bank(pb), [PSB[pb]], [(wq[:, kc, hq * 128:(hq + 1) * 128], HS[:, kc, ts], [wq_b, HS_bs[blk]])
                                                   for kc in range(8)])
                    sumsq_rstd([bank(pb)], [PSB[pb]], 512, 128, rstd, rstd_b, sq, sq_b, 2 + pb)
                    gcol = 0 if hq < 4 else 1
                    if hq < 4:
                        dst = QT[:, hq, ts]
                        dst_b = QT_b
                    else:
                        s_i, t0 = (blk * 512) // L, (blk * 512) % L
                        dst_b = KT_b
                    if not smp:
                        if hq < 4:
                            E.op("dve", lambda h: h.scalar_tensor_tensor(
                                out=dst, in0=bank(pb), scalar=qkn[:, 0:1], in1=rstd, op0=ALU.mult, op1=ALU.mult),
                                reads=[PSB[pb], qkn_b, rstd_b], writes=[dst_b])
                        else:
                            g = hq - 4
                            E.op("dve", lambda h: h.scalar_tensor_tensor(
                                out=KN[:, g, ts], in0=bank(pb), scalar=qkn[:, 1:2], in1=rstd, op0=ALU.mult, op1=ALU.mult),
                                reads=[PSB[pb], qkn_b, rstd_b], writes=[KN_b])
                            nsq = 512 // L
                            E.op("act", lambda h: h.activation(
                                out=KT[:, g, s_i:s_i + nsq, 0:L], in_=KN[:, g, ts].rearrange("p (s t) -> p s t", t=L),
                                func=AF.Copy), reads=[KN_b], writes=[KT_b])
                    else:
                        E.op("dve", lambda h: h.scalar_tensor_tensor(
                            out=qn, in0=bank(pb), scalar=qkn[:, gcol:gcol + 1], in1=rstd, op0=ALU.mult, op1=ALU.mult),
                            reads=[PSB[pb], qkn_b, rstd_b], writes=[qn_b])
                        E.op("act", lambda h: h.activation(out=qb16, in_=qn, func=AF.Copy), reads=[qn_b], writes=[qb16_b])
                        mm_group(bank(4 + pb), [PSB[4 + pb]], [(swp, qb16, [swp_b, qb16_b])])
                        E.op("dve", lambda h: h.tensor_tensor(out=r1, in0=bank(4 + pb), in1=rs[:, ts], op=ALU.mult),
                             reads=[PSB[4 + pb], rs_b], writes=[r1_b])
                        E.op("pool", lambda h: h.tensor_tensor(out=qn, in0=qn, in1=rc[:, ts], op=ALU.mult),
                             reads=[qn_b, rc_b], writes=[qn_b])
                        if hq < 4:
                            d2 = dst
                        else:
                            d2 = KT[:, hq - 4, 0, t0:t0 + 512]
                        E.op("dve", lambda h: h.tensor_tensor(out=d2, in0=qn, in1=r1, op=ALU.add),
                             reads=[qn_b, r1_b], writes=[dst_b])
            if G.get("kout") is not None:
                E.dma("sp", G["kout"].rearrange("p (g t) -> p g t", g=2), KN, reads=[KN_b])
            if KSUB in (2, 3):
                return
            for c in range(T // 128):
                pb = 4 + c % 2
                blk = c // 4
                s_i, cc = (c * 128) // L, ((c * 128) % L) // 128
                mm_group(bank(pb, 256), [PSB[pb]], [(HS[:, kc, c * 128:(c + 1) * 128], wq[:, kc, 768:1024], [wq_b, HS_bs[blk]])
                                                    for kc in range(8)])
                if not smp and KSUB != 5:
                    E.op("dve", lambda h: h.tensor_copy(out=VT[:, c, :], in_=bank(pb, 256)),
                         reads=[PSB[pb]], writes=[VT_b])
                if KSUB != 6:
                    E.op("act", lambda h: h.activation(out=VTb[:, s_i * (nkv // 128) + cc, :], in_=bank(pb, 256), func=AF.Copy),
                         reads=[PSB[pb]], writes=[VTb_b])
            if G.get("vout") is not None:
                E.dma("sp", G["vout"].rearrange("p (c e) -> p c e", e=256), VT, reads=[VT_b])
            A.release(mk_q)
            if stage == 3:
                return
            Pt, Pt_bs = A.alloc("Pt", [128, 2, 512], BF16, nbufs=2)
            rden, rden_b = A.alloc("rden", [128, 512], F32)
            nq = min(512, L)
            nkc = nkv // 128
            att_scale = 1.0 / math.sqrt(128.0)
            for s_i in range(nseq):
                for hd in range(4):
                    g = hd // 2
                    for qb in range(L // nq):
                        q0 = s_i * L + qb * nq
                        def att_front(kc):
                            sb = kc % 2
                            mm_group(bank(sb, nq), [PSB[sb]], [(KT[:, g, s_i, kc * 128:(kc + 1) * 128], QT[:, hd, q0:q0 + nq],
                                                                [KT_b, QT_b])])
                            E.op("act", lambda h: h.activation(out=Pt[:, sb, :nq], in_=bank(sb, nq), func=AF.Exp,
                                                               scale=att_scale), reads=[PSB[sb]], writes=[Pt_bs[sb]])

                        def att_back(kc):
                            sb = kc % 2
                            E.op("pe", lambda h: h.matmul(bank(2, nq), lhsT=VTb[:, s_i * nkc + kc, g * 128:(g + 1) * 128],
                                                          rhs=Pt[:, sb, :nq], start=(kc == 0), stop=(kc == nkc - 1)),
                                 reads=[VTb_b, Pt_bs[sb]], writes=[PSB[2]], pe_accum=True)
                            E.op("pe", lambda h: h.matmul(bank(3, nq), lhsT=ones, rhs=Pt[:, sb, :nq],
                                                          start=(kc == 0), stop=(kc == nkc - 1)),
                                 reads=[ones_b, Pt_bs[sb]], writes=[PSB[3]], pe_accum=True)
                        att_front(0)
                        for kc in range(1, nkc):
                            att_front(kc)
                            att_back(kc - 1)
                        att_back(nkc - 1)
                        E.op("act", lambda h: h.activation(out=rden[:, :nq], in_=bank(3, nq), func=AF.Ln), reads=[PSB[3]], writes=[rden_b])
                        E.op("act", lambda h: h.activation(out=rden[:, :nq], in_=rden[:, :nq], func=AF.Exp, scale=-1.0),
                             reads=[rden_b], writes=[rden_b])
                        E.op("dve", lambda h: h.tensor_tensor(out=MIX[:, hd, q0:q0 + nq], in0=bank(2, nq), in1=rden[:, :nq],
                                                              op=ALU.mult), reads=[PSB[2], rden_b], writes=[MIX_bs[hd]])
            A.release(mk_a)
            if stage == 4:
                return

            nsc = L // 128
            nfc = 2 * nsc + 1
            X0, X0_b = A.alloc("X0", [128, 4, T], BF16, top=True)
            VPT, VPT_b = A.alloc("VPT", [128, nseq, nsc, 512], BF16, top=True)
            mk_h = A.mark()
            wu, wu_bs = A.alloc("wu", [128, 2, 8, 128], BF16, nbufs=2)
            wu_it = [0]
            U, U_b = A.alloc("U", [128, T], F32)
            CU, CU_bs = A.alloc("CU", [128, 2, T], F32, nbufs=2)
            VPc, VPc_b = A.alloc("VPc", [128, T], BF16)
            for c in range(4):
                for ti, which in enumerate((1, 2, 0)):
                    ch = which * 4 + c
                    wsl = wu_it[0] % 2
                    wu_it[0] += 1
                    E.dma("sp", wu[:, wsl], WB["ah_w_in"][0].rearrange("p (kc n) -> p kc n", kc=8)[:, :, 1024 + ch * 128: 1024 + (ch + 1) * 128],
                          reads=[WB["ah_w_in"][1]], writes=[wu_bs[wsl]])
                    for blk in range(nblk):
                        ts = slice(blk * 512, (blk + 1) * 512)
                        pb = blk % 2
                        mm_group(bank(pb), [PSB[pb]], [(wu[:, wsl, kc, :], HS[:, kc, ts], [wu_bs[wsl], HS_bs[blk]])
                                                       for kc in range(8)])
                        E.op("act", lambda h: h.activation(out=U[:, ts], in_=bank(pb), func=AF.Copy),
                             reads=[PSB[pb]], writes=[U_b])
                    ci = ti % 2
                    cu, cu_b = CU[:, ci], CU_bs[ci]
                    E.op("act", lambda h: h.activation(out=cu, in_=U, func=AF.Identity, bias=hcw[:, ch, 3:4],
                                                       scale=hcw[:, ch, 1:2]), reads=[U_b, hcw_b], writes=[cu_b])
                    c3 = cu.rearrange("p (s t) -> p s t", t=L)
                    u3 = U.rearrange("p (s t) -> p s t", t=L)
                    E.op("dve", lambda h: h.scalar_tensor_tensor(out=c3[:, :, 1:L], in0=u3[:, :, 0:L - 1], scalar=hcw[:, ch, 0:1],
                                                                 in1=c3[:, :, 1:L], op0=ALU.mult, op1=ALU.add),
                         reads=[U_b, hcw_b, cu_b], writes=[cu_b])
                    E.op("dve", lambda h: h.scalar_tensor_tensor(out=c3[:, :, 0:L - 1], in0=u3[:, :, 1:L], scalar=hcw[:, ch, 2:3],
                                                                 in1=c3[:, :, 0:L - 1], op0=ALU.mult, op1=ALU.add),
                         reads=[U_b, hcw_b, cu_b], writes=[cu_b])
                    if which == 2:
                        E.op("pool", lambda h: h.tensor_tensor(out=VPc, in0=CU[:, 0], in1=CU[:, 1], op=ALU.mult),
                             reads=[CU_bs[0], CU_bs[1]], writes=[VPc_b])
                    if which == 0:
                        E.op("pool", lambda h: h.tensor_copy(out=X0[:, c, :], in_=cu), reads=[cu_b], writes=[X0_b])
                for tcn in range(T // 128):
                    s_i, cc = (tcn * 128) // L, ((tcn * 128) % L) // 128
                    pb = 4 + tcn % 2
                    E.op("pe", lambda h: h.transpose(PSbf[pb], VPc[:, tcn * 128:(tcn + 1) * 128], ident),
                         reads=[VPc_b, ident_b], writes=[PSbf_b[pb]])
                    E.op("act", lambda h: h.activation(out=VPT[:, s_i, cc, c * 128:(c + 1) * 128], in_=PSbf[pb], func=AF.Copy),
                         reads=[PSbf_b[pb]], writes=[VPT_b])
            A.release(mk_m)
            if stage == 5:
                return
            YR, YR_b = A.alloc("YR", [128, nsc + 1, 512], BF16)
            YI, YI_b = A.alloc("YI", [128, nsc, 512], BF16)
            vr, vr_b = A.alloc("vr", [128, 512], F32)
            vi, vi_b = A.alloc("vi", [128, 512], F32)
            pa, pa_b = A.alloc("pa", [128, 512], F32)
            pb_, pb_b = A.alloc("pb", [128, 512], F32)
            pc, pc_b = A.alloc("pc", [128, 512], F32)
            pd, pd_b = A.alloc("pd", [128, 512], F32)
            fw, fw_bs = A.alloc("fwst", [128, 2, nsc, 128], BF16, nbufs=2)
            nt = min(L, 256)
            ntb = L // nt
            iv, iv_b = A.alloc("invst", [128, nfc, nt], BF16)
            for s_i in range(nseq):
                for fc in range(nsc + 1):
                    E.dma("sp", fw[:, 0], tabs[L]["fwd"][fc].rearrange("p (s f) -> p s f", f=128), writes=[fw_bs[0]])
                    mm_group(bank(0), [PSB[0]], [(fw[:, 0, s_, :], VPT[:, s_i, s_, :], [fw_bs[0], VPT_b]) for s_ in range(nsc)])
                    E.op("act", lambda h: h.activation(out=vr, in_=bank(0), func=AF.Copy), reads=[PSB[0]], writes=[vr_b])
                    if fc < nsc:
                        E.dma("sp", fw[:, 1], tabs[L]["fwd"][nsc + 1 + fc].rearrange("p (s f) -> p s f", f=128),
                              writes=[fw_bs[1]])
                        mm_group(bank(1), [PSB[1]], [(fw[:, 1, s_, :], VPT[:, s_i, s_, :], [fw_bs[1], VPT_b])
                                                     for s_ in range(nsc)])
                        E.op("act", lambda h: h.activation(out=vi, in_=bank(1), func=AF.Copy), reads=[PSB[1]], writes=[vi_b])
                        E.op("dve", lambda h: h.tensor_tensor(out=pa, in0=vr, in1=GR[:, fc, :], op=ALU.mult),
                             reads=[vr_b, GR_b], writes=[pa_b])
                        E.op("pool", lambda h: h.tensor_tensor(out=pb_, in0=vi, in1=GI[:, fc, :], op=ALU.mult),
                             reads=[vi_b, GI_b], writes=[pb_b])
                        E.op("dve", lambda h: h.tensor_tensor(out=YR[:, fc, :], in0=pa, in1=pb_, op=ALU.subtract),
                             reads=[pa_b, pb_b], writes=[YR_b])
                        E.op("pool", lambda h: h.tensor_tensor(out=pc, in0=vr, in1=GI[:, fc, :], op=ALU.mult),
                             reads=[vr_b, GI_b], writes=[pc_b])
                        E.op("dve", lambda h: h.tensor_tensor(out=pd, in0=vi, in1=GR[:, fc, :], op=ALU.mult),
                             reads=[vi_b, GR_b], writes=[pd_b])
                        E.op("pool", lambda h: h.tensor_tensor(out=YI[:, fc, :], in0=pc, in1=pd, op=ALU.add),
                             reads=[pc_b, pd_b], writes=[YI_b])
                    else:
                        E.op("dve", lambda h: h.tensor_tensor(out=YR[:, fc, :], in0=vr, in1=GR[:, fc, :], op=ALU.mult),
                             reads=[vr_b, GR_b], writes=[YR_b])
                for tb in range(ntb):
                    tw = min(L, 512)
                    E.dma("sp", iv, tabs[L]["inv"][(tb * nt) // tw].rearrange("p (f t) -> p f t", t=tw)[:, :, (tb * nt) % tw:(tb * nt) % tw + nt],
                          writes=[iv_b])
                    t0 = s_i * L + tb * nt
                    for cc in range(4):
                        pbk = 4 + cc % 2
                        terms = [(YR[:, f_, cc * 128:(cc + 1) * 128], iv[:, f_, :], [YR_b, iv_b]) for f_ in range(nsc + 1)]
                        terms += [(YI[:, f_, cc * 128:(cc + 1) * 128], iv[:, nsc + 1 + f_, :], [YI_b, iv_b]) for f_ in range(nsc)]
                        mm_group(bank(pbk, nt), [PSB[pbk]], terms)
                        E.op("dve", lambda h: h.tensor_tensor(out=MIX[:, 4 + cc, t0:t0 + nt], in0=bank(pbk, nt),
                                                              in1=X0[:, cc, t0:t0 + nt], op=ALU.mult),
                             reads=[PSB[pbk], X0_b], writes=[MIX_bs[4 + cc]])
            A.release(mk_m)
            A.release_top(mk_top)
            if stage == 6:
                dbg = P.dout("dbg_" + gname, [128, 8 * T], BF16)
                E.dma("sp", dbg.rearrange("p (j t) -> p j t", j=8), MIX, reads=MIX_bs)
                return

            HF, HF_bs = A.alloc(gname + "_hf", [128, 8, T], BF16, nbufs=nblk)
            mk_o = A.mark()
            wo, wo_b = A.alloc("wo", [128, 8, 1024], BF16)
            E.dma("sp", wo, WB["ah_w_out"][0].rearrange("p (kc n) -> p kc n", kc=8), reads=[WB["ah_w_out"][1]], writes=[wo_b])
            O, O_b = A.alloc("O", [128, 8, 512], F32)
            sq, sq_b = A.alloc("sq", [128, 8, 512], BF16)
            rstd, rstd_b = A.alloc("rstd", [128, 512], F32)
            tmp, tmp_b = A.alloc("tmp", [128, 512], F32)
            xb, xb_b = A.alloc("xb", [128, 8, 512], F32)
            wk = (sq, sq_b, rstd, rstd_b, tmp, tmp_b, xb, xb_b)
            for blk in range(nblk):
                ts = slice(blk * 512, (blk + 1) * 512)
                for m in range(8):
                    pbk = m % 2
                    mm_group(bank(pbk), [PSB[pbk]], [(wo[:, kc, m * 128:(m + 1) * 128], MIX[:, kc, ts], [wo_b, MIX_bs[kc]])
                                                     for kc in range(8)])
                    E.op("act", lambda h: h.activation(out=O[:, m, :], in_=bank(pbk), func=AF.Copy), reads=[PSB[pbk]], writes=[O_b])
                epilogue(O, O_b, 512, X0d, X0d_b, X1d, X1d_b, blk * 512, T, 0, 0, v,
                         (0, 1, lambda j: HF[:, j, ts], HF_bs[blk]), wk)
            A.release(mk_o)
            if stage == 7:
                dbg = P.dout("dbg_" + gname, [128, 8 * T], F32)
                E.dma("sp", dbg, X1d, reads=[X1d_b], writes=[])
                return
            p_after_hf = A.p
            A.release(mk_g)
            HS1, HS1_bs = A.alloc(gname + "_hs1", [128, 8, T], BF16, nbufs=nblk)
            p_after_hs1 = A.p
            A.p = p_after_hf
            X2d, X2d_b = dscratch(gname + "_x2b", [128, 8 * T])
            ffn(G, 0, HF, HF_bs, X1d, X1d_b, X2d, X2d_b,
                lambda tok0: (1, 0, (lambda j: HS1[:, j, tok0:tok0 + 512]), HS1_bs[tok0 // 512]), NB=(512 if smp else 1024))
            if stage == 8:
                E.dma("sp", G["y"], X2d, reads=[X2d_b])
                A.release(mk_g)
                return
            A.release(p_after_hs1)
            layer1(G, HS1, HS1_bs, X2d, X2d_b)
            A.release(mk_g)

        def layer1(G, HS1, HS1_bs, X2d, X2d_b):
            gname, nseq, L, v = G["name"], G["nseq"], G["L"], G["v"]
            smp = G["sample"]
            T = nseq * L
            nblk = T // 512
            nch = L // 128
            X3d, X3d_b = dscratch(gname + "_x3", [128, 8 * T])
            mk_l = A.mark()
            MLM, MLM_bs = A.alloc("mlmix", [128, 8, T], BF16, nbufs=8)
            mk_2 = A.mark()
            RT, RT_b = A.alloc("RT", [40, T], F32)
            EM, EM_b = A.alloc("EM", [40, T], F32)
            WI, WI_b = A.alloc("WI", [40, T], F32)
            ATK, ATK_b = A.alloc("ATK", [128, T // 128, 40], F32)
            m0c, m0c_b = A.alloc("m0c", [40, 2], F32)
            onesf, onesf_b = A.alloc("onesf", [40, 128], F32)
            E.op("dve", lambda h: h.memset(onesf, 1.0), writes=[onesf_b])
            if G.get("states"):
                WTK, WTK_b = A.alloc("WTK", [128, T // 128, 40], F32)
            mk_r = A.mark()
            wg, wg_b = A.alloc("wg", [128, 8, 80], BF16)
            E.dma("pool", wg, mlwg_d.rearrange("p (kc n) -> p kc n", kc=8), writes=[wg_b])
            IG, IG_b = A.alloc("IG", [40, T], F32)
            LF, LF_b = A.alloc("LF", [40, T], F32)
            for blk in range(nblk):
                ts = slice(blk * 512, (blk + 1) * 512)
                for gi in range(2):
                    mm_group(PS[0:40, gi * 512:(gi + 1) * 512], [PSB[gi]],
                             [(wg[:, kc, gi * 40:(gi + 1) * 40], HS1[:, kc, ts], [wg_b, HS1_bs[blk]]) for kc in range(8)])
                E.op("act", lambda h: h.activation(out=IG[:, ts], in_=PS[0:40, 0:512], func=AF.Identity, bias=mbg[:, 0:1], scale=1.0),
                     reads=[PSB[0], mbg_b], writes=[IG_b])
                E.op("act", lambda h: h.activation(out=LF[:, ts], in_=PS[0:40, 512:1024], func=AF.Identity, bias=mbg[:, 1:2], scale=1.0),
                     reads=[PSB[1], mbg_b], writes=[LF_b])
            E.op("act", lambda h: h.activation(out=LF, in_=LF, func=AF.Exp, scale=-1.0), reads=[LF_b], writes=[LF_b])
            E.op("act", lambda h: h.activation(out=LF, in_=LF, func=AF.Ln, bias=onec[0:40, 0:1], scale=1.0),
                 reads=[LF_b, onec_b], writes=[LF_b])
            E.op("dve", lambda h: h.tensor_scalar(out=LF, in0=LF, scalar1=-1.0, scalar2=None, op0=ALU.mult), reads=[LF_b], writes=[LF_b])
            if smp:
                E.dma("sp", m0c[:, 0:1], m0_d, writes=[m0c_b])
            else:
                E.op("dve", lambda h: h.memset(m0c, 0.0), writes=[m0c_b])
            BT, BT_b = A.alloc("BT", [40, T], F32)
            AA, AA_b = A.alloc("AA", [40, T], F32)
            CM, CM_b = A.alloc("CM", [40, T], F32)
            onesr, onesr_b = A.alloc("onesr", [40, L], F32)
            E.op("dve", lambda h: h.memset(onesr, 1.0), writes=[onesr_b])
            for (tt, tb_) in ((BT, BT_b), (AA, AA_b), (CM, CM_b)):
                E.op("pool", lambda h: h.memset(tt, 0.0), writes=[tb_])
            for s_i in range(nseq):
                sl = slice(s_i * L, (s_i + 1) * L)
                for (p0, rev) in ((0, False), (32, True)):
                    pr = slice(p0, p0 + 8)

                    def V_(ap):
                        a2 = ap[pr, sl]
                        return a2[:, ::-1] if rev else a2
                    E.op("dve", lambda h: h.tensor_tensor_scan(out=V_(BT), data0=onesr[pr, :], data1=V_(LF), initial=0.0,
                                                               op0=ALU.mult, op1=ALU.add),
                         reads=[LF_b, onesr_b], writes=[BT_b])
                    E.op("dve", lambda h: h.tensor_tensor(out=AA[pr, sl], in0=IG[pr, sl], in1=BT[pr, sl], op=ALU.subtract),
                         reads=[IG_b, BT_b], writes=[AA_b])
                    E.op("dve", lambda h: h.tensor_tensor_scan(out=V_(CM), data0=V_(AA), data1=V_(AA), initial=m0c[pr, 0:1],
                                                               op0=ALU.max, op1=ALU.max), reads=[AA_b, m0c_b], writes=[CM_b])
            E.op("dve", lambda h: h.tensor_scalar(out=RT, in0=CM, scalar1=-1.0, scalar2=None, op0=ALU.mult), reads=[CM_b], writes=[RT_b])
            E.op("dve", lambda h: h.tensor_tensor(out=EM, in0=BT, in1=CM, op=ALU.add), reads=[BT_b, CM_b], writes=[EM_b])
            if G.get("states"):
                MTk, MTk_b = A.alloc("MTk", [40, nseq], F32)
                E.op("dve", lambda h: h.memset(MTk, 0.0), writes=[MTk_b])
                for s_i in range(nseq):
                    E.op("dve", lambda h: h.tensor_copy(out=MTk[0:8, s_i:s_i + 1], in_=EM[0:8, (s_i + 1) * L - 1:(s_i + 1) * L]),
                         reads=[EM_b], writes=[MTk_b])
                    E.op("dve", lambda h: h.tensor_copy(out=MTk[32:40, s_i:s_i + 1], in_=EM[32:40, s_i * L:s_i * L + 1]),
                         reads=[EM_b], writes=[MTk_b])
                E.dma("sp", mst_d, MTk, reads=[MTk_b])
            E.op("act", lambda h: h.activation(out=EM, in_=EM, func=AF.Exp, scale=-1.0), reads=[EM_b], writes=[EM_b])
            E.op("act", lambda h: h.activation(out=WI, in_=CM, func=AF.Exp, scale=-1.0, bias=m0c[:, 0:1]),
                 reads=[CM_b, m0c_b], writes=[WI_b])
            for c in range(T // 128):
                E.op("pe", lambda h: h.transpose(PS[:, 1024:1064], AA[:, c * 128:(c + 1) * 128], id32[0:40, 0:40]),
                     reads=[AA_b, id32_b], writes=[PSB[2]])
                E.op("dve", lambda h: h.tensor_copy(out=ATK[:, c, :], in_=PS[:, 1024:1064]), reads=[PSB[2]], writes=[ATK_b])
            if G.get("states"):
                dg, dg_b = A.alloc("dg", [40, nseq, 40], F32)
                for s_i in range(nseq):
                    E.op("dve", lambda h: h.memset(dg[:, s_i, :], 0.0), writes=[dg_b])
                    E.op("dve", lambda h: h.tensor_scalar(out=dg[0:8, s_i, :], in0=id32[0:8, 0:40], scalar1=RT[0:8, (s_i + 1) * L - 1:(s_i + 1) * L],
                                                          scalar2=None, op0=ALU.mult), reads=[id32_b, RT_b], writes=[dg_b])
                    E.op("dve", lambda h: h.tensor_scalar(out=dg[32:40, s_i, :], in0=id32[32:40, 0:40], scalar1=RT[32:40, s_i * L:s_i * L + 1],
                                                          scalar2=None, op0=ALU.mult), reads=[id32_b, RT_b], writes=[dg_b])
                    mm_group(PS[:, 1024:1064], [PSB[2]], [(onesf, dg[:, s_i, :], [onesf_b, dg_b])])
                    for cc in range(nch):
                        c = s_i * nch + cc
                        E.op("dve", lambda h: h.tensor_tensor(out=WTK[:, c, :], in0=ATK[:, c, :], in1=PS[:, 1024:1064], op=ALU.add),
                             reads=[ATK_b, PSB[2]], writes=[WTK_b])
                E.op("act", lambda h: h.activation(out=WTK, in_=WTK, func=AF.Exp), reads=[WTK_b], writes=[WTK_b])
            A.release(mk_r)
            wh, wh_b = A.alloc("wh", [128, 8, 4, 128], BF16)
            U, U_b = A.alloc("U", [128, T], F32)
            cu, cu_b = A.alloc("cu1", [128, T], F32)
            QT, QT_b = A.alloc("QT1", [128, T], BF16)
            KT, KT_b = A.alloc("KT1", [128, T], BF16)
            SG, SG_b = A.alloc("SG", [128, T], BF16)
            VK, VK_b = A.alloc("VK", [128, T // 128, 128], BF16)
            KK, KK_b = A.alloc("KK", [128, T // 128, 128], BF16)
            HSUM, HSUM_b = A.alloc("HSUM", [128, T], F32)
            nq = min(512, L)
            rtb, rtb_b = A.alloc("rtb", [128, nq], F32)
            emb, emb_b = A.alloc("emb", [128, nq], F32)
            wexp, wexp_bs = A.alloc("wexp", [128, 2, nq], F32, nbufs=2)
            Pm, Pm_bs = A.alloc("Pm", [128, 2, nq], BF16, nbufs=2)
            dn, dn_b = A.alloc("dn", [128, nq], F32)
            ht, ht_b = A.alloc("ht", [128, nq], F32)
            sq, sq_b = A.alloc("sq", [128, 1, 512], BF16)
            rstd, rstd_b = A.alloc("rstd", [128, 512], F32)
            vw, vw_b = A.alloc("vw", [128, 128], BF16)
            cst, cst_bs = A.alloc("cst", [128, 2, 128], F32, nbufs=2)
            nst, nst_bs = A.alloc("nst", [1, 2, 128], F32, nbufs=2)
            if smp:
                qp, qp_b = A.alloc("qp", [128, nq], BF16)
                c0t, c0t_b = A.alloc("c0t", [128, 16, 128], BF16)
                n0b, n0b_b = A.alloc("n0b", [128, 16, 128], BF16)
                E.dma("pool", c0t, c0t_d.rearrange("p (r m) -> p r m", m=128), writes=[c0t_b])
                E.dma("pool", n0b, n0b_d.rearrange("p (r m) -> p r m", m=128), writes=[n0b_b])
            wv = WB["ml_w_in"][0].rearrange("p (kc n) -> p kc n", kc=8)
            for hd in range(8):
                for wi in range(4):
                    E.dma("sp", wh[:, :, wi, :], wv[:, :, wi * 1024 + hd * 128: wi * 1024 + (hd + 1) * 128], reads=[WB["ml_w_in"][1]], writes=[wh_b])
                for wi, (dst, dst_b) in ((0, (QT, QT_b)), (1, (KT, KT_b)), (3, (SG, SG_b))):
                    for blk in range(nblk):
                        ts = slice(blk * 512, (blk + 1) * 512)
                        pb = blk % 2
                        mm_group(bank(pb), [PSB[pb]], [(wh[:, kc, wi, :], HS1[:, kc, ts], [wh_b, HS1_bs[blk]]) for kc in range(8)])
                        if wi == 3:
                            E.op("act", lambda h: h.activation(out=SG[:, ts], in_=bank(pb), func=AF.Sigmoid), reads=[PSB[pb]], writes=[SG_b])
                        else:
                            E.op("act", lambda h: h.activation(out=U[:, ts], in_=bank(pb), func=AF.Copy), reads=[PSB[pb]], writes=[U_b])
                    if wi == 3:
                        continue
                    ch = wi * 8 + hd
                    E.op("act", lambda h: h.activation(out=cu, in_=U, func=AF.Identity, bias=mcw[:, ch, 3:4], scale=mcw[:, ch, 1:2]),
                         reads=[U_b, mcw_b], writes=[cu_b])
                    c3 = cu.rearrange("p (s t) -> p s t", t=L)
                    u3 = U.rearrange("p (s t) -> p s t", t=L)
                    E.op("dve", lambda h: h.scalar_tensor_tensor(out=c3[:, :, 1:L], in0=u3[:, :, 0:L - 1], scalar=mcw[:, ch, 0:1],
                                                                 in1=c3[:, :, 1:L], op0=ALU.mult, op1=ALU.add),
                         reads=[U_b, mcw_b, cu_b], writes=[cu_b])
                    E.op("dve", lambda h: h.scalar_tensor_tensor(out=c3[:, :, 0:L - 1], in0=u3[:, :, 1:L], scalar=mcw[:, ch, 2:3],
                                                                 in1=c3[:, :, 0:L - 1], op0=ALU.mult, op1=ALU.add),
                         reads=[U_b, mcw_b, cu_b], writes=[cu_b])
                    if wi == 0:
                        E.op("act", lambda h: h.activation(out=QT, in_=cu, func=AF.Silu), reads=[cu_b], writes=[QT_b])
                    else:
                        E.op("act", lambda h: h.activation(out=cu, in_=cu, func=AF.Silu), reads=[cu_b], writes=[cu_b])
                        E.op("pool", lambda h: h.tensor_scalar(out=KT, in0=cu, scalar1=128.0 ** -0.5, scalar2=None, op0=ALU.mult),
                             reads=[cu_b], writes=[KT_b])
                for c in range(T // 128):
                    pb = 4 + c % 2
                    mm_group(bank(pb, 128), [PSB[pb]], [(HS1[:, kc, c * 128:(c + 1) * 128], wh[:, kc, 2, :], [wh_b, HS1_bs[c // 4]])
                                                        for kc in range(8)])
                    E.op("act", lambda h: h.activation(out=VK[:, c, :], in_=bank(pb, 128), func=AF.Copy), reads=[PSB[pb]], writes=[VK_b])
                    if G.get("states"):
                        E.op("pe", lambda h: h.transpose(PSbf[4], KT[:, c * 128:(c + 1) * 128], ident), reads=[KT_b, ident_b], writes=[PSbf_b[4]])
                        E.op("act", lambda h: h.activation(out=KK[:, c, :], in_=PSbf[4], func=AF.Copy), reads=[PSbf_b[4]], writes=[KK_b])
                for s_i in range(nseq):
                    for di in range(2):
                        r = di * 32 + hd
                        ridx = di * 8 + hd
                        for qb in range(L // nq):
                            q0 = s_i * L + qb * nq
                            mm_group(bank(4, nq), [PSB[4]], [(sel[:, ridx, :], RT[:, q0:q0 + nq], [sel_b, RT_b])])
                            E.op("act", lambda h: h.activation(out=rtb, in_=bank(4, nq), func=AF.Copy), reads=[PSB[4]], writes=[rtb_b])
                            mm_group(bank(5, nq), [PSB[5]], [(sel[:, ridx, :], EM[:, q0:q0 + nq], [sel_b, EM_b])])
                            E.op("act", lambda h: h.activation(out=emb, in_=bank(5, nq), func=AF.Copy), reads=[PSB[5]], writes=[emb_b])
                            if di == 0:
                                kcs = [kc for kc in range(nch) if kc * 128 <= qb * nq + nq - 1]
                            else:
                                kcs = [kc for kc in range(nch) if kc * 128 + 127 >= qb * nq]
                            nterm = len(kcs) + (1 if smp else 0)
                            ti = 0
                            if smp:
                                mm_group(bank(4, nq), [PSB[4]], [(sel[:, ridx, :], WI[:, q0:q0 + nq], [sel_b, WI_b])])
                                E.op("dve", lambda h: h.tensor_tensor(out=qp, in0=QT[:, q0:q0 + nq], in1=bank(4, nq), op=ALU.mult),
                                     reads=[QT_b, PSB[4]], writes=[qp_b])
                                E.op("pe", lambda h: h.matmul(bank(2, nq), lhsT=c0t[:, ridx, :], rhs=qp, start=True, stop=(nterm == 1)),
                                     reads=[c0t_b, qp_b], writes=[PSB[2]], pe_accum=True)
                                E.op("pe", lambda h: h.matmul(bank(3, nq), lhsT=n0b[:, ridx, :], rhs=qp, start=True, stop=(nterm == 1)),
                                     reads=[n0b_b, qp_b], writes=[PSB[3]], pe_accum=True)
                                ti = 1
                            def ml_front(idx, kc):
                                c = s_i * nch + kc
                                sb = idx % 2
                                off = kc * 128 - qb * nq
                                mm_group(bank(sb, nq), [PSB[sb]], [(KT[:, c * 128:(c + 1) * 128], QT[:, q0:q0 + nq], [KT_b, QT_b])])
                                E.op("act", lambda h: h.activation(out=wexp[:, sb, :], in_=rtb, func=AF.Exp, bias=ATK[:, c, r:r + 1], scale=1.0),
                                     reads=[rtb_b, ATK_b], writes=[wexp_bs[sb]])
                                E.op("dve", lambda h: h.tensor_tensor(out=Pm[:, sb, :], in0=bank(sb, nq), in1=wexp[:, sb, :], op=ALU.mult),
                                     reads=[PSB[sb], wexp_bs[sb]], writes=[Pm_bs[sb]])
                                if 0 <= off < nq and (off // 128) < 4:
                                    E.op("pool", lambda h: h.tensor_tensor(out=Pm[:, sb, :], in0=Pm[:, sb, :], in1=msk[:, di, off // 128, 0:nq], op=ALU.mult),
                                         reads=[Pm_bs[sb], msk_b], writes=[Pm_bs[sb]])

                            def ml_back(idx, kc, tpos):
                                c = s_i * nch + kc
                                sb = idx % 2
                                E.op("pe", lambda h: h.matmul(bank(2, nq), lhsT=VK[:, c, :], rhs=Pm[:, sb, :], start=(tpos == 0), stop=(tpos == nterm - 1)),
                                     reads=[VK_b, Pm_bs[sb]], writes=[PSB[2]], pe_accum=True)
                                E.op("pe", lambda h: h.matmul(bank(3, nq), lhsT=ones, rhs=Pm[:, sb, :], start=(tpos == 0), stop=(tpos == nterm - 1)),
                                     reads=[ones_b, Pm_bs[sb]], writes=[PSB[3]], pe_accum=True)
                            ml_front(0, kcs[0])
                            for idx in range(1, len(kcs)):
                                ml_front(idx, kcs[idx])
                                ml_back(idx - 1, kcs[idx - 1], ti + idx - 1)
                            ml_back(len(kcs) - 1, kcs[-1], ti + len(kcs) - 1)
                            E.op("dve", lambda h: h.tensor_scalar(out=dn, in0=bank(3, nq), scalar1=-1.0, scalar2=None, op0=ALU.mult),
                                 reads=[PSB[3]], writes=[dn_b])
                            E.op("dve", lambda h: h.tensor_tensor(out=dn, in0=dn, in1=bank(3, nq), op=ALU.max), reads=[dn_b, PSB[3]], writes=[dn_b])
                            E.op("dve", lambda h: h.tensor_tensor(out=dn, in0=dn, in1=emb, op=ALU.max), reads=[dn_b, emb_b], writes=[dn_b])
                            E.op("act", lambda h: h.activation(out=dn, in_=dn, func=AF.Ln), reads=[dn_b], writes=[dn_b])
                            E.op("act", lambda h: h.activation(out=dn, in_=dn, func=AF.Exp, scale=-1.0), reads=[dn_b], writes=[dn_b])
                            if di == 0:
                                E.op("dve", lambda h: h.tensor_tensor(out=HSUM[:, q0:q0 + nq], in0=bank(2, nq), in1=dn, op=ALU.mult),
                                     reads=[PSB[2], dn_b], writes=[HSUM_b])
                            else:
                                E.op("dve", lambda h: h.tensor_tensor(out=ht, in0=bank(2, nq), in1=dn, op=ALU.mult),
                                     reads=[PSB[2], dn_b], writes=[ht_b])
                                E.op("pool", lambda h: h.tensor_tensor(out=HSUM[:, q0:q0 + nq], in0=HSUM[:, q0:q0 + nq], in1=ht, op=ALU.add),
                                     reads=[ht_b, HSUM_b], writes=[HSUM_b])
                        if G.get("states"):
                            so = (s_i * 16 + ridx) * 128
                            slot = ridx % 2
                            for cc in range(nch):
                                c = s_i * nch + cc
                                E.op("dve", lambda h: h.tensor_scalar(out=vw, in0=VK[:, c, :], scalar1=WTK[:, c, r:r + 1], scalar2=None, op0=ALU.mult),
                                     reads=[VK_b, WTK_b], writes=[vw_b])
                                E.op("pe", lambda h: h.matmul(bank(5, 128), lhsT=vw, rhs=KK[:, c, :], start=(cc == 0), stop=(cc == nch - 1)),
                                     reads=[vw_b, KK_b], writes=[PSB[5]], pe_accum=True)
                            E.op("act", lambda h: h.activation(out=cst[:, slot, :], in_=bank(5, 128), func=AF.Copy), reads=[PSB[5]], writes=[cst_bs[slot]])
                            E.dma("sp", cst_d[:, so:so + 128], cst[:, slot, :], reads=[cst_bs[slot]])
                            wtb, wtb_b = vw, vw_b
                            for cc in range(nch):
                                c = s_i * nch + cc
                                E.op("dve", lambda h: h.tensor_copy(out=vw[:, 0:1], in_=WTK[:, c, r:r + 1]), reads=[WTK_b], writes=[vw_b])
                                E.op("pe", lambda h: h.matmul(PS[0:1, 5 * 512 + 128: 5 * 512 + 256], lhsT=vw[:, 0:1], rhs=KK[:, c, :],
                                                              start=(cc == 0), stop=(cc == nch - 1)),
                                     reads=[vw_b, KK_b], writes=[PSB[5]], pe_accum=True)
                            E.op("act", lambda h: h.activation(out=nst[0:1, slot, :], in_=PS[0:1, 5 * 512 + 128: 5 * 512 + 256], func=AF.Copy),
                                 reads=[PSB[5]], writes=[nst_bs[slot]])
                            E.dma("sp", nst_d[0:1, so:so + 128], nst[0:1, slot, :], reads=[nst_bs[slot]])
                for blk in range(nblk):
                    ts = slice(blk * 512, (blk + 1) * 512)
                    sumsq_rstd([HSUM[:, ts]], [HSUM_b], 512, 128, rstd, rstd_b, sq, sq_b, 6)
                    E.op("dve", lambda h: h.scalar_tensor_tensor(out=U[:, ts], in0=HSUM[:, ts], scalar=mhn[:, hd:hd + 1], in1=rstd,
                                                                 op0=ALU.mult, op1=ALU.mult), reads=[HSUM_b, mhn_b, rstd_b], writes=[U_b])
                    E.op("pool", lambda h: h.tensor_tensor(out=MLM[:, hd, ts], in0=U[:, ts], in1=SG[:, ts], op=ALU.mult),
                         reads=[U_b, SG_b], writes=[MLM_bs[hd]])
            A.release(mk_2)
            if stage == 9:
                dbg = P.dout("dbg_" + gname, [128, 8 * T], BF16)
                E.dma("sp", dbg.rearrange("p (j t) -> p j t", j=8), MLM, reads=MLM_bs)
                A.release(mk_l)
                return
            mk_top1 = A.mark_top()
            HF1, HF1_bs = A.alloc(gname + "_hf1", [128, 8, T], BF16, nbufs=nblk, top=True)
            mk_o = A.mark()
            wo, wo_b = A.alloc("wo1", [128, 8, 1024], BF16)
            E.dma("sp", wo, WB["ml_w_out"][0].rearrange("p (kc n) -> p kc n", kc=8), reads=[WB["ml_w_out"][1]], writes=[wo_b])
            O, O_b = A.alloc("O", [128, 8, 512], F32)
            sq, sq_b = A.alloc("sq", [128, 8, 512], BF16)
            rstd, rstd_b = A.alloc("rstd", [128, 512], F32)
            tmp, tmp_b = A.alloc("tmp", [128, 512], F32)
            xb, xb_b = A.alloc("xb", [128, 8, 512], F32)
            wk = (sq, sq_b, rstd, rstd_b, tmp, tmp_b, xb, xb_b)
            for blk in range(nblk):
                ts = slice(blk * 512, (blk + 1) * 512)
                for m in range(8):
                    pbk = m % 2
                    mm_group(bank(pbk), [PSB[pbk]], [(wo[:, kc, m * 128:(m + 1) * 128], MLM[:, kc, ts], [wo_b, MLM_bs[kc]])
                                                     for kc in range(8)])
                    E.op("act", lambda h: h.activation(out=O[:, m, :], in_=bank(pbk), func=AF.Copy), reads=[PSB[pbk]], writes=[O_b])
                epilogue(O, O_b, 512, X2d, X2d_b, X3d, X3d_b, blk * 512, T, 1, 0, v,
                         (1, 1, lambda j: HF1[:, j, ts], HF1_bs[blk]), wk)
            A.release(mk_l)
            ffn(G, 1, HF1, HF1_bs, X3d, X3d_b, G["y"], Buf("y"), None, NB=(512 if smp else 1024))
            A.release_top(mk_top1)

        def ffn(G, l, HF, HF_bs, xsrc, xsrc_b, xdst, xdst_b, nxt, NB=1024):
            gname, nseq, L, v = G["name"], G["nseq"], G["L"], G["v"]
            T = nseq * L
            mk_f = A.mark()
            Gb, Gb_b = A.alloc("G", [128, 22, NB], BF16)
            wu, wu_bs = A.alloc("wup", [128, 2, 8, 2, 256], BF16, nbufs=2)
            wd, wd_bs = A.alloc("wdn", [128, 2, 22, 256], BF16, nbufs=2)
            cu, cu_bs = A.alloc("cu", [128, 2, NB], F32, nbufs=2)
            O, O_b = A.alloc("O", [128, 8, 512], F32)
            sq, sq_b = A.alloc("sq", [128, 8, 512], BF16)
            rstd, rstd_b = A.alloc("rstd", [128, 512], F32)
            tmp, tmp_b = A.alloc("tmp", [128, 512], F32)
            xb, xb_b = A.alloc("xb", [128, 8, 512], F32)
            wk = (sq, sq_b, rstd, rstd_b, tmp, tmp_b, xb, xb_b)
            upv = WB["ffn_up%d" % l][0].rearrange("p (kc n) -> p kc n", kc=8)
            dnv = WB["ffn_dn%d" % l][0].rearrange("p (kc n) -> p kc n", kc=22)
            upb, dnb = WB["ffn_up%d" % l][1], WB["ffn_dn%d" % l][1]
            for a in range(0, T, NB):
                n = NB
                seq_start = (a % L == 0)
                seq_end = ((a + n) % L == 0)
                aligned = (L <= n)
                lo = 1 if (seq_start or aligned) else 0
                hi = n + 1 if (seq_end or aligned) else n + 2
                pieces = []
                c0 = lo
                while c0 < hi:
                    c1 = min(hi, (c0 // 512 + 1) * 512)
                    pieces.append((c0, c1))
                    c0 = c1
                for jp in range(11):
                    slot = jp % 2
                    for half in range(2):
                        E.dma("sp", wu[:, slot, :, half, :], upv[:, :, half * DFF + jp * 256: half * DFF + (jp + 1) * 256],
                              reads=[upb], writes=[wu_bs[slot]])
                    for jj in range(2):
                        j = jp * 2 + jj
                        for half in range(2):
                            hb = 3 * half
                            hp = PS[:, hb * 512: hb * 512 + 1536]
                            hbufs = [PSB[hb], PSB[hb + 1], PSB[hb + 2]]
                            for (c0, c1) in pieces:
                                bk = hb + c0 // 512
                                mm_group(hp[:, c0:c1], [PSB[bk]],
                                         [(wu[:, slot, kc, half, jj * 128:(jj + 1) * 128], HF[:, kc, a - 1 + c0: a - 1 + c1],
                                           [wu_bs[slot]] + HF_bs) for kc in range(8)])
                            ch = half * 22 + j
                            cw = fcw[:, l, ch, :]
                            E.op("act", lambda h: h.activation(out=cu[:, half, :n], in_=hp[:, 1:n + 1], func=AF.Identity,
                                                               bias=cw[:, 3:4], scale=cw[:, 1:2]),
                                 reads=hbufs + [fcw_b], writes=[cu_bs[half]])
                            if aligned:
                                c3 = cu[:, half, :n].rearrange("p (s t) -> p s t", t=L)
                                h3 = hp[:, 1:n + 1].rearrange("p (s t) -> p s t", t=L)
                                E.op("dve", lambda h: h.scalar_tensor_tensor(
                                    out=c3[:, :, 1:L], in0=h3[:, :, 0:L - 1], scalar=cw[:, 0:1], in1=c3[:, :, 1:L],
                                    op0=ALU.mult, op1=ALU.add), reads=hbufs + [fcw_b, cu_bs[half]], writes=[cu_bs[half]])
                                E.op("dve", lambda h: h.scalar_tensor_tensor(
                                    out=c3[:, :, 0:L - 1], in0=h3[:, :, 1:L], scalar=cw[:, 2:3], in1=c3[:, :, 0:L - 1],
                                    op0=ALU.mult, op1=ALU.add), reads=hbufs + [fcw_b, cu_bs[half]], writes=[cu_bs[half]])
                            else:
                                i0 = 1 if seq_start else 0
                                i1 = n - 1 if seq_end else n
                                E.op("dve", lambda h: h.scalar_tensor_tensor(
                                    out=cu[:, half, i0:n], in0=hp[:, i0:n], scalar=cw[:, 0:1], in1=cu[:, half, i0:n],
                                    op0=ALU.mult, op1=ALU.add), reads=hbufs + [fcw_b, cu_bs[half]], writes=[cu_bs[half]])
                                E.op("dve", lambda h: h.scalar_tensor_tensor(
                                    out=cu[:, half, 0:i1], in0=hp[:, 2:i1 + 2], scalar=cw[:, 2:3], in1=cu[:, half, 0:i1],
                                    op0=ALU.mult, op1=ALU.add), reads=hbufs + [fcw_b, cu_bs[half]], writes=[cu_bs[half]])
                        E.op("act", lambda h: h.activation(out=cu[:, 0, :n], in_=cu[:, 0, :n], func=AF.Gelu),
                             reads=[cu_bs[0]], writes=[cu_bs[0]])
                        E.op("pool", lambda h: h.tensor_tensor(out=Gb[:, j, :n], in0=cu[:, 0, :n], in1=cu[:, 1, :n], op=ALU.mult),
                             reads=[cu_bs[0], cu_bs[1]], writes=[Gb_b])
                for piece in range(n // 512):
                    tsl = slice(piece * 512, (piece + 1) * 512)
                    for mp in range(4):
                        slot = mp % 2
                        E.dma("sp", wd[:, slot], dnv[:, :, mp * 256:(mp + 1) * 256], reads=[dnb], writes=[wd_bs[slot]])
                        for mm_ in range(2):
                            m = mp * 2 + mm_
                            pbk = m % 2
                            mm_group(bank(pbk), [PSB[pbk]], [(wd[:, slot, j, mm_ * 128:(mm_ + 1) * 128], Gb[:, j, tsl],
                                                              [wd_bs[slot], Gb_b]) for j in range(22)])
                            E.op("act", lambda h: h.activation(out=O[:, m, :], in_=bank(pbk), func=AF.Copy),
                                 reads=[PSB[pbk]], writes=[O_b])
                    epilogue(O, O_b, 512, xsrc, xsrc_b, xdst, xdst_b, a + piece * 512, T, l, 1, v,
                             None if nxt is None else nxt(a + piece * 512), wk)
            A.release(mk_f)

        GP = dict(name="p", nseq=NPSEQ, L=SEQ, v=0, sample=False, xin=xp_d, kout=kout_d, vout=vout_d, y=yp_d, states=True)
        gstage = [stage]
        if stage >= 20:
            stage = 99
        run_group(GP)
        stage = gstage[0] - 20 if 20 <= gstage[0] < 40 else stage
        if gstage[0] >= 20:
            GS = dict(name="s", nseq=1, L=DEC_SEQ, v=1, sample=True, xin=xs_d, kout=None, vout=None, y=ys_d, states=False)
            run_group(GS)
        E.finish()
    return P


def _fm(x2d):
    T, F = x2d.shape
    return np.ascontiguousarray(x2d.reshape(T, F // 128, 128).transpose(2, 1, 0).reshape(128, -1))


def _unfm(a, T):
    return np.ascontiguousarray(a.reshape(128, 8, T).transpose(2, 1, 0).reshape(T, 1024))


def _wl(w):
    K, N = w.shape
    return np.ascontiguousarray(w.reshape(K // 128, 128, N).transpose(1, 0, 2).reshape(128, -1))


_PROG_CACHE = {}
_CONST_CACHE = {}
DBG = None


def _consts():
    if _CONST_CACHE:
        return _CONST_CACHE
    import ml_dtypes
    c = {}
    c["ident_bf"] = np.eye(128, dtype=np.float32).astype(ml_dtypes.bfloat16)
    c["ones_bf"] = np.ones((128, 128), dtype=np.float32).astype(ml_dtypes.bfloat16)
    sw = np.zeros((128, 128), np.float32)
    for m in range(128):
        sw[(m + 64) % 128, m] = 1.0
    c["swap_bf"] = sw.astype(ml_dtypes.bfloat16)
    for L in (SEQ, DEC_SEQ):
        fwd, inv = dft_tables(L)
        nsc, nfc = L // 128, (2 * L + 128) // 128
        f3 = fwd.reshape(128, nsc, nfc, 128).transpose(2, 0, 1, 3).reshape(nfc, 128, nsc * 128)
        c["dft_fwd_%d" % L] = np.ascontiguousarray(f3)
        nt = min(L, 512)
        ntb = max(1, L // 512)
        i3 = inv.reshape(128, nfc, ntb, nt).transpose(2, 0, 1, 3).reshape(ntb, 128, nfc * nt)
        c["dft_inv_%d" % L] = np.ascontiguousarray(i3)
        zT, tcol = hyena_pos_tables(L)
        c["hy_zT_%d" % L] = zT
        c["hy_tcol_%d" % L] = tcol
    selm = np.zeros((40, 16, 128), np.float32)
    for ridx in range(16):
        selm[(ridx // 8) * 32 + ridx % 8, ridx, :] = 1.0
    c["sel40"] = selm.reshape(40, -1)
    sp = np.arange(128)[:, None]
    tp = np.arange(512)[None, :]
    mk = np.zeros((128, 2, 4, 512), np.float32)
    for o in range(4):
        mk[:, 0, o, :] = (sp + 128 * o <= tp)
        mk[:, 1, o, :] = (sp + 128 * o >= tp)
    c["masks"] = mk.reshape(128, -1).astype(ml_dtypes.bfloat16)
    c["ident_f32"] = np.eye(128, dtype=np.float32)
    cosT, sinT = rope_tables(DEC_SEQ)
    c["rope_cos"] = cosT
    c["rope_sin"] = sinT
    c["hy_delta"] = np.ascontiguousarray(np.broadcast_to(np.abs(np.linspace(HY_MIN_DECAY, HY_MAX_DECAY, 512, dtype=np.float32)).reshape(1, 512), (128, 512)))
    _CONST_CACHE.update(c)
    return _CONST_CACHE


def kernel(**inp):
    global DBG
    inp = {k: np.asarray(v) for k, v in inp.items()}
    stage = int(os.environ.get("KSTAGE", "99"))
    if stage not in _PROG_CACHE:
        _PROG_CACHE[stage] = build_program(stage)
    P = _PROG_CACHE[stage]
    TP = NPSEQ * SEQ
    B = inp["x_prompt"].shape[0]
    f32 = np.float32

    sh = dict(_consts())
    sh["w_mod"] = np.stack([_wl(inp["w_mod"][l]) for l in range(2)], axis=0)
    sh["b_mod"] = np.ascontiguousarray(inp["b_mod"].reshape(2, 48, 128).transpose(2, 0, 1).reshape(128, -1))
    norms = np.stack([inp["norm_mix_pre"], inp["norm_mix_post"], inp["norm_ffn_pre"], inp["norm_ffn_post"]], 0)
    sh["norms"] = np.ascontiguousarray(norms.reshape(4, 2, 8, 128).transpose(3, 0, 1, 2).reshape(128, -1))
    sh["ah_w_in"] = _wl(inp["ah_w_in"][0])
    sh["ah_w_out"] = _wl(inp["ah_w_out"][0])
    sh["qk_norm"] = np.ascontiguousarray(np.stack([inp["attn_q_norm"][0], inp["attn_k_norm"][0]], axis=1))
    hc = np.concatenate([inp["hy_conv_w"][0], inp["hy_conv_b"][0][None]], axis=0)
    sh["hy_conv"] = np.ascontiguousarray(hc.reshape(4, 12, 128).transpose(2, 1, 0).reshape(128, -1))
    sh["hy_w1"] = np.ascontiguousarray(inp["hy_w1"][0])
    sh["hy_w2"] = np.ascontiguousarray(inp["hy_w2"][0])
    sh["hy_w3"] = np.ascontiguousarray(inp["hy_w3"][0])
    sh["hy_b12f"] = np.ascontiguousarray(np.stack([inp["hy_b1"][0], inp["hy_b2"][0], inp["hy_sin_freq"][0, 0],
                                                   inp["hy_sin_freq"][0, 1]], axis=1))
    sh["hy_b3"] = np.ascontiguousarray(np.broadcast_to(inp["hy_b3"][0].reshape(1, 1024), (128, 1024)))
    sh["hy_skip"] = np.ascontiguousarray(np.broadcast_to(inp["hy_skip"][0].reshape(1, 512), (128, 512)))
    sh["ffn_w_up"] = np.stack([_wl(inp["ffn_w_up"][l]) for l in range(2)], axis=0)
    sh["ffn_w_down"] = np.stack([_wl(inp["ffn_w_down"][l]) for l in range(2)], axis=0)
    fc = np.concatenate([inp["ffn_conv_w"], inp["ffn_conv_b"][:, None, :]], axis=1)
    sh["ffn_conv"] = np.ascontiguousarray(fc.reshape(2, 4, 44, 128).transpose(3, 0, 2, 1).reshape(128, -1))

    mw = inp["ml_w_in"][0]
    sh["ml_w_in"] = _wl(np.ascontiguousarray(mw[:, :4096]))
    wgp = np.zeros((1024, 80), f32)
    bgp = np.zeros((40, 2), f32)
    bg = inp["ml_b_gates"][0]
    for kind in range(4):
        base = (kind // 2) * 40 + (kind % 2) * 32
        wgp[:, base:base + 8] = mw[:, 4096 + kind * 8: 4096 + kind * 8 + 8]
        bgp[(kind % 2) * 32:(kind % 2) * 32 + 8, kind // 2] = bg[kind * 8:kind * 8 + 8]
    sh["ml_w_g"] = _wl(wgp)
    sh["ml_b_g"] = bgp
    mc = np.concatenate([inp["ml_conv_w"][0], inp["ml_conv_b"][0][None]], axis=0)
    sh["ml_conv"] = np.ascontiguousarray(mc.reshape(4, 16, 128).transpose(2, 1, 0).reshape(128, -1))
    sh["ml_head_norm"] = np.ascontiguousarray(inp["ml_head_norm"][0].reshape(8, 128).T)
    sh["ml_w_out"] = _wl(inp["ml_w_out"][0])

    in_maps = []
    for r in range(NCORES):
        b = r // 4
        m = dict(sh)
        m["xp"] = _fm(inp["x_prompt"][NPSEQ * r:NPSEQ * (r + 1)].reshape(TP, D))
        m["xs"] = _fm(inp["x_sample"][b])
        cvec = np.stack([inp["c_ctx"], inp["c"][b]], axis=0)
        m["cvec"] = np.ascontiguousarray(cvec.reshape(2, 8, 128).transpose(2, 1, 0).reshape(128, -1))
        ck = inp["cache_attn_k"][b, 0]
        m["cache_kT"] = np.ascontiguousarray(ck.transpose(2, 1, 0).reshape(128, -1))
        cvv = inp["cache_attn_v"][b, 0]
        m["cache_vt"] = np.ascontiguousarray(cvv.reshape(4, 128, 256).transpose(1, 0, 2).reshape(128, -1))
        m0 = np.zeros((40, 1), f32)
        sm = inp["state_mlstm_m"][b, 0]
        m0[0:8, 0] = sm[0]
        m0[32:40, 0] = sm[1]
        m["ml_m0"] = m0
        sC = inp["state_mlstm_C"][b, 0].reshape(16, 128, 128)
        m["ml_c0t"] = np.ascontiguousarray(sC.transpose(2, 0, 1).reshape(128, -1))
        sn = inp["state_mlstm_n"][b, 0].reshape(16, 128)
        m["ml_n0b"] = np.ascontiguousarray(np.broadcast_to(sn.T[:, :, None], (128, 16, 128)).reshape(128, -1))
        in_maps.append({k: np.ascontiguousarray(m[k]) for k in P.ins})
    res = run_bass_kernel_spmd(P.nc, in_maps, core_ids=list(range(NCORES)))
    R = res.results
    DBG = R

    y_prompt = np.zeros((B, SEQ, D), f32)
    y_sample = np.zeros((2, DEC_SEQ, D), f32)
    new_k = np.zeros((B, 1, SEQ, 2, 128), f32)
    new_v = np.zeros((B, 1, SEQ, 2, 128), f32)
    new_C = np.zeros((B, 1, 2, 8, 128, 128), f32)
    new_n = np.zeros((B, 1, 2, 8, 128), f32)
    new_m = np.zeros((B, 1, 2, 8), f32)
    for r in range(NCORES):
        o = R[r]
        sl = slice(NPSEQ * r, NPSEQ * (r + 1))
        new_k[sl, 0] = o["k_out"].reshape(128, 2, NPSEQ, SEQ).transpose(2, 3, 1, 0)
        new_v[sl, 0] = o["v_out"].reshape(128, TP // 128, 2, 128).transpose(1, 0, 2, 3).reshape(NPSEQ, SEQ, 2, 128)
        y_prompt[sl] = _unfm(o["yp"], TP).reshape(NPSEQ, SEQ, D)
        new_C[sl, 0] = o["c_out"].reshape(128, NPSEQ, 2, 8, 128).transpose(1, 2, 3, 0, 4)
        new_n[sl, 0] = o["n_out"].reshape(NPSEQ, 2, 8, 128)
        mo = o["m_out"]
        new_m[sl, 0, 0] = mo[0:8].T
        new_m[sl, 0, 1] = mo[32:40].T
        if r % 4 == 0:
            y_sample[r // 4] = _unfm(o["ys"], DEC_SEQ)
    return (y_prompt, y_sample, new_k, new_v, new_C, new_n, new_m)
```
